# Optimizing a Trainium2 kernel written in Bass

```python
import math
import jax
import jax.numpy as jnp
from jax import lax
import numpy as np

D_MODEL = 1024
BATCH = 16
SEQ = 256
DEPTH = 4
DEC_BATCH = 2
DEC_SEQ = 4096
PAST_LEN = 512

GRID_W = 64
N_BRANCH = 3
HY_WIDTH = 256
HY_ORDER = 2
HY_SHORT_K = 3
HY_POS_EMB = 33
HY_POS_BANDS = (HY_POS_EMB - 1) // 2
HY_FILTER_HIDDEN = 64
HY_DECAY_FAST = -math.log(1e-2) / 0.3
HY_DECAY_SLOW = -math.log(1e-2) / 1.5
S5_WIDTH = 256
S5_GROUP = 16
S5_GROUPS = S5_WIDTH // S5_GROUP
S5_STATE = 64
N_HEADS = 8
N_KV_HEADS = 2
Q_PER_KV = N_HEADS // N_KV_HEADS
HEAD_DIM = 64
ATTN_WIDTH = N_HEADS * HEAD_DIM
KV_WIDTH = N_KV_HEADS * HEAD_DIM
WINDOW = 128
BLOCK = 128
ROPE_BASE = 10000.0
FFN_HIDDEN = ((8 * D_MODEL // 3 + 255) // 256) * 256
HY_IN = (HY_ORDER + 1) * HY_WIDTH
GATE_IN = N_BRANCH * D_MODEL
W_IN_COLS = HY_IN + S5_WIDTH + ATTN_WIDTH + 2 * KV_WIDTH + GATE_IN
SPLIT_S5 = HY_IN
SPLIT_Q = SPLIT_S5 + S5_WIDTH
SPLIT_K = SPLIT_Q + ATTN_WIDTH
SPLIT_V = SPLIT_K + KV_WIDTH
SPLIT_G = SPLIT_V + KV_WIDTH
IN_SPLITS = (SPLIT_S5, SPLIT_Q, SPLIT_K, SPLIT_V, SPLIT_G)

F32 = jnp.float32
EPS = 1e-6
NEG_INF = -1e30

kernel_name = 'hybrid_diffusion_trunk_step'


def _rmsnorm(x, g):
    xf = x.astype(F32)
    xf = xf * lax.rsqrt(jnp.mean(xf * xf, axis=-1, keepdims=True) + EPS)
    return (xf * g.astype(F32)).astype(x.dtype)


def _short_conv(u, w, b):
    L = u.shape[1]
    pad = HY_SHORT_K // 2
    up = jnp.pad(u, ((0, 0), (pad, pad), (0, 0)))
    w = w.astype(u.dtype)
    out = b.astype(u.dtype)
    for j in range(HY_SHORT_K):
        out = out + up[:, j:j + L] * w[j]
    return out


def _hyena_filters_fft(L, p):
    t = jnp.linspace(0.0, 1.0, L, dtype=F32)
    w = 2.0 * math.pi * jnp.arange(L, dtype=F32) / L
    bands = jnp.linspace(1e-4, HY_POS_BANDS - 1, HY_POS_BANDS, dtype=F32)
    ang = w[:, None] * bands[None, :]
    feat = jnp.concatenate([t[:, None], jnp.cos(ang), -jnp.sin(ang)], axis=-1)
    freq = p['hy_sin_freq'].astype(F32)
    h = jnp.sin(freq * (feat @ p['hy_pos_w1'].astype(F32) + p['hy_pos_b1'].astype(F32)))
    h = jnp.sin(freq * (h @ p['hy_pos_w2'].astype(F32) + p['hy_pos_b2'].astype(F32)))
    h = h @ p['hy_pos_w3'].astype(F32)
    h = h * jnp.exp(-t[:, None] * jnp.abs(p['hy_decay'].astype(F32)))
    h = h.reshape(L, HY_ORDER, 2, HY_WIDTH)
    h_fwd, h_bwd = h[:, :, 0], h[:, :, 1]
    k2 = jnp.concatenate([h_fwd, jnp.zeros((1, HY_ORDER, HY_WIDTH), F32), jnp.flip(h_bwd[1:], axis=0)], axis=0)
    k2 = k2 / jnp.sum(jnp.abs(k2), axis=0, keepdims=True)
    return jnp.fft.rfft(k2, axis=0)


def _fft_conv(u, kf):
    L = u.shape[1]
    uf = jnp.fft.rfft(u, n=2 * L, axis=1)
    return jnp.fft.irfft(uf * kf[None], n=2 * L, axis=1)[:, :L]


def _hyena(zh, p):
    L = zh.shape[1]
    u = _short_conv(zh, p['hy_conv_w'], p['hy_conv_b']).astype(F32)
    v, *gates = jnp.split(u, HY_ORDER + 1, axis=-1)
    kf = _hyena_filters_fft(L, p)
    skip = p['hy_skip'].astype(F32)
    z = v
    for o, gate in enumerate(gates):
        z = gate * (_fft_conv(z, kf[:, o]) + skip[o] * z)
    return z.astype(zh.dtype)


def _linear_combine(e1, e2):
    a1, b1 = e1
    a2, b2 = e2
    return a1 * a2, a2 * b1 + b2


def _s5(zb, p, h0):
    B, L, _ = zb.shape
    u = zb.astype(F32).reshape(B, L, S5_GROUPS, S5_GROUP)
    uc = u.astype(jnp.complex64)
    y = u * p['s5_skip'].astype(F32).reshape(S5_GROUPS, S5_GROUP)
    finals = []
    for d in range(2):
        lam = lax.complex(p['s5_lam_re'][d].astype(F32), p['s5_lam_im'][d].astype(F32))
        step = jnp.exp(p['s5_log_step'][d].astype(F32))[:, None]
        lam_bar = jnp.exp(lam * step)
        b_mat = lax.complex(p['s5_b_re'][d].astype(F32), p['s5_b_im'][d].astype(F32))
        b_bar = ((lam_bar - 1.0) / lam)[..., None] * b_mat
        bu = jnp.einsum('blgc,gpc->blgp', uc, b_bar)
        reverse = d == 1
        edge = L - 1 if reverse else 0
        if h0 is not None:
            bu = bu.at[:, edge].add(lam_bar * h0[:, d])
        a = jnp.broadcast_to(lam_bar, bu.shape)
        _, hs = lax.associative_scan(_linear_combine, (a, bu), reverse=reverse, axis=1)
        finals.append(hs[:, L - 1 - edge])
        c_mat = lax.complex(p['s5_c_re'][d].astype(F32), p['s5_c_im'][d].astype(F32))
        y = y + jnp.real(jnp.einsum('blgp,gcp->blgc', hs, c_mat))
    y = y.reshape(B, L, S5_WIDTH)
    a_lin, b_gate = jnp.split(y @ p['s5_glu_w'].astype(F32) + p['s5_glu_b'].astype(F32), 2, axis=-1)
    out = a_lin * jax.nn.sigmoid(b_gate)
    return out.astype(zb.dtype), jnp.stack(finals, axis=1)


def _axial_rope(x):
    L = x.shape[1]
    rows = L // GRID_W
    row = jnp.repeat(jnp.arange(rows, dtype=F32), GRID_W)
    col = jnp.tile(jnp.arange(GRID_W, dtype=F32), rows)
    n_freq = HEAD_DIM // 4
    inv = ROPE_BASE ** (-jnp.arange(n_freq, dtype=F32) / n_freq)

    def rot(xp, pos):
        ang = pos[:, None] * inv
        cos = jnp.cos(ang)[None, :, None, :]
        sin = jnp.sin(ang)[None, :, None, :]
        x1, x2 = jnp.split(xp.astype(F32), 2, axis=-1)
        return jnp.concatenate([x1 * cos - x2 * sin, x1 * sin + x2 * cos], axis=-1)

    half = HEAD_DIM // 2
    return jnp.concatenate([rot(x[..., :half], row), rot(x[..., half:], col)], axis=-1).astype(x.dtype)


def _attend(q, k, v, sink, mask):
    s = jnp.einsum('bqgrd,bkgd->bgrqk', q, k).astype(F32) * (HEAD_DIM ** -0.5)
    if mask is not None:
        s = jnp.where(mask, s, NEG_INF)
    sk = jnp.broadcast_to(sink.astype(F32)[None, :, :, None, None], s.shape[:-1] + (1,))
    pr = jax.nn.softmax(jnp.concatenate([s, sk], axis=-1), axis=-1)[..., :-1]
    return jnp.einsum('bgrqk,bkgd->bqgrd', pr.astype(v.dtype), v)


def _context_attention(q, k, v, sink):
    B, L = q.shape[:2]
    nb = L // BLOCK
    qb = q.reshape(B, nb, BLOCK, N_KV_HEADS, Q_PER_KV, HEAD_DIM).swapaxes(0, 1)
    out = lax.map(lambda qi: _attend(qi, k, v, sink, None), qb)
    return out.swapaxes(0, 1).reshape(B, L, ATTN_WIDTH)


def _latent_attention(q, k, v, kc, vc, sink):
    B, L = q.shape[:2]
    nb = L // BLOCK
    span = BLOCK + 2 * WINDOW
    lc = kc.shape[1]
    idx = jnp.arange(nb)[:, None] * BLOCK + jnp.arange(span)[None, :]
    kp = jnp.pad(k, ((0, 0), (WINDOW, WINDOW), (0, 0), (0, 0)))
    vp = jnp.pad(v, ((0, 0), (WINDOW, WINDOW), (0, 0), (0, 0)))
    kw = kp[:, idx].swapaxes(0, 1)
    vw = vp[:, idx].swapaxes(0, 1)
    qpos = jnp.arange(nb)[:, None] * BLOCK + jnp.arange(BLOCK)[None, :]
    kpos = idx - WINDOW
    win = (kpos[:, None, :] >= 0) & (kpos[:, None, :] < L) & (jnp.abs(qpos[:, :, None] - kpos[:, None, :]) <= WINDOW)
    mask = jnp.concatenate([jnp.ones((nb, BLOCK, lc), dtype=bool), win], axis=-1)
    qb = q.reshape(B, nb, BLOCK, N_KV_HEADS, Q_PER_KV, HEAD_DIM).swapaxes(0, 1)

    def one(args):
        qi, kwi, vwi, mi = args
        return _attend(qi, jnp.concatenate([kc, kwi], axis=1), jnp.concatenate([vc, vwi], axis=1), sink, mi)

    out = lax.map(one, (qb, kw, vw, mask))
    return out.swapaxes(0, 1).reshape(B, L, ATTN_WIDTH)


def _layer(x, mod, p, ctx_kv, s5_h0):
    B, L, _ = x.shape
    shift1, scale1, gate1, shift2, scale2, gate2 = jnp.split(mod, 6, axis=-1)
    h = _rmsnorm(x, p['norm1_g']) * (1 + scale1) + shift1
    z = h @ p['w_in']
    za, zb, zq, zk, zv, zg = jnp.split(z, IN_SPLITS, axis=-1)
    ya = _hyena(za, p)
    yb, s5_final = _s5(zb, p, s5_h0)
    q = zq.reshape(B, L, N_HEADS, HEAD_DIM)
    k = zk.reshape(B, L, N_KV_HEADS, HEAD_DIM)
    v = zv.reshape(B, L, N_KV_HEADS, HEAD_DIM)
    sink = p['attn_sink'].reshape(N_KV_HEADS, Q_PER_KV)
    if ctx_kv is None:
        yc = _context_attention(q.reshape(B, L, N_KV_HEADS, Q_PER_KV, HEAD_DIM), k, v, sink)
    else:
        q_rot = _axial_rope(q).reshape(B, L, N_KV_HEADS, Q_PER_KV, HEAD_DIM)
        yc = _latent_attention(q_rot, _axial_rope(k), v, ctx_kv[0], ctx_kv[1], sink)
    ga, gb, gc = jnp.split(jax.nn.sigmoid(zg), N_BRANCH, axis=-1)
    merged = ga * (ya @ p['proj_a']) + gb * (yb @ p['proj_b']) + gc * (yc @ p['proj_c'])
    x = x + gate1 * (merged @ p['w_out'])
    h = _rmsnorm(x, p['norm2_g']) * (1 + scale2) + shift2
    x = x + gate2 * ((jax.nn.silu(h @ p['ffn_w_gate']) * (h @ p['ffn_w_up'])) @ p['ffn_w_down'])
    return x, k, v, s5_final


def setup_inputs(seed: int = 0) -> dict:
    key = jax.random.key(seed)
    ks = iter(jax.random.split(key, 48))

    def nrm(shape, scale=1.0):
        return jax.random.normal(next(ks), shape, F32) * scale

    D = D_MODEL
    hy_decay = jnp.tile(jnp.linspace(HY_DECAY_FAST, HY_DECAY_SLOW, HY_WIDTH, dtype=F32), (DEPTH, 2 * HY_ORDER))
    lam_im = jnp.broadcast_to(math.pi * jnp.arange(S5_STATE, dtype=F32), (DEPTH, 2, S5_GROUPS, S5_STATE))
    return {
        'x_prompt': nrm((BATCH, SEQ, D)),
        'x_sample': nrm((DEC_BATCH, DEC_SEQ, D)),
        'cache_k': nrm((DEC_BATCH, DEPTH, PAST_LEN, N_KV_HEADS, HEAD_DIM)),
        'cache_v': nrm((DEC_BATCH, DEPTH, PAST_LEN, N_KV_HEADS, HEAD_DIM)),
        'state_s5_re': nrm((DEC_BATCH, DEPTH, 2, S5_GROUPS, S5_STATE), 0.1),
        'state_s5_im': nrm((DEC_BATCH, DEPTH, 2, S5_GROUPS, S5_STATE), 0.1),
        'c': nrm((DEC_BATCH, D)),
        'c_ctx': nrm((D,)),
        'ada_w': nrm((DEPTH, D, 6 * D), 0.5 * D ** -0.5),
        'ada_b': nrm((DEPTH, 6 * D), 0.01),
        'norm1_g': 1.0 + nrm((DEPTH, D), 0.05),
        'w_in': nrm((DEPTH, D, W_IN_COLS), D ** -0.5),
        'hy_conv_w': nrm((DEPTH, HY_SHORT_K, HY_IN), HY_SHORT_K ** -0.5),
        'hy_conv_b': nrm((DEPTH, HY_IN), 0.01),
        'hy_pos_w1': nrm((DEPTH, HY_POS_EMB, HY_FILTER_HIDDEN), HY_POS_EMB ** -0.5),
        'hy_pos_b1': nrm((DEPTH, HY_FILTER_HIDDEN), 0.1),
        'hy_pos_w2': nrm((DEPTH, HY_FILTER_HIDDEN, HY_FILTER_HIDDEN), HY_FILTER_HIDDEN ** -0.5),
        'hy_pos_b2': nrm((DEPTH, HY_FILTER_HIDDEN), 0.1),
        'hy_pos_w3': nrm((DEPTH, HY_FILTER_HIDDEN, 2 * HY_ORDER * HY_WIDTH), HY_FILTER_HIDDEN ** -0.5),
        'hy_sin_freq': 1.0 + nrm((DEPTH, HY_FILTER_HIDDEN), 0.1),
        'hy_decay': hy_decay + nrm((DEPTH, 2 * HY_ORDER * HY_WIDTH), 0.1),
        'hy_skip': nrm((DEPTH, HY_ORDER, HY_WIDTH)),
        's5_lam_re': -0.5 + nrm((DEPTH, 2, S5_GROUPS, S5_STATE), 0.01),
        's5_lam_im': lam_im + nrm((DEPTH, 2, S5_GROUPS, S5_STATE), 0.01),
        's5_log_step': jax.random.uniform(next(ks), (DEPTH, 2, S5_GROUPS), F32, math.log(1e-3), math.log(1e-1)),
        's5_b_re': nrm((DEPTH, 2, S5_GROUPS, S5_STATE, S5_GROUP), (2 * S5_GROUP) ** -0.5),
        's5_b_im': nrm((DEPTH, 2, S5_GROUPS, S5_STATE, S5_GROUP), (2 * S5_GROUP) ** -0.5),
        's5_c_re': nrm((DEPTH, 2, S5_GROUPS, S5_GROUP, S5_STATE), S5_STATE ** -0.5),
        's5_c_im': nrm((DEPTH, 2, S5_GROUPS, S5_GROUP, S5_STATE), S5_STATE ** -0.5),
        's5_skip': nrm((DEPTH, S5_WIDTH)),
        's5_glu_w': nrm((DEPTH, S5_WIDTH, 2 * S5_WIDTH), S5_WIDTH ** -0.5),
        's5_glu_b': nrm((DEPTH, 2 * S5_WIDTH), 0.01),
        'attn_sink': nrm((DEPTH, N_HEADS), 0.5),
        'proj_a': nrm((DEPTH, HY_WIDTH, D), HY_WIDTH ** -0.5),
        'proj_b': nrm((DEPTH, S5_WIDTH, D), S5_WIDTH ** -0.5),
        'proj_c': nrm((DEPTH, ATTN_WIDTH, D), ATTN_WIDTH ** -0.5),
        'w_out': nrm((DEPTH, D, D), D ** -0.5),
        'norm2_g': 1.0 + nrm((DEPTH, D), 0.05),
        'ffn_w_gate': nrm((DEPTH, D, FFN_HIDDEN), D ** -0.5),
        'ffn_w_up': nrm((DEPTH, D, FFN_HIDDEN), D ** -0.5),
        'ffn_w_down': nrm((DEPTH, FFN_HIDDEN, D), FFN_HIDDEN ** -0.5),
        'final_norm_g': 1.0 + nrm((D,), 0.05),
    }


def reference(x_prompt, x_sample, cache_k, cache_v, state_s5_re, state_s5_im, c, c_ctx,
              ada_w, ada_b, norm1_g, w_in, hy_conv_w, hy_conv_b, hy_pos_w1, hy_pos_b1,
              hy_pos_w2, hy_pos_b2, hy_pos_w3, hy_sin_freq, hy_decay, hy_skip,
              s5_lam_re, s5_lam_im, s5_log_step, s5_b_re, s5_b_im, s5_c_re, s5_c_im,
              s5_skip, s5_glu_w, s5_glu_b, attn_sink, proj_a, proj_b, proj_c, w_out,
              norm2_g, ffn_w_gate, ffn_w_up, ffn_w_down, final_norm_g):
    y_p = x_prompt
    y_s = x_sample
    ks_out, vs_out, sre_out, sim_out = [], [], [], []
    for l in range(DEPTH):
        p = {
            'norm1_g': norm1_g[l], 'w_in': w_in[l],
            'hy_conv_w': hy_conv_w[l], 'hy_conv_b': hy_conv_b[l],
            'hy_pos_w1': hy_pos_w1[l], 'hy_pos_b1': hy_pos_b1[l],
            'hy_pos_w2': hy_pos_w2[l], 'hy_pos_b2': hy_pos_b2[l], 'hy_pos_w3': hy_pos_w3[l],
            'hy_sin_freq': hy_sin_freq[l], 'hy_decay': hy_decay[l], 'hy_skip': hy_skip[l],
            's5_lam_re': s5_lam_re[l], 's5_lam_im': s5_lam_im[l], 's5_log_step': s5_log_step[l],
            's5_b_re': s5_b_re[l], 's5_b_im': s5_b_im[l], 's5_c_re': s5_c_re[l], 's5_c_im': s5_c_im[l],
            's5_skip': s5_skip[l], 's5_glu_w': s5_glu_w[l], 's5_glu_b': s5_glu_b[l],
            'attn_sink': attn_sink[l], 'proj_a': proj_a[l], 'proj_b': proj_b[l], 'proj_c': proj_c[l],
            'w_out': w_out[l], 'norm2_g': norm2_g[l],
            'ffn_w_gate': ffn_w_gate[l], 'ffn_w_up': ffn_w_up[l], 'ffn_w_down': ffn_w_down[l],
        }
        mod_ctx = (jax.nn.silu(c_ctx) @ ada_w[l] + ada_b[l]).reshape(1, 1, 6 * D_MODEL)
        y_p, k_l, v_l, s5_l = _layer(y_p, mod_ctx, p, None, None)
        ks_out.append(k_l)
        vs_out.append(v_l)
        sre_out.append(jnp.real(s5_l).astype(x_prompt.dtype))
        sim_out.append(jnp.imag(s5_l).astype(x_prompt.dtype))
        mod_lat = (jax.nn.silu(c) @ ada_w[l] + ada_b[l])[:, None, :]
        h0 = lax.complex(state_s5_re[:, l].astype(F32), state_s5_im[:, l].astype(F32))
        y_s, _, _, _ = _layer(y_s, mod_lat, p, (cache_k[:, l], cache_v[:, l]), h0)
    y_prompt = _rmsnorm(y_p, final_norm_g)
    y_sample = _rmsnorm(y_s, final_norm_g)
    new_cache_k = jnp.stack(ks_out, axis=1)
    new_cache_v = jnp.stack(vs_out, axis=1)
    new_state_s5_re = jnp.stack(sre_out, axis=1)
    new_state_s5_im = jnp.stack(sim_out, axis=1)
    return (y_prompt, y_sample, new_cache_k, new_cache_v, new_state_s5_re, new_state_s5_im)
```

```python
from contextlib import ExitStack
import math
import numpy as np
import concourse.bass as bass
import concourse.mybir as mybir
from concourse.bass_utils import run_bass_kernel_spmd

F32 = mybir.dt.float32
BF16 = mybir.dt.bfloat16
I32 = mybir.dt.int32
AF = mybir.ActivationFunctionType
ALU = mybir.AluOpType
AX = mybir.AxisListType

D = 1024
DEPTH = 4
NP = 512
LS = 4096
NT = NP + LS
TT = 512
NTILE = NT // TT
FF = 2816
NJ = FF // 128
WIN = 4864
WEXT = WIN + 640
EPS = 1e-6


class _Eng:
    def __init__(self, name, h, sem):
        self.name, self.h, self.sem = name, h, sem
        self.count = 0
        self.seen = {}


class _Buf:
    __slots__ = ("w", "r")

    def __init__(self):
        self.w = None
        self.r = []


class FW:
    NDMA = 48

    def __init__(self, nc, es):
        self.nc = nc
        mk = lambda n: es.enter_context(nc.semaphore(n))
        self.sync = _Eng("sync", nc.sync, mk("e_sync"))
        self.act = _Eng("act", nc.scalar, mk("e_act"))
        self.dve = _Eng("dve", nc.vector, mk("e_dve"))
        self.pool = _Eng("pool", nc.gpsimd, mk("e_pool"))
        self.pe = _Eng("pe", nc.tensor, mk("e_pe"))
        self.engs = [self.sync, self.act, self.dve, self.pool, self.pe]
        self.dsems = [mk(f"d{i}") for i in range(self.NDMA)]
        self.dcount = [0] * self.NDMA
        self.dnext = 0
        self.NSW = 16
        self.dnext_sw = 0
        self.bufs = {}
        self.out_events = []
        self.n_inst = 0

    def _rec(self, key):
        b = self.bufs.get(key)
        if b is None:
            b = self.bufs[key] = _Buf()
        return b

    def _deps(self, reads, writes):
        deps = []
        for k in reads:
            b = self._rec(k)
            if b.w is not None:
                deps.append(b.w)
        for k in writes:
            b = self._rec(k)
            if b.w is not None:
                deps.append(b.w)
            deps.extend(b.r)
        return deps

    def _wait(self, eng, deps):
        need = {}
        for (sem, val) in deps:
            i = id(sem)
            if need.get(i, (None, 0))[1] < val:
                need[i] = (sem, val)
        for i, (sem, val) in need.items():
            if eng.seen.get(i, 0) >= val:
                continue
            eng.h.wait_ge(sem, val)
            eng.seen[i] = val

    def _commit(self, ev, reads, writes):
        for k in reads:
            b = self._rec(k)
            for i_, (s_, v_) in enumerate(b.r):
                if s_ is ev[0]:
                    if v_ < ev[1]:
                        b.r[i_] = ev
                    break
            else:
                b.r.append(ev)
        for k in writes:
            b = self._rec(k)
            b.w = ev
            b.r = []

    def op(self, eng, fn, reads=(), writes=(), inc=True):
        pr = [k for k in reads if isinstance(k, tuple) and k[0] == "pb"]
        if pr:
            reads = [k for k in reads if k not in pr]
            writes = list(writes) + pr
        deps = self._deps(reads, writes)
        if eng is self.pe:
            deps = [d for d in deps if d[0] is not eng.sem]
        self._wait(eng, deps)
        ins = fn(eng.h)
        if inc:
            eng.count += 1
            ins.then_inc(eng.sem, 1)
            ev = (eng.sem, eng.count)
        else:
            ev = (eng.sem, eng.count + 1)
        self._commit(ev, reads, writes)
        self.n_inst += 1
        return ins

    def dma(self, out, in_, reads=(), writes=(), q=None, is_output=False, **kw):
        q = q or self.sync
        if q is self.pool:
            i = self.NDMA - self.NSW + self.dnext_sw
            self.dnext_sw = (self.dnext_sw + 1) % self.NSW
        else:
            i = self.dnext
            self.dnext = (self.dnext + 1) % (self.NDMA - self.NSW)
        sem = self.dsems[i]
        deps = self._deps(reads, writes)
        if self.dcount[i] > 0:
            deps.append((sem, self.dcount[i]))
        self._wait(q, deps)
        ins = q.h.dma_start(out=out, in_=in_, **kw)
        self.dcount[i] += 16
        ins.then_inc(sem, 16)
        ev = (sem, self.dcount[i])
        self._commit(ev, reads, writes)
        if is_output:
            self.out_events.append(ev)
        self.n_inst += 1
        return ins

    def barrier(self):
        evs = [(e.sem, e.count) for e in self.engs if e.count > 0]
        evs += [(self.dsems[i], self.dcount[i]) for i in range(self.NDMA) if self.dcount[i] > 0]
        for e in self.engs:
            self._wait(e, evs)
        self.bufs = {}

    def finish(self):
        evs = list(self.out_events)
        evs += [(self.dsems[i], self.dcount[i]) for i in range(self.NDMA) if self.dcount[i] > 0]
        evs += [(e.sem, e.count) for e in self.engs if e.count > 0 and e is not self.sync]
        self._wait(self.sync, evs)


def _perm_cols():
    def partner(d):
        return d + 16 if (d % 32) < 16 else d - 16
    SPLIT_Q, SPLIT_K = 1024, 1536
    pq = [SPLIT_Q + h * 64 + partner(d) for h in range(8) for d in range(64)]
    pk = [SPLIT_K + g * 64 + partner(d) for g in range(2) for d in range(64)]
    return np.array(pq + pk, dtype=np.int64)


def _rope_tables():
    t = np.arange(LS)
    row = (t // 64).astype(np.float64)
    col = (t % 64).astype(np.float64)
    inv = 10000.0 ** (-np.arange(16, dtype=np.float64) / 16)
    cos = np.zeros((64, LS)); sin = np.zeros((64, LS))
    for d in range(64):
        pos = row if d < 32 else col
        ang = pos * inv[d % 16]
        cos[d] = np.cos(ang)
        sin[d] = np.sin(ang) * (-1.0 if (d % 32) < 16 else 1.0)
    cos2 = np.concatenate([cos, cos], 0).astype(np.float32)
    sin2 = np.concatenate([sin, sin], 0).astype(np.float32)
    return cos2, sin2


def _hy_tables(L):
    NH, KA, N2 = L // 128, L // 64, 2 * L
    n_hi = np.arange(128)[:, None, None]; n_lo = np.arange(128)[None, :, None]
    FA = np.zeros((128, 128, 128))
    ka = np.arange(KA)[None, None, :]
    th = 2 * np.pi * (n_lo + 128 * n_hi) * (ka + 0.5) / N2
    valid = (n_hi < NH)
    FA[:, :, :KA] = np.cos(th) * valid
    FA[:, :, 64:64 + KA] = -np.sin(th) * valid
    FAi = np.zeros((128, 128, NH))
    ka2 = np.arange(KA)[:, None, None]; nl = np.arange(128)[None, :, None]; nh = np.arange(NH)[None, None, :]
    th2 = 2 * np.pi * (nl + 128 * nh) * (ka2 + 0.5) / N2
    FAi[:KA] = np.cos(th2) / L
    FAi[64:64 + KA] = -np.sin(th2) / L
    return FA, FAi


def _hy_gb():
    nb = np.arange(128)[:, None]; kb = np.arange(64)[None, :]
    C = np.cos(2 * np.pi * nb * kb / 128); S = np.sin(2 * np.pi * nb * kb / 128)
    cat = lambda a, b: np.concatenate([a, b], 1)
    lst = [cat(C, -S), cat(S, C), cat(-S, C), cat(C, S),
           cat(C, C), cat(S, S),
           cat(S, -S), cat(-C, C), cat(-S, S), cat(C, -C),
           np.concatenate([C.T, -S.T], 0), np.concatenate([S.T, C.T], 0)]
    return np.stack(lst, 1)


def _hy_feat(L):
    t = np.linspace(0.0, 1.0, L)
    w = 2.0 * np.pi * np.arange(L) / L
    bands = np.linspace(1e-4, 15, 16)
    ang = w[:, None] * bands[None, :]
    feat = np.concatenate([t[:, None], np.cos(ang), -np.sin(ang)], -1)
    fT = np.zeros((128, L)); fT[:33] = feat.T
    negt = -t.reshape(L // 128, 128).T
    return fT, negt


class Prog:
    def __init__(self, depth=DEPTH, mixers=("attn", "hyena", "s5"), taps=(), stop=None):
        self.stop = stop
        self.depth = depth
        self.mixers = mixers
        self.taps = taps
        self.nc = bass.Bass("TRN2", target_bir_lowering=False)
        self.din = {}
        self.dout = {}
        self.build()

    def uniq(self, n):
        self._uid = getattr(self, "_uid", 0) + 1
        return f"{n}_{self._uid}"

    def inp(self, name, shape, dt=F32):
        t = self.nc.dram_tensor(name, list(shape), dt, kind="ExternalInput").ap()
        self.din[name] = (tuple(shape), dt)
        return t

    def outp(self, name, shape, dt=F32):
        t = self.nc.dram_tensor(name, list(shape), dt, kind="ExternalOutput").ap()
        self.dout[name] = (tuple(shape), dt)
        return t

    def scratch(self, name, shape, dt=F32):
        return self.nc.dram_tensor(name, list(shape), dt, kind="Internal").ap()

    def build(self):
        nc = self.nc
        L = self.depth
        self.xT_in = self.inp("xT_in", [D, NT])
        self.cT = self.inp("cT", [128, 8, 2])
        self.ada_w = self.inp("ada_w", [L, D, 6 * D])
        self.ada_b = self.inp("ada_b", [L, 128, 48])
        self.norm1_g = self.inp("norm1_g", [L, 128, 8])
        self.norm2_g = self.inp("norm2_g", [L, 128, 8])
        self.final_g = self.inp("final_g", [128, 8])
        self.w_in = self.inp("w_in", [L, D, WEXT])
        self.proj_a = self.inp("proj_a", [L, 256, D])
        self.proj_b = self.inp("proj_b", [L, 256, D])
        self.proj_c = self.inp("proj_c", [L, 512, D])
        self.w_out = self.inp("w_out", [L, D, D])
        self.w_gate = self.inp("w_gate", [L, D, FF])
        self.w_up = self.inp("w_up", [L, D, FF])
        self.w_down = self.inp("w_down", [L, FF, D])
        self.rope_cos = self.inp("rope_cos", [128, LS])
        self.rope_sin = self.inp("rope_sin", [128, LS])
        self.hy_FA = {256: self.inp("hy_FA256", [128, 128, 128], BF16), 4096: self.inp("hy_FA4096", [128, 128, 128], BF16)}
        self.hy_FAi = {256: self.inp("hy_FAi256", [128, 128, 2], BF16), 4096: self.inp("hy_FAi4096", [128, 128, 32], BF16)}
        self.hy_GB = self.inp("hy_GB", [128, 12, 128], BF16)
        self.hy_featT = {256: self.inp("hy_featT256", [128, 256]), 4096: self.inp("hy_featT4096", [128, 4096])}
        self.hy_negt = {256: self.inp("hy_negt256", [128, 2]), 4096: self.inp("hy_negt4096", [128, 32])}
        self.hy_w1 = self.inp("hy_w1", [L, 128, 128])
        self.hy_w2 = self.inp("hy_w2", [L, 128, 128])
        self.hy_w3 = self.inp("hy_w3", [L, 128, 1024])
        self.hy_vec = self.inp("hy_vec", [L, 128, 3])
        self.hy_decay = self.inp("hy_decay", [L, 1024])
        self.hy_cw = self.inp("hy_cw", [L, 128, 6, 4])
        self.hy_skip = self.inp("hy_skip", [L, 128, 2, 2])
        self.s5_sc = self.inp("s5_sc", [L, 128, 3, 16])
        self.s5_B = self.inp("s5_B", [L, 2, 8, 2, 128, 128])
        self.s5_C = self.inp("s5_C", [L, 2, 8, 2, 128, 128])
        self.s5_h0 = self.inp("s5_h0", [L, 128, 2, 16])
        self.s5_skip = self.inp("s5_skip", [L, 128, 2])
        self.s5_gw = self.inp("s5_gw", [L, 256, 512])
        self.s5_gb = self.inp("s5_gb", [L, 128, 4])
        self.s5_idx = self.inp("s5_idx", [128, 513])
        self.cache_k = self.inp("cache_k", [L, 512, 128])
        self.cache_v = self.inp("cache_v", [L, 512, 128])
        self.sink = self.inp("sink", [L, 8])
        self.mask_prev = self.inp("mask_prev", [128, 128])
        self.mask_next = self.inp("mask_next", [128, 128])
        self.yT = self.outp("yT", [D, NT])
        self.ok = self.outp("ok", [L, NP, 128])
        self.ov = self.outp("ov", [L, NP, 128])
        self.osf = self.outp("osf", [L, 2, 2, 2, 16, 64])
        self.xT = self.scratch("xT", [D, NT])
        self.zaT = self.scratch("zaT", [768, NT])
        self.zbT = self.scratch("zbT", [256, NT], BF16)
        self.gT = self.scratch("gT", [3072, NT], BF16)
        self.qA = self.scratch("qA", [8, 65, NT], BF16)
        self.kA = self.scratch("kA", [2, 65, NT], BF16)
        self.vtok = self.scratch("vtok", [NT, 128], BF16)
        self.nk = self.scratch("nk", [NT, 2])
        self.yaT = self.scratch("yaT", [256, NT], BF16)
        self.ybT = self.scratch("ybT", [256, NT], BF16)
        self.ycT = self.scratch("ycT", [512, NT], BF16)
        self.aT = self.scratch("aT", [FF, NT], BF16)
        self.ugT = self.scratch("ugT", [512, NT])
        self.hzT = self.scratch("hzT", [256, NT])
        self.hyT = self.scratch("hyT", [256, NT])
        self.ztok = self.scratch("ztok", [NT, 256], BF16)
        self.hraw = {256: self.scratch("hraw256", [256, 1024]), 4096: self.scratch("hraw4096", [4096, 1024])}
        self.ftok = {256: self.scratch("ftok256", [256, 1024], BF16), 4096: self.scratch("ftok4096", [4096, 1024], BF16)}
        self.Ascr = self.scratch("Ascr", [128, 128, 512], BF16)
        self.Bscr = self.scratch("Bscr", [128, 2, 64, 256], BF16)
        self.Ksp = {256: self.scratch("Ksp256", [2, 2, 128, 4, 256]), 4096: self.scratch("Ksp4096", [2, 2, 128, 64, 256])}
        self.tapo = {}
        for name, shape in self.taps:
            self.tapo[name] = self.outp("tap_" + name, shape)

        with ExitStack() as es:
            self.fw = fw = FW(nc, es)
            self.pb = [es.enter_context(nc.psum_tensor(f"pb{i}", [128, 512], F32)) for i in range(8)]
            sb = lambda n, s, d=F32: es.enter_context(nc.sbuf_tensor(self.uniq(n), s, d))
            self.ones32 = sb("ones32", [128, 128])
            self.ident = sb("ident", [128, 128])
            self.blk2 = sb("blk2", [128, 128])
            self.sc = sb("silu_c", [128, 8, 2])
            self.modT = sb("modT", [128, 48, 2])
            self.a1 = sb("a1", [128, 8, 2])
            self.a2 = sb("a2", [128, 8, 2])
            self.g1 = sb("g1", [128, 8])
            self.g2 = sb("g2", [128, 8])
            self.adab = sb("adab", [128, 48])
            self.epsc = sb("epsc", [128, 1])
            self.init_consts()
            for k in range(8):
                fw.dma(self.xT[k * 128:(k + 1) * 128, :], self.xT_in[k * 128:(k + 1) * 128, :], writes=[("xT", k)])
            fw.barrier()
            phases = [("mod", self.phase_mod), ("A", self.phase_A), ("B", self.phase_B), ("C1", self.phase_C1),
                      ("C2", self.phase_C2), ("C3", self.phase_C3)]
            done = False
            for l in range(L):
                for nm, ph in phases:
                    ph(l)
                    fw.barrier()
                    if self.stop == nm:
                        done = True
                        break
                if done:
                    break
            self.phase_final()
            fw.finish()

    def init_consts(self):
        nc, fw = self.nc, self.fw
        fw.op(fw.pool, lambda g: g.memset(self.ones32[:], 1.0), writes=["ones32"])
        fw.op(fw.pool, lambda g: g.memset(self.ident[:], 0.0), writes=["ident"])
        fw.op(fw.pool, lambda g: g.affine_select(out=self.ident[:], in_=self.ident[:], pattern=[[-1, 128]],
                                                 compare_op=ALU.not_equal, fill=1.0, base=0, channel_multiplier=1),
              reads=["ident"], writes=["ident"])
        fw.op(fw.pool, lambda g: g.memset(self.blk2[:], 0.0), writes=["blk2"])
        fw.op(fw.pool, lambda g: g.memset(self.blk2[0:64, 0:1], 1.0), reads=["blk2"], writes=["blk2"])
        fw.op(fw.pool, lambda g: g.memset(self.blk2[64:128, 1:2], 1.0), reads=["blk2"], writes=["blk2"])
        fw.op(fw.pool, lambda g: g.memset(self.epsc[:], EPS), writes=["epsc"])
        fw.dma(self.sc[:], self.cT[:, :, :], writes=["sc"])
        fw.op(fw.act, lambda a: a.activation(out=self.sc[:], in_=self.sc[:], func=AF.Silu), reads=["sc"], writes=["sc"])
        es2 = ExitStack()
        with es2:
            onesb = es2.enter_context(nc.sbuf_tensor(self.uniq("onesb"), [1, NT], BF16))
            fw.op(fw.pool, lambda g: g.memset(onesb[:], 1.0), writes=["onesb"])
            for g_ in range(2):
                fw.dma(self.kA[g_, 64:65, :], onesb[:], reads=["onesb"], writes=[("kA", g_)])
            fw.barrier()

    def phase_mod(self, l):
        nc, fw = self.nc, self.fw
        with ExitStack() as es:
            wblk = [es.enter_context(nc.sbuf_tensor(self.uniq(f"adaw{i}"), [128, 8, 1024], F32)) for i in range(2)]
            fw.dma(self.adab[:], self.ada_b[l], writes=["adab"])
            fw.dma(self.g1[:], self.norm1_g[l], writes=["g1"])
            fw.dma(self.g2[:], self.norm2_g[l], writes=["g2"])
            for blk in range(6):
                wb = wblk[blk % 2]
                key = ("adaw", blk % 2)
                for k in range(8):
                    fw.dma(wb[:, k, :], self.ada_w[l, k * 128:(k + 1) * 128, blk * 1024:(blk + 1) * 1024], writes=[key])
                for mm in range(8):
                    m = blk * 8 + mm
                    ps = self.pb[m % 4]
                    pk = ("pb", m % 4)
                    for k in range(8):
                        fw.op(fw.pe, lambda t, k=k, mm=mm, ps=ps, wb=wb: t.matmul(ps[:, 0:2], lhsT=wb[:, k, mm * 128:(mm + 1) * 128], rhs=self.sc[:, k, :],
                                                                                 start=(k == 0), stop=(k == 7)),
                              reads=[key, "sc"], writes=[pk], inc=(k == 7))
                    fw.op(fw.act, lambda a, m=m, ps=ps: a.activation(out=self.modT[:, m, :], in_=ps[:, 0:2], func=AF.Identity,
                                                                     bias=self.adab[:, m:m + 1], scale=1.0),
                          reads=[pk, "adab"], writes=["modT"])
            for (a_t, g_t, base, nm) in ((self.a1, self.g1, 8, "a1"), (self.a2, self.g2, 32, "a2")):
                for j in range(2):
                    fw.op(fw.dve, lambda v, a_t=a_t, g_t=g_t, base=base, j=j: v.scalar_tensor_tensor(
                        out=a_t[:, :, j], in0=self.modT[:, base:base + 8, j], scalar=1.0, in1=g_t[:, :], op0=ALU.add, op1=ALU.mult),
                        reads=["modT", "g1", "g2"], writes=[nm])

    def norm_mod(self, es, xt, a_t, shift_base, j, hT, tagsfx):
        nc, fw = self.nc, self.fw
        xsq = self._xsq
        rin = self._rin
        ss = self.pb[7]
        fw.op(fw.act, lambda a: a.activation(out=xsq[:], in_=xt[:], func=AF.Square), reads=["xt"], writes=["xsq"])
        for k in range(8):
            fw.op(fw.pe, lambda t, k=k: t.matmul(ss[:], lhsT=self.ones32[:], rhs=xsq[:, k, :], start=(k == 0), stop=(k == 7)),
                  reads=["xsq", "ones32"], writes=[("pb", 7)], inc=(k == 7))
        fw.op(fw.act, lambda a: a.activation(out=rin[:], in_=ss[:], func=AF.Sqrt, scale=1.0 / D, bias=self.epsc[:, 0:1]),
              reads=[("pb", 7), "epsc"], writes=["rin"])
        fw.op(fw.dve, lambda v: v.reciprocal(out=rin[:], in_=rin[:]), reads=["rin"], writes=["rin"])
        fw.op(fw.dve, lambda v: v.tensor_tensor(out=xsq[:], in0=xt[:], in1=rin[:].unsqueeze(1).to_broadcast([128, 8, TT]), op=ALU.mult),
              reads=["xt", "rin", "xsq"], writes=["xsq"])
        for k in range(8):
            if shift_base is None:
                fw.op(fw.act, lambda a, k=k: a.activation(out=hT[:, k, :], in_=xsq[:, k, :], func=AF.Identity, scale=a_t[:, k:k + 1]),
                      reads=["xsq", "gfin"], writes=["hT"])
            else:
                fw.op(fw.act, lambda a, k=k: a.activation(out=hT[:, k, :], in_=xsq[:, k, :], func=AF.Identity,
                                                          scale=a_t[:, k, j:j + 1], bias=self.modT[:, shift_base + k, j:j + 1]),
                      reads=["xsq", "a1", "a2", "modT"], writes=["hT"])

    def load_w_bf16(self, dst, src, key, nchunk):
        fw = self.fw
        K = dst.shape[1]
        N = dst.shape[2]
        cw = N // nchunk
        assert cw <= 2048 and cw * nchunk == N
        for k in range(K):
            for c in range(nchunk):
                fw.dma(dst[:, k, c * cw:(c + 1) * cw], src[k * 128:(k + 1) * 128, c * cw:(c + 1) * cw], writes=[key], q=fw.pool)

    def phase_A(self, l):
        nc, fw = self.nc, self.fw
        with ExitStack() as es:
            sb = lambda n, s, d=F32: es.enter_context(nc.sbuf_tensor(self.uniq(n), s, d))
            WinB = sb("WinB", [128, 8, WEXT], BF16)
            xt = sb("xtA", [128, 8, TT])
            self._xsq = sb("xsqA", [128, 8, TT])
            self._rin = sb("rinA", [128, TT])
            hT = sb("hTA", [128, 8, TT], BF16)
            cosT = sb("cosT", [128, TT])
            sinT = sb("sinT", [128, TT])
            NST = 4
            st32 = [sb(f"st32_{i}", [128, TT]) for i in range(NST)]
            st16 = [sb(f"st16_{i}", [128, TT], BF16) for i in range(NST)]
            tmp32 = [sb(f"tmp32_{i}", [128, TT]) for i in range(2)]
            nrm = [sb(f"nrm_{i}", [2, TT], BF16) for i in range(2)]
            kv32 = [sb(f"kv32_{i}", [128, 256]) for i in range(2)]
            v16 = [sb(f"v16_{i}", [128, 128], BF16) for i in range(2)]
            nkt = [sb(f"nkt_{i}", [128, 2]) for i in range(2)]
            junk = sb("junkA", [128, 64])
            self.load_w_bf16(WinB, self.w_in[l], "WinB", 4)
            cnt = {"s32": 0, "s16": 0, "ps": 0, "t32": 0, "nrm": 0, "kv": 0}

            def next_ps():
                i = cnt["ps"] % 6
                cnt["ps"] += 1
                return self.pb[i], ("pb", i)

            def mm_tile(m, ps, pk):
                for k in range(8):
                    fw.op(fw.pe, lambda t, k=k: t.matmul(ps[:], lhsT=WinB[:, k, m * 128:(m + 1) * 128], rhs=hT[:, k, :], start=(k == 0), stop=(k == 7)),
                          reads=["WinB", "hT"], writes=[pk], inc=(k == 7))

            import os
            for tt in range(int(os.environ.get('DBGT', NTILE))):
                c0 = tt * TT
                j = 0 if tt == 0 else 1
                for k in range(8):
                    fw.dma(xt[:, k, :], self.xT[k * 128:(k + 1) * 128, c0:c0 + TT], reads=[("xT", k)], writes=["xt"])
                if j == 1:
                    p0 = c0 - NP
                    fw.dma(cosT[:], self.rope_cos[:, p0:p0 + TT], writes=["cosT"])
                    fw.dma(sinT[:], self.rope_sin[:, p0:p0 + TT], writes=["sinT"])
                self.norm_mod(es, xt, self.a1, 0, j, hT, "A")
                evac_flip = 0
                import os
                lvl = int(os.environ.get('DBGA', '9'))
                mlist = list(range(0, 13)) + list(range(14, 38))
                if lvl == 0: mlist = list(range(0, 6))
                if lvl == 1: mlist = list(range(0, 8))
                if lvl == 2: mlist = list(range(0, 13))
                for m in mlist:
                    ps, pk = next_ps()
                    mm_tile(m, ps, pk)
                    eng = fw.act if (evac_flip % 2 == 0) else fw.dve
                    evac_flip += 1
                    if m < 6:
                        i = cnt["s32"] % NST; cnt["s32"] += 1
                        s = st32[i]; sk = ("st32", i)
                        if eng is fw.act:
                            fw.op(eng, lambda a, s=s, ps=ps: a.copy(out=s[:], in_=ps[:]), reads=[pk], writes=[sk])
                        else:
                            fw.op(eng, lambda v, s=s, ps=ps: v.tensor_copy(out=s[:], in_=ps[:]), reads=[pk], writes=[sk])
                        fw.dma(self.zaT[m * 128:(m + 1) * 128, c0:c0 + TT], s[:], reads=[sk], writes=[("zaT", m)])
                    elif m < 8:
                        i = cnt["s16"] % NST; cnt["s16"] += 1
                        s = st16[i]; sk = ("st16", i)
                        fw.op(fw.act, lambda a, s=s, ps=ps: a.copy(out=s[:], in_=ps[:]), reads=[pk], writes=[sk])
                        fw.dma(self.zbT[(m - 6) * 128:(m - 5) * 128, c0:c0 + TT], s[:], reads=[sk], writes=[("zbT", m - 6)])
                    elif m < 13:
                        i = cnt["s16"] % NST; cnt["s16"] += 1
                        s = st16[i]; sk = ("st16", i)
                        ti = cnt["t32"] % 2; cnt["t32"] += 1
                        t32 = tmp32[ti]; tk = ("tmp32", ti)
                        if m < 12 and os.environ.get('DBGQ') != '1':
                            fw.op(fw.act, lambda a, t32=t32, ps=ps: a.activation(out=t32[:], in_=ps[:], func=AF.Square), reads=[pk], writes=[tk])
                            pn, pnk = self.pb[6], ("pb", 6)
                            fw.op(fw.pe, lambda t, t32=t32: t.matmul(pn[:, :], lhsT=self.blk2[:], rhs=t32[:], start=True, stop=True),
                                  reads=[tk, "blk2"], writes=[pnk])
                            ni = cnt["nrm"] % 2; cnt["nrm"] += 1
                            nr = nrm[ni]; nrk = ("nrm", ni)
                            if os.environ.get('DBGQ') != '2':
                                fw.op(fw.act, lambda a, nr=nr: a.mul(out=nr[:], in_=pn[0:2, :], mul=-0.5), reads=[pnk], writes=[nrk])
                            for hh in range(2 if os.environ.get('DBGQ') not in ('2', '3') else 0):
                                h = 2 * (m - 8) + hh
                                fw.dma(self.qA[h, 64:65, c0:c0 + TT], nr[hh:hh + 1, :], reads=[nrk], writes=[("qA", h)])
                        if j == 0:
                            fw.op(fw.dve, lambda v, s=s, ps=ps: v.tensor_copy(out=s[:], in_=ps[:]), reads=[pk], writes=[sk])
                        else:
                            mp = 38 + (m - 8)
                            ps2, pk2 = next_ps()
                            mm_tile(mp, ps2, pk2)
                            fw.op(fw.dve, lambda v, t32=t32, ps=ps: v.tensor_tensor(out=t32[:], in0=ps[:], in1=cosT[:], op=ALU.mult),
                                  reads=[pk, "cosT", tk], writes=[tk])
                            i2 = cnt["s32"] % NST; cnt["s32"] += 1
                            s2 = st32[i2]; s2k = ("st32", i2)
                            fw.op(fw.dve, lambda v, s2=s2, ps2=ps2: v.tensor_tensor(out=s2[:], in0=ps2[:], in1=sinT[:], op=ALU.mult),
                                  reads=[pk2, "sinT"], writes=[s2k])
                            fw.op(fw.pool, lambda g, s=s, t32=t32, s2=s2: g.tensor_tensor(out=s[:], in0=t32[:], in1=s2[:], op=ALU.add),
                                  reads=[tk, s2k], writes=[sk])
                        for hh in range(2):
                            if m < 12:
                                h = 2 * (m - 8) + hh
                                fw.dma(self.qA[h, 0:64, c0:c0 + TT], s[hh * 64:(hh + 1) * 64, :], reads=[sk], writes=[("qA", h)])
                            else:
                                fw.dma(self.kA[hh, 0:64, c0:c0 + TT], s[hh * 64:(hh + 1) * 64, :], reads=[sk], writes=[("kA", hh)])
                    else:
                        i = cnt["s16"] % NST; cnt["s16"] += 1
                        s = st16[i]; sk = ("st16", i)
                        fw.op(fw.act, lambda a, s=s, ps=ps: a.activation(out=s[:], in_=ps[:], func=AF.Sigmoid), reads=[pk], writes=[sk])
                        fw.dma(self.gT[(m - 14) * 128:(m - 13) * 128, c0:c0 + TT], s[:], reads=[sk], writes=[("gT", m - 14)])
                for ts in range(4 if lvl >= 4 else 0):
                    ps, pk = next_ps()
                    for k in range(8):
                        fw.op(fw.pe, lambda t, k=k, ts=ts, ps=ps: t.matmul(ps[:, 0:256], lhsT=hT[:, k, ts * 128:(ts + 1) * 128], rhs=WinB[:, k, 1536:1792],
                                                                         start=(k == 0), stop=(k == 7)),
                              reads=["WinB", "hT"], writes=[pk], inc=(k == 7))
                    i = cnt["kv"] % 2; cnt["kv"] += 1
                    r0 = c0 + ts * 128
                    fw.op(fw.dve, lambda v, i=i, ps=ps: v.tensor_copy(out=kv32[i][:], in_=ps[:, 0:256]), reads=[pk], writes=[("kv32", i)])
                    fw.op(fw.act, lambda a, i=i, ps=ps: a.copy(out=v16[i][:], in_=ps[:, 128:256]), reads=[pk], writes=[("v16", i)])
                    for g_ in range(2):
                        fw.op(fw.act, lambda a, i=i, g_=g_: a.activation(out=junk[:], in_=kv32[i][:, g_ * 64:(g_ + 1) * 64], func=AF.Square,
                                                                          accum_out=nkt[i][:, g_:g_ + 1]),
                              reads=[("kv32", i)], writes=["junkA", ("nkt", i)])
                    fw.op(fw.dve, lambda v, i=i: v.tensor_scalar(out=nkt[i][:], in0=nkt[i][:], scalar1=1.0 / 16.0, scalar2=None, op0=ALU.mult),
                          reads=[("nkt", i)], writes=[("nkt", i)])
                    fw.dma(self.nk[r0:r0 + 128, :], nkt[i][:], reads=[("nkt", i)], writes=["nk"])
                    fw.dma(self.vtok[r0:r0 + 128, :], v16[i][:], reads=[("v16", i)], writes=["vtok"])
                    if tt == 0:
                        fw.dma(self.ok[l, r0:r0 + 128, :], kv32[i][:, 0:128], reads=[("kv32", i)], is_output=True)
                        fw.dma(self.ov[l, r0:r0 + 128, :], kv32[i][:, 128:256], reads=[("kv32", i)], is_output=True)

    def phase_B(self, l):
        nc, fw = self.nc, self.fw
        with ExitStack() as es:
            sb = lambda n, s, d=F32: es.enter_context(nc.sbuf_tensor(self.uniq(n), s, d))
            t32 = sb("bp32", [128, NT])
            t16 = sb("bp16", [128, NT], BF16)
            z16 = sb("bpz", [128, NT], BF16)
            if "hyena" not in self.mixers:
                for m in range(2):
                    fw.dma(t32[:], self.zaT[m * 128:(m + 1) * 128, :], reads=[("zaT", m)], writes=["bp32"])
                    fw.op(fw.dve, lambda v: v.tensor_copy(out=t16[:], in_=t32[:]), reads=["bp32"], writes=["bp16"])
                    fw.dma(self.yaT[m * 128:(m + 1) * 128, :], t16[:], reads=["bp16"], writes=[("yaT", m)])
            if "s5" not in self.mixers:
                for m in range(2):
                    fw.dma(t16[:], self.zbT[m * 128:(m + 1) * 128, :], reads=[("zbT", m)], writes=["bp16"])
                    fw.dma(self.ybT[m * 128:(m + 1) * 128, :], t16[:], reads=["bp16"], writes=[("ybT", m)])
            if "attn" not in self.mixers:
                fw.op(fw.pool, lambda g: g.memset(z16[:], 0.0), writes=["bpz"])
                for m in range(4):
                    fw.dma(self.ycT[m * 128:(m + 1) * 128, :], z16[:], reads=["bpz"], writes=[("ycT", m)])
        if "attn" in self.mixers:
            fw.barrier()
            self.attention(l)
        if "hyena" in self.mixers:
            fw.barrier()
            self.hyena(l)
        if "s5" in self.mixers:
            fw.barrier()
            self.s5(l)

    def attention(self, l):
        nc, fw = self.nc, self.fw
        with ExitStack() as es:
            sb = lambda n, s, d=F32: es.enter_context(nc.sbuf_tensor(self.uniq(n), s, d))
            NKT = NT // 128
            kAs = sb("kAs", [65, NT], BF16)
            qAs = sb("qAs", [65, 4, NT], BF16)
            vt = sb("vt", [128, NKT, 128], BF16)
            nkS = sb("nkS", [128, NKT, 2])
            en = sb("enS", [128, NKT, 2])
            nnk = sb("nnkS", [128, NKT, 2])
            Vp = sb("Vp", [128, NKT, 65], BF16)
            ck32 = sb("ck32", [128, 4, 128])
            cv32 = sb("cv32", [128, 4, 128])
            kcA = sb("kcA", [65, 2, 512], BF16)
            nkc = sb("nkc", [128, 4, 2])
            enc = sb("enc", [128, 4, 2])
            nnkc = sb("nnkc", [128, 4, 2])
            Vpc = sb("Vpc", [128, 2, 4, 65], BF16)
            junk = sb("junkB", [128, 64])
            sinkt = sb("sinkt", [65, 8])
            mprev = sb("mprev", [128, 128], BF16)
            mnext = sb("mnext", [128, 128], BF16)
            m32 = sb("m32", [128, 128])
            sel64 = sb("sel64", [128, 128])
            Rrow = sb("Rrow", [128, 1024])
            PT = [sb(f"PT{i}", [128, 1024], BF16) for i in range(2)]
            sk_t = sb("sk_t", [65, 1024])
            bcs = sb("bcs", [64, 1024])
            ycS = sb("ycS", [64, 4, NT], BF16)

            fw.dma(sinkt[:], self.sink[l:l + 1, :].partition_broadcast(65), writes=["sinkt"])
            fw.dma(m32[:], self.mask_prev[:, :], writes=["m32"])
            fw.op(fw.dve, lambda v: v.tensor_copy(out=mprev[:], in_=m32[:]), reads=["m32"], writes=["mprev"])
            fw.dma(m32[:], self.mask_next[:, :], reads=[], writes=["m32"])
            fw.op(fw.dve, lambda v: v.tensor_copy(out=mnext[:], in_=m32[:]), reads=["m32"], writes=["mnext"])
            fw.op(fw.pool, lambda g: g.memset(sel64[:], 0.0), writes=["sel64"])
            fw.op(fw.pool, lambda g: g.memset(sel64[64:65, :], 1.0), reads=["sel64"], writes=["sel64"])
            fw.op(fw.pool, lambda g: g.memset(Rrow[:], 0.0), writes=["Rrow"])
            fw.dma(vt[:], self.vtok.rearrange("(t p) c -> p t c", p=128), reads=["vtok"], writes=["vt"])
            fw.dma(nkS[:], self.nk.rearrange("(t p) c -> p t c", p=128), reads=["nk"], writes=["nkS"])
            fw.op(fw.act, lambda a: a.activation(out=en[:], in_=nkS[:], func=AF.Exp), reads=["nkS"], writes=["en"])
            fw.op(fw.dve, lambda v: v.tensor_scalar(out=nnk[:], in0=nkS[:], scalar1=-1.0, scalar2=None, op0=ALU.mult), reads=["nkS"], writes=["nnk"])
            fw.dma(ck32[:], self.cache_k[l].rearrange("(t p) c -> p t c", p=128), writes=["ck32"])
            fw.dma(cv32[:], self.cache_v[l].rearrange("(t p) c -> p t c", p=128), writes=["cv32"])
            fw.op(fw.pool, lambda g: g.memset(kcA[64:65, :, :], 1.0), writes=["kcA"])
            for t in range(4):
                for g_ in range(2):
                    fw.op(fw.act, lambda a, t=t, g_=g_: a.activation(out=junk[:], in_=ck32[:, t, g_ * 64:(g_ + 1) * 64], func=AF.Square,
                                                                      accum_out=nkc[:, t, g_:g_ + 1]), reads=["ck32"], writes=["junkB", "nkc"])
                pt, ptk = self.pb[5], ("pb", 5)
                fw.op(fw.pe, lambda t_, t=t: t_.transpose(pt[:, 0:128], ck32[:, t, :], self.ident[:]), reads=["ck32", "ident"], writes=[ptk])
                for g_ in range(2):
                    fw.op(fw.dve, lambda v, t=t, g_=g_: v.tensor_copy(out=kcA[0:64, g_, t * 128:(t + 1) * 128], in_=pt[g_ * 64:(g_ + 1) * 64, 0:128]),
                          reads=[ptk], writes=["kcA"])
            fw.op(fw.dve, lambda v: v.tensor_scalar(out=nkc[:], in0=nkc[:], scalar1=1.0 / 16.0, scalar2=None, op0=ALU.mult), reads=["nkc"], writes=["nkc"])
            fw.op(fw.act, lambda a: a.activation(out=enc[:], in_=nkc[:], func=AF.Exp), reads=["nkc"], writes=["enc"])
            fw.op(fw.dve, lambda v: v.tensor_scalar(out=nnkc[:], in0=nkc[:], scalar1=-1.0, scalar2=None, op0=ALU.mult), reads=["nkc"], writes=["nnkc"])

            for g_ in range(2):
                fw.dma(kAs[:], self.kA[g_], reads=[("kA", g_)], writes=["kAs"])
                for hh in range(4):
                    fw.dma(qAs[:, hh, :], self.qA[4 * g_ + hh], reads=[("qA", 4 * g_ + hh)], writes=["qAs"])
                for t in range(NKT):
                    fw.op(fw.dve, lambda v, t=t: v.tensor_scalar(out=Vp[:, t, 0:64], in0=vt[:, t, g_ * 64:(g_ + 1) * 64], scalar1=en[:, t, g_:g_ + 1],
                                                                  scalar2=None, op0=ALU.mult), reads=["vt", "en"], writes=["Vp"])
                fw.op(fw.dve, lambda v: v.tensor_copy(out=Vp[:, :, 64], in_=en[:, :, g_]), reads=["en"], writes=["Vp"])
                for t in range(4):
                    fw.op(fw.dve, lambda v, t=t: v.tensor_scalar(out=Vpc[:, g_, t, 0:64], in0=cv32[:, t, g_ * 64:(g_ + 1) * 64], scalar1=enc[:, t, g_:g_ + 1],
                                                                  scalar2=None, op0=ALU.mult), reads=["cv32", "enc"], writes=["Vpc"])
                fw.op(fw.dve, lambda v: v.tensor_copy(out=Vpc[:, g_, :, 64], in_=enc[:, :, g_]), reads=["enc"], writes=["Vpc"])

                groups = []
                for s_ in range(2):
                    groups.append((s_ * 256, 256, [("tok", 2 * s_), ("tok", 2 * s_ + 1)]))
                for qb in range(32):
                    kts = [("cache", t) for t in range(4)]
                    if qb > 0:
                        kts.append(("prev", 4 + qb - 1))
                    kts.append(("tok", 4 + qb))
                    if qb < 31:
                        kts.append(("next", 4 + qb + 1))
                    groups.append((NP + qb * 128, 128, kts))
                for gi, (q0, nq, kts) in enumerate(groups):
                    ncol = 4 * nq
                    nb = ncol // 512
                    po = [self.pb[2 + 2 * (gi % 2) + b_] for b_ in range(nb)]
                    pok = [("pb", 2 + 2 * (gi % 2) + b_) for b_ in range(nb)]
                    for ki, (kind, t) in enumerate(kts):
                        pt_ = PT[ki % 2]; ptk_ = ("PT", ki % 2)
                        pss = [self.pb[0], self.pb[1]]; pssk = [("pb", 0), ("pb", 1)]
                        if nb == 1:
                            pss = [self.pb[ki % 2]]; pssk = [("pb", ki % 2)]
                        if kind == "cache":
                            lhs = kcA[0:65, g_, t * 128:(t + 1) * 128]
                            bias = nnkc[:, t, g_:g_ + 1]
                            vprime = Vpc[:, g_, t, :]
                            rk = ["kcA"]; bk = "nnkc"; vk = "Vpc"
                        else:
                            lhs = kAs[0:65, t * 128:(t + 1) * 128]
                            bias = nnk[:, t, g_:g_ + 1]
                            vprime = Vp[:, t, :]
                            rk = ["kAs"]; bk = "nnk"; vk = "Vp"
                        for hh in range(4):
                            col = hh * nq
                            b_ = col // 512
                            fw.op(fw.pe, lambda t_, hh=hh, col=col, b_=b_: t_.matmul(pss[b_][:, col % 512:col % 512 + nq], lhsT=lhs, rhs=qAs[0:65, hh, q0:q0 + nq],
                                                                                   start=True, stop=True),
                                  reads=rk + ["qAs"], writes=[pssk[b_]], inc=(hh % (4 // nb) == (4 // nb) - 1))
                        for b_ in range(nb):
                            fw.op(fw.act, lambda a, b_=b_: a.activation(out=pt_[:, b_ * 512:(b_ + 1) * 512], in_=pss[b_][:], func=AF.Exp, scale=0.125, bias=bias),
                                  reads=[pssk[b_], bk], writes=[ptk_])
                        if kind in ("prev", "next"):
                            mk_ = mprev if kind == "prev" else mnext
                            fw.op(fw.pool, lambda g: g.tensor_tensor(out=pt_[:, 0:512].rearrange("p (h q) -> p h q", h=4),
                                                                     in0=pt_[:, 0:512].rearrange("p (h q) -> p h q", h=4),
                                                                     in1=mk_[:].unsqueeze(1).to_broadcast([128, 4, 128]), op=ALU.mult),
                                  reads=[ptk_, "mprev", "mnext"], writes=[ptk_])
                        for b_ in range(nb):
                            fw.op(fw.pe, lambda t_, b_=b_: t_.matmul(po[b_][0:65, :], lhsT=vprime[:, 0:65], rhs=pt_[:, b_ * 512:(b_ + 1) * 512],
                                                                     start=(ki == 0), stop=(ki == len(kts) - 1)),
                                  reads=[vk, ptk_], writes=[pok[b_]], inc=(ki == len(kts) - 1))
                    for hh in range(4):
                        h = 4 * g_ + hh
                        fw.op(fw.act, lambda a, hh=hh, h=h: a.activation(out=sk_t[64:65, hh * nq:(hh + 1) * nq], in_=qAs[64:65, hh, q0:q0 + nq], func=AF.Exp,
                                                                          scale=0.125, bias=sinkt[64:65, h:h + 1]), reads=["qAs", "sinkt"], writes=["sk_t"])
                    for b_ in range(nb):
                        cs = slice(b_ * 512, (b_ + 1) * 512)
                        fw.op(fw.dve, lambda v, b_=b_, cs=cs: v.tensor_tensor(out=sk_t[64:65, cs], in0=po[b_][64:65, :], in1=sk_t[64:65, cs], op=ALU.add),
                              reads=[pok[b_], "sk_t"], writes=["sk_t"])
                        fw.op(fw.dve, lambda v, cs=cs: v.reciprocal(out=Rrow[64:65, cs], in_=sk_t[64:65, cs]), reads=["sk_t"], writes=["Rrow"])
                        pbc, pbck = self.pb[6 + b_], ("pb", 6 + b_)
                        fw.op(fw.pe, lambda t_, cs=cs, pbc=pbc: t_.matmul(pbc[:], lhsT=sel64[:], rhs=Rrow[:, cs], start=True, stop=True),
                              reads=["sel64", "Rrow"], writes=[pbck])
                        fw.op(fw.act, lambda a, cs=cs, pbc=pbc: a.copy(out=bcs[:, cs], in_=pbc[0:64, :]), reads=[pbck], writes=["bcs"])
                        for hh in range(4 // nb):
                            hg = b_ * (4 // nb) + hh
                            fw.op(fw.dve, lambda v, hh=hh, hg=hg, b_=b_: v.tensor_tensor(out=ycS[:, hg, q0:q0 + nq], in0=po[b_][0:64, hh * nq:(hh + 1) * nq],
                                                                                         in1=bcs[:, b_ * 512 + hh * nq:b_ * 512 + (hh + 1) * nq], op=ALU.mult),
                                  reads=[pok[b_], "bcs"], writes=["ycS"])
                for hh in range(4):
                    h = 4 * g_ + hh
                    fw.dma(self.ycT[h * 64:(h + 1) * 64, :], ycS[:, hh, :], reads=["ycS"], writes=[("ycT", h // 2)])

    HSEG = [(0, 256), (256, 256), (NP, LS)]

    def hyena(self, l):
        nc, fw = self.nc, self.fw
        if not hasattr(self, "hzT"):
            pass
        with ExitStack() as es:
            sb = lambda n, s, d=F32: es.enter_context(nc.sbuf_tensor(self.uniq(n), s, d))
            za = sb("hy_za", [128, NT]); U = sb("hy_U", [128, NT]); cw = sb("hy_cwS", [128, 6, 4])
            fw.dma(cw[:], self.hy_cw[l], writes=["hy_cwS"])
            for m in range(6):
                fw.dma(za[:], self.zaT[m * 128:(m + 1) * 128, :], reads=[("zaT", m)], writes=["hy_za"])
                fw.op(fw.act, lambda a: a.activation(out=U[:], in_=za[:], func=AF.Identity, scale=cw[:, m, 1:2], bias=cw[:, m, 3:4]),
                      reads=["hy_za", "hy_cwS"], writes=["hy_U"])
                for (t0, Ls) in self.HSEG:
                    fw.op(fw.dve, lambda v: v.scalar_tensor_tensor(out=U[:, t0 + 1:t0 + Ls], in0=za[:, t0:t0 + Ls - 1], scalar=cw[:, m, 0:1],
                                                                   in1=U[:, t0 + 1:t0 + Ls], op0=ALU.mult, op1=ALU.add),
                          reads=["hy_za", "hy_cwS", "hy_U"], writes=["hy_U"])
                    fw.op(fw.dve, lambda v: v.scalar_tensor_tensor(out=U[:, t0:t0 + Ls - 1], in0=za[:, t0 + 1:t0 + Ls], scalar=cw[:, m, 2:3],
                                                                   in1=U[:, t0:t0 + Ls - 1], op0=ALU.mult, op1=ALU.add),
                          reads=["hy_za", "hy_cwS", "hy_U"], writes=["hy_U"])
                if m < 2:
                    fw.dma(self.hzT[m * 128:(m + 1) * 128, :], U[:], reads=["hy_U"], writes=[("hzT", m)])
                else:
                    fw.dma(self.ugT[(m - 2) * 128:(m - 1) * 128, :], U[:], reads=["hy_U"], writes=[("ugT", m - 2)])
        fw.barrier()
        for L_ in (256, 4096):
            self.hy_filters(l, L_)
            fw.barrier()
        for o in range(2):
            self.hy_to_tok()
            fw.barrier()
            for (t0, Ls) in self.HSEG:
                self.hy_conv(o, t0, Ls)
                fw.barrier()
            with ExitStack() as es:
                sb = lambda n, s, d=F32: es.enter_context(nc.sbuf_tensor(self.uniq(n), s, d))
                z = sb("hy_z", [128, NT]); y = sb("hy_y", [128, NT]); g = sb("hy_g", [128, NT]); skp = sb("hy_skp", [128, 2, 2])
                z16 = sb("hy_z16", [128, NT], BF16)
                fw.dma(skp[:], self.hy_skip[l], writes=["hy_skp"])
                for ch in range(2):
                    fw.dma(z[:], self.hzT[ch * 128:(ch + 1) * 128, :], reads=[("hzT", ch)], writes=["hy_z"])
                    fw.dma(y[:], self.hyT[ch * 128:(ch + 1) * 128, :], reads=[("hyT", ch)], writes=["hy_y"])
                    fw.dma(g[:], self.ugT[(2 * o + ch) * 128:(2 * o + ch + 1) * 128, :], reads=[("ugT", 2 * o + ch)], writes=["hy_g"])
                    fw.op(fw.dve, lambda v: v.scalar_tensor_tensor(out=y[:], in0=z[:], scalar=skp[:, o, ch:ch + 1], in1=y[:], op0=ALU.mult, op1=ALU.add),
                          reads=["hy_z", "hy_skp", "hy_y"], writes=["hy_y"])
                    if o == 0:
                        fw.op(fw.pool, lambda g_: g_.tensor_tensor(out=z[:], in0=y[:], in1=g[:], op=ALU.mult), reads=["hy_y", "hy_g"], writes=["hy_z"])
                        fw.dma(self.hzT[ch * 128:(ch + 1) * 128, :], z[:], reads=["hy_z"], writes=[("hzT", ch)])
                    else:
                        fw.op(fw.pool, lambda g_: g_.tensor_tensor(out=z16[:], in0=y[:], in1=g[:], op=ALU.mult), reads=["hy_y", "hy_g"], writes=["hy_z16"])
                        fw.dma(self.yaT[ch * 128:(ch + 1) * 128, :], z16[:], reads=["hy_z16"], writes=[("yaT", ch)])
            fw.barrier()

    def hy_to_tok(self):
        nc, fw = self.nc, self.fw
        with ExitStack() as es:
            sb = lambda n, s, d=F32: es.enter_context(nc.sbuf_tensor(self.uniq(n), s, d))
            z = sb("ht_z", [128, 2, NT])
            st = [sb(f"ht_st{i}", [128, 256], BF16) for i in range(3)]
            for ch in range(2):
                fw.dma(z[:, ch, :], self.hzT[ch * 128:(ch + 1) * 128, :], reads=[("hzT", ch)], writes=["ht_z"])
            for t in range(NT // 128):
                ps, pk = self.pb[t % 4], ("pb", t % 4)
                for ch in range(2):
                    fw.op(fw.pe, lambda t_, ch=ch: t_.transpose(ps[:, ch * 128:(ch + 1) * 128], z[:, ch, t * 128:(t + 1) * 128], self.ident[:]),
                          reads=["ht_z", "ident"], writes=[pk], inc=(ch == 1))
                s_ = st[t % 3]; sk = ("ht_st", t % 3)
                if t % 2 == 0:
                    fw.op(fw.dve, lambda v: v.tensor_copy(out=s_[:], in_=ps[:, 0:256]), reads=[pk], writes=[sk])
                else:
                    fw.op(fw.act, lambda a: a.activation(out=s_[:], in_=ps[:, 0:256], func=AF.Identity), reads=[pk], writes=[sk])
                fw.dma(self.ztok[t * 128:(t + 1) * 128, :], s_[:], reads=[sk], writes=["ztok"])

    def hy_stageA(self, es, L_, src, C, FAt, xt, stg):
        nc, fw = self.nc, self.fw
        NH = L_ // 128
        srcv = src.rearrange("(h l) c -> h l c", l=128)
        for ch in range(8):
            fw.dma(xt[0:NH, :, 0:C], srcv[:, ch * 16:(ch + 1) * 16, :], reads=["hy_src"], writes=["hy_xt"])
            sg = stg[ch % 2]; sgk = ("hy_stg", ch % 2)
            for i in range(16):
                n_lo = ch * 16 + i
                ps, pk = self.pb[i % 4], ("pb", i % 4)
                fw.op(fw.pe, lambda t_: t_.matmul(ps[:, 0:C], lhsT=FAt[:, n_lo, :], rhs=xt[:, i, 0:C], start=True, stop=True),
                      reads=["hy_FAt", "hy_xt"], writes=[pk])
                if i % 2 == 0:
                    fw.op(fw.dve, lambda v: v.tensor_copy(out=sg[:, i, 0:C], in_=ps[:, 0:C]), reads=[pk], writes=[sgk])
                else:
                    fw.op(fw.act, lambda a: a.activation(out=sg[:, i, 0:C], in_=ps[:, 0:C], func=AF.Identity), reads=[pk], writes=[sgk])
            fw.dma(self.Ascr[:, ch * 16:(ch + 1) * 16, 0:C], sg[:, :, 0:C], reads=[sgk], writes=["Ascr"])

    def hy_filters(self, l, L_):
        nc, fw = self.nc, self.fw
        NJT = L_ // 128
        KA = L_ // 64
        cwid = min(512, L_)
        TWO_PI = 2.0 * math.pi
        with ExitStack() as es:
            sb = lambda n, s, d=F32: es.enter_context(nc.sbuf_tensor(self.uniq(n), s, d))
            featT = sb("hf_feat", [128, L_]); negt = sb("hf_negt", [128, NJT])
            w1 = sb("hf_w1", [128, 128]); w2 = sb("hf_w2", [128, 128]); w3 = sb("hf_w3", [128, 1024])
            vec = sb("hf_vec", [128, 3]); fb = sb("hf_fb", [128, 2]); dabs = sb("hf_dabs", [128, 1024])
            h1T = sb("hf_h1T", [128, L_]); h2T = sb("hf_h2T", [128, L_])
            yt = sb("hf_yt", [128, 512]); ki = sb("hf_ki", [128, 512], I32); kf = sb("hf_kf", [128, 512])
            fw.dma(featT[:], self.hy_featT[L_][:, :], writes=["hf_feat"])
            fw.dma(negt[:], self.hy_negt[L_][:, :], writes=["hf_negt"])
            fw.dma(w1[:], self.hy_w1[l], writes=["hf_w1"]); fw.dma(w2[:], self.hy_w2[l], writes=["hf_w2"]); fw.dma(w3[:], self.hy_w3[l], writes=["hf_w3"])
            fw.dma(vec[:], self.hy_vec[l], writes=["hf_vec"])
            fw.dma(dabs[:], self.hy_decay[l:l + 1, :].partition_broadcast(128), writes=["hf_dabs"])
            fw.op(fw.act, lambda a: a.activation(out=dabs[:], in_=dabs[:], func=AF.Abs), reads=["hf_dabs"], writes=["hf_dabs"])
            fw.op(fw.dve, lambda v: v.tensor_tensor(out=fb[:], in0=vec[:, 0:2], in1=vec[:, 2:3].to_broadcast([128, 2]), op=ALU.mult), reads=["hf_vec"], writes=["hf_fb"])

            def sin_layer(srcT, W, col, dstT, sk, wk, dk):
                for ch in range(L_ // cwid):
                    cs = slice(ch * cwid, (ch + 1) * cwid)
                    ps, pk = self.pb[ch % 2], ("pb", ch % 2)
                    fw.op(fw.pe, lambda t_: t_.matmul(ps[:, 0:cwid], lhsT=W[:], rhs=srcT[:, cs], start=True, stop=True), reads=[sk, wk], writes=[pk])
                    fw.op(fw.act, lambda a: a.activation(out=yt[:, 0:cwid], in_=ps[:, 0:cwid], func=AF.Identity, scale=vec[:, 2:3], bias=fb[:, col:col + 1]),
                          reads=[pk, "hf_vec", "hf_fb"], writes=["hf_yt"])
                    fw.op(fw.dve, lambda v: v.tensor_scalar(out=ki[:, 0:cwid], in0=yt[:, 0:cwid], scalar1=1.0 / TWO_PI, scalar2=None, op0=ALU.mult),
                          reads=["hf_yt"], writes=["hf_ki"])
                    fw.op(fw.dve, lambda v: v.tensor_copy(out=kf[:, 0:cwid], in_=ki[:, 0:cwid]), reads=["hf_ki"], writes=["hf_kf"])
                    fw.op(fw.dve, lambda v: v.scalar_tensor_tensor(out=yt[:, 0:cwid], in0=kf[:, 0:cwid], scalar=-TWO_PI, in1=yt[:, 0:cwid], op0=ALU.mult, op1=ALU.add),
                          reads=["hf_kf", "hf_yt"], writes=["hf_yt"])
                    fw.op(fw.dve, lambda v: v.tensor_scalar(out=yt[:, 0:cwid], in0=yt[:, 0:cwid], scalar1=3.141592, scalar2=-3.141592, op0=ALU.min, op1=ALU.max),
                          reads=["hf_yt"], writes=["hf_yt"])
                    fw.op(fw.act, lambda a: a.activation(out=dstT[:, cs], in_=yt[:, 0:cwid], func=AF.Sin), reads=["hf_yt"], writes=[dk])

            sin_layer(featT, w1, 0, h1T, "hf_feat", "hf_w1", "hf_h1T")
            sin_layer(h1T, w2, 1, h2T, "hf_h1T", "hf_w2", "hf_h2T")

            hrow = [sb(f"hf_hrow{i}", [128, 1024]) for i in range(2)]
            habs = sb("hf_habs", [128, 1024]); Et = sb("hf_E", [128, 512])
            pS = [self.pb[6], self.pb[7]]; pSk = [("pb", 6), ("pb", 7)]
            for j in range(NJT):
                hr = hrow[j % 2]; hk = ("hf_hrow", j % 2)
                for half in range(2):
                    cs = slice(half * 512, (half + 1) * 512)
                    ps, pk = self.pb[2 + half], ("pb", 2 + half)
                    fw.op(fw.pe, lambda t_: t_.matmul(ps[:], lhsT=h2T[:, j * 128:(j + 1) * 128], rhs=w3[:, cs], start=True, stop=True),
                          reads=["hf_h2T", "hf_w3"], writes=[pk])
                    fw.op(fw.act, lambda a: a.activation(out=Et[:], in_=dabs[:, cs], func=AF.Exp, scale=negt[:, j:j + 1]), reads=["hf_dabs", "hf_negt"], writes=["hf_E"])
                    fw.op(fw.dve, lambda v: v.tensor_tensor(out=hr[:, cs], in0=ps[:], in1=Et[:], op=ALU.mult), reads=[pk, "hf_E"], writes=[hk])
                if j == 0:
                    fw.op(fw.pool, lambda g: g.memset(hr[0:1, 256:512], 0.0), reads=[hk], writes=[hk])
                    fw.op(fw.pool, lambda g: g.memset(hr[0:1, 768:1024], 0.0), reads=[hk], writes=[hk])
                fw.op(fw.act, lambda a: a.activation(out=habs[:], in_=hr[:], func=AF.Abs), reads=[hk], writes=["hf_habs"])
                for half in range(2):
                    fw.op(fw.pe, lambda t_: t_.matmul(pS[half][:], lhsT=self.ones32[:], rhs=habs[:, half * 512:(half + 1) * 512], start=(j == 0), stop=(j == NJT - 1)),
                          reads=["hf_habs", "ones32"], writes=[pSk[half]])
                fw.dma(self.hraw[L_][j * 128:(j + 1) * 128, :], hr[:], reads=[hk], writes=["hraw"])
            Ssb = sb("hf_S", [128, 1024]); rn = sb("hf_rn", [128, 2, 256]); rnb = sb("hf_rnb", [128, 1024])
            for half in range(2):
                fw.op(fw.act, lambda a: a.copy(out=Ssb[:, half * 512:(half + 1) * 512], in_=pS[half][:]), reads=[pSk[half]], writes=["hf_S"])
            Sv = Ssb[:].rearrange("p (o d c) -> p o d c", o=2, d=2)
            fw.op(fw.dve, lambda v: v.tensor_tensor(out=rn[:], in0=Sv[:, :, 0, :], in1=Sv[:, :, 1, :], op=ALU.add), reads=["hf_S"], writes=["hf_rn"])
            fw.op(fw.dve, lambda v: v.reciprocal(out=rn[:], in_=rn[:]), reads=["hf_rn"], writes=["hf_rn"])
            rv = rnb[:].rearrange("p (o d c) -> p o d c", o=2, d=2)
            for d_ in range(2):
                fw.op(fw.dve, lambda v: v.tensor_copy(out=rv[:, :, d_, :], in_=rn[:]), reads=["hf_rn"], writes=["hf_rnb"])
            hb16 = [sb(f"hf_hb16{i}", [128, 1024], BF16) for i in range(2)]
            for j in range(NJT):
                hr = hrow[j % 2]; hk = ("hf_hrow", j % 2)
                fw.dma(hr[:], self.hraw[L_][j * 128:(j + 1) * 128, :], reads=["hraw"], writes=[hk])
                fw.op(fw.dve, lambda v: v.tensor_tensor(out=hb16[j % 2][:], in0=hr[:], in1=rnb[:], op=ALU.mult), reads=[hk, "hf_rnb"], writes=[("hf_hb16", j % 2)])
                fw.dma(self.ftok[L_][j * 128:(j + 1) * 128, :], hb16[j % 2][:], reads=[("hf_hb16", j % 2)], writes=["ftok"])
        fw.barrier()
        with ExitStack() as es:
            sb = lambda n, s, d=F32: es.enter_context(nc.sbuf_tensor(self.uniq(n), s, d))
            FAt = sb("hy_FAt", [128, 128, 128], BF16); GB = sb("hy_GBs", [128, 12, 128], BF16)
            xt = sb("hy_xt", [128, 16, 512], BF16); stg = [sb(f"hy_stg{i}", [128, 16, 512], BF16) for i in range(2)]
            A2 = sb("hy_A2", [128, 2, 4, 512], BF16)
            kst = [sb(f"hy_kst{i}", [128, 2, 4, 256]) for i in range(2)]
            fw.dma(FAt[:], self.hy_FA[L_][:, :, :], writes=["hy_FAt"])
            fw.dma(GB[:], self.hy_GB[:, :, :], writes=["hy_GBs"])
            fw.op(fw.pool, lambda g: g.memset(xt[:], 0.0), writes=["hy_xt"])
            for o in range(2):
                self.hy_stageA(es, L_, self.ftok[L_][:, o * 512:(o + 1) * 512], 512, FAt, xt, stg)
                nkc = min(4, KA)
                for kc in range(KA // nkc):
                    for r in range(2):
                        fw.dma(A2[:, r, 0:nkc, :], self.Ascr[r * 64 + kc * nkc:r * 64 + (kc + 1) * nkc, :, :].rearrange("k l c -> l k c"),
                               reads=["Ascr"], writes=["hy_A2"])
                    ks = kst[kc % 2]; ksk = ("hy_kst", kc % 2)
                    for kk in range(nkc):
                        pa, pak = self.pb[(2 * kk) % 8], ("pb", (2 * kk) % 8)
                        pb_, pbk = self.pb[(2 * kk + 1) % 8], ("pb", (2 * kk + 1) % 8)
                        ops = [(0, 0, 4), (1, 0, 5), (0, 256, 4), (1, 256, 5)]
                        for ii, (r, c0_, gi) in enumerate(ops):
                            fw.op(fw.pe, lambda t_: t_.matmul(pa[:, 0:256], lhsT=GB[:, gi, :], rhs=A2[:, r, kk, c0_:c0_ + 256], start=(ii == 0), stop=(ii == 3)),
                                  reads=["hy_GBs", "hy_A2"], writes=[pak], inc=(ii == 3))
                        ops = [(0, 0, 6), (1, 0, 7), (0, 256, 8), (1, 256, 9)]
                        for ii, (r, c0_, gi) in enumerate(ops):
                            fw.op(fw.pe, lambda t_: t_.matmul(pb_[:, 0:256], lhsT=GB[:, gi, :], rhs=A2[:, r, kk, c0_:c0_ + 256], start=(ii == 0), stop=(ii == 3)),
                                  reads=["hy_GBs", "hy_A2"], writes=[pbk], inc=(ii == 3))
                        fw.op(fw.act, lambda a: a.copy(out=ks[:, 0, kk, :], in_=pa[:, 0:256]), reads=[pak], writes=[ksk])
                        fw.op(fw.dve, lambda v: v.tensor_copy(out=ks[:, 1, kk, :], in_=pb_[:, 0:256]), reads=[pbk], writes=[ksk])
                    for ab in range(2):
                        fw.dma(self.Ksp[L_][o, ab, :, kc * nkc:(kc + 1) * nkc, :], ks[:, ab, 0:nkc, :], reads=[ksk], writes=["Ksp"])
                fw.barrier()

    def hy_conv(self, o, t0, L_):
        nc, fw = self.nc, self.fw
        NH = L_ // 128
        KA = L_ // 64
        with ExitStack() as es:
            sb = lambda n, s, d=F32: es.enter_context(nc.sbuf_tensor(self.uniq(n), s, d))
            FAt = sb("hy_FAt", [128, 128, 128], BF16); GB = sb("hy_GBs", [128, 12, 128], BF16)
            FAi = sb("hy_FAi", [128, 128, NH], BF16)
            xt = sb("hy_xt", [128, 16, 256], BF16); stg = [sb(f"hy_stg{i}", [128, 16, 256], BF16) for i in range(2)]
            A2 = sb("hy_A2", [128, 2, 8, 256], BF16)
            Kab = sb("hy_Kab", [128, 2, 8, 256])
            Y = sb("hy_Y", [128, KA, 256], BF16)
            t1 = [sb(f"hy_t1{i}", [128, 256]) for i in range(2)]; t2 = [sb(f"hy_t2{i}", [128, 256]) for i in range(2)]
            stB = [sb(f"hy_stB{i}", [128, 2, 8, 256], BF16) for i in range(2)]
            Bq2 = sb("hy_Bq2", [128, 16, 256], BF16)
            y32 = sb("hy_y32", [128, 2, L_])
            fw.dma(FAt[:], self.hy_FA[L_][:, :, :], writes=["hy_FAt"])
            fw.dma(FAi[:], self.hy_FAi[L_][:, :, :], writes=["hy_FAi"])
            fw.dma(GB[:], self.hy_GB[:, :, :], writes=["hy_GBs"])
            fw.op(fw.pool, lambda g: g.memset(xt[:], 0.0), writes=["hy_xt"])
            fw.op(fw.pool, lambda g: g.memset(Bq2[:], 0.0), writes=["hy_Bq2"])
            self.hy_stageA(es, L_, self.ztok[t0:t0 + L_, :], 256, FAt, xt, stg)
            nkc = min(8, KA)
            for kc in range(KA // nkc):
                for r in range(2):
                    fw.dma(A2[:, r, 0:nkc, :], self.Ascr[r * 64 + kc * nkc:r * 64 + (kc + 1) * nkc, :, 0:256].rearrange("k l c -> l k c"),
                           reads=["Ascr"], writes=["hy_A2"])
                    fw.dma(Kab[:, r, 0:nkc, :], self.Ksp[L_][o, r, :, kc * nkc:(kc + 1) * nkc, :], reads=["Ksp"], writes=["hy_Kab"])
                for kk in range(nkc):
                    ka = kc * nkc + kk
                    pa, pak = self.pb[(2 * kk) % 4], ("pb", (2 * kk) % 4)
                    pw, pwk = self.pb[(2 * kk + 1) % 4], ("pb", (2 * kk + 1) % 4)
                    for ii, (r, gi) in enumerate([(0, 0), (1, 1)]):
                        fw.op(fw.pe, lambda t_: t_.matmul(pa[:, 0:256], lhsT=GB[:, gi, :], rhs=A2[:, r, kk, :], start=(ii == 0), stop=(ii == 1)),
                              reads=["hy_GBs", "hy_A2"], writes=[pak], inc=(ii == 1))
                    for ii, (r, gi) in enumerate([(0, 2), (1, 3)]):
                        fw.op(fw.pe, lambda t_: t_.matmul(pw[:, 0:256], lhsT=GB[:, gi, :], rhs=A2[:, r, kk, :], start=(ii == 0), stop=(ii == 1)),
                              reads=["hy_GBs", "hy_A2"], writes=[pwk], inc=(ii == 1))
                    a_, b_ = t1[kk % 2], t2[kk % 2]
                    fw.op(fw.dve, lambda v: v.tensor_tensor(out=a_[:], in0=pa[:, 0:256], in1=Kab[:, 0, kk, :], op=ALU.mult), reads=[pak, "hy_Kab"], writes=[("hy_t1", kk % 2)])
                    fw.op(fw.dve, lambda v: v.tensor_tensor(out=b_[:], in0=pw[:, 0:256], in1=Kab[:, 1, kk, :], op=ALU.mult), reads=[pwk, "hy_Kab"], writes=[("hy_t2", kk % 2)])
                    fw.op(fw.pool, lambda g: g.tensor_tensor(out=Y[:, ka, :], in0=a_[:], in1=b_[:], op=ALU.add), reads=[("hy_t1", kk % 2), ("hy_t2", kk % 2)], writes=["hy_Y"])
            for kc in range(KA // nkc):
                sB = stB[kc % 2]; sBk = ("hy_stB", kc % 2)
                for kk in range(nkc):
                    ka = kc * nkc + kk
                    for r in range(2):
                        ps, pk = self.pb[4 + (2 * kk + r) % 4], ("pb", 4 + (2 * kk + r) % 4)
                        fw.op(fw.pe, lambda t_: t_.matmul(ps[:, 0:256], lhsT=GB[:, 10 + r, :], rhs=Y[:, ka, :], start=True, stop=True), reads=["hy_GBs", "hy_Y"], writes=[pk])
                        if r == 0:
                            fw.op(fw.dve, lambda v: v.tensor_copy(out=sB[:, r, kk, :], in_=ps[:, 0:256]), reads=[pk], writes=[sBk])
                        else:
                            fw.op(fw.act, lambda a: a.activation(out=sB[:, r, kk, :], in_=ps[:, 0:256], func=AF.Identity), reads=[pk], writes=[sBk])
                for r in range(2):
                    fw.dma(self.Bscr[:, r, kc * nkc:(kc + 1) * nkc, :], sB[:, r, 0:nkc, :], reads=[sBk], writes=["Bscr"])
            for ch in range(8):
                for r in range(2):
                    fw.dma(Bq2[r * 64:r * 64 + KA, :, :], self.Bscr[ch * 16:(ch + 1) * 16, r, 0:KA, :].rearrange("l k c -> k l c"), reads=["Bscr"], writes=["hy_Bq2"])
                for chalf in range(2):
                    ps, pk = self.pb[(2 * ch + chalf) % 4], ("pb", (2 * ch + chalf) % 4)
                    for i in range(16):
                        n_lo = ch * 16 + i
                        fw.op(fw.pe, lambda t_: t_.matmul(ps[:, i * NH:(i + 1) * NH], lhsT=Bq2[:, i, chalf * 128:(chalf + 1) * 128], rhs=FAi[:, n_lo, :], start=True, stop=True),
                              reads=["hy_Bq2", "hy_FAi"], writes=[pk], inc=(i == 15))
                    ov_ = y32[:, chalf, :].rearrange("p (h l) -> p l h", l=128)[:, ch * 16:(ch + 1) * 16, :]
                    iv_ = ps[:, 0:16 * NH].rearrange("p (j h) -> p j h", h=NH)
                    if chalf == 0:
                        fw.op(fw.dve, lambda v: v.tensor_copy(out=ov_, in_=iv_), reads=[pk], writes=["hy_y32"])
                    else:
                        fw.op(fw.act, lambda a: a.activation(out=ov_, in_=iv_, func=AF.Identity), reads=[pk], writes=["hy_y32"])
            for chalf in range(2):
                fw.dma(self.hyT[chalf * 128:(chalf + 1) * 128, t0:t0 + L_], y32[:, chalf, :], reads=["hy_y32"], writes=[("hyT", chalf)])

    def s5(self, l):
        nc, fw = self.nc, self.fw
        NA = NT // 8
        TWO_PI = 2.0 * math.pi
        with ExitStack() as es:
            sb = lambda n, s, d=F32: es.enter_context(nc.sbuf_tensor(self.uniq(n), s, d))
            zb = sb("s5_zb", [128, 2, NT], BF16)
            CQ = sb("s5_CQ", [128, 16, 2, 8, 128], BF16)
            BD = sb("s5_BD", [128, 4, 8, 128], BF16)
            HM = sb("s5_HM", [128, 16, 2, NA], BF16)
            sc = sb("s5_scS", [128, 3, 16]); h0 = sb("s5_h0S", [128, 2, 16]); skp = sb("s5_skS", [128, 2])
            idx = sb("s5_idxS", [128, 513])
            W = {}
            for nm in ("st", "ar", "th", "rho", "c1", "s1", "pr1", "pi1", "nr", "dd", "fr", "fi", "t1", "t2", "rho8", "phr", "th2"):
                W[nm] = sb("s5w_" + nm, [128, 16])
            wi = sb("s5w_i", [128, 16], I32)
            PWr = sb("s5_PWr", [128, 9, 16]); PWi = sb("s5_PWi", [128, 9, 16]); NPWi = sb("s5_NPWi", [128, 9, 16])
            for h in range(2):
                fw.dma(zb[:, h, :], self.zbT[h * 128:(h + 1) * 128, :], reads=[("zbT", h)], writes=["s5_zb"])
            fw.dma(sc[:], self.s5_sc[l], writes=["s5_sc"]); fw.dma(h0[:], self.s5_h0[l], writes=["s5_h0"])
            fw.dma(skp[:], self.s5_skip[l], writes=["s5_skp"]); fw.dma(idx[:], self.s5_idx[:, :], writes=["s5_idx"])
            K = "s5_small"
            dv = lambda fn, r=(K, "s5_sc"), w=(K,): fw.op(fw.dve, fn, reads=list(r), writes=list(w))
            ac = lambda fn, r=(K, "s5_sc"), w=(K,): fw.op(fw.act, fn, reads=list(r), writes=list(w))
            lre, lim, lst = sc[:, 0, :], sc[:, 1, :], sc[:, 2, :]

            def reduce_sin(dst, src, tmpf, tmpi):
                dv(lambda v: v.tensor_scalar(out=tmpi, in0=src, scalar1=1.0 / TWO_PI, scalar2=None, op0=ALU.mult))
                dv(lambda v: v.tensor_copy(out=tmpf, in_=tmpi))
                dv(lambda v: v.scalar_tensor_tensor(out=src, in0=tmpf, scalar=-TWO_PI, in1=src, op0=ALU.mult, op1=ALU.add))
                dv(lambda v: v.tensor_scalar(out=src, in0=src, scalar1=3.141592, scalar2=-3.141592, op0=ALU.min, op1=ALU.max))
                ac(lambda a: a.activation(out=dst, in_=src, func=AF.Sin))

            ac(lambda a: a.activation(out=W["st"][:], in_=lst, func=AF.Exp))
            dv(lambda v: v.tensor_tensor(out=W["ar"][:], in0=lre, in1=W["st"][:], op=ALU.mult))
            dv(lambda v: v.tensor_tensor(out=W["th"][:], in0=lim, in1=W["st"][:], op=ALU.mult))
            ac(lambda a: a.activation(out=W["rho"][:], in_=W["ar"][:], func=AF.Exp))
            ac(lambda a: a.activation(out=W["rho8"][:], in_=W["ar"][:], func=AF.Exp, scale=8.0))
            dv(lambda v: v.tensor_scalar(out=W["phr"][:], in0=W["th"][:], scalar1=8.0, scalar2=None, op0=ALU.mult))
            dv(lambda v: v.tensor_scalar(out=W["th2"][:], in0=W["th"][:], scalar1=math.pi / 2, scalar2=None, op0=ALU.add))
            reduce_sin(W["s1"][:], W["th"][:], W["t1"][:], wi[:])
            reduce_sin(W["c1"][:], W["th2"][:], W["t1"][:], wi[:])
            dv(lambda v: v.tensor_scalar(out=wi[:], in0=W["phr"][:], scalar1=1.0 / TWO_PI, scalar2=None, op0=ALU.mult))
            dv(lambda v: v.tensor_copy(out=W["t1"][:], in_=wi[:]))
            dv(lambda v: v.scalar_tensor_tensor(out=W["phr"][:], in0=W["t1"][:], scalar=-TWO_PI, in1=W["phr"][:], op0=ALU.mult, op1=ALU.add))
            dv(lambda v: v.tensor_tensor(out=W["pr1"][:], in0=W["rho"][:], in1=W["c1"][:], op=ALU.mult))
            dv(lambda v: v.tensor_tensor(out=W["pi1"][:], in0=W["rho"][:], in1=W["s1"][:], op=ALU.mult))
            dv(lambda v: v.tensor_scalar(out=W["nr"][:], in0=W["pr1"][:], scalar1=-1.0, scalar2=None, op0=ALU.add))
            dv(lambda v: v.tensor_tensor(out=W["dd"][:], in0=lre, in1=lre, op=ALU.mult))
            dv(lambda v: v.tensor_tensor(out=W["t1"][:], in0=lim, in1=lim, op=ALU.mult))
            dv(lambda v: v.tensor_tensor(out=W["dd"][:], in0=W["dd"][:], in1=W["t1"][:], op=ALU.add))
            dv(lambda v: v.reciprocal(out=W["dd"][:], in_=W["dd"][:]))
            dv(lambda v: v.tensor_tensor(out=W["t1"][:], in0=W["nr"][:], in1=lre, op=ALU.mult))
            dv(lambda v: v.tensor_tensor(out=W["t2"][:], in0=W["pi1"][:], in1=lim, op=ALU.mult))
            dv(lambda v: v.tensor_tensor(out=W["t1"][:], in0=W["t1"][:], in1=W["t2"][:], op=ALU.add))
            dv(lambda v: v.tensor_tensor(out=W["fr"][:], in0=W["t1"][:], in1=W["dd"][:], op=ALU.mult))
            dv(lambda v: v.tensor_tensor(out=W["t1"][:], in0=W["pi1"][:], in1=lre, op=ALU.mult))
            dv(lambda v: v.tensor_tensor(out=W["t2"][:], in0=W["nr"][:], in1=lim, op=ALU.mult))
            dv(lambda v: v.tensor_tensor(out=W["t1"][:], in0=W["t1"][:], in1=W["t2"][:], op=ALU.subtract))
            dv(lambda v: v.tensor_tensor(out=W["fi"][:], in0=W["t1"][:], in1=W["dd"][:], op=ALU.mult))
            dv(lambda v: v.memset(PWr[:, 0, :], 1.0)); dv(lambda v: v.memset(PWi[:, 0, :], 0.0))
            for j in range(8):
                dv(lambda v: v.tensor_tensor(out=W["t1"][:], in0=PWr[:, j, :], in1=W["pr1"][:], op=ALU.mult))
                dv(lambda v: v.tensor_tensor(out=W["t2"][:], in0=PWi[:, j, :], in1=W["pi1"][:], op=ALU.mult))
                dv(lambda v: v.tensor_tensor(out=PWr[:, j + 1, :], in0=W["t1"][:], in1=W["t2"][:], op=ALU.subtract))
                dv(lambda v: v.tensor_tensor(out=W["t1"][:], in0=PWr[:, j, :], in1=W["pi1"][:], op=ALU.mult))
                dv(lambda v: v.tensor_tensor(out=W["t2"][:], in0=PWi[:, j, :], in1=W["pr1"][:], op=ALU.mult))
                dv(lambda v: v.tensor_tensor(out=PWi[:, j + 1, :], in0=W["t1"][:], in1=W["t2"][:], op=ALU.add))
            dv(lambda v: v.tensor_scalar(out=NPWi[:], in0=PWi[:], scalar1=-1.0, scalar2=None, op0=ALU.mult))

            def ucols(d, h, r, a_lo, n):
                if d == 0:
                    s0 = r + 8 * a_lo
                    return zb[:, h, s0:s0 + 8 * (n - 1) + 1:8]
                s0 = NT - 1 - r - 8 * a_lo
                e0 = s0 - 8 * n
                return zb[:, h, s0:e0:-8] if e0 >= 0 else zb[:, h, s0::-8]

            with ExitStack() as es2:
                sb2 = lambda n, s, d=F32: es2.enter_context(nc.sbuf_tensor(self.uniq(n), s, d))
                Bm = [sb2(f"s5_Bm{i}", [128, 2, 128]) for i in range(2)]
                Cm = [sb2(f"s5_Cm{i}", [128, 2, 128]) for i in range(2)]
                nCi = [sb2(f"s5_nCi{i}", [128, 128]) for i in range(2)]
                Bb = sb2("s5_Bb", [128, 2, 128]); tA = sb2("s5_tA", [128, 128]); tB = sb2("s5_tB", [128, 128])
                Pm = [sb2("s5_Pm0", [128, 2, 8, 128])] * 2
                WV = [sb2("s5_WV0", [128, 2, 8, 128], BF16)] * 2
                Vs = sb2("s5_Vs", [128, 2, NA])
                Tt = sb2("s5_Tt", [128, 2, 513]); ti = sb2("s5_ti", [128, 2, 513], I32); tf = sb2("s5_tf", [128, 2, 513])
                TA = sb2("s5_TA", [128, 2, NA]); TH = sb2("s5_TH", [128, 2, NA + 3])
                Wr = sb2("s5_Wr", [128, 2, NA]); w1 = sb2("s5_w1", [128, NA + 3]); w2 = sb2("s5_w2", [128, NA + 3])
                Gm = sb2("s5_Gm", [128, 2, NA + 3]); Hf = sb2("s5_Hf", [128, 2, NA + 3])
                for d in range(2):
                    if d == 0:
                        segs = [(0, 32, 0), (32, 32, 1), (64, 512, 2)]
                    else:
                        segs = [(0, 512, 2), (512, 32, 1), (544, 32, 0)]
                    for pr in range(8):
                        col = d * 8 + pr
                        h = pr // 4
                        q = pr % 2
                        Bk, Ck, Pk, WVk = ("s5_Bm", q), ("s5_Cm", q), ("s5_Pm", 0), ("s5_WV", 0)
                        for ri in range(2):
                            fw.dma(Bm[q][:, ri, :], self.s5_B[l, d, pr, ri], writes=[Bk])
                            fw.dma(Cm[q][:, ri, :], self.s5_C[l, d, pr, ri], writes=[Ck])
                        fr, fi = W["fr"][:, col:col + 1], W["fi"][:, col:col + 1]
                        fw.op(fw.dve, lambda v: v.tensor_scalar(out=tA[:], in0=Bm[q][:, 1, :], scalar1=fi, scalar2=None, op0=ALU.mult), reads=[Bk, K], writes=["s5_tA"])
                        fw.op(fw.dve, lambda v: v.scalar_tensor_tensor(out=Bb[:, 0, :], in0=Bm[q][:, 0, :], scalar=fr, in1=tA[:], op0=ALU.mult, op1=ALU.subtract),
                              reads=[Bk, K, "s5_tA"], writes=["s5_Bb"])
                        fw.op(fw.dve, lambda v: v.tensor_scalar(out=tB[:], in0=Bm[q][:, 0, :], scalar1=fi, scalar2=None, op0=ALU.mult), reads=[Bk, K], writes=["s5_tB"])
                        fw.op(fw.dve, lambda v: v.scalar_tensor_tensor(out=Bb[:, 1, :], in0=Bm[q][:, 1, :], scalar=fr, in1=tB[:], op0=ALU.mult, op1=ALU.add),
                              reads=[Bk, K, "s5_tB"], writes=["s5_Bb"])
                        fw.op(fw.pool, lambda g: g.tensor_scalar(out=nCi[q][:], in0=Cm[q][:, 1, :], scalar1=-1.0, scalar2=None, op0=ALU.mult), reads=[Ck], writes=[("s5_nCi", q)])
                        for j in range(8):
                            pr_j, pi_j = PWr[:, j, col:col + 1], PWi[:, j, col:col + 1]
                            fw.op(fw.dve, lambda v: v.tensor_scalar(out=tA[:], in0=Bb[:, 1, :], scalar1=pi_j, scalar2=None, op0=ALU.mult), reads=["s5_Bb", K], writes=["s5_tA"])
                            fw.op(fw.dve, lambda v: v.scalar_tensor_tensor(out=Pm[q][:, 0, j, :], in0=Bb[:, 0, :], scalar=pr_j, in1=tA[:], op0=ALU.mult, op1=ALU.subtract),
                                  reads=["s5_Bb", K, "s5_tA"], writes=[Pk])
                            fw.op(fw.dve, lambda v: v.tensor_scalar(out=tB[:], in0=Bb[:, 0, :], scalar1=pi_j, scalar2=None, op0=ALU.mult), reads=["s5_Bb", K], writes=["s5_tB"])
                            fw.op(fw.dve, lambda v: v.scalar_tensor_tensor(out=Pm[q][:, 1, j, :], in0=Bb[:, 1, :], scalar=pr_j, in1=tB[:], op0=ALU.mult, op1=ALU.add),
                                  reads=["s5_Bb", K, "s5_tB"], writes=[Pk])
                        for r in range(8):
                            p1, q1, nq1 = PWr[:, r + 1, col:col + 1], PWi[:, r + 1, col:col + 1], NPWi[:, r + 1, col:col + 1]
                            fw.op(fw.dve, lambda v: v.tensor_scalar(out=tA[:], in0=Cm[q][:, 1, :], scalar1=q1, scalar2=None, op0=ALU.mult), reads=[Ck, K], writes=["s5_tA"])
                            fw.op(fw.dve, lambda v: v.scalar_tensor_tensor(out=CQ[:, col, 0, r, :], in0=Cm[q][:, 0, :], scalar=p1, in1=tA[:], op0=ALU.mult, op1=ALU.subtract),
                                  reads=[Ck, K, "s5_tA"], writes=["s5_CQ"])
                            fw.op(fw.dve, lambda v: v.tensor_scalar(out=tB[:], in0=Cm[q][:, 1, :], scalar1=p1, scalar2=None, op0=ALU.mult), reads=[Ck, K], writes=["s5_tB"])
                            fw.op(fw.dve, lambda v: v.scalar_tensor_tensor(out=CQ[:, col, 1, r, :], in0=Cm[q][:, 0, :], scalar=nq1, in1=tB[:], op0=ALU.mult, op1=ALU.subtract),
                                  reads=[Ck, K, "s5_tB"], writes=["s5_CQ"])
                        for ri in range(2):
                            for half in range(2):
                                ps, pk = self.pb[half], ("pb", half)
                                for jj in range(4):
                                    rp = half * 4 + jj
                                    fw.op(fw.pe, lambda t_: t_.transpose(ps[:, jj * 128:(jj + 1) * 128], Pm[q][:, ri, 7 - rp, :], self.ident[:]),
                                          reads=[Pk, "ident"], writes=[pk], inc=(jj == 3))
                                fw.op(fw.act, lambda a: a.activation(out=WV[q][:, ri, half * 4:(half + 1) * 4, :], in_=ps[:].rearrange("p (j c) -> p j c", j=4), func=AF.Identity),
                                      reads=[pk], writes=[WVk])
                        for j in range(8):
                            ps, pk = self.pb[6 + j // 4], ("pb", 6 + j // 4)
                            first = (pr % 4 == 0) and (j % 4 == 0)
                            last = (pr % 4 == 3) and (j % 4 == 3)
                            fw.op(fw.pe, lambda t_: t_.matmul(ps[:, (j % 4) * 128:(j % 4 + 1) * 128], lhsT=Pm[q][:, 0, j, :], rhs=Cm[q][:, 0, :], start=first, stop=False,
                                                             skip_group_check=True), reads=[Pk, Ck], writes=[pk], inc=False)
                            fw.op(fw.pe, lambda t_: t_.matmul(ps[:, (j % 4) * 128:(j % 4 + 1) * 128], lhsT=Pm[q][:, 1, j, :], rhs=nCi[q][:], start=False, stop=last,
                                                             skip_group_check=True), reads=[Pk, ("s5_nCi", q)], writes=[pk], inc=(j % 4 == 3))
                        if pr % 4 == 3:
                            for hb in range(2):
                                fw.op(fw.act, lambda a: a.activation(out=BD[:, d * 2 + h, hb * 4:(hb + 1) * 4, :], in_=self.pb[6 + hb][:].rearrange("p (j c) -> p j c", j=4),
                                                                      func=AF.Identity), reads=[("pb", 6 + hb)], writes=["s5_BD"])
                        for ri in range(2):
                            for (b0, nb_, bank) in ((0, 512, 2 + 2 * ri), (512, 64, 3 + 2 * ri)):
                                ps, pk = self.pb[bank], ("pb", bank)
                                for rp in range(8):
                                    fw.op(fw.pe, lambda t_: t_.matmul(ps[:, 0:nb_], lhsT=WV[q][:, ri, rp, :], rhs=ucols(d, h, rp, b0, nb_), start=(rp == 0), stop=(rp == 7)),
                                          reads=[WVk, "s5_zb"], writes=[pk], inc=(rp == 7))
                                if ri == 0:
                                    fw.op(fw.act, lambda a: a.copy(out=Vs[:, ri, b0:b0 + nb_], in_=ps[:, 0:nb_]), reads=[pk], writes=["s5_Vs"])
                                else:
                                    fw.op(fw.dve, lambda v: v.tensor_copy(out=Vs[:, ri, b0:b0 + nb_], in_=ps[:, 0:nb_]), reads=[pk], writes=["s5_Vs"])
                        fw.op(fw.dve, lambda v: v.tensor_scalar(out=Tt[:, 1, :], in0=idx[:], scalar1=W["phr"][:, col:col + 1], scalar2=None, op0=ALU.mult), reads=["s5_idx", K], writes=["s5_Tt"])
                        fw.op(fw.dve, lambda v: v.tensor_scalar(out=Tt[:, 0, :], in0=Tt[:, 1, :], scalar1=math.pi / 2, scalar2=None, op0=ALU.add), reads=["s5_Tt"], writes=["s5_Tt"])
                        fw.op(fw.dve, lambda v: v.tensor_scalar(out=ti[:], in0=Tt[:], scalar1=1.0 / TWO_PI, scalar2=None, op0=ALU.mult), reads=["s5_Tt"], writes=["s5_ti"])
                        fw.op(fw.dve, lambda v: v.tensor_copy(out=tf[:], in_=ti[:]), reads=["s5_ti"], writes=["s5_tf"])
                        fw.op(fw.dve, lambda v: v.scalar_tensor_tensor(out=Tt[:], in0=tf[:], scalar=-TWO_PI, in1=Tt[:], op0=ALU.mult, op1=ALU.add), reads=["s5_tf", "s5_Tt"], writes=["s5_Tt"])
                        fw.op(fw.dve, lambda v: v.tensor_scalar(out=Tt[:], in0=Tt[:], scalar1=3.141592, scalar2=-3.141592, op0=ALU.min, op1=ALU.max), reads=["s5_Tt"], writes=["s5_Tt"])
                        fw.op(fw.act, lambda a: a.activation(out=Tt[:], in_=Tt[:], func=AF.Sin), reads=["s5_Tt"], writes=["s5_Tt"])
                        hoff = 0
                        for (a0, A1, sid) in segs:
                            fw.op(fw.act, lambda a: a.copy(out=TA[:, :, a0:a0 + A1], in_=Tt[:, :, 1:A1 + 1]), reads=["s5_Tt"], writes=["s5_TA"])
                            fw.op(fw.pool, lambda g: g.tensor_copy(out=TH[:, :, hoff:hoff + A1 + 1], in_=Tt[:, :, 0:A1 + 1]), reads=["s5_Tt"], writes=["s5_TH"])
                            hoff += A1 + 1
                        fw.op(fw.dve, lambda v: v.tensor_tensor(out=w1[:, 0:NA], in0=Vs[:, 0, :], in1=TA[:, 0, :], op=ALU.mult), reads=["s5_Vs", "s5_TA"], writes=["s5_w1"])
                        fw.op(fw.dve, lambda v: v.tensor_tensor(out=w2[:, 0:NA], in0=Vs[:, 1, :], in1=TA[:, 1, :], op=ALU.mult), reads=["s5_Vs", "s5_TA"], writes=["s5_w2"])
                        fw.op(fw.pool, lambda g: g.tensor_tensor(out=Wr[:, 0, :], in0=w1[:, 0:NA], in1=w2[:, 0:NA], op=ALU.add), reads=["s5_w1", "s5_w2"], writes=["s5_Wr"])
                        fw.op(fw.dve, lambda v: v.tensor_tensor(out=w1[:, 0:NA], in0=Vs[:, 1, :], in1=TA[:, 0, :], op=ALU.mult), reads=["s5_Vs", "s5_TA", "s5_w1"], writes=["s5_w1"])
                        fw.op(fw.dve, lambda v: v.tensor_tensor(out=w2[:, 0:NA], in0=Vs[:, 0, :], in1=TA[:, 1, :], op=ALU.mult), reads=["s5_Vs", "s5_TA", "s5_w2"], writes=["s5_w2"])
                        fw.op(fw.pool, lambda g: g.tensor_tensor(out=Wr[:, 1, :], in0=w1[:, 0:NA], in1=w2[:, 0:NA], op=ALU.subtract), reads=["s5_w1", "s5_w2"], writes=["s5_Wr"])
                        hoff = 0
                        for (a0, A1, sid) in segs:
                            for ri in range(2):
                                if sid == 2:
                                    fw.op(fw.act, lambda a: a.copy(out=Gm[:, ri, hoff:hoff + 1], in_=h0[:, ri, col:col + 1]), reads=["s5_h0"], writes=["s5_Gm"])
                                    init = h0[:, ri, col:col + 1]
                                else:
                                    fw.op(fw.pool, lambda g: g.memset(Gm[:, ri, hoff:hoff + 1], 0.0), writes=["s5_Gm"])
                                    init = 0.0
                                fw.op(fw.dve, lambda v: v.tensor_tensor_scan(out=Gm[:, ri, hoff + 1:hoff + 1 + A1], data0=W["rho8"][:, col:col + 1].to_broadcast([128, A1]),
                                                                               data1=Wr[:, ri, a0:a0 + A1], initial=init, op0=ALU.mult, op1=ALU.add),
                                      reads=["s5_Wr", K, "s5_h0"], writes=["s5_Gm"])
                            hoff += A1 + 1
                        NH_ = NA + 3
                        fw.op(fw.dve, lambda v: v.tensor_tensor(out=w1[:], in0=Gm[:, 0, :], in1=TH[:, 0, :], op=ALU.mult), reads=["s5_Gm", "s5_TH", "s5_w1"], writes=["s5_w1"])
                        fw.op(fw.dve, lambda v: v.tensor_tensor(out=w2[:], in0=Gm[:, 1, :], in1=TH[:, 1, :], op=ALU.mult), reads=["s5_Gm", "s5_TH", "s5_w2"], writes=["s5_w2"])
                        fw.op(fw.pool, lambda g: g.tensor_tensor(out=Hf[:, 0, :], in0=w1[:], in1=w2[:], op=ALU.subtract), reads=["s5_w1", "s5_w2"], writes=["s5_Hf"])
                        fw.op(fw.dve, lambda v: v.tensor_tensor(out=w1[:], in0=Gm[:, 0, :], in1=TH[:, 1, :], op=ALU.mult), reads=["s5_Gm", "s5_TH", "s5_w1"], writes=["s5_w1"])
                        fw.op(fw.dve, lambda v: v.tensor_tensor(out=w2[:], in0=Gm[:, 1, :], in1=TH[:, 0, :], op=ALU.mult), reads=["s5_Gm", "s5_TH", "s5_w2"], writes=["s5_w2"])
                        fw.op(fw.pool, lambda g: g.tensor_tensor(out=Hf[:, 1, :], in0=w1[:], in1=w2[:], op=ALU.add), reads=["s5_w1", "s5_w2"], writes=["s5_Hf"])
                        hoff = 0
                        for (a0, A1, sid) in segs:
                            fw.op(fw.act, lambda a: a.activation(out=HM[:, col, :, a0:a0 + A1], in_=Hf[:, :, hoff:hoff + A1], func=AF.Identity), reads=["s5_Hf"], writes=["s5_HM"])
                            if sid < 2:
                                for ri in range(2):
                                    fw.dma(self.osf[l, sid, ri, d, 2 * pr:2 * pr + 2, :].rearrange("e (n o) -> (e n) o", o=1), Hf[:, ri, hoff + A1:hoff + A1 + 1],
                                           reads=["s5_Hf"], is_output=True)
                            hoff += A1 + 1
            fw.barrier()
            with ExitStack() as es3:
                sb3 = lambda n, s, d=F32: es3.enter_context(nc.sbuf_tensor(self.uniq(n), s, d))
                GW = sb3("s5_GW", [128, 2, 512], BF16); gb = sb3("s5_gbS", [128, 4])
                y16 = [sb3(f"s5_y16{i}", [128, 2, 512], BF16) for i in range(2)]
                sg = [sb3(f"s5_sg{i}", [128, 512]) for i in range(2)]
                yo = [sb3(f"s5_yo{i}", [128, 512], BF16) for i in range(2)]
                self.load_w_bf16(GW, self.s5_gw[l], "s5_GW", 1)
                fw.dma(gb[:], self.s5_gb[l], writes=["s5_gb"])
                for blk in range(NT // 512):
                    c0 = blk * 512
                    yb_ = y16[blk % 2]; ybk = ("s5_y16", blk % 2)
                    for h in range(2):
                        ps, pk = self.pb[h], ("pb", h)
                        first = True
                        for d in range(2):
                            a_lo = blk * 64 if d == 0 else NA - 64 * (blk + 1)
                            for r in range(8):
                                if d == 0:
                                    oap = ps[:, r:r + 8 * 63 + 1:8]
                                else:
                                    s0 = 511 - r
                                    oap = ps[:, s0:(7 - r) - 1:-8] if (7 - r) - 1 >= 0 else ps[:, s0::-8]
                                for r2 in range(r + 1):
                                    fw.op(fw.pe, lambda t_: t_.matmul(oap, lhsT=BD[:, d * 2 + h, r - r2, :], rhs=ucols(d, h, r2, a_lo, 64), start=first, stop=False,
                                                                     skip_group_check=True), reads=["s5_BD", "s5_zb"], writes=[pk], inc=False)
                                    first = False
                                for pr in range(4 * h, 4 * h + 4):
                                    col = d * 8 + pr
                                    for ri in range(2):
                                        lastmm = (d == 1 and r == 7 and pr == 4 * h + 3 and ri == 1)
                                        fw.op(fw.pe, lambda t_: t_.matmul(oap, lhsT=CQ[:, col, ri, r, :], rhs=HM[:, col, ri, a_lo:a_lo + 64], start=False, stop=lastmm,
                                                                         skip_group_check=True), reads=["s5_CQ", "s5_HM"], writes=[pk], inc=lastmm)
                        fw.op(fw.dve, lambda v: v.scalar_tensor_tensor(out=yb_[:, h, :], in0=zb[:, h, c0:c0 + 512], scalar=skp[:, h:h + 1], in1=ps[:], op0=ALU.mult, op1=ALU.add),
                              reads=["s5_zb", "s5_skp", pk], writes=[ybk])
                    for mt in range(2):
                        pa, pak = self.pb[2 + mt], ("pb", 2 + mt)
                        pg, pgk = self.pb[4 + mt], ("pb", 4 + mt)
                        for h in range(2):
                            fw.op(fw.pe, lambda t_: t_.matmul(pa[:], lhsT=GW[:, h, mt * 128:(mt + 1) * 128], rhs=yb_[:, h, :], start=(h == 0), stop=(h == 1)),
                                  reads=["s5_GW", ybk], writes=[pak], inc=(h == 1))
                        for h in range(2):
                            fw.op(fw.pe, lambda t_: t_.matmul(pg[:], lhsT=GW[:, h, (2 + mt) * 128:(3 + mt) * 128], rhs=yb_[:, h, :], start=(h == 0), stop=(h == 1)),
                                  reads=["s5_GW", ybk], writes=[pgk], inc=(h == 1))
                        fw.op(fw.act, lambda a: a.activation(out=sg[mt][:], in_=pg[:], func=AF.Sigmoid, bias=gb[:, 2 + mt:3 + mt], scale=1.0), reads=[pgk, "s5_gb"], writes=[("s5_sg", mt)])
                        fw.op(fw.dve, lambda v: v.scalar_tensor_tensor(out=yo[mt][:], in0=pa[:], scalar=gb[:, mt:mt + 1], in1=sg[mt][:], op0=ALU.add, op1=ALU.mult),
                              reads=[pak, "s5_gb", ("s5_sg", mt)], writes=[("s5_yo", mt)])
                        fw.dma(self.ybT[mt * 128:(mt + 1) * 128, c0:c0 + 512], yo[mt][:], reads=[("s5_yo", mt)], writes=[("ybT", mt)])

    def phase_C1(self, l):
        nc, fw = self.nc, self.fw
        with ExitStack() as es:
            sb = lambda n, s, d=F32: es.enter_context(nc.sbuf_tensor(self.uniq(n), s, d))
            PA = sb("PA", [128, 2, D], BF16)
            PB = sb("PB", [128, 2, D], BF16)
            PC = sb("PC", [128, 4, D], BF16)
            WO = sb("WO", [128, 8, D], BF16)
            xt = sb("xtC", [128, 8, TT])
            ya = sb("yaC", [128, 2, TT], BF16)
            yb = sb("ybC", [128, 2, TT], BF16)
            yc = sb("ycC", [128, 4, TT], BF16)
            gt = sb("gtC", [128, 24, TT], BF16)
            mg = sb("mgC", [128, 8, TT], BF16)
            t1 = [sb(f"t1C{i}", [128, TT]) for i in range(2)]
            t2 = [sb(f"t2C{i}", [128, TT]) for i in range(2)]
            t3 = [sb(f"t3C{i}", [128, TT]) for i in range(2)]
            self.load_w_bf16(PA, self.proj_a[l], "PA", 1)
            self.load_w_bf16(PB, self.proj_b[l], "PB", 1)
            self.load_w_bf16(PC, self.proj_c[l], "PC", 1)
            self.load_w_bf16(WO, self.w_out[l], "WO", 1)
            for tt in range(NTILE):
                c0 = tt * TT
                j = 0 if tt == 0 else 1
                for k in range(8):
                    fw.dma(xt[:, k, :], self.xT[k * 128:(k + 1) * 128, c0:c0 + TT], reads=[("xT", k)], writes=["xt"])
                for k in range(2):
                    fw.dma(ya[:, k, :], self.yaT[k * 128:(k + 1) * 128, c0:c0 + TT], reads=[("yaT", k)], writes=["ya"])
                    fw.dma(yb[:, k, :], self.ybT[k * 128:(k + 1) * 128, c0:c0 + TT], reads=[("ybT", k)], writes=["yb"])
                for k in range(4):
                    fw.dma(yc[:, k, :], self.ycT[k * 128:(k + 1) * 128, c0:c0 + TT], reads=[("ycT", k)], writes=["yc"])
                for m in range(24):
                    fw.dma(gt[:, m, :], self.gT[m * 128:(m + 1) * 128, c0:c0 + TT], reads=[("gT", m)], writes=["gt"])
                for mf in range(8):
                    par = mf % 2
                    pa, pb_, pc = self.pb[par * 3 + 0], self.pb[par * 3 + 1], self.pb[par * 3 + 2]
                    ka, kb, kc = ("pb", par * 3), ("pb", par * 3 + 1), ("pb", par * 3 + 2)
                    cs = slice(mf * 128, (mf + 1) * 128)
                    for k in range(2):
                        fw.op(fw.pe, lambda t, k=k: t.matmul(pa[:], lhsT=PA[:, k, cs], rhs=ya[:, k, :], start=(k == 0), stop=(k == 1)),
                              reads=["PA", "ya"], writes=[ka], inc=(k == 1))
                    for k in range(2):
                        fw.op(fw.pe, lambda t, k=k: t.matmul(pb_[:], lhsT=PB[:, k, cs], rhs=yb[:, k, :], start=(k == 0), stop=(k == 1)),
                              reads=["PB", "yb"], writes=[kb], inc=(k == 1))
                    for k in range(4):
                        fw.op(fw.pe, lambda t, k=k: t.matmul(pc[:], lhsT=PC[:, k, cs], rhs=yc[:, k, :], start=(k == 0), stop=(k == 3)),
                              reads=["PC", "yc"], writes=[kc], inc=(k == 3))
                    a_, b_, c_ = t1[par], t2[par], t3[par]
                    fw.op(fw.dve, lambda v: v.tensor_tensor(out=a_[:], in0=pa[:], in1=gt[:, mf, :], op=ALU.mult), reads=[ka, "gt"], writes=[("t1", par)])
                    fw.op(fw.dve, lambda v: v.tensor_tensor(out=b_[:], in0=pb_[:], in1=gt[:, 8 + mf, :], op=ALU.mult), reads=[kb, "gt"], writes=[("t2", par)])
                    fw.op(fw.dve, lambda v: v.tensor_tensor(out=c_[:], in0=pc[:], in1=gt[:, 16 + mf, :], op=ALU.mult), reads=[kc, "gt"], writes=[("t3", par)])
                    fw.op(fw.pool, lambda g: g.tensor_tensor(out=a_[:], in0=a_[:], in1=b_[:], op=ALU.add), reads=[("t1", par), ("t2", par)], writes=[("t1", par)])
                    fw.op(fw.pool, lambda g: g.tensor_tensor(out=mg[:, mf, :], in0=a_[:], in1=c_[:], op=ALU.add), reads=[("t1", par), ("t3", par)], writes=["mg"])
                for mf in range(8):
                    ps, pk = self.pb[6 + mf % 2], ("pb", 6 + mf % 2)
                    for k in range(8):
                        fw.op(fw.pe, lambda t, k=k: t.matmul(ps[:], lhsT=WO[:, k, mf * 128:(mf + 1) * 128], rhs=mg[:, k, :], start=(k == 0), stop=(k == 7)),
                              reads=["WO", "mg"], writes=[pk], inc=(k == 7))
                    fw.op(fw.dve, lambda v: v.scalar_tensor_tensor(out=xt[:, mf, :], in0=ps[:], scalar=self.modT[:, 16 + mf, j:j + 1], in1=xt[:, mf, :],
                                                                   op0=ALU.mult, op1=ALU.add), reads=[pk, "modT", "xt"], writes=["xt"])
                for k in range(8):
                    fw.dma(self.xT[k * 128:(k + 1) * 128, c0:c0 + TT], xt[:, k, :], reads=["xt"], writes=[("xT", k)])

    def phase_C2(self, l):
        nc, fw = self.nc, self.fw
        with ExitStack() as es:
            sb = lambda n, s, d=F32: es.enter_context(nc.sbuf_tensor(self.uniq(n), s, d))
            WG = sb("WG", [128, 8, FF], BF16)
            WU = sb("WU", [128, 8, FF], BF16)
            xt = sb("xtD", [128, 8, TT])
            self._xsq = sb("xsqD", [128, 8, TT])
            self._rin = sb("rinD", [128, TT])
            hT = sb("hTD", [128, 8, TT], BF16)
            sl = [sb(f"slD{i}", [128, TT]) for i in range(2)]
            ao = [sb(f"aoD{i}", [128, TT], BF16) for i in range(3)]
            self.load_w_bf16(WG, self.w_gate[l], "WG", 2)
            self.load_w_bf16(WU, self.w_up[l], "WU", 2)
            for tt in range(NTILE):
                c0 = tt * TT
                j = 0 if tt == 0 else 1
                for k in range(8):
                    fw.dma(xt[:, k, :], self.xT[k * 128:(k + 1) * 128, c0:c0 + TT], reads=[("xT", k)], writes=["xt"])
                self.norm_mod(es, xt, self.a2, 24, j, hT, "D")
                for jj in range(NJ):
                    par = jj % 3
                    pg, pu = self.pb[par * 2], self.pb[par * 2 + 1]
                    kg, ku = ("pb", par * 2), ("pb", par * 2 + 1)
                    cs = slice(jj * 128, (jj + 1) * 128)
                    for k in range(8):
                        fw.op(fw.pe, lambda t, k=k: t.matmul(pg[:], lhsT=WG[:, k, cs], rhs=hT[:, k, :], start=(k == 0), stop=(k == 7)),
                              reads=["WG", "hT"], writes=[kg], inc=(k == 7))
                    for k in range(8):
                        fw.op(fw.pe, lambda t, k=k: t.matmul(pu[:], lhsT=WU[:, k, cs], rhs=hT[:, k, :], start=(k == 0), stop=(k == 7)),
                              reads=["WU", "hT"], writes=[ku], inc=(k == 7))
                    s_ = sl[jj % 2]; sk = ("sl", jj % 2)
                    a_ = ao[jj % 3]; ak = ("ao", jj % 3)
                    fw.op(fw.act, lambda a: a.activation(out=s_[:], in_=pg[:], func=AF.Silu), reads=[kg], writes=[sk])
                    fw.op(fw.dve, lambda v: v.tensor_tensor(out=a_[:], in0=pu[:], in1=s_[:], op=ALU.mult), reads=[ku, sk], writes=[ak])
                    fw.dma(self.aT[jj * 128:(jj + 1) * 128, c0:c0 + TT], a_[:], reads=[ak], writes=[("aT", jj)])

    def phase_C3(self, l):
        nc, fw = self.nc, self.fw
        with ExitStack() as es:
            sb = lambda n, s, d=F32: es.enter_context(nc.sbuf_tensor(self.uniq(n), s, d))
            WD = sb("WD", [128, NJ, D], BF16)
            xt = sb("xtE", [128, 8, TT])
            at = sb("atE", [128, NJ, TT], BF16)
            self.load_w_bf16(WD, self.w_down[l], "WD", 1)
            for tt in range(NTILE):
                c0 = tt * TT
                j = 0 if tt == 0 else 1
                for k in range(8):
                    fw.dma(xt[:, k, :], self.xT[k * 128:(k + 1) * 128, c0:c0 + TT], reads=[("xT", k)], writes=["xt"])
                for jj in range(NJ):
                    fw.dma(at[:, jj, :], self.aT[jj * 128:(jj + 1) * 128, c0:c0 + TT], reads=[("aT", jj)], writes=["at"])
                for mf in range(8):
                    ps, pk = self.pb[mf % 4], ("pb", mf % 4)
                    for jj in range(NJ):
                        fw.op(fw.pe, lambda t, jj=jj: t.matmul(ps[:], lhsT=WD[:, jj, mf * 128:(mf + 1) * 128], rhs=at[:, jj, :], start=(jj == 0), stop=(jj == NJ - 1)),
                              reads=["WD", "at"], writes=[pk], inc=(jj == NJ - 1))
                    fw.op(fw.dve, lambda v: v.scalar_tensor_tensor(out=xt[:, mf, :], in0=ps[:], scalar=self.modT[:, 40 + mf, j:j + 1], in1=xt[:, mf, :],
                                                                   op0=ALU.mult, op1=ALU.add), reads=[pk, "modT", "xt"], writes=["xt"])
                for k in range(8):
                    fw.dma(self.xT[k * 128:(k + 1) * 128, c0:c0 + TT], xt[:, k, :], reads=["xt"], writes=[("xT", k)])

    def phase_final(self):
        nc, fw = self.nc, self.fw
        with ExitStack() as es:
            sb = lambda n, s, d=F32: es.enter_context(nc.sbuf_tensor(self.uniq(n), s, d))
            xt = sb("xtF", [128, 8, TT])
            self._xsq = sb("xsqF", [128, 8, TT])
            self._rin = sb("rinF", [128, TT])
            hT = sb("hTF", [128, 8, TT])
            gf = sb("gfin", [128, 8])
            fw.dma(gf[:], self.final_g[:, :], writes=["gfin"])
            for tt in range(NTILE):
                c0 = tt * TT
                for k in range(8):
                    fw.dma(xt[:, k, :], self.xT[k * 128:(k + 1) * 128, c0:c0 + TT], reads=[("xT", k)], writes=["xt"])
                self.norm_mod(es, xt, gf, None, 0, hT, "F")
                for k in range(8):
                    fw.dma(self.yT[k * 128:(k + 1) * 128, c0:c0 + TT], hT[:, k, :], reads=["hT"], is_output=True)


_SHARED_KEYS_PER_CORE = ("xT_in", "cT", "cache_k", "cache_v", "s5_h0")


def host_inputs(inputs, core, depth=DEPTH, shared=None):
    if shared is not None and shared.get("ready"):
        return _host_core_only(inputs, core, depth, shared)
    m = _host_inputs_full(inputs, core, depth)
    if shared is not None:
        shared.update({k: v for k, v in m.items() if k not in _SHARED_KEYS_PER_CORE})
        shared["ready"] = True
    return m


def _host_core_only(inputs, core, depth, shared):
    f = lambda a: np.ascontiguousarray(np.asarray(a, dtype=np.float32))
    xp = np.asarray(inputs["x_prompt"])[2 * core:2 * core + 2].reshape(NP, D)
    b = core % 2
    xs = np.asarray(inputs["x_sample"])[b]
    xT = np.concatenate([xp, xs], 0).T
    cT = np.stack([np.asarray(inputs["c_ctx"]), np.asarray(inputs["c"])[b]], 1)

    def pair(a):
        return a.reshape(depth, 2, 8, 2, 64).transpose(0, 3, 4, 1, 2).reshape(depth, 128, 16)
    m = {k: v for k, v in shared.items() if k != "ready"}
    m["xT_in"] = f(xT)
    m["cT"] = f(cT.reshape(8, 128, 2).transpose(1, 0, 2))
    m["cache_k"] = f(np.asarray(inputs["cache_k"])[b, :depth].reshape(depth, 512, 128))
    m["cache_v"] = f(np.asarray(inputs["cache_v"])[b, :depth].reshape(depth, 512, 128))
    m["s5_h0"] = f(np.stack([pair(np.asarray(inputs["state_s5_re"])[b, :depth]), pair(np.asarray(inputs["state_s5_im"])[b, :depth])], 2))
    return m


def _host_inputs_full(inputs, core, depth=DEPTH):
    f = lambda a: np.ascontiguousarray(np.asarray(a, dtype=np.float32))
    xp = np.asarray(inputs["x_prompt"])[2 * core:2 * core + 2].reshape(NP, D)
    b = core % 2
    xs = np.asarray(inputs["x_sample"])[b]
    xT = np.concatenate([xp, xs], 0).T
    cT = np.stack([np.asarray(inputs["c_ctx"]), np.asarray(inputs["c"])[b]], 1)
    perm = _perm_cols()
    w_in = np.asarray(inputs["w_in"])[:depth]
    w_ext = np.concatenate([w_in, w_in[:, :, perm]], axis=2)
    cos2, sin2 = _rope_tables()
    def pair(a):
        return a.reshape(depth, 2, 8, 2, 64).transpose(0, 3, 4, 1, 2).reshape(depth, 128, 16)
    lre = pair(np.asarray(inputs["s5_lam_re"])[:depth]); lim = pair(np.asarray(inputs["s5_lam_im"])[:depth])
    lst = pair(np.broadcast_to(np.asarray(inputs["s5_log_step"])[:depth][..., None], (depth, 2, 16, 64)))
    s5_sc = np.stack([lre, lim, lst], 2)
    s5_h0 = np.stack([pair(np.asarray(inputs["state_s5_re"])[b, :depth]), pair(np.asarray(inputs["state_s5_im"])[b, :depth])], 2)
    s5_B = np.zeros((depth, 2, 8, 2, 128, 128), np.float32); s5_C = np.zeros((depth, 2, 8, 2, 128, 128), np.float32)
    bre = np.asarray(inputs["s5_b_re"])[:depth]; bim = np.asarray(inputs["s5_b_im"])[:depth]
    cre = np.asarray(inputs["s5_c_re"])[:depth]; cim = np.asarray(inputs["s5_c_im"])[:depth]
    for g_ in range(16):
        pr_, e_, gl_ = g_ // 2, g_ % 2, g_ % 8
        s5_B[:, :, pr_, 0, e_ * 64:(e_ + 1) * 64, gl_ * 16:(gl_ + 1) * 16] = bre[:, :, g_]
        s5_B[:, :, pr_, 1, e_ * 64:(e_ + 1) * 64, gl_ * 16:(gl_ + 1) * 16] = bim[:, :, g_]
        s5_C[:, :, pr_, 0, e_ * 64:(e_ + 1) * 64, gl_ * 16:(gl_ + 1) * 16] = cre[:, :, g_].transpose(0, 1, 3, 2)
        s5_C[:, :, pr_, 1, e_ * 64:(e_ + 1) * 64, gl_ * 16:(gl_ + 1) * 16] = cim[:, :, g_].transpose(0, 1, 3, 2)
    pk = lambda a: a.reshape(a.shape[0], -1, 128).transpose(0, 2, 1)
    import ml_dtypes
    bf = lambda a: np.ascontiguousarray(np.asarray(a, dtype=np.float32).astype(ml_dtypes.bfloat16))

    def zp(a, r, c):
        o = np.zeros(a.shape[:-2] + (r, c), np.float32)
        o[..., :a.shape[-2], :a.shape[-1]] = a
        return o
    m = {
        "xT_in": f(xT), "cT": f(cT.reshape(8, 128, 2).transpose(1, 0, 2)),
        "ada_w": f(np.asarray(inputs["ada_w"])[:depth]), "ada_b": f(pk(np.asarray(inputs["ada_b"])[:depth])),
        "norm1_g": f(pk(np.asarray(inputs["norm1_g"])[:depth])), "norm2_g": f(pk(np.asarray(inputs["norm2_g"])[:depth])),
        "final_g": f(pk(np.asarray(inputs["final_norm_g"])[None])[0]),
        "w_in": f(w_ext),
        "proj_a": f(np.asarray(inputs["proj_a"])[:depth]), "proj_b": f(np.asarray(inputs["proj_b"])[:depth]),
        "proj_c": f(np.asarray(inputs["proj_c"])[:depth]), "w_out": f(np.asarray(inputs["w_out"])[:depth]),
        "w_gate": f(np.asarray(inputs["ffn_w_gate"])[:depth]), "w_up": f(np.asarray(inputs["ffn_w_up"])[:depth]),
        "w_down": f(np.asarray(inputs["ffn_w_down"])[:depth]),
        "rope_cos": cos2, "rope_sin": sin2,
        "hy_GB": bf(_hy_gb()),
        "hy_w1": f(zp(np.asarray(inputs["hy_pos_w1"])[:depth], 128, 128)),
        "hy_w2": f(zp(np.asarray(inputs["hy_pos_w2"])[:depth], 128, 128)),
        "hy_w3": f(zp(np.asarray(inputs["hy_pos_w3"])[:depth], 128, 1024)),
        "hy_vec": f(zp(np.stack([np.asarray(inputs["hy_pos_b1"])[:depth], np.asarray(inputs["hy_pos_b2"])[:depth],
                                 np.asarray(inputs["hy_sin_freq"])[:depth]], -1), 128, 3)),
        "hy_decay": f(np.asarray(inputs["hy_decay"])[:depth]),
        "hy_cw": f(np.concatenate([np.asarray(inputs["hy_conv_w"])[:depth].transpose(0, 2, 1),
                                   np.asarray(inputs["hy_conv_b"])[:depth][:, :, None]], -1).reshape(depth, 6, 128, 4).transpose(0, 2, 1, 3)),
        "hy_skip": f(np.asarray(inputs["hy_skip"])[:depth].reshape(depth, 2, 2, 128).transpose(0, 3, 1, 2)),
        "s5_sc": f(s5_sc), "s5_B": f(s5_B), "s5_C": f(s5_C), "s5_h0": f(s5_h0),
        "s5_skip": f(np.asarray(inputs["s5_skip"])[:depth].reshape(depth, 2, 128).transpose(0, 2, 1)),
        "s5_gw": f(np.asarray(inputs["s5_glu_w"])[:depth]),
        "s5_gb": f(np.asarray(inputs["s5_glu_b"])[:depth].reshape(depth, 4, 128).transpose(0, 2, 1)),
        "s5_idx": f(np.broadcast_to(np.arange(513, dtype=np.float32), (128, 513))),
        "cache_k": f(np.asarray(inputs["cache_k"])[b, :depth].reshape(depth, 512, 128)),
        "cache_v": f(np.asarray(inputs["cache_v"])[b, :depth].reshape(depth, 512, 128)),
        "sink": f(np.asarray(inputs["attn_sink"])[:depth]),
        "mask_prev": f(np.tril(np.ones((128, 128)))),
        "mask_next": f(np.triu(np.ones((128, 128)))),
    }
    for L_ in (256, 4096):
        FA, FAi = _hy_tables(L_)
        fT, negt = _hy_feat(L_)
        m[f"hy_FA{L_}"] = bf(FA); m[f"hy_FAi{L_}"] = bf(FAi)
        m[f"hy_featT{L_}"] = f(fT); m[f"hy_negt{L_}"] = f(negt)
    return m


_PROG = {}


def get_prog(depth=DEPTH, mixers=("attn", "hyena", "s5"), taps=()):
    key = (depth, tuple(mixers), tuple(taps))
    if key not in _PROG:
        _PROG[key] = Prog(depth, mixers, taps)
    return _PROG[key]


def kernel(**inputs):
    prog = get_prog()
    in_maps = []
    shared = {}
    for c in range(8):
        m = host_inputs(inputs, c, DEPTH, shared)
        in_maps.append({k: m[k] for k in prog.din})
    res = run_bass_kernel_spmd(prog.nc, in_maps, core_ids=list(range(8)))
    R = res.results
    B, S = 16, 256
    y_prompt = np.zeros((B, S, D), np.float32)
    y_sample = np.zeros((2, LS, D), np.float32)
    nk_ = np.zeros((B, DEPTH, S, 2, 64), np.float32)
    nv_ = np.zeros((B, DEPTH, S, 2, 64), np.float32)
    sre = np.zeros((B, DEPTH, 2, 16, 64), np.float32)
    sim = np.zeros((B, DEPTH, 2, 16, 64), np.float32)
    for c in range(8):
        yT = R[c]["yT"]
        y_prompt[2 * c:2 * c + 2] = yT[:, :NP].T.reshape(2, S, D)
        if c < 2:
            y_sample[c] = yT[:, NP:].T
        ok = R[c]["ok"].reshape(DEPTH, 2, S, 2, 64)
        ov = R[c]["ov"].reshape(DEPTH, 2, S, 2, 64)
        nk_[2 * c:2 * c + 2] = ok.transpose(1, 0, 2, 3, 4)
        nv_[2 * c:2 * c + 2] = ov.transpose(1, 0, 2, 3, 4)
        osf = R[c]["osf"]
        sre[2 * c:2 * c + 2] = osf[:, :, 0].transpose(1, 0, 2, 3, 4)
        sim[2 * c:2 * c + 2] = osf[:, :, 1].transpose(1, 0, 2, 3, 4)
    return (y_prompt, y_sample, nk_, nv_, sre, sim)
```

```python
from contextlib import ExitStack
import math
import numpy as np
import concourse.bass as bass
import concourse.mybir as mybir
from concourse.bass_utils import run_bass_kernel_spmd

F32 = mybir.dt.float32
BF16 = mybir.dt.bfloat16
I32 = mybir.dt.int32
AF = mybir.ActivationFunctionType
ALU = mybir.AluOpType
AX = mybir.AxisListType

D = 1024
DEPTH = 4
NP = 512
LS = 4096
NT = NP + LS
TT = 512
NTILE = NT // TT
FF = 2816
NJ = FF // 128
WIN = 4864
WEXT = WIN + 640
EPS = 1e-6


class _Eng:
    def __init__(self, name, h, sem):
        self.name, self.h, self.sem = name, h, sem
        self.count = 0
        self.seen = {}


class _Buf:
    __slots__ = ("w", "r")

    def __init__(self):
        self.w = None
        self.r = []


class FW:
    NDMA = 48

    def __init__(self, nc, es):
        self.nc = nc
        mk = lambda n: es.enter_context(nc.semaphore(n))
        self.sync = _Eng("sync", nc.sync, mk("e_sync"))
        self.act = _Eng("act", nc.scalar, mk("e_act"))
        self.dve = _Eng("dve", nc.vector, mk("e_dve"))
        self.pool = _Eng("pool", nc.gpsimd, mk("e_pool"))
        self.pe = _Eng("pe", nc.tensor, mk("e_pe"))
        self.engs = [self.sync, self.act, self.dve, self.pool, self.pe]
        self.dsems = [mk(f"d{i}") for i in range(self.NDMA)]
        self.dcount = [0] * self.NDMA
        self.dnext = 0
        self.NSW = 16
        self.dnext_sw = 0
        self.bufs = {}
        self.out_events = []
        self.n_inst = 0

    def _rec(self, key):
        b = self.bufs.get(key)
        if b is None:
            b = self.bufs[key] = _Buf()
        return b

    def _deps(self, reads, writes):
        deps = []
        for k in reads:
            b = self._rec(k)
            if b.w is not None:
                deps.append(b.w)
        for k in writes:
            b = self._rec(k)
            if b.w is not None:
                deps.append(b.w)
            deps.extend(b.r)
        return deps

    def _wait(self, eng, deps):
        need = {}
        for (sem, val) in deps:
            i = id(sem)
            if need.get(i, (None, 0))[1] < val:
                need[i] = (sem, val)
        for i, (sem, val) in need.items():
            if eng.seen.get(i, 0) >= val:
                continue
            eng.h.wait_ge(sem, val)
            eng.seen[i] = val

    def _commit(self, ev, reads, writes):
        for k in reads:
            b = self._rec(k)
            for i_, (s_, v_) in enumerate(b.r):
                if s_ is ev[0]:
                    if v_ < ev[1]:
                        b.r[i_] = ev
                    break
            else:
                b.r.append(ev)
        for k in writes:
            b = self._rec(k)
            b.w = ev
            b.r = []

    def op(self, eng, fn, reads=(), writes=(), inc=True):
        pr = [k for k in reads if isinstance(k, tuple) and k[0] == "pb"]
        if pr:
            reads = [k for k in reads if k not in pr]
            writes = list(writes) + pr
        deps = self._deps(reads, writes)
        if eng is self.pe:
            deps = [d for d in deps if d[0] is not eng.sem]
        self._wait(eng, deps)
        ins = fn(eng.h)
        if inc:
            eng.count += 1
            ins.then_inc(eng.sem, 1)
            ev = (eng.sem, eng.count)
        else:
            ev = (eng.sem, eng.count + 1)
        self._commit(ev, reads, writes)
        self.n_inst += 1
        return ins

    def dma(self, out, in_, reads=(), writes=(), q=None, is_output=False, **kw):
        q = q or self.sync
        if q is self.pool:
            i = self.NDMA - self.NSW + self.dnext_sw
            self.dnext_sw = (self.dnext_sw + 1) % self.NSW
        else:
            i = self.dnext
            self.dnext = (self.dnext + 1) % (self.NDMA - self.NSW)
        sem = self.dsems[i]
        deps = self._deps(reads, writes)
        if self.dcount[i] > 0:
            deps.append((sem, self.dcount[i]))
        self._wait(q, deps)
        ins = q.h.dma_start(out=out, in_=in_, **kw)
        self.dcount[i] += 16
        ins.then_inc(sem, 16)
        ev = (sem, self.dcount[i])
        self._commit(ev, reads, writes)
        if is_output:
            self.out_events.append(ev)
        self.n_inst += 1
        return ins

    def barrier(self):
        evs = [(e.sem, e.count) for e in self.engs if e.count > 0]
        evs += [(self.dsems[i], self.dcount[i]) for i in range(self.NDMA) if self.dcount[i] > 0]
        for e in self.engs:
            self._wait(e, evs)
        self.bufs = {}

    def finish(self):
        evs = list(self.out_events)
        evs += [(self.dsems[i], self.dcount[i]) for i in range(self.NDMA) if self.dcount[i] > 0]
        evs += [(e.sem, e.count) for e in self.engs if e.count > 0 and e is not self.sync]
        self._wait(self.sync, evs)


def _perm_cols():
    def partner(d):
        return d + 16 if (d % 32) < 16 else d - 16
    SPLIT_Q, SPLIT_K = 1024, 1536
    pq = [SPLIT_Q + h * 64 + partner(d) for h in range(8) for d in range(64)]
    pk = [SPLIT_K + g * 64 + partner(d) for g in range(2) for d in range(64)]
    return np.array(pq + pk, dtype=np.int64)


def _rope_tables():
    t = np.arange(LS)
    row = (t // 64).astype(np.float64)
    col = (t % 64).astype(np.float64)
    inv = 10000.0 ** (-np.arange(16, dtype=np.float64) / 16)
    cos = np.zeros((64, LS)); sin = np.zeros((64, LS))
    for d in range(64):
        pos = row if d < 32 else col
        ang = pos * inv[d % 16]
        cos[d] = np.cos(ang)
        sin[d] = np.sin(ang) * (-1.0 if (d % 32) < 16 else 1.0)
    cos2 = np.concatenate([cos, cos], 0).astype(np.float32)
    sin2 = np.concatenate([sin, sin], 0).astype(np.float32)
    return cos2, sin2


def _hy_tables(L):
    NH, KA, N2 = L // 128, L // 64, 2 * L
    n_hi = np.arange(128)[:, None, None]; n_lo = np.arange(128)[None, :, None]
    FA = np.zeros((128, 128, 128))
    ka = np.arange(KA)[None, None, :]
    th = 2 * np.pi * (n_lo + 128 * n_hi) * (ka + 0.5) / N2
    valid = (n_hi < NH)
    FA[:, :, :KA] = np.cos(th) * valid
    FA[:, :, 64:64 + KA] = -np.sin(th) * valid
    FAi = np.zeros((128, 128, NH))
    ka2 = np.arange(KA)[:, None, None]; nl = np.arange(128)[None, :, None]; nh = np.arange(NH)[None, None, :]
    th2 = 2 * np.pi * (nl + 128 * nh) * (ka2 + 0.5) / N2
    FAi[:KA] = np.cos(th2) / L
    FAi[64:64 + KA] = -np.sin(th2) / L
    return FA, FAi


def _hy_gb():
    nb = np.arange(128)[:, None]; kb = np.arange(64)[None, :]
    C = np.cos(2 * np.pi * nb * kb / 128); S = np.sin(2 * np.pi * nb * kb / 128)
    cat = lambda a, b: np.concatenate([a, b], 1)
    lst = [cat(C, -S), cat(S, C), cat(-S, C), cat(C, S),
           cat(C, C), cat(S, S),
           cat(S, -S), cat(-C, C), cat(-S, S), cat(C, -C),
           np.concatenate([C.T, -S.T], 0), np.concatenate([S.T, C.T], 0)]
    return np.stack(lst, 1)


def _hy_t256():
    L = 256
    n = (np.arange(128)[:, None, None, None] + 128 * np.arange(2)[None, :, None, None])
    k = np.arange(4)[None, None, :, None] + 4 * np.arange(64)[None, None, None, :]
    th = 2 * np.pi * n * (k + 0.5) / 512.0
    C, S = np.cos(th), np.sin(th)
    cat = lambda a, b: np.concatenate([a, b], -1).reshape(128, 1024)
    T = np.stack([cat(C, -S), cat(-S, C), cat(C, C), cat(S, -S), cat(-S, S)], 1)
    kk = np.arange(4)[None, :, None] + 4 * np.arange(64)[:, None, None]
    nn = np.arange(256)[None, None, :]
    th2 = 2 * np.pi * nn * (kk + 0.5) / 512.0
    Ti = np.concatenate([np.cos(th2) / L, -np.sin(th2) / L], 0).reshape(128, 1024)
    return np.concatenate([T, Ti[:, None, :]], 1)


def _hy_feat(L):
    t = np.linspace(0.0, 1.0, L)
    w = 2.0 * np.pi * np.arange(L) / L
    bands = np.linspace(1e-4, 15, 16)
    ang = w[:, None] * bands[None, :]
    feat = np.concatenate([t[:, None], np.cos(ang), -np.sin(ang)], -1)
    fT = np.zeros((128, L)); fT[:33] = feat.T
    negt = -t.reshape(L // 128, 128).T
    return fT, negt


class Prog:
    def __init__(self, depth=DEPTH, mixers=("attn", "hyena", "s5"), taps=(), stop=None):
        self.stop = stop
        self.depth = depth
        self.mixers = mixers
        self.taps = taps
        self.nc = bass.Bass("TRN2", target_bir_lowering=False)
        self.din = {}
        self.dout = {}
        self.build()

    def uniq(self, n):
        self._uid = getattr(self, "_uid", 0) + 1
        return f"{n}_{self._uid}"

    def inp(self, name, shape, dt=F32):
        t = self.nc.dram_tensor(name, list(shape), dt, kind="ExternalInput").ap()
        self.din[name] = (tuple(shape), dt)
        return t

    def outp(self, name, shape, dt=F32):
        t = self.nc.dram_tensor(name, list(shape), dt, kind="ExternalOutput").ap()
        self.dout[name] = (tuple(shape), dt)
        return t

    def scratch(self, name, shape, dt=F32):
        return self.nc.dram_tensor(name, list(shape), dt, kind="Internal").ap()

    def build(self):
        nc = self.nc
        L = self.depth
        self.xT_in = self.inp("xT_in", [D, NT])
        self.cT = self.inp("cT", [128, 8, 2])
        self.ada_w = self.inp("ada_w", [L, D, 6 * D])
        self.ada_b = self.inp("ada_b", [L, 128, 48])
        self.norm1_g = self.inp("norm1_g", [L, 128, 8])
        self.norm2_g = self.inp("norm2_g", [L, 128, 8])
        self.final_g = self.inp("final_g", [128, 8])
        self.w_in = self.inp("w_in", [L, D, WEXT])
        self.proj_a = self.inp("proj_a", [L, 256, D])
        self.proj_b = self.inp("proj_b", [L, 256, D])
        self.proj_c = self.inp("proj_c", [L, 512, D])
        self.w_out = self.inp("w_out", [L, D, D])
        self.w_gate = self.inp("w_gate", [L, D, FF])
        self.w_up = self.inp("w_up", [L, D, FF])
        self.w_down = self.inp("w_down", [L, FF, D])
        self.rope_cos = self.inp("rope_cos", [128, LS])
        self.rope_sin = self.inp("rope_sin", [128, LS])
        self.hy_FA = {256: self.inp("hy_FA256", [128, 128, 128], BF16), 4096: self.inp("hy_FA4096", [128, 128, 128], BF16)}
        self.hy_FAi = {256: self.inp("hy_FAi256", [128, 128, 2], BF16), 4096: self.inp("hy_FAi4096", [128, 128, 32], BF16)}
        self.hy_GB = self.inp("hy_GB", [128, 12, 128], BF16)
        self.hy_T256 = self.inp("hy_T256", [128, 6, 1024], BF16)
        self.hy_featT = {256: self.inp("hy_featT256", [128, 256]), 4096: self.inp("hy_featT4096", [128, 4096])}
        self.hy_negt = {256: self.inp("hy_negt256", [128, 2]), 4096: self.inp("hy_negt4096", [128, 32])}
        self.hy_w1 = self.inp("hy_w1", [L, 128, 128])
        self.hy_w2 = self.inp("hy_w2", [L, 128, 128])
        self.hy_w3 = self.inp("hy_w3", [L, 128, 1024])
        self.hy_vec = self.inp("hy_vec", [L, 128, 3])
        self.hy_decay = self.inp("hy_decay", [L, 1024])
        self.hy_cw = self.inp("hy_cw", [L, 128, 6, 4])
        self.hy_skip = self.inp("hy_skip", [L, 128, 2, 2])
        self.s5_sc = self.inp("s5_sc", [L, 128, 3, 16])
        self.s5_B = self.inp("s5_B", [L, 2, 8, 2, 128, 128])
        self.s5_C = self.inp("s5_C", [L, 2, 8, 2, 128, 128])
        self.s5_h0 = self.inp("s5_h0", [L, 128, 2, 16])
        self.s5_skip = self.inp("s5_skip", [L, 128, 2])
        self.s5_gw = self.inp("s5_gw", [L, 256, 512])
        self.s5_gb = self.inp("s5_gb", [L, 128, 4])
        self.s5_idx = self.inp("s5_idx", [128, 513])
        self.cache_k = self.inp("cache_k", [L, 512, 128])
        self.cache_v = self.inp("cache_v", [L, 512, 128])
        self.sink = self.inp("sink", [L, 8])
        self.mask_prev = self.inp("mask_prev", [128, 128])
        self.mask_next = self.inp("mask_next", [128, 128])
        self.yT = self.outp("yT", [D, NT])
        self.ok = self.outp("ok", [L, NP, 128])
        self.ov = self.outp("ov", [L, NP, 128])
        self.osf = self.outp("osf", [L, 2, 2, 2, 16, 64])
        self.xT = self.scratch("xT", [D, NT])
        self.zaT = self.scratch("zaT", [768, NT])
        self.zbT = self.scratch("zbT", [256, NT], BF16)
        self.gT = self.scratch("gT", [3072, NT], BF16)
        self.qA = self.scratch("qA", [8, 65, NT], BF16)
        self.kA = self.scratch("kA", [2, 65, NT], BF16)
        self.vtok = self.scratch("vtok", [NT, 128], BF16)
        self.nk = self.scratch("nk", [NT, 2])
        self.yaT = self.scratch("yaT", [256, NT], BF16)
        self.ybT = self.scratch("ybT", [256, NT], BF16)
        self.ycT = self.scratch("ycT", [512, NT], BF16)
        self.aT = self.scratch("aT", [FF, NT], BF16)
        self.ugT = self.scratch("ugT", [512, NT])
        self.hzT = self.scratch("hzT", [256, NT])
        self.hyT = self.scratch("hyT", [256, NT])
        self.ztok = self.scratch("ztok", [NT, 256], BF16)
        self.hraw = {256: self.scratch("hraw256", [256, 1024]), 4096: self.scratch("hraw4096", [4096, 1024])}
        self.ftok = {256: self.scratch("ftok256", [256, 1024], BF16), 4096: self.scratch("ftok4096", [4096, 1024], BF16)}
        self.Ascr = self.scratch("Ascr", [128, 128, 512], BF16)
        self.Bscr = self.scratch("Bscr", [128, 2, 64, 256], BF16)
        self.Ksp = {256: self.scratch("Ksp256", [2, 2, 128, 4, 256]), 4096: self.scratch("Ksp4096", [2, 2, 128, 64, 256])}
        self.tapo = {}
        for name, shape in self.taps:
            self.tapo[name] = self.outp("tap_" + name, shape)

        with ExitStack() as es:
            self.fw = fw = FW(nc, es)
            self.pb = [es.enter_context(nc.psum_tensor(f"pb{i}", [128, 512], F32)) for i in range(8)]
            sb = lambda n, s, d=F32: es.enter_context(nc.sbuf_tensor(self.uniq(n), s, d))
            self.ones32 = sb("ones32", [128, 128])
            self.ident = sb("ident", [128, 128])
            self.blk2 = sb("blk2", [128, 128])
            self.sc = sb("silu_c", [128, 8, 2])
            self.modT = sb("modT", [128, 48, 2])
            self.a1 = sb("a1", [128, 8, 2])
            self.a2 = sb("a2", [128, 8, 2])
            self.g1 = sb("g1", [128, 8])
            self.g2 = sb("g2", [128, 8])
            self.adab = sb("adab", [128, 48])
            self.epsc = sb("epsc", [128, 1])
            self.init_consts()
            for k in range(8):
                fw.dma(self.xT[k * 128:(k + 1) * 128, :], self.xT_in[k * 128:(k + 1) * 128, :], writes=[("xT", k)])
            fw.barrier()
            phases = [("mod", self.phase_mod), ("A", self.phase_A), ("B", self.phase_B), ("C1", self.phase_C1),
                      ("C2", self.phase_C2), ("C3", self.phase_C3)]
            done = False
            for l in range(L):
                for nm, ph in phases:
                    ph(l)
                    fw.barrier()
                    if self.stop == nm:
                        done = True
                        break
                if done:
                    break
            self.phase_final()
            fw.finish()

    def init_consts(self):
        nc, fw = self.nc, self.fw
        fw.op(fw.pool, lambda g: g.memset(self.ones32[:], 1.0), writes=["ones32"])
        fw.op(fw.pool, lambda g: g.memset(self.ident[:], 0.0), writes=["ident"])
        fw.op(fw.pool, lambda g: g.affine_select(out=self.ident[:], in_=self.ident[:], pattern=[[-1, 128]],
                                                 compare_op=ALU.not_equal, fill=1.0, base=0, channel_multiplier=1),
              reads=["ident"], writes=["ident"])
        fw.op(fw.pool, lambda g: g.memset(self.blk2[:], 0.0), writes=["blk2"])
        fw.op(fw.pool, lambda g: g.memset(self.blk2[0:64, 0:1], 1.0), reads=["blk2"], writes=["blk2"])
        fw.op(fw.pool, lambda g: g.memset(self.blk2[64:128, 1:2], 1.0), reads=["blk2"], writes=["blk2"])
        fw.op(fw.pool, lambda g: g.memset(self.epsc[:], EPS), writes=["epsc"])
        fw.dma(self.sc[:], self.cT[:, :, :], writes=["sc"])
        fw.op(fw.act, lambda a: a.activation(out=self.sc[:], in_=self.sc[:], func=AF.Silu), reads=["sc"], writes=["sc"])
        es2 = ExitStack()
        with es2:
            onesb = es2.enter_context(nc.sbuf_tensor(self.uniq("onesb"), [1, NT], BF16))
            fw.op(fw.pool, lambda g: g.memset(onesb[:], 1.0), writes=["onesb"])
            for g_ in range(2):
                fw.dma(self.kA[g_, 64:65, :], onesb[:], reads=["onesb"], writes=[("kA", g_)])
            fw.barrier()

    def phase_mod(self, l):
        nc, fw = self.nc, self.fw
        with ExitStack() as es:
            wblk = [es.enter_context(nc.sbuf_tensor(self.uniq(f"adaw{i}"), [128, 8, 1024], F32)) for i in range(2)]
            fw.dma(self.adab[:], self.ada_b[l], writes=["adab"])
            fw.dma(self.g1[:], self.norm1_g[l], writes=["g1"])
            fw.dma(self.g2[:], self.norm2_g[l], writes=["g2"])
            for blk in range(6):
                wb = wblk[blk % 2]
                key = ("adaw", blk % 2)
                for k in range(8):
                    fw.dma(wb[:, k, :], self.ada_w[l, k * 128:(k + 1) * 128, blk * 1024:(blk + 1) * 1024], writes=[key])
                for mm in range(8):
                    m = blk * 8 + mm
                    ps = self.pb[m % 4]
                    pk = ("pb", m % 4)
                    for k in range(8):
                        fw.op(fw.pe, lambda t, k=k, mm=mm, ps=ps, wb=wb: t.matmul(ps[:, 0:2], lhsT=wb[:, k, mm * 128:(mm + 1) * 128], rhs=self.sc[:, k, :],
                                                                                 start=(k == 0), stop=(k == 7)),
                              reads=[key, "sc"], writes=[pk], inc=(k == 7))
                    fw.op(fw.act, lambda a, m=m, ps=ps: a.activation(out=self.modT[:, m, :], in_=ps[:, 0:2], func=AF.Identity,
                                                                     bias=self.adab[:, m:m + 1], scale=1.0),
                          reads=[pk, "adab"], writes=["modT"])
            for (a_t, g_t, base, nm) in ((self.a1, self.g1, 8, "a1"), (self.a2, self.g2, 32, "a2")):
                for j in range(2):
                    fw.op(fw.dve, lambda v, a_t=a_t, g_t=g_t, base=base, j=j: v.scalar_tensor_tensor(
                        out=a_t[:, :, j], in0=self.modT[:, base:base + 8, j], scalar=1.0, in1=g_t[:, :], op0=ALU.add, op1=ALU.mult),
                        reads=["modT", "g1", "g2"], writes=[nm])

    def norm_mod(self, es, xt, a_t, shift_base, j, hT, tagsfx, bufs=None, sfx=""):
        nc, fw = self.nc, self.fw
        xsq, rin = bufs if bufs is not None else (self._xsq, self._rin)
        kxt, kxs, kri, khT = "xt" + sfx, "xsq" + sfx, "rin" + sfx, "hT" + sfx
        ss = self.pb[7]
        fw.op(fw.act, lambda a: a.activation(out=xsq[:], in_=xt[:], func=AF.Square), reads=[kxt], writes=[kxs])
        for k in range(8):
            fw.op(fw.pe, lambda t, k=k: t.matmul(ss[:], lhsT=self.ones32[:], rhs=xsq[:, k, :], start=(k == 0), stop=(k == 7)),
                  reads=[kxs, "ones32"], writes=[("pb", 7)], inc=(k == 7))
        fw.op(fw.act, lambda a: a.activation(out=rin[:], in_=ss[:], func=AF.Sqrt, scale=1.0 / D, bias=self.epsc[:, 0:1]),
              reads=[("pb", 7), "epsc"], writes=[kri])
        fw.op(fw.dve, lambda v: v.reciprocal(out=rin[:], in_=rin[:]), reads=[kri], writes=[kri])
        fw.op(fw.dve, lambda v: v.tensor_tensor(out=xsq[:], in0=xt[:], in1=rin[:].unsqueeze(1).to_broadcast([128, 8, TT]), op=ALU.mult),
              reads=[kxt, kri, kxs], writes=[kxs])
        for k in range(8):
            if shift_base is None:
                fw.op(fw.act, lambda a, k=k: a.activation(out=hT[:, k, :], in_=xsq[:, k, :], func=AF.Identity, scale=a_t[:, k:k + 1]),
                      reads=[kxs, "gfin"], writes=[khT])
            else:
                fw.op(fw.act, lambda a, k=k: a.activation(out=hT[:, k, :], in_=xsq[:, k, :], func=AF.Identity,
                                                          scale=a_t[:, k, j:j + 1], bias=self.modT[:, shift_base + k, j:j + 1]),
                      reads=[kxs, "a1", "a2", "modT"], writes=[khT])

    def load_w_bf16(self, dst, src, key, nchunk):
        fw = self.fw
        K = dst.shape[1]
        N = dst.shape[2]
        cw = N // nchunk
        assert cw <= 2048 and cw * nchunk == N
        for k in range(K):
            for c in range(nchunk):
                fw.dma(dst[:, k, c * cw:(c + 1) * cw], src[k * 128:(k + 1) * 128, c * cw:(c + 1) * cw], writes=[key], q=fw.pool)

    def phase_A(self, l):
        nc, fw = self.nc, self.fw
        with ExitStack() as es:
            sb = lambda n, s, d=F32: es.enter_context(nc.sbuf_tensor(self.uniq(n), s, d))
            WinB = sb("WinB", [128, 8, WEXT], BF16)
            xtb = [sb(f"xtA{i}", [128, 8, TT]) for i in range(2)]
            xsqb = [sb(f"xsqA{i}", [128, 8, TT]) for i in range(2)]
            rinb = [sb(f"rinA{i}", [128, TT]) for i in range(2)]
            hTb = [sb(f"hTA{i}", [128, 8, TT], BF16) for i in range(2)]
            cosT = sb("cosT", [128, TT])
            sinT = sb("sinT", [128, TT])
            NST = 4
            st32 = [sb(f"st32_{i}", [128, TT]) for i in range(NST)]
            st16 = [sb(f"st16_{i}", [128, TT], BF16) for i in range(NST)]
            tmp32 = [sb(f"tmp32_{i}", [128, TT]) for i in range(2)]
            nrm = [sb(f"nrm_{i}", [2, TT], BF16) for i in range(2)]
            kv32 = [sb(f"kv32_{i}", [128, 256]) for i in range(2)]
            v16 = [sb(f"v16_{i}", [128, 128], BF16) for i in range(2)]
            nkt = [sb(f"nkt_{i}", [128, 2]) for i in range(2)]
            junk = sb("junkA", [128, 64])
            self.load_w_bf16(WinB, self.w_in[l], "WinB", 4)
            cnt = {"s32": 0, "s16": 0, "ps": 0, "t32": 0, "nrm": 0, "kv": 0}

            def next_ps():
                i = cnt["ps"] % 6
                cnt["ps"] += 1
                return self.pb[i], ("pb", i)

            cur = {"hT": None, "hk": None}

            def mm_tile(m, ps, pk):
                hT, hk = cur["hT"], cur["hk"]
                for k in range(8):
                    fw.op(fw.pe, lambda t, k=k: t.matmul(ps[:], lhsT=WinB[:, k, m * 128:(m + 1) * 128], rhs=hT[:, k, :], start=(k == 0), stop=(k == 7)),
                          reads=["WinB", hk], writes=[pk], inc=(k == 7))

            def prep(tt):
                bi = tt % 2
                for k in range(8):
                    fw.dma(xtb[bi][:, k, :], self.xT[k * 128:(k + 1) * 128, tt * TT:(tt + 1) * TT], reads=[("xT", k)], writes=[f"xt{bi}"])
                self.norm_mod(es, xtb[bi], self.a1, 0, 0 if tt == 0 else 1, hTb[bi], "A", bufs=(xsqb[bi], rinb[bi]), sfx=str(bi))

            import os
            prep(0)
            for tt in range(int(os.environ.get('DBGT', NTILE))):
                c0 = tt * TT
                j = 0 if tt == 0 else 1
                if tt + 1 < NTILE:
                    prep(tt + 1)
                hT = hTb[tt % 2]
                cur["hT"], cur["hk"] = hT, f"hT{tt % 2}"
                if j == 1:
                    p0 = c0 - NP
                    fw.dma(cosT[:], self.rope_cos[:, p0:p0 + TT], writes=["cosT"])
                    fw.dma(sinT[:], self.rope_sin[:, p0:p0 + TT], writes=["sinT"])
                evac_flip = 0
                import os
                lvl = int(os.environ.get('DBGA', '9'))
                mlist = list(range(0, 13)) + list(range(14, 38))
                if lvl == 0: mlist = list(range(0, 6))
                if lvl == 1: mlist = list(range(0, 8))
                if lvl == 2: mlist = list(range(0, 13))
                for m in mlist:
                    ps, pk = next_ps()
                    mm_tile(m, ps, pk)
                    eng = fw.act if (evac_flip % 2 == 0) else fw.dve
                    evac_flip += 1
                    if m < 6:
                        i = cnt["s32"] % NST; cnt["s32"] += 1
                        s = st32[i]; sk = ("st32", i)
                        if eng is fw.act:
                            fw.op(eng, lambda a, s=s, ps=ps: a.copy(out=s[:], in_=ps[:]), reads=[pk], writes=[sk])
                        else:
                            fw.op(eng, lambda v, s=s, ps=ps: v.tensor_copy(out=s[:], in_=ps[:]), reads=[pk], writes=[sk])
                        fw.dma(self.zaT[m * 128:(m + 1) * 128, c0:c0 + TT], s[:], reads=[sk], writes=[("zaT", m)])
                    elif m < 8:
                        i = cnt["s16"] % NST; cnt["s16"] += 1
                        s = st16[i]; sk = ("st16", i)
                        fw.op(fw.act, lambda a, s=s, ps=ps: a.copy(out=s[:], in_=ps[:]), reads=[pk], writes=[sk])
                        fw.dma(self.zbT[(m - 6) * 128:(m - 5) * 128, c0:c0 + TT], s[:], reads=[sk], writes=[("zbT", m - 6)])
                    elif m < 13:
                        i = cnt["s16"] % NST; cnt["s16"] += 1
                        s = st16[i]; sk = ("st16", i)
                        ti = cnt["t32"] % 2; cnt["t32"] += 1
                        t32 = tmp32[ti]; tk = ("tmp32", ti)
                        if m < 12 and os.environ.get('DBGQ') != '1':
                            fw.op(fw.act, lambda a, t32=t32, ps=ps: a.activation(out=t32[:], in_=ps[:], func=AF.Square), reads=[pk], writes=[tk])
                            pn, pnk = self.pb[6], ("pb", 6)
                            fw.op(fw.pe, lambda t, t32=t32: t.matmul(pn[:, :], lhsT=self.blk2[:], rhs=t32[:], start=True, stop=True),
                                  reads=[tk, "blk2"], writes=[pnk])
                            ni = cnt["nrm"] % 2; cnt["nrm"] += 1
                            nr = nrm[ni]; nrk = ("nrm", ni)
                            if os.environ.get('DBGQ') != '2':
                                fw.op(fw.act, lambda a, nr=nr: a.mul(out=nr[:], in_=pn[0:2, :], mul=-0.5), reads=[pnk], writes=[nrk])
                            for hh in range(2 if os.environ.get('DBGQ') not in ('2', '3') else 0):
                                h = 2 * (m - 8) + hh
                                fw.dma(self.qA[h, 64:65, c0:c0 + TT], nr[hh:hh + 1, :], reads=[nrk], writes=[("qA", h)])
                        if j == 0:
                            fw.op(fw.dve, lambda v, s=s, ps=ps: v.tensor_copy(out=s[:], in_=ps[:]), reads=[pk], writes=[sk])
                        else:
                            mp = 38 + (m - 8)
                            ps2, pk2 = next_ps()
                            mm_tile(mp, ps2, pk2)
                            fw.op(fw.dve, lambda v, t32=t32, ps=ps: v.tensor_tensor(out=t32[:], in0=ps[:], in1=cosT[:], op=ALU.mult),
                                  reads=[pk, "cosT", tk], writes=[tk])
                            i2 = cnt["s32"] % NST; cnt["s32"] += 1
                            s2 = st32[i2]; s2k = ("st32", i2)
                            fw.op(fw.dve, lambda v, s2=s2, ps2=ps2: v.tensor_tensor(out=s2[:], in0=ps2[:], in1=sinT[:], op=ALU.mult),
                                  reads=[pk2, "sinT"], writes=[s2k])
                            fw.op(fw.pool, lambda g, s=s, t32=t32, s2=s2: g.tensor_tensor(out=s[:], in0=t32[:], in1=s2[:], op=ALU.add),
                                  reads=[tk, s2k], writes=[sk])
                        for hh in range(2):
                            if m < 12:
                                h = 2 * (m - 8) + hh
                                fw.dma(self.qA[h, 0:64, c0:c0 + TT], s[hh * 64:(hh + 1) * 64, :], reads=[sk], writes=[("qA", h)])
                            else:
                                fw.dma(self.kA[hh, 0:64, c0:c0 + TT], s[hh * 64:(hh + 1) * 64, :], reads=[sk], writes=[("kA", hh)])
                    else:
                        i = cnt["s16"] % NST; cnt["s16"] += 1
                        s = st16[i]; sk = ("st16", i)
                        fw.op(fw.act, lambda a, s=s, ps=ps: a.activation(out=s[:], in_=ps[:], func=AF.Sigmoid), reads=[pk], writes=[sk])
                        fw.dma(self.gT[(m - 14) * 128:(m - 13) * 128, c0:c0 + TT], s[:], reads=[sk], writes=[("gT", m - 14)])
                for ts in range(4 if lvl >= 4 else 0):
                    ps, pk = next_ps()
                    for k in range(8):
                        fw.op(fw.pe, lambda t, k=k, ts=ts, ps=ps: t.matmul(ps[:, 0:256], lhsT=hT[:, k, ts * 128:(ts + 1) * 128], rhs=WinB[:, k, 1536:1792],
                                                                         start=(k == 0), stop=(k == 7)),
                              reads=["WinB", cur["hk"]], writes=[pk], inc=(k == 7))
                    i = cnt["kv"] % 2; cnt["kv"] += 1
                    r0 = c0 + ts * 128
                    fw.op(fw.dve, lambda v, i=i, ps=ps: v.tensor_copy(out=kv32[i][:], in_=ps[:, 0:256]), reads=[pk], writes=[("kv32", i)])
                    fw.op(fw.act, lambda a, i=i, ps=ps: a.copy(out=v16[i][:], in_=ps[:, 128:256]), reads=[pk], writes=[("v16", i)])
                    for g_ in range(2):
                        fw.op(fw.act, lambda a, i=i, g_=g_: a.activation(out=junk[:], in_=kv32[i][:, g_ * 64:(g_ + 1) * 64], func=AF.Square,
                                                                          accum_out=nkt[i][:, g_:g_ + 1]),
                              reads=[("kv32", i)], writes=["junkA", ("nkt", i)])
                    fw.op(fw.dve, lambda v, i=i: v.tensor_scalar(out=nkt[i][:], in0=nkt[i][:], scalar1=1.0 / 16.0, scalar2=None, op0=ALU.mult),
                          reads=[("nkt", i)], writes=[("nkt", i)])
                    fw.dma(self.nk[r0:r0 + 128, :], nkt[i][:], reads=[("nkt", i)], writes=["nk"])
                    fw.dma(self.vtok[r0:r0 + 128, :], v16[i][:], reads=[("v16", i)], writes=["vtok"])
                    if tt == 0:
                        fw.dma(self.ok[l, r0:r0 + 128, :], kv32[i][:, 0:128], reads=[("kv32", i)], is_output=True)
                        fw.dma(self.ov[l, r0:r0 + 128, :], kv32[i][:, 128:256], reads=[("kv32", i)], is_output=True)

    def phase_B(self, l):
        nc, fw = self.nc, self.fw
        with ExitStack() as es:
            sb = lambda n, s, d=F32: es.enter_context(nc.sbuf_tensor(self.uniq(n), s, d))
            t32 = sb("bp32", [128, NT])
            t16 = sb("bp16", [128, NT], BF16)
            z16 = sb("bpz", [128, NT], BF16)
            if "hyena" not in self.mixers:
                for m in range(2):
                    fw.dma(t32[:], self.zaT[m * 128:(m + 1) * 128, :], reads=[("zaT", m)], writes=["bp32"])
                    fw.op(fw.dve, lambda v: v.tensor_copy(out=t16[:], in_=t32[:]), reads=["bp32"], writes=["bp16"])
                    fw.dma(self.yaT[m * 128:(m + 1) * 128, :], t16[:], reads=["bp16"], writes=[("yaT", m)])
            if "s5" not in self.mixers:
                for m in range(2):
                    fw.dma(t16[:], self.zbT[m * 128:(m + 1) * 128, :], reads=[("zbT", m)], writes=["bp16"])
                    fw.dma(self.ybT[m * 128:(m + 1) * 128, :], t16[:], reads=["bp16"], writes=[("ybT", m)])
            if "attn" not in self.mixers:
                fw.op(fw.pool, lambda g: g.memset(z16[:], 0.0), writes=["bpz"])
                for m in range(4):
                    fw.dma(self.ycT[m * 128:(m + 1) * 128, :], z16[:], reads=["bpz"], writes=[("ycT", m)])
        if "attn" in self.mixers:
            fw.barrier()
            self.attention(l)
        if "hyena" in self.mixers:
            fw.barrier()
            self.hyena(l)
        if "s5" in self.mixers:
            fw.barrier()
            self.s5(l)

    def attention(self, l):
        nc, fw = self.nc, self.fw
        with ExitStack() as es:
            sb = lambda n, s, d=F32: es.enter_context(nc.sbuf_tensor(self.uniq(n), s, d))
            NKT = NT // 128
            kAs = sb("kAs", [65, NT], BF16)
            qAs = sb("qAs", [65, 4, NT], BF16)
            vt = sb("vt", [128, NKT, 128], BF16)
            nkS = sb("nkS", [128, NKT, 2])
            en = sb("enS", [128, NKT, 2])
            nnk = sb("nnkS", [128, NKT, 2])
            Vp = sb("Vp", [128, NKT, 65], BF16)
            ck32 = sb("ck32", [128, 4, 128])
            cv32 = sb("cv32", [128, 4, 128])
            kcA = sb("kcA", [65, 2, 512], BF16)
            nkc = sb("nkc", [128, 4, 2])
            enc = sb("enc", [128, 4, 2])
            nnkc = sb("nnkc", [128, 4, 2])
            Vpc = sb("Vpc", [128, 2, 4, 65], BF16)
            junk = sb("junkB", [128, 64])
            sinkt = sb("sinkt", [65, 8])
            mprev = sb("mprev", [128, 128], BF16)
            mnext = sb("mnext", [128, 128], BF16)
            m32 = sb("m32", [128, 128])
            sel64 = sb("sel64", [128, 128])
            Rrow = sb("Rrow", [128, 1024])
            PT = [sb(f"PT{i}", [128, 1024], BF16) for i in range(2)]
            sk_t = sb("sk_t", [65, 1024])
            bcs = sb("bcs", [64, 1024])
            ycS = sb("ycS", [64, 4, NT], BF16)

            fw.dma(sinkt[:], self.sink[l:l + 1, :].partition_broadcast(65), writes=["sinkt"])
            fw.dma(m32[:], self.mask_prev[:, :], writes=["m32"])
            fw.op(fw.dve, lambda v: v.tensor_copy(out=mprev[:], in_=m32[:]), reads=["m32"], writes=["mprev"])
            fw.dma(m32[:], self.mask_next[:, :], reads=[], writes=["m32"])
            fw.op(fw.dve, lambda v: v.tensor_copy(out=mnext[:], in_=m32[:]), reads=["m32"], writes=["mnext"])
            fw.op(fw.pool, lambda g: g.memset(sel64[:], 0.0), writes=["sel64"])
            fw.op(fw.pool, lambda g: g.memset(sel64[64:65, :], 1.0), reads=["sel64"], writes=["sel64"])
            fw.op(fw.pool, lambda g: g.memset(Rrow[:], 0.0), writes=["Rrow"])
            fw.dma(vt[:], self.vtok.rearrange("(t p) c -> p t c", p=128), reads=["vtok"], writes=["vt"])
            fw.dma(nkS[:], self.nk.rearrange("(t p) c -> p t c", p=128), reads=["nk"], writes=["nkS"])
            fw.op(fw.act, lambda a: a.activation(out=en[:], in_=nkS[:], func=AF.Exp), reads=["nkS"], writes=["en"])
            fw.op(fw.dve, lambda v: v.tensor_scalar(out=nnk[:], in0=nkS[:], scalar1=-1.0, scalar2=None, op0=ALU.mult), reads=["nkS"], writes=["nnk"])
            fw.dma(ck32[:], self.cache_k[l].rearrange("(t p) c -> p t c", p=128), writes=["ck32"])
            fw.dma(cv32[:], self.cache_v[l].rearrange("(t p) c -> p t c", p=128), writes=["cv32"])
            fw.op(fw.pool, lambda g: g.memset(kcA[64:65, :, :], 1.0), writes=["kcA"])
            for t in range(4):
                for g_ in range(2):
                    fw.op(fw.act, lambda a, t=t, g_=g_: a.activation(out=junk[:], in_=ck32[:, t, g_ * 64:(g_ + 1) * 64], func=AF.Square,
                                                                      accum_out=nkc[:, t, g_:g_ + 1]), reads=["ck32"], writes=["junkB", "nkc"])
                pt, ptk = self.pb[5], ("pb", 5)
                fw.op(fw.pe, lambda t_, t=t: t_.transpose(pt[:, 0:128], ck32[:, t, :], self.ident[:]), reads=["ck32", "ident"], writes=[ptk])
                for g_ in range(2):
                    fw.op(fw.dve, lambda v, t=t, g_=g_: v.tensor_copy(out=kcA[0:64, g_, t * 128:(t + 1) * 128], in_=pt[g_ * 64:(g_ + 1) * 64, 0:128]),
                          reads=[ptk], writes=["kcA"])
            fw.op(fw.dve, lambda v: v.tensor_scalar(out=nkc[:], in0=nkc[:], scalar1=1.0 / 16.0, scalar2=None, op0=ALU.mult), reads=["nkc"], writes=["nkc"])
            fw.op(fw.act, lambda a: a.activation(out=enc[:], in_=nkc[:], func=AF.Exp), reads=["nkc"], writes=["enc"])
            fw.op(fw.dve, lambda v: v.tensor_scalar(out=nnkc[:], in0=nkc[:], scalar1=-1.0, scalar2=None, op0=ALU.mult), reads=["nkc"], writes=["nnkc"])

            for g_ in range(2):
                fw.dma(kAs[:], self.kA[g_], reads=[("kA", g_)], writes=["kAs"])
                for hh in range(4):
                    fw.dma(qAs[:, hh, :], self.qA[4 * g_ + hh], reads=[("qA", 4 * g_ + hh)], writes=["qAs"])
                for t in range(NKT):
                    fw.op(fw.dve, lambda v, t=t: v.tensor_scalar(out=Vp[:, t, 0:64], in0=vt[:, t, g_ * 64:(g_ + 1) * 64], scalar1=en[:, t, g_:g_ + 1],
                                                                  scalar2=None, op0=ALU.mult), reads=["vt", "en"], writes=["Vp"])
                fw.op(fw.dve, lambda v: v.tensor_copy(out=Vp[:, :, 64], in_=en[:, :, g_]), reads=["en"], writes=["Vp"])
                for t in range(4):
                    fw.op(fw.dve, lambda v, t=t: v.tensor_scalar(out=Vpc[:, g_, t, 0:64], in0=cv32[:, t, g_ * 64:(g_ + 1) * 64], scalar1=enc[:, t, g_:g_ + 1],
                                                                  scalar2=None, op0=ALU.mult), reads=["cv32", "enc"], writes=["Vpc"])
                fw.op(fw.dve, lambda v: v.tensor_copy(out=Vpc[:, g_, :, 64], in_=enc[:, :, g_]), reads=["enc"], writes=["Vpc"])

                groups = []
                for s_ in range(2):
                    groups.append((s_ * 256, 256, [("tok", 2 * s_), ("tok", 2 * s_ + 1)]))
                for qb in range(32):
                    kts = [("cache", t) for t in range(4)]
                    if qb > 0:
                        kts.append(("prev", 4 + qb - 1))
                    kts.append(("tok", 4 + qb))
                    if qb < 31:
                        kts.append(("next", 4 + qb + 1))
                    groups.append((NP + qb * 128, 128, kts))
                for gi, (q0, nq, kts) in enumerate(groups):
                    ncol = 4 * nq
                    nb = ncol // 512
                    po = [self.pb[2 + 2 * (gi % 2) + b_] for b_ in range(nb)]
                    pok = [("pb", 2 + 2 * (gi % 2) + b_) for b_ in range(nb)]
                    for ki, (kind, t) in enumerate(kts):
                        pt_ = PT[ki % 2]; ptk_ = ("PT", ki % 2)
                        pss = [self.pb[0], self.pb[1]]; pssk = [("pb", 0), ("pb", 1)]
                        if nb == 1:
                            pss = [self.pb[ki % 2]]; pssk = [("pb", ki % 2)]
                        if kind == "cache":
                            lhs = kcA[0:65, g_, t * 128:(t + 1) * 128]
                            bias = nnkc[:, t, g_:g_ + 1]
                            vprime = Vpc[:, g_, t, :]
                            rk = ["kcA"]; bk = "nnkc"; vk = "Vpc"
                        else:
                            lhs = kAs[0:65, t * 128:(t + 1) * 128]
                            bias = nnk[:, t, g_:g_ + 1]
                            vprime = Vp[:, t, :]
                            rk = ["kAs"]; bk = "nnk"; vk = "Vp"
                        for hh in range(4):
                            col = hh * nq
                            b_ = col // 512
                            fw.op(fw.pe, lambda t_, hh=hh, col=col, b_=b_: t_.matmul(pss[b_][:, col % 512:col % 512 + nq], lhsT=lhs, rhs=qAs[0:65, hh, q0:q0 + nq],
                                                                                   start=True, stop=True),
                                  reads=rk + ["qAs"], writes=[pssk[b_]], inc=(hh % (4 // nb) == (4 // nb) - 1))
                        for b_ in range(nb):
                            fw.op(fw.act, lambda a, b_=b_: a.activation(out=pt_[:, b_ * 512:(b_ + 1) * 512], in_=pss[b_][:], func=AF.Exp, scale=0.125, bias=bias),
                                  reads=[pssk[b_], bk], writes=[ptk_])
                        if kind in ("prev", "next"):
                            mk_ = mprev if kind == "prev" else mnext
                            fw.op(fw.pool, lambda g: g.tensor_tensor(out=pt_[:, 0:512].rearrange("p (h q) -> p h q", h=4),
                                                                     in0=pt_[:, 0:512].rearrange("p (h q) -> p h q", h=4),
                                                                     in1=mk_[:].unsqueeze(1).to_broadcast([128, 4, 128]), op=ALU.mult),
                                  reads=[ptk_, "mprev", "mnext"], writes=[ptk_])
                        for b_ in range(nb):
                            fw.op(fw.pe, lambda t_, b_=b_: t_.matmul(po[b_][0:65, :], lhsT=vprime[:, 0:65], rhs=pt_[:, b_ * 512:(b_ + 1) * 512],
                                                                     start=(ki == 0), stop=(ki == len(kts) - 1)),
                                  reads=[vk, ptk_], writes=[pok[b_]], inc=(ki == len(kts) - 1))
                    for hh in range(4):
                        h = 4 * g_ + hh
                        fw.op(fw.act, lambda a, hh=hh, h=h: a.activation(out=sk_t[64:65, hh * nq:(hh + 1) * nq], in_=qAs[64:65, hh, q0:q0 + nq], func=AF.Exp,
                                                                          scale=0.125, bias=sinkt[64:65, h:h + 1]), reads=["qAs", "sinkt"], writes=["sk_t"])
                    for b_ in range(nb):
                        cs = slice(b_ * 512, (b_ + 1) * 512)
                        fw.op(fw.dve, lambda v, b_=b_, cs=cs: v.tensor_tensor(out=sk_t[64:65, cs], in0=po[b_][64:65, :], in1=sk_t[64:65, cs], op=ALU.add),
                              reads=[pok[b_], "sk_t"], writes=["sk_t"])
                        fw.op(fw.dve, lambda v, cs=cs: v.reciprocal(out=Rrow[64:65, cs], in_=sk_t[64:65, cs]), reads=["sk_t"], writes=["Rrow"])
                        pbc, pbck = self.pb[6 + b_], ("pb", 6 + b_)
                        fw.op(fw.pe, lambda t_, cs=cs, pbc=pbc: t_.matmul(pbc[:], lhsT=sel64[:], rhs=Rrow[:, cs], start=True, stop=True),
                              reads=["sel64", "Rrow"], writes=[pbck])
                        fw.op(fw.act, lambda a, cs=cs, pbc=pbc: a.copy(out=bcs[:, cs], in_=pbc[0:64, :]), reads=[pbck], writes=["bcs"])
                        for hh in range(4 // nb):
                            hg = b_ * (4 // nb) + hh
                            fw.op(fw.dve, lambda v, hh=hh, hg=hg, b_=b_: v.tensor_tensor(out=ycS[:, hg, q0:q0 + nq], in0=po[b_][0:64, hh * nq:(hh + 1) * nq],
                                                                                         in1=bcs[:, b_ * 512 + hh * nq:b_ * 512 + (hh + 1) * nq], op=ALU.mult),
                                  reads=[pok[b_], "bcs"], writes=["ycS"])
                for hh in range(4):
                    h = 4 * g_ + hh
                    fw.dma(self.ycT[h * 64:(h + 1) * 64, :], ycS[:, hh, :], reads=["ycS"], writes=[("ycT", h // 2)])

    HSEG = [(0, 256), (256, 256), (NP, LS)]

    def hyena(self, l):
        nc, fw = self.nc, self.fw
        if not hasattr(self, "hzT"):
            pass
        with ExitStack() as es:
            sb = lambda n, s, d=F32: es.enter_context(nc.sbuf_tensor(self.uniq(n), s, d))
            za = sb("hy_za", [128, NT]); U = sb("hy_U", [128, NT]); cw = sb("hy_cwS", [128, 6, 4])
            fw.dma(cw[:], self.hy_cw[l], writes=["hy_cwS"])
            for m in range(6):
                fw.dma(za[:], self.zaT[m * 128:(m + 1) * 128, :], reads=[("zaT", m)], writes=["hy_za"])
                fw.op(fw.act, lambda a: a.activation(out=U[:], in_=za[:], func=AF.Identity, scale=cw[:, m, 1:2], bias=cw[:, m, 3:4]),
                      reads=["hy_za", "hy_cwS"], writes=["hy_U"])
                for (t0, Ls) in self.HSEG:
                    fw.op(fw.dve, lambda v: v.scalar_tensor_tensor(out=U[:, t0 + 1:t0 + Ls], in0=za[:, t0:t0 + Ls - 1], scalar=cw[:, m, 0:1],
                                                                   in1=U[:, t0 + 1:t0 + Ls], op0=ALU.mult, op1=ALU.add),
                          reads=["hy_za", "hy_cwS", "hy_U"], writes=["hy_U"])
                    fw.op(fw.dve, lambda v: v.scalar_tensor_tensor(out=U[:, t0:t0 + Ls - 1], in0=za[:, t0 + 1:t0 + Ls], scalar=cw[:, m, 2:3],
                                                                   in1=U[:, t0:t0 + Ls - 1], op0=ALU.mult, op1=ALU.add),
                          reads=["hy_za", "hy_cwS", "hy_U"], writes=["hy_U"])
                if m < 2:
                    fw.dma(self.hzT[m * 128:(m + 1) * 128, :], U[:], reads=["hy_U"], writes=[("hzT", m)])
                else:
                    fw.dma(self.ugT[(m - 2) * 128:(m - 1) * 128, :], U[:], reads=["hy_U"], writes=[("ugT", m - 2)])
        fw.barrier()
        for L_ in (256, 4096):
            self.hy_filters(l, L_)
            fw.barrier()
        for o in range(2):
            self.hy_to_tok()
            fw.barrier()
            self.hy_conv256(o)
            fw.barrier()
            self.hy_conv(o, NP, LS)
            fw.barrier()
            with ExitStack() as es:
                sb = lambda n, s, d=F32: es.enter_context(nc.sbuf_tensor(self.uniq(n), s, d))
                z = sb("hy_z", [128, NT]); y = sb("hy_y", [128, NT]); g = sb("hy_g", [128, NT]); skp = sb("hy_skp", [128, 2, 2])
                z16 = sb("hy_z16", [128, NT], BF16)
                fw.dma(skp[:], self.hy_skip[l], writes=["hy_skp"])
                for ch in range(2):
                    fw.dma(z[:], self.hzT[ch * 128:(ch + 1) * 128, :], reads=[("hzT", ch)], writes=["hy_z"])
                    fw.dma(y[:], self.hyT[ch * 128:(ch + 1) * 128, :], reads=[("hyT", ch)], writes=["hy_y"])
                    fw.dma(g[:], self.ugT[(2 * o + ch) * 128:(2 * o + ch + 1) * 128, :], reads=[("ugT", 2 * o + ch)], writes=["hy_g"])
                    fw.op(fw.dve, lambda v: v.scalar_tensor_tensor(out=y[:], in0=z[:], scalar=skp[:, o, ch:ch + 1], in1=y[:], op0=ALU.mult, op1=ALU.add),
                          reads=["hy_z", "hy_skp", "hy_y"], writes=["hy_y"])
                    if o == 0:
                        fw.op(fw.pool, lambda g_: g_.tensor_tensor(out=z[:], in0=y[:], in1=g[:], op=ALU.mult), reads=["hy_y", "hy_g"], writes=["hy_z"])
                        fw.dma(self.hzT[ch * 128:(ch + 1) * 128, :], z[:], reads=["hy_z"], writes=[("hzT", ch)])
                    else:
                        fw.op(fw.pool, lambda g_: g_.tensor_tensor(out=z16[:], in0=y[:], in1=g[:], op=ALU.mult), reads=["hy_y", "hy_g"], writes=["hy_z16"])
                        fw.dma(self.yaT[ch * 128:(ch + 1) * 128, :], z16[:], reads=["hy_z16"], writes=[("yaT", ch)])
            fw.barrier()

    def hy_to_tok(self):
        nc, fw = self.nc, self.fw
        with ExitStack() as es:
            sb = lambda n, s, d=F32: es.enter_context(nc.sbuf_tensor(self.uniq(n), s, d))
            z = sb("ht_z", [128, 2, NT])
            st = [sb(f"ht_st{i}", [128, 256], BF16) for i in range(3)]
            for ch in range(2):
                fw.dma(z[:, ch, :], self.hzT[ch * 128:(ch + 1) * 128, :], reads=[("hzT", ch)], writes=["ht_z"])
            for t in range(NT // 128):
                ps, pk = self.pb[t % 4], ("pb", t % 4)
                for ch in range(2):
                    fw.op(fw.pe, lambda t_, ch=ch: t_.transpose(ps[:, ch * 128:(ch + 1) * 128], z[:, ch, t * 128:(t + 1) * 128], self.ident[:]),
                          reads=["ht_z", "ident"], writes=[pk], inc=(ch == 1))
                s_ = st[t % 3]; sk = ("ht_st", t % 3)
                if t % 2 == 0:
                    fw.op(fw.dve, lambda v: v.tensor_copy(out=s_[:], in_=ps[:, 0:256]), reads=[pk], writes=[sk])
                else:
                    fw.op(fw.act, lambda a: a.activation(out=s_[:], in_=ps[:, 0:256], func=AF.Identity), reads=[pk], writes=[sk])
                fw.dma(self.ztok[t * 128:(t + 1) * 128, :], s_[:], reads=[sk], writes=["ztok"])

    def hy_stageA(self, es, L_, src, C, FAt, xt, stg):
        nc, fw = self.nc, self.fw
        NH = L_ // 128
        srcv = src.rearrange("(h l) c -> h l c", l=128)
        for ch in range(8):
            fw.dma(xt[0:NH, :, 0:C], srcv[:, ch * 16:(ch + 1) * 16, :], reads=["hy_src"], writes=["hy_xt"])
            sg = stg[ch % 2]; sgk = ("hy_stg", ch % 2)
            for i in range(16):
                n_lo = ch * 16 + i
                ps, pk = self.pb[i % 4], ("pb", i % 4)
                fw.op(fw.pe, lambda t_: t_.matmul(ps[:, 0:C], lhsT=FAt[:, n_lo, :], rhs=xt[:, i, 0:C], start=True, stop=True),
                      reads=["hy_FAt", "hy_xt"], writes=[pk])
                if i % 2 == 0:
                    fw.op(fw.dve, lambda v: v.tensor_copy(out=sg[:, i, 0:C], in_=ps[:, 0:C]), reads=[pk], writes=[sgk])
                else:
                    fw.op(fw.act, lambda a: a.activation(out=sg[:, i, 0:C], in_=ps[:, 0:C], func=AF.Identity), reads=[pk], writes=[sgk])
            fw.dma(self.Ascr[:, ch * 16:(ch + 1) * 16, 0:C], sg[:, :, 0:C], reads=[sgk], writes=["Ascr"])

    def hy_filters(self, l, L_):
        nc, fw = self.nc, self.fw
        NJT = L_ // 128
        KA = L_ // 64
        cwid = min(512, L_)
        TWO_PI = 2.0 * math.pi
        with ExitStack() as es:
            sb = lambda n, s, d=F32: es.enter_context(nc.sbuf_tensor(self.uniq(n), s, d))
            featT = sb("hf_feat", [128, L_]); negt = sb("hf_negt", [128, NJT])
            w1 = sb("hf_w1", [128, 128]); w2 = sb("hf_w2", [128, 128]); w3 = sb("hf_w3", [128, 1024])
            vec = sb("hf_vec", [128, 3]); fb = sb("hf_fb", [128, 2]); dabs = sb("hf_dabs", [128, 1024])
            h1T = sb("hf_h1T", [128, L_]); h2T = sb("hf_h2T", [128, L_])
            yt = sb("hf_yt", [128, 512]); ki = sb("hf_ki", [128, 512], I32); kf = sb("hf_kf", [128, 512])
            fw.dma(featT[:], self.hy_featT[L_][:, :], writes=["hf_feat"])
            fw.dma(negt[:], self.hy_negt[L_][:, :], writes=["hf_negt"])
            fw.dma(w1[:], self.hy_w1[l], writes=["hf_w1"]); fw.dma(w2[:], self.hy_w2[l], writes=["hf_w2"]); fw.dma(w3[:], self.hy_w3[l], writes=["hf_w3"])
            fw.dma(vec[:], self.hy_vec[l], writes=["hf_vec"])
            fw.dma(dabs[:], self.hy_decay[l:l + 1, :].partition_broadcast(128), writes=["hf_dabs"])
            fw.op(fw.act, lambda a: a.activation(out=dabs[:], in_=dabs[:], func=AF.Abs), reads=["hf_dabs"], writes=["hf_dabs"])
            fw.op(fw.dve, lambda v: v.tensor_tensor(out=fb[:], in0=vec[:, 0:2], in1=vec[:, 2:3].to_broadcast([128, 2]), op=ALU.mult), reads=["hf_vec"], writes=["hf_fb"])

            def sin_layer(srcT, W, col, dstT, sk, wk, dk):
                for ch in range(L_ // cwid):
                    cs = slice(ch * cwid, (ch + 1) * cwid)
                    ps, pk = self.pb[ch % 2], ("pb", ch % 2)
                    fw.op(fw.pe, lambda t_: t_.matmul(ps[:, 0:cwid], lhsT=W[:], rhs=srcT[:, cs], start=True, stop=True), reads=[sk, wk], writes=[pk])
                    fw.op(fw.act, lambda a: a.activation(out=yt[:, 0:cwid], in_=ps[:, 0:cwid], func=AF.Identity, scale=vec[:, 2:3], bias=fb[:, col:col + 1]),
                          reads=[pk, "hf_vec", "hf_fb"], writes=["hf_yt"])
                    fw.op(fw.dve, lambda v: v.tensor_scalar(out=ki[:, 0:cwid], in0=yt[:, 0:cwid], scalar1=1.0 / TWO_PI, scalar2=None, op0=ALU.mult),
                          reads=["hf_yt"], writes=["hf_ki"])
                    fw.op(fw.dve, lambda v: v.tensor_copy(out=kf[:, 0:cwid], in_=ki[:, 0:cwid]), reads=["hf_ki"], writes=["hf_kf"])
                    fw.op(fw.dve, lambda v: v.scalar_tensor_tensor(out=yt[:, 0:cwid], in0=kf[:, 0:cwid], scalar=-TWO_PI, in1=yt[:, 0:cwid], op0=ALU.mult, op1=ALU.add),
                          reads=["hf_kf", "hf_yt"], writes=["hf_yt"])
                    fw.op(fw.dve, lambda v: v.tensor_scalar(out=yt[:, 0:cwid], in0=yt[:, 0:cwid], scalar1=3.141592, scalar2=-3.141592, op0=ALU.min, op1=ALU.max),
                          reads=["hf_yt"], writes=["hf_yt"])
                    fw.op(fw.act, lambda a: a.activation(out=dstT[:, cs], in_=yt[:, 0:cwid], func=AF.Sin), reads=["hf_yt"], writes=[dk])

            sin_layer(featT, w1, 0, h1T, "hf_feat", "hf_w1", "hf_h1T")
            sin_layer(h1T, w2, 1, h2T, "hf_h1T", "hf_w2", "hf_h2T")

            hrow = [sb(f"hf_hrow{i}", [128, 1024]) for i in range(2)]
            habs = sb("hf_habs", [128, 1024]); Et = sb("hf_E", [128, 512])
            pS = [self.pb[6], self.pb[7]]; pSk = [("pb", 6), ("pb", 7)]
            for j in range(NJT):
                hr = hrow[j % 2]; hk = ("hf_hrow", j % 2)
                for half in range(2):
                    cs = slice(half * 512, (half + 1) * 512)
                    ps, pk = self.pb[2 + half], ("pb", 2 + half)
                    fw.op(fw.pe, lambda t_: t_.matmul(ps[:], lhsT=h2T[:, j * 128:(j + 1) * 128], rhs=w3[:, cs], start=True, stop=True),
                          reads=["hf_h2T", "hf_w3"], writes=[pk])
                    fw.op(fw.act, lambda a: a.activation(out=Et[:], in_=dabs[:, cs], func=AF.Exp, scale=negt[:, j:j + 1]), reads=["hf_dabs", "hf_negt"], writes=["hf_E"])
                    fw.op(fw.dve, lambda v: v.tensor_tensor(out=hr[:, cs], in0=ps[:], in1=Et[:], op=ALU.mult), reads=[pk, "hf_E"], writes=[hk])
                if j == 0:
                    fw.op(fw.pool, lambda g: g.memset(hr[0:1, 256:512], 0.0), reads=[hk], writes=[hk])
                    fw.op(fw.pool, lambda g: g.memset(hr[0:1, 768:1024], 0.0), reads=[hk], writes=[hk])
                fw.op(fw.act, lambda a: a.activation(out=habs[:], in_=hr[:], func=AF.Abs), reads=[hk], writes=["hf_habs"])
                for half in range(2):
                    fw.op(fw.pe, lambda t_: t_.matmul(pS[half][:], lhsT=self.ones32[:], rhs=habs[:, half * 512:(half + 1) * 512], start=(j == 0), stop=(j == NJT - 1)),
                          reads=["hf_habs", "ones32"], writes=[pSk[half]])
                fw.dma(self.hraw[L_][j * 128:(j + 1) * 128, :], hr[:], reads=[hk], writes=["hraw"])
            Ssb = sb("hf_S", [128, 1024]); rn = sb("hf_rn", [128, 2, 256]); rnb = sb("hf_rnb", [128, 1024])
            for half in range(2):
                fw.op(fw.act, lambda a: a.copy(out=Ssb[:, half * 512:(half + 1) * 512], in_=pS[half][:]), reads=[pSk[half]], writes=["hf_S"])
            Sv = Ssb[:].rearrange("p (o d c) -> p o d c", o=2, d=2)
            fw.op(fw.dve, lambda v: v.tensor_tensor(out=rn[:], in0=Sv[:, :, 0, :], in1=Sv[:, :, 1, :], op=ALU.add), reads=["hf_S"], writes=["hf_rn"])
            fw.op(fw.dve, lambda v: v.reciprocal(out=rn[:], in_=rn[:]), reads=["hf_rn"], writes=["hf_rn"])
            rv = rnb[:].rearrange("p (o d c) -> p o d c", o=2, d=2)
            for d_ in range(2):
                fw.op(fw.dve, lambda v: v.tensor_copy(out=rv[:, :, d_, :], in_=rn[:]), reads=["hf_rn"], writes=["hf_rnb"])
            hb16 = [sb(f"hf_hb16{i}", [128, 1024], BF16) for i in range(2)]
            for j in range(NJT):
                hr = hrow[j % 2]; hk = ("hf_hrow", j % 2)
                fw.dma(hr[:], self.hraw[L_][j * 128:(j + 1) * 128, :], reads=["hraw"], writes=[hk])
                fw.op(fw.dve, lambda v: v.tensor_tensor(out=hb16[j % 2][:], in0=hr[:], in1=rnb[:], op=ALU.mult), reads=[hk, "hf_rnb"], writes=[("hf_hb16", j % 2)])
                fw.dma(self.ftok[L_][j * 128:(j + 1) * 128, :], hb16[j % 2][:], reads=[("hf_hb16", j % 2)], writes=["ftok"])
        fw.barrier()
        if L_ == 256:
            self.hy_spectra256()
            fw.barrier()
            return
        with ExitStack() as es:
            sb = lambda n, s, d=F32: es.enter_context(nc.sbuf_tensor(self.uniq(n), s, d))
            FAt = sb("hy_FAt", [128, 128, 128], BF16); GB = sb("hy_GBs", [128, 12, 128], BF16)
            xt = sb("hy_xt", [128, 16, 512], BF16); stg = [sb(f"hy_stg{i}", [128, 16, 512], BF16) for i in range(2)]
            A2 = sb("hy_A2", [128, 2, 4, 512], BF16)
            kst = [sb(f"hy_kst{i}", [128, 2, 4, 256]) for i in range(2)]
            fw.dma(FAt[:], self.hy_FA[L_][:, :, :], writes=["hy_FAt"])
            fw.dma(GB[:], self.hy_GB[:, :, :], writes=["hy_GBs"])
            fw.op(fw.pool, lambda g: g.memset(xt[:], 0.0), writes=["hy_xt"])
            for o in range(2):
                self.hy_stageA(es, L_, self.ftok[L_][:, o * 512:(o + 1) * 512], 512, FAt, xt, stg)
                nkc = min(4, KA)
                for kc in range(KA // nkc):
                    for r in range(2):
                        fw.dma(A2[:, r, 0:nkc, :], self.Ascr[r * 64 + kc * nkc:r * 64 + (kc + 1) * nkc, :, :].rearrange("k l c -> l k c"),
                               reads=["Ascr"], writes=["hy_A2"])
                    ks = kst[kc % 2]; ksk = ("hy_kst", kc % 2)
                    for kk in range(nkc):
                        pa, pak = self.pb[(2 * kk) % 8], ("pb", (2 * kk) % 8)
                        pb_, pbk = self.pb[(2 * kk + 1) % 8], ("pb", (2 * kk + 1) % 8)
                        ops = [(0, 0, 4), (1, 0, 5), (0, 256, 4), (1, 256, 5)]
                        for ii, (r, c0_, gi) in enumerate(ops):
                            fw.op(fw.pe, lambda t_: t_.matmul(pa[:, 0:256], lhsT=GB[:, gi, :], rhs=A2[:, r, kk, c0_:c0_ + 256], start=(ii == 0), stop=(ii == 3)),
                                  reads=["hy_GBs", "hy_A2"], writes=[pak], inc=(ii == 3))
                        ops = [(0, 0, 6), (1, 0, 7), (0, 256, 8), (1, 256, 9)]
                        for ii, (r, c0_, gi) in enumerate(ops):
                            fw.op(fw.pe, lambda t_: t_.matmul(pb_[:, 0:256], lhsT=GB[:, gi, :], rhs=A2[:, r, kk, c0_:c0_ + 256], start=(ii == 0), stop=(ii == 3)),
                                  reads=["hy_GBs", "hy_A2"], writes=[pbk], inc=(ii == 3))
                        fw.op(fw.act, lambda a: a.copy(out=ks[:, 0, kk, :], in_=pa[:, 0:256]), reads=[pak], writes=[ksk])
                        fw.op(fw.dve, lambda v: v.tensor_copy(out=ks[:, 1, kk, :], in_=pb_[:, 0:256]), reads=[pbk], writes=[ksk])
                    for ab in range(2):
                        fw.dma(self.Ksp[L_][o, ab, :, kc * nkc:(kc + 1) * nkc, :], ks[:, ab, 0:nkc, :], reads=[ksk], writes=["Ksp"])
                fw.barrier()

    def hy_spectra256(self):
        nc, fw = self.nc, self.fw
        with ExitStack() as es:
            sb = lambda n, s, d=F32: es.enter_context(nc.sbuf_tensor(self.uniq(n), s, d))
            T = sb("hy_T", [128, 6, 1024], BF16)
            ft = sb("hy_ft", [128, 2, 1024], BF16)
            ks = sb("hy_ks", [128, 2, 2, 4, 256])
            fw.dma(T[:], self.hy_T256[:, :, :], writes=["hy_T"])
            fw.dma(ft[:], self.ftok[256].rearrange("(h p) c -> p h c", p=128), reads=["ftok"], writes=["hy_ft"])
            Tv = lambda var, nh, ka: T[:, var, (nh * 4 + ka) * 128:(nh * 4 + ka + 1) * 128]
            for o in range(2):
                for ka in range(4):
                    pa, pak = self.pb[(2 * ka) % 8], ("pb", (2 * ka) % 8)
                    pb_, pbk = self.pb[(2 * ka + 1) % 8], ("pb", (2 * ka + 1) % 8)
                    seq = [(2, nh, 0) for nh in range(2)] + [(2, nh, 256) for nh in range(2)]
                    for ii, (var, nh, c0_) in enumerate(seq):
                        fw.op(fw.pe, lambda t_: t_.matmul(pa[:, 0:256], lhsT=Tv(var, nh, ka), rhs=ft[:, nh, o * 512 + c0_:o * 512 + c0_ + 256], start=(ii == 0), stop=(ii == 3)),
                              reads=["hy_T", "hy_ft"], writes=[pak], inc=(ii == 3))
                    seq = [(3, nh, 0) for nh in range(2)] + [(4, nh, 256) for nh in range(2)]
                    for ii, (var, nh, c0_) in enumerate(seq):
                        fw.op(fw.pe, lambda t_: t_.matmul(pb_[:, 0:256], lhsT=Tv(var, nh, ka), rhs=ft[:, nh, o * 512 + c0_:o * 512 + c0_ + 256], start=(ii == 0), stop=(ii == 3)),
                              reads=["hy_T", "hy_ft"], writes=[pbk], inc=(ii == 3))
                    fw.op(fw.act, lambda a: a.copy(out=ks[:, o, 0, ka, :], in_=pa[:, 0:256]), reads=[pak], writes=["hy_ks"])
                    fw.op(fw.dve, lambda v: v.tensor_copy(out=ks[:, o, 1, ka, :], in_=pb_[:, 0:256]), reads=[pbk], writes=["hy_ks"])
                for ab in range(2):
                    fw.dma(self.Ksp[256][o, ab], ks[:, o, ab], reads=["hy_ks"], writes=["Ksp"])

    def hy_conv256(self, o):
        nc, fw = self.nc, self.fw
        with ExitStack() as es:
            sb = lambda n, s, d=F32: es.enter_context(nc.sbuf_tensor(self.uniq(n), s, d))
            T = sb("hy_T", [128, 6, 1024], BF16)
            xt = sb("hy_x256", [128, 4, 256], BF16)
            Kab = sb("hy_K256", [128, 2, 4, 256])
            Y = sb("hy_Y256", [128, 2, 4, 256], BF16)
            t1 = [sb(f"hy_t1{i}", [128, 256]) for i in range(2)]; t2 = [sb(f"hy_t2{i}", [128, 256]) for i in range(2)]
            y32 = sb("hy_y256", [128, 2, 512])
            fw.dma(T[:], self.hy_T256[:, :, :], writes=["hy_T"])
            fw.dma(xt[:], self.ztok[0:512, :].rearrange("(h p) c -> p h c", p=128), reads=["ztok"], writes=["hy_x256"])
            for ab in range(2):
                fw.dma(Kab[:, ab], self.Ksp[256][o, ab], reads=["Ksp"], writes=["hy_K256"])
            Tv = lambda var, nh, ka: T[:, var, (nh * 4 + ka) * 128:(nh * 4 + ka + 1) * 128]
            for sq in range(2):
                for ka in range(4):
                    i_ = (sq * 4 + ka) % 2
                    pa, pak = self.pb[2 * i_], ("pb", 2 * i_)
                    pw, pwk = self.pb[2 * i_ + 1], ("pb", 2 * i_ + 1)
                    for nh in range(2):
                        fw.op(fw.pe, lambda t_: t_.matmul(pa[:, 0:256], lhsT=Tv(0, nh, ka), rhs=xt[:, 2 * sq + nh, :], start=(nh == 0), stop=(nh == 1)),
                              reads=["hy_T", "hy_x256"], writes=[pak], inc=(nh == 1))
                    for nh in range(2):
                        fw.op(fw.pe, lambda t_: t_.matmul(pw[:, 0:256], lhsT=Tv(1, nh, ka), rhs=xt[:, 2 * sq + nh, :], start=(nh == 0), stop=(nh == 1)),
                              reads=["hy_T", "hy_x256"], writes=[pwk], inc=(nh == 1))
                    a_, b_ = t1[i_], t2[i_]
                    fw.op(fw.dve, lambda v: v.tensor_tensor(out=a_[:], in0=pa[:, 0:256], in1=Kab[:, 0, ka, :], op=ALU.mult), reads=[pak, "hy_K256"], writes=[("hy_t1", i_)])
                    fw.op(fw.dve, lambda v: v.tensor_tensor(out=b_[:], in0=pw[:, 0:256], in1=Kab[:, 1, ka, :], op=ALU.mult), reads=[pwk, "hy_K256"], writes=[("hy_t2", i_)])
                    fw.op(fw.pool, lambda g: g.tensor_tensor(out=Y[:, sq, ka, :], in0=a_[:], in1=b_[:], op=ALU.add), reads=[("hy_t1", i_), ("hy_t2", i_)], writes=["hy_Y256"])
                for chalf in range(2):
                    ps, pk = self.pb[4 + (2 * sq + chalf) % 4], ("pb", 4 + (2 * sq + chalf) % 4)
                    for ka in range(4):
                        fw.op(fw.pe, lambda t_: t_.matmul(ps[:, 0:256], lhsT=Y[:, sq, ka, chalf * 128:(chalf + 1) * 128], rhs=T[:, 5, ka * 256:(ka + 1) * 256], start=(ka == 0), stop=(ka == 3)),
                              reads=["hy_Y256", "hy_T"], writes=[pk], inc=(ka == 3))
                    if chalf == 0:
                        fw.op(fw.dve, lambda v: v.tensor_copy(out=y32[:, chalf, sq * 256:(sq + 1) * 256], in_=ps[:, 0:256]), reads=[pk], writes=["hy_y256"])
                    else:
                        fw.op(fw.act, lambda a: a.copy(out=y32[:, chalf, sq * 256:(sq + 1) * 256], in_=ps[:, 0:256]), reads=[pk], writes=["hy_y256"])
            for chalf in range(2):
                fw.dma(self.hyT[chalf * 128:(chalf + 1) * 128, 0:512], y32[:, chalf, :], reads=["hy_y256"], writes=[("hyT", chalf)])

    def hy_conv(self, o, t0, L_):
        nc, fw = self.nc, self.fw
        NH = L_ // 128
        KA = L_ // 64
        with ExitStack() as es:
            sb = lambda n, s, d=F32: es.enter_context(nc.sbuf_tensor(self.uniq(n), s, d))
            FAt = sb("hy_FAt", [128, 128, 128], BF16); GB = sb("hy_GBs", [128, 12, 128], BF16)
            FAi = sb("hy_FAi", [128, 128, NH], BF16)
            xt = sb("hy_xt", [128, 16, 256], BF16); stg = [sb(f"hy_stg{i}", [128, 16, 256], BF16) for i in range(2)]
            A2 = sb("hy_A2", [128, 2, 8, 256], BF16)
            Kab = sb("hy_Kab", [128, 2, 8, 256])
            Y = sb("hy_Y", [128, KA, 256], BF16)
            t1 = [sb(f"hy_t1{i}", [128, 256]) for i in range(2)]; t2 = [sb(f"hy_t2{i}", [128, 256]) for i in range(2)]
            stB = [sb(f"hy_stB{i}", [128, 2, 8, 256], BF16) for i in range(2)]
            Bq2 = sb("hy_Bq2", [128, 16, 256], BF16)
            y32 = sb("hy_y32", [128, 2, L_])
            fw.dma(FAt[:], self.hy_FA[L_][:, :, :], writes=["hy_FAt"])
            fw.dma(FAi[:], self.hy_FAi[L_][:, :, :], writes=["hy_FAi"])
            fw.dma(GB[:], self.hy_GB[:, :, :], writes=["hy_GBs"])
            fw.op(fw.pool, lambda g: g.memset(xt[:], 0.0), writes=["hy_xt"])
            fw.op(fw.pool, lambda g: g.memset(Bq2[:], 0.0), writes=["hy_Bq2"])
            self.hy_stageA(es, L_, self.ztok[t0:t0 + L_, :], 256, FAt, xt, stg)
            nkc = min(8, KA)
            for kc in range(KA // nkc):
                for r in range(2):
                    fw.dma(A2[:, r, 0:nkc, :], self.Ascr[r * 64 + kc * nkc:r * 64 + (kc + 1) * nkc, :, 0:256].rearrange("k l c -> l k c"),
                           reads=["Ascr"], writes=["hy_A2"])
                    fw.dma(Kab[:, r, 0:nkc, :], self.Ksp[L_][o, r, :, kc * nkc:(kc + 1) * nkc, :], reads=["Ksp"], writes=["hy_Kab"])
                for kk in range(nkc):
                    ka = kc * nkc + kk
                    pa, pak = self.pb[(2 * kk) % 4], ("pb", (2 * kk) % 4)
                    pw, pwk = self.pb[(2 * kk + 1) % 4], ("pb", (2 * kk + 1) % 4)
                    for ii, (r, gi) in enumerate([(0, 0), (1, 1)]):
                        fw.op(fw.pe, lambda t_: t_.matmul(pa[:, 0:256], lhsT=GB[:, gi, :], rhs=A2[:, r, kk, :], start=(ii == 0), stop=(ii == 1)),
                              reads=["hy_GBs", "hy_A2"], writes=[pak], inc=(ii == 1))
                    for ii, (r, gi) in enumerate([(0, 2), (1, 3)]):
                        fw.op(fw.pe, lambda t_: t_.matmul(pw[:, 0:256], lhsT=GB[:, gi, :], rhs=A2[:, r, kk, :], start=(ii == 0), stop=(ii == 1)),
                              reads=["hy_GBs", "hy_A2"], writes=[pwk], inc=(ii == 1))
                    a_, b_ = t1[kk % 2], t2[kk % 2]
                    fw.op(fw.dve, lambda v: v.tensor_tensor(out=a_[:], in0=pa[:, 0:256], in1=Kab[:, 0, kk, :], op=ALU.mult), reads=[pak, "hy_Kab"], writes=[("hy_t1", kk % 2)])
                    fw.op(fw.dve, lambda v: v.tensor_tensor(out=b_[:], in0=pw[:, 0:256], in1=Kab[:, 1, kk, :], op=ALU.mult), reads=[pwk, "hy_Kab"], writes=[("hy_t2", kk % 2)])
                    fw.op(fw.pool, lambda g: g.tensor_tensor(out=Y[:, ka, :], in0=a_[:], in1=b_[:], op=ALU.add), reads=[("hy_t1", kk % 2), ("hy_t2", kk % 2)], writes=["hy_Y"])
            for kc in range(KA // nkc):
                sB = stB[kc % 2]; sBk = ("hy_stB", kc % 2)
                for kk in range(nkc):
                    ka = kc * nkc + kk
                    for r in range(2):
                        ps, pk = self.pb[4 + (2 * kk + r) % 4], ("pb", 4 + (2 * kk + r) % 4)
                        fw.op(fw.pe, lambda t_: t_.matmul(ps[:, 0:256], lhsT=GB[:, 10 + r, :], rhs=Y[:, ka, :], start=True, stop=True), reads=["hy_GBs", "hy_Y"], writes=[pk])
                        if r == 0:
                            fw.op(fw.dve, lambda v: v.tensor_copy(out=sB[:, r, kk, :], in_=ps[:, 0:256]), reads=[pk], writes=[sBk])
                        else:
                            fw.op(fw.act, lambda a: a.activation(out=sB[:, r, kk, :], in_=ps[:, 0:256], func=AF.Identity), reads=[pk], writes=[sBk])
                for r in range(2):
                    fw.dma(self.Bscr[:, r, kc * nkc:(kc + 1) * nkc, :], sB[:, r, 0:nkc, :], reads=[sBk], writes=["Bscr"])
            for ch in range(8):
                for r in range(2):
                    fw.dma(Bq2[r * 64:r * 64 + KA, :, :], self.Bscr[ch * 16:(ch + 1) * 16, r, 0:KA, :].rearrange("l k c -> k l c"), reads=["Bscr"], writes=["hy_Bq2"])
                for chalf in range(2):
                    ps, pk = self.pb[(2 * ch + chalf) % 4], ("pb", (2 * ch + chalf) % 4)
                    for i in range(16):
                        n_lo = ch * 16 + i
                        fw.op(fw.pe, lambda t_: t_.matmul(ps[:, i * NH:(i + 1) * NH], lhsT=Bq2[:, i, chalf * 128:(chalf + 1) * 128], rhs=FAi[:, n_lo, :], start=True, stop=True),
                              reads=["hy_Bq2", "hy_FAi"], writes=[pk], inc=(i == 15))
                    ov_ = y32[:, chalf, :].rearrange("p (h l) -> p l h", l=128)[:, ch * 16:(ch + 1) * 16, :]
                    iv_ = ps[:, 0:16 * NH].rearrange("p (j h) -> p j h", h=NH)
                    if chalf == 0:
                        fw.op(fw.dve, lambda v: v.tensor_copy(out=ov_, in_=iv_), reads=[pk], writes=["hy_y32"])
                    else:
                        fw.op(fw.act, lambda a: a.activation(out=ov_, in_=iv_, func=AF.Identity), reads=[pk], writes=["hy_y32"])
            for chalf in range(2):
                fw.dma(self.hyT[chalf * 128:(chalf + 1) * 128, t0:t0 + L_], y32[:, chalf, :], reads=["hy_y32"], writes=[("hyT", chalf)])

    def s5(self, l):
        nc, fw = self.nc, self.fw
        NA = NT // 8
        TWO_PI = 2.0 * math.pi
        with ExitStack() as es:
            sb = lambda n, s, d=F32: es.enter_context(nc.sbuf_tensor(self.uniq(n), s, d))
            zb = sb("s5_zb", [128, 2, NT], BF16)
            CQ = sb("s5_CQ", [128, 16, 2, 8, 128], BF16)
            BD = sb("s5_BD", [128, 4, 8, 128], BF16)
            HM = sb("s5_HM", [128, 16, 2, NA], BF16)
            sc = sb("s5_scS", [128, 3, 16]); h0 = sb("s5_h0S", [128, 2, 16]); skp = sb("s5_skS", [128, 2])
            idx = sb("s5_idxS", [128, 513])
            W = {}
            for nm in ("st", "ar", "th", "rho", "c1", "s1", "pr1", "pi1", "nr", "dd", "fr", "fi", "t1", "t2", "rho8", "phr", "th2"):
                W[nm] = sb("s5w_" + nm, [128, 16])
            wi = sb("s5w_i", [128, 16], I32)
            PWr = sb("s5_PWr", [128, 9, 16]); PWi = sb("s5_PWi", [128, 9, 16]); NPWi = sb("s5_NPWi", [128, 9, 16])
            for h in range(2):
                fw.dma(zb[:, h, :], self.zbT[h * 128:(h + 1) * 128, :], reads=[("zbT", h)], writes=["s5_zb"])
            fw.dma(sc[:], self.s5_sc[l], writes=["s5_sc"]); fw.dma(h0[:], self.s5_h0[l], writes=["s5_h0"])
            fw.dma(skp[:], self.s5_skip[l], writes=["s5_skp"]); fw.dma(idx[:], self.s5_idx[:, :], writes=["s5_idx"])
            K = "s5_small"
            dv = lambda fn, r=(K, "s5_sc"), w=(K,): fw.op(fw.dve, fn, reads=list(r), writes=list(w))
            ac = lambda fn, r=(K, "s5_sc"), w=(K,): fw.op(fw.act, fn, reads=list(r), writes=list(w))
            lre, lim, lst = sc[:, 0, :], sc[:, 1, :], sc[:, 2, :]

            def reduce_sin(dst, src, tmpf, tmpi):
                dv(lambda v: v.tensor_scalar(out=tmpi, in0=src, scalar1=1.0 / TWO_PI, scalar2=None, op0=ALU.mult))
                dv(lambda v: v.tensor_copy(out=tmpf, in_=tmpi))
                dv(lambda v: v.scalar_tensor_tensor(out=src, in0=tmpf, scalar=-TWO_PI, in1=src, op0=ALU.mult, op1=ALU.add))
                dv(lambda v: v.tensor_scalar(out=src, in0=src, scalar1=3.141592, scalar2=-3.141592, op0=ALU.min, op1=ALU.max))
                ac(lambda a: a.activation(out=dst, in_=src, func=AF.Sin))

            ac(lambda a: a.activation(out=W["st"][:], in_=lst, func=AF.Exp))
            dv(lambda v: v.tensor_tensor(out=W["ar"][:], in0=lre, in1=W["st"][:], op=ALU.mult))
            dv(lambda v: v.tensor_tensor(out=W["th"][:], in0=lim, in1=W["st"][:], op=ALU.mult))
            ac(lambda a: a.activation(out=W["rho"][:], in_=W["ar"][:], func=AF.Exp))
            ac(lambda a: a.activation(out=W["rho8"][:], in_=W["ar"][:], func=AF.Exp, scale=8.0))
            dv(lambda v: v.tensor_scalar(out=W["phr"][:], in0=W["th"][:], scalar1=8.0, scalar2=None, op0=ALU.mult))
            dv(lambda v: v.tensor_scalar(out=W["th2"][:], in0=W["th"][:], scalar1=math.pi / 2, scalar2=None, op0=ALU.add))
            reduce_sin(W["s1"][:], W["th"][:], W["t1"][:], wi[:])
            reduce_sin(W["c1"][:], W["th2"][:], W["t1"][:], wi[:])
            dv(lambda v: v.tensor_scalar(out=wi[:], in0=W["phr"][:], scalar1=1.0 / TWO_PI, scalar2=None, op0=ALU.mult))
            dv(lambda v: v.tensor_copy(out=W["t1"][:], in_=wi[:]))
            dv(lambda v: v.scalar_tensor_tensor(out=W["phr"][:], in0=W["t1"][:], scalar=-TWO_PI, in1=W["phr"][:], op0=ALU.mult, op1=ALU.add))
            dv(lambda v: v.tensor_tensor(out=W["pr1"][:], in0=W["rho"][:], in1=W["c1"][:], op=ALU.mult))
            dv(lambda v: v.tensor_tensor(out=W["pi1"][:], in0=W["rho"][:], in1=W["s1"][:], op=ALU.mult))
            dv(lambda v: v.tensor_scalar(out=W["nr"][:], in0=W["pr1"][:], scalar1=-1.0, scalar2=None, op0=ALU.add))
            dv(lambda v: v.tensor_tensor(out=W["dd"][:], in0=lre, in1=lre, op=ALU.mult))
            dv(lambda v: v.tensor_tensor(out=W["t1"][:], in0=lim, in1=lim, op=ALU.mult))
            dv(lambda v: v.tensor_tensor(out=W["dd"][:], in0=W["dd"][:], in1=W["t1"][:], op=ALU.add))
            dv(lambda v: v.reciprocal(out=W["dd"][:], in_=W["dd"][:]))
            dv(lambda v: v.tensor_tensor(out=W["t1"][:], in0=W["nr"][:], in1=lre, op=ALU.mult))
            dv(lambda v: v.tensor_tensor(out=W["t2"][:], in0=W["pi1"][:], in1=lim, op=ALU.mult))
            dv(lambda v: v.tensor_tensor(out=W["t1"][:], in0=W["t1"][:], in1=W["t2"][:], op=ALU.add))
            dv(lambda v: v.tensor_tensor(out=W["fr"][:], in0=W["t1"][:], in1=W["dd"][:], op=ALU.mult))
            dv(lambda v: v.tensor_tensor(out=W["t1"][:], in0=W["pi1"][:], in1=lre, op=ALU.mult))
            dv(lambda v: v.tensor_tensor(out=W["t2"][:], in0=W["nr"][:], in1=lim, op=ALU.mult))
            dv(lambda v: v.tensor_tensor(out=W["t1"][:], in0=W["t1"][:], in1=W["t2"][:], op=ALU.subtract))
            dv(lambda v: v.tensor_tensor(out=W["fi"][:], in0=W["t1"][:], in1=W["dd"][:], op=ALU.mult))
            dv(lambda v: v.memset(PWr[:, 0, :], 1.0)); dv(lambda v: v.memset(PWi[:, 0, :], 0.0))
            for j in range(8):
                dv(lambda v: v.tensor_tensor(out=W["t1"][:], in0=PWr[:, j, :], in1=W["pr1"][:], op=ALU.mult))
                dv(lambda v: v.tensor_tensor(out=W["t2"][:], in0=PWi[:, j, :], in1=W["pi1"][:], op=ALU.mult))
                dv(lambda v: v.tensor_tensor(out=PWr[:, j + 1, :], in0=W["t1"][:], in1=W["t2"][:], op=ALU.subtract))
                dv(lambda v: v.tensor_tensor(out=W["t1"][:], in0=PWr[:, j, :], in1=W["pi1"][:], op=ALU.mult))
                dv(lambda v: v.tensor_tensor(out=W["t2"][:], in0=PWi[:, j, :], in1=W["pr1"][:], op=ALU.mult))
                dv(lambda v: v.tensor_tensor(out=PWi[:, j + 1, :], in0=W["t1"][:], in1=W["t2"][:], op=ALU.add))
            dv(lambda v: v.tensor_scalar(out=NPWi[:], in0=PWi[:], scalar1=-1.0, scalar2=None, op0=ALU.mult))

            def ucols(d, h, r, a_lo, n):
                if d == 0:
                    s0 = r + 8 * a_lo
                    return zb[:, h, s0:s0 + 8 * (n - 1) + 1:8]
                s0 = NT - 1 - r - 8 * a_lo
                e0 = s0 - 8 * n
                return zb[:, h, s0:e0:-8] if e0 >= 0 else zb[:, h, s0::-8]

            with ExitStack() as es2:
                sb2 = lambda n, s, d=F32: es2.enter_context(nc.sbuf_tensor(self.uniq(n), s, d))
                Bm = [sb2(f"s5_Bm{i}", [128, 2, 128]) for i in range(2)]
                Cm = [sb2(f"s5_Cm{i}", [128, 2, 128]) for i in range(2)]
                nCi = [sb2(f"s5_nCi{i}", [128, 128]) for i in range(2)]
                Bb = sb2("s5_Bb", [128, 2, 128]); tA = sb2("s5_tA", [128, 128]); tB = sb2("s5_tB", [128, 128])
                Pm = [sb2("s5_Pm0", [128, 2, 8, 128])] * 2
                WV = [sb2("s5_WV0", [128, 2, 8, 128], BF16)] * 2
                Vs = sb2("s5_Vs", [128, 2, NA])
                Tt = sb2("s5_Tt", [128, 2, 513]); ti = sb2("s5_ti", [128, 2, 513], I32); tf = sb2("s5_tf", [128, 2, 513])
                TA = sb2("s5_TA", [128, 2, NA]); TH = sb2("s5_TH", [128, 2, NA + 3])
                Wr = sb2("s5_Wr", [128, 2, NA]); w1 = sb2("s5_w1", [128, NA + 3]); w2 = sb2("s5_w2", [128, NA + 3])
                Gm = sb2("s5_Gm", [128, 2, NA + 3]); Hf = sb2("s5_Hf", [128, 2, NA + 3])
                for d in range(2):
                    if d == 0:
                        segs = [(0, 32, 0), (32, 32, 1), (64, 512, 2)]
                    else:
                        segs = [(0, 512, 2), (512, 32, 1), (544, 32, 0)]
                    for pr in range(8):
                        col = d * 8 + pr
                        h = pr // 4
                        q = pr % 2
                        Bk, Ck, Pk, WVk = ("s5_Bm", q), ("s5_Cm", q), ("s5_Pm", 0), ("s5_WV", 0)
                        for ri in range(2):
                            fw.dma(Bm[q][:, ri, :], self.s5_B[l, d, pr, ri], writes=[Bk])
                            fw.dma(Cm[q][:, ri, :], self.s5_C[l, d, pr, ri], writes=[Ck])
                        fr, fi = W["fr"][:, col:col + 1], W["fi"][:, col:col + 1]
                        fw.op(fw.dve, lambda v: v.tensor_scalar(out=tA[:], in0=Bm[q][:, 1, :], scalar1=fi, scalar2=None, op0=ALU.mult), reads=[Bk, K], writes=["s5_tA"])
                        fw.op(fw.dve, lambda v: v.scalar_tensor_tensor(out=Bb[:, 0, :], in0=Bm[q][:, 0, :], scalar=fr, in1=tA[:], op0=ALU.mult, op1=ALU.subtract),
                              reads=[Bk, K, "s5_tA"], writes=["s5_Bb"])
                        fw.op(fw.dve, lambda v: v.tensor_scalar(out=tB[:], in0=Bm[q][:, 0, :], scalar1=fi, scalar2=None, op0=ALU.mult), reads=[Bk, K], writes=["s5_tB"])
                        fw.op(fw.dve, lambda v: v.scalar_tensor_tensor(out=Bb[:, 1, :], in0=Bm[q][:, 1, :], scalar=fr, in1=tB[:], op0=ALU.mult, op1=ALU.add),
                              reads=[Bk, K, "s5_tB"], writes=["s5_Bb"])
                        fw.op(fw.pool, lambda g: g.tensor_scalar(out=nCi[q][:], in0=Cm[q][:, 1, :], scalar1=-1.0, scalar2=None, op0=ALU.mult), reads=[Ck], writes=[("s5_nCi", q)])
                        for j in range(8):
                            pr_j, pi_j = PWr[:, j, col:col + 1], PWi[:, j, col:col + 1]
                            fw.op(fw.dve, lambda v: v.tensor_scalar(out=tA[:], in0=Bb[:, 1, :], scalar1=pi_j, scalar2=None, op0=ALU.mult), reads=["s5_Bb", K], writes=["s5_tA"])
                            fw.op(fw.dve, lambda v: v.scalar_tensor_tensor(out=Pm[q][:, 0, j, :], in0=Bb[:, 0, :], scalar=pr_j, in1=tA[:], op0=ALU.mult, op1=ALU.subtract),
                                  reads=["s5_Bb", K, "s5_tA"], writes=[Pk])
                            fw.op(fw.dve, lambda v: v.tensor_scalar(out=tB[:], in0=Bb[:, 0, :], scalar1=pi_j, scalar2=None, op0=ALU.mult), reads=["s5_Bb", K], writes=["s5_tB"])
                            fw.op(fw.dve, lambda v: v.scalar_tensor_tensor(out=Pm[q][:, 1, j, :], in0=Bb[:, 1, :], scalar=pr_j, in1=tB[:], op0=ALU.mult, op1=ALU.add),
                                  reads=["s5_Bb", K, "s5_tB"], writes=[Pk])
                        for r in range(8):
                            p1, q1, nq1 = PWr[:, r + 1, col:col + 1], PWi[:, r + 1, col:col + 1], NPWi[:, r + 1, col:col + 1]
                            fw.op(fw.dve, lambda v: v.tensor_scalar(out=tA[:], in0=Cm[q][:, 1, :], scalar1=q1, scalar2=None, op0=ALU.mult), reads=[Ck, K], writes=["s5_tA"])
                            fw.op(fw.dve, lambda v: v.scalar_tensor_tensor(out=CQ[:, col, 0, r, :], in0=Cm[q][:, 0, :], scalar=p1, in1=tA[:], op0=ALU.mult, op1=ALU.subtract),
                                  reads=[Ck, K, "s5_tA"], writes=["s5_CQ"])
                            fw.op(fw.dve, lambda v: v.tensor_scalar(out=tB[:], in0=Cm[q][:, 1, :], scalar1=p1, scalar2=None, op0=ALU.mult), reads=[Ck, K], writes=["s5_tB"])
                            fw.op(fw.dve, lambda v: v.scalar_tensor_tensor(out=CQ[:, col, 1, r, :], in0=Cm[q][:, 0, :], scalar=nq1, in1=tB[:], op0=ALU.mult, op1=ALU.subtract),
                                  reads=[Ck, K, "s5_tB"], writes=["s5_CQ"])
                        for ri in range(2):
                            for half in range(2):
                                ps, pk = self.pb[half], ("pb", half)
                                for jj in range(4):
                                    rp = half * 4 + jj
                                    fw.op(fw.pe, lambda t_: t_.transpose(ps[:, jj * 128:(jj + 1) * 128], Pm[q][:, ri, 7 - rp, :], self.ident[:]),
                                          reads=[Pk, "ident"], writes=[pk], inc=(jj == 3))
                                fw.op(fw.act, lambda a: a.activation(out=WV[q][:, ri, half * 4:(half + 1) * 4, :], in_=ps[:].rearrange("p (j c) -> p j c", j=4), func=AF.Identity),
                                      reads=[pk], writes=[WVk])
                        for j in range(8):
                            ps, pk = self.pb[6 + j // 4], ("pb", 6 + j // 4)
                            first = (pr % 4 == 0) and (j % 4 == 0)
                            last = (pr % 4 == 3) and (j % 4 == 3)
                            fw.op(fw.pe, lambda t_: t_.matmul(ps[:, (j % 4) * 128:(j % 4 + 1) * 128], lhsT=Pm[q][:, 0, j, :], rhs=Cm[q][:, 0, :], start=first, stop=False,
                                                             skip_group_check=True), reads=[Pk, Ck], writes=[pk], inc=False)
                            fw.op(fw.pe, lambda t_: t_.matmul(ps[:, (j % 4) * 128:(j % 4 + 1) * 128], lhsT=Pm[q][:, 1, j, :], rhs=nCi[q][:], start=False, stop=last,
                                                             skip_group_check=True), reads=[Pk, ("s5_nCi", q)], writes=[pk], inc=(j % 4 == 3))
                        if pr % 4 == 3:
                            for hb in range(2):
                                fw.op(fw.act, lambda a: a.activation(out=BD[:, d * 2 + h, hb * 4:(hb + 1) * 4, :], in_=self.pb[6 + hb][:].rearrange("p (j c) -> p j c", j=4),
                                                                      func=AF.Identity), reads=[("pb", 6 + hb)], writes=["s5_BD"])
                        for ri in range(2):
                            for (b0, nb_, bank) in ((0, 512, 2 + 2 * ri), (512, 64, 3 + 2 * ri)):
                                ps, pk = self.pb[bank], ("pb", bank)
                                for rp in range(8):
                                    fw.op(fw.pe, lambda t_: t_.matmul(ps[:, 0:nb_], lhsT=WV[q][:, ri, rp, :], rhs=ucols(d, h, rp, b0, nb_), start=(rp == 0), stop=(rp == 7)),
                                          reads=[WVk, "s5_zb"], writes=[pk], inc=(rp == 7))
                                if ri == 0:
                                    fw.op(fw.act, lambda a: a.copy(out=Vs[:, ri, b0:b0 + nb_], in_=ps[:, 0:nb_]), reads=[pk], writes=["s5_Vs"])
                                else:
                                    fw.op(fw.dve, lambda v: v.tensor_copy(out=Vs[:, ri, b0:b0 + nb_], in_=ps[:, 0:nb_]), reads=[pk], writes=["s5_Vs"])
                        fw.op(fw.dve, lambda v: v.tensor_scalar(out=Tt[:, 1, :], in0=idx[:], scalar1=W["phr"][:, col:col + 1], scalar2=None, op0=ALU.mult), reads=["s5_idx", K], writes=["s5_Tt"])
                        fw.op(fw.dve, lambda v: v.tensor_scalar(out=Tt[:, 0, :], in0=Tt[:, 1, :], scalar1=math.pi / 2, scalar2=None, op0=ALU.add), reads=["s5_Tt"], writes=["s5_Tt"])
                        fw.op(fw.dve, lambda v: v.tensor_scalar(out=ti[:], in0=Tt[:], scalar1=1.0 / TWO_PI, scalar2=None, op0=ALU.mult), reads=["s5_Tt"], writes=["s5_ti"])
                        fw.op(fw.dve, lambda v: v.tensor_copy(out=tf[:], in_=ti[:]), reads=["s5_ti"], writes=["s5_tf"])
                        fw.op(fw.dve, lambda v: v.scalar_tensor_tensor(out=Tt[:], in0=tf[:], scalar=-TWO_PI, in1=Tt[:], op0=ALU.mult, op1=ALU.add), reads=["s5_tf", "s5_Tt"], writes=["s5_Tt"])
                        fw.op(fw.dve, lambda v: v.tensor_scalar(out=Tt[:], in0=Tt[:], scalar1=3.141592, scalar2=-3.141592, op0=ALU.min, op1=ALU.max), reads=["s5_Tt"], writes=["s5_Tt"])
                        fw.op(fw.act, lambda a: a.activation(out=Tt[:], in_=Tt[:], func=AF.Sin), reads=["s5_Tt"], writes=["s5_Tt"])
                        hoff = 0
                        for (a0, A1, sid) in segs:
                            fw.op(fw.act, lambda a: a.copy(out=TA[:, :, a0:a0 + A1], in_=Tt[:, :, 1:A1 + 1]), reads=["s5_Tt"], writes=["s5_TA"])
                            fw.op(fw.pool, lambda g: g.tensor_copy(out=TH[:, :, hoff:hoff + A1 + 1], in_=Tt[:, :, 0:A1 + 1]), reads=["s5_Tt"], writes=["s5_TH"])
                            hoff += A1 + 1
                        fw.op(fw.dve, lambda v: v.tensor_tensor(out=w1[:, 0:NA], in0=Vs[:, 0, :], in1=TA[:, 0, :], op=ALU.mult), reads=["s5_Vs", "s5_TA"], writes=["s5_w1"])
                        fw.op(fw.dve, lambda v: v.tensor_tensor(out=w2[:, 0:NA], in0=Vs[:, 1, :], in1=TA[:, 1, :], op=ALU.mult), reads=["s5_Vs", "s5_TA"], writes=["s5_w2"])
                        fw.op(fw.pool, lambda g: g.tensor_tensor(out=Wr[:, 0, :], in0=w1[:, 0:NA], in1=w2[:, 0:NA], op=ALU.add), reads=["s5_w1", "s5_w2"], writes=["s5_Wr"])
                        fw.op(fw.dve, lambda v: v.tensor_tensor(out=w1[:, 0:NA], in0=Vs[:, 1, :], in1=TA[:, 0, :], op=ALU.mult), reads=["s5_Vs", "s5_TA", "s5_w1"], writes=["s5_w1"])
                        fw.op(fw.dve, lambda v: v.tensor_tensor(out=w2[:, 0:NA], in0=Vs[:, 0, :], in1=TA[:, 1, :], op=ALU.mult), reads=["s5_Vs", "s5_TA", "s5_w2"], writes=["s5_w2"])
                        fw.op(fw.pool, lambda g: g.tensor_tensor(out=Wr[:, 1, :], in0=w1[:, 0:NA], in1=w2[:, 0:NA], op=ALU.subtract), reads=["s5_w1", "s5_w2"], writes=["s5_Wr"])
                        hoff = 0
                        for (a0, A1, sid) in segs:
                            for ri in range(2):
                                if sid == 2:
                                    fw.op(fw.act, lambda a: a.copy(out=Gm[:, ri, hoff:hoff + 1], in_=h0[:, ri, col:col + 1]), reads=["s5_h0"], writes=["s5_Gm"])
                                    init = h0[:, ri, col:col + 1]
                                else:
                                    fw.op(fw.pool, lambda g: g.memset(Gm[:, ri, hoff:hoff + 1], 0.0), writes=["s5_Gm"])
                                    init = 0.0
                                fw.op(fw.dve, lambda v: v.tensor_tensor_scan(out=Gm[:, ri, hoff + 1:hoff + 1 + A1], data0=W["rho8"][:, col:col + 1].to_broadcast([128, A1]),
                                                                               data1=Wr[:, ri, a0:a0 + A1], initial=init, op0=ALU.mult, op1=ALU.add),
                                      reads=["s5_Wr", K, "s5_h0"], writes=["s5_Gm"])
                            hoff += A1 + 1
                        NH_ = NA + 3
                        fw.op(fw.dve, lambda v: v.tensor_tensor(out=w1[:], in0=Gm[:, 0, :], in1=TH[:, 0, :], op=ALU.mult), reads=["s5_Gm", "s5_TH", "s5_w1"], writes=["s5_w1"])
                        fw.op(fw.dve, lambda v: v.tensor_tensor(out=w2[:], in0=Gm[:, 1, :], in1=TH[:, 1, :], op=ALU.mult), reads=["s5_Gm", "s5_TH", "s5_w2"], writes=["s5_w2"])
                        fw.op(fw.pool, lambda g: g.tensor_tensor(out=Hf[:, 0, :], in0=w1[:], in1=w2[:], op=ALU.subtract), reads=["s5_w1", "s5_w2"], writes=["s5_Hf"])
                        fw.op(fw.dve, lambda v: v.tensor_tensor(out=w1[:], in0=Gm[:, 0, :], in1=TH[:, 1, :], op=ALU.mult), reads=["s5_Gm", "s5_TH", "s5_w1"], writes=["s5_w1"])
                        fw.op(fw.dve, lambda v: v.tensor_tensor(out=w2[:], in0=Gm[:, 1, :], in1=TH[:, 0, :], op=ALU.mult), reads=["s5_Gm", "s5_TH", "s5_w2"], writes=["s5_w2"])
                        fw.op(fw.pool, lambda g: g.tensor_tensor(out=Hf[:, 1, :], in0=w1[:], in1=w2[:], op=ALU.add), reads=["s5_w1", "s5_w2"], writes=["s5_Hf"])
                        hoff = 0
                        for (a0, A1, sid) in segs:
                            fw.op(fw.act, lambda a: a.activation(out=HM[:, col, :, a0:a0 + A1], in_=Hf[:, :, hoff:hoff + A1], func=AF.Identity), reads=["s5_Hf"], writes=["s5_HM"])
                            if sid < 2:
                                for ri in range(2):
                                    fw.dma(self.osf[l, sid, ri, d, 2 * pr:2 * pr + 2, :].rearrange("e (n o) -> (e n) o", o=1), Hf[:, ri, hoff + A1:hoff + A1 + 1],
                                           reads=["s5_Hf"], is_output=True)
                            hoff += A1 + 1
            fw.barrier()
            with ExitStack() as es3:
                sb3 = lambda n, s, d=F32: es3.enter_context(nc.sbuf_tensor(self.uniq(n), s, d))
                GW = sb3("s5_GW", [128, 2, 512], BF16); gb = sb3("s5_gbS", [128, 4])
                y16 = [sb3(f"s5_y16{i}", [128, 2, 512], BF16) for i in range(2)]
                sg = [sb3(f"s5_sg{i}", [128, 512]) for i in range(2)]
                yo = [sb3(f"s5_yo{i}", [128, 512], BF16) for i in range(2)]
                self.load_w_bf16(GW, self.s5_gw[l], "s5_GW", 1)
                fw.dma(gb[:], self.s5_gb[l], writes=["s5_gb"])
                for blk in range(NT // 512):
                    c0 = blk * 512
                    yb_ = y16[blk % 2]; ybk = ("s5_y16", blk % 2)
                    for h in range(2):
                        ps, pk = self.pb[h], ("pb", h)
                        first = True
                        for d in range(2):
                            a_lo = blk * 64 if d == 0 else NA - 64 * (blk + 1)
                            for r in range(8):
                                if d == 0:
                                    oap = ps[:, r:r + 8 * 63 + 1:8]
                                else:
                                    s0 = 511 - r
                                    oap = ps[:, s0:(7 - r) - 1:-8] if (7 - r) - 1 >= 0 else ps[:, s0::-8]
                                for r2 in range(r + 1):
                                    fw.op(fw.pe, lambda t_: t_.matmul(oap, lhsT=BD[:, d * 2 + h, r - r2, :], rhs=ucols(d, h, r2, a_lo, 64), start=first, stop=False,
                                                                     skip_group_check=True), reads=["s5_BD", "s5_zb"], writes=[pk], inc=False)
                                    first = False
                                for pr in range(4 * h, 4 * h + 4):
                                    col = d * 8 + pr
                                    for ri in range(2):
                                        lastmm = (d == 1 and r == 7 and pr == 4 * h + 3 and ri == 1)
                                        fw.op(fw.pe, lambda t_: t_.matmul(oap, lhsT=CQ[:, col, ri, r, :], rhs=HM[:, col, ri, a_lo:a_lo + 64], start=False, stop=lastmm,
                                                                         skip_group_check=True), reads=["s5_CQ", "s5_HM"], writes=[pk], inc=lastmm)
                        fw.op(fw.dve, lambda v: v.scalar_tensor_tensor(out=yb_[:, h, :], in0=zb[:, h, c0:c0 + 512], scalar=skp[:, h:h + 1], in1=ps[:], op0=ALU.mult, op1=ALU.add),
                              reads=["s5_zb", "s5_skp", pk], writes=[ybk])
                    for mt in range(2):
                        pa, pak = self.pb[2 + mt], ("pb", 2 + mt)
                        pg, pgk = self.pb[4 + mt], ("pb", 4 + mt)
                        for h in range(2):
                            fw.op(fw.pe, lambda t_: t_.matmul(pa[:], lhsT=GW[:, h, mt * 128:(mt + 1) * 128], rhs=yb_[:, h, :], start=(h == 0), stop=(h == 1)),
                                  reads=["s5_GW", ybk], writes=[pak], inc=(h == 1))
                        for h in range(2):
                            fw.op(fw.pe, lambda t_: t_.matmul(pg[:], lhsT=GW[:, h, (2 + mt) * 128:(3 + mt) * 128], rhs=yb_[:, h, :], start=(h == 0), stop=(h == 1)),
                                  reads=["s5_GW", ybk], writes=[pgk], inc=(h == 1))
                        fw.op(fw.act, lambda a: a.activation(out=sg[mt][:], in_=pg[:], func=AF.Sigmoid, bias=gb[:, 2 + mt:3 + mt], scale=1.0), reads=[pgk, "s5_gb"], writes=[("s5_sg", mt)])
                        fw.op(fw.dve, lambda v: v.scalar_tensor_tensor(out=yo[mt][:], in0=pa[:], scalar=gb[:, mt:mt + 1], in1=sg[mt][:], op0=ALU.add, op1=ALU.mult),
                              reads=[pak, "s5_gb", ("s5_sg", mt)], writes=[("s5_yo", mt)])
                        fw.dma(self.ybT[mt * 128:(mt + 1) * 128, c0:c0 + 512], yo[mt][:], reads=[("s5_yo", mt)], writes=[("ybT", mt)])

    def phase_C1(self, l):
        nc, fw = self.nc, self.fw
        with ExitStack() as es:
            sb = lambda n, s, d=F32: es.enter_context(nc.sbuf_tensor(self.uniq(n), s, d))
            PA = sb("PA", [128, 2, D], BF16)
            PB = sb("PB", [128, 2, D], BF16)
            PC = sb("PC", [128, 4, D], BF16)
            WO = sb("WO", [128, 8, D], BF16)
            xt = sb("xtC", [128, 8, TT])
            ya = sb("yaC", [128, 2, TT], BF16)
            yb = sb("ybC", [128, 2, TT], BF16)
            yc = sb("ycC", [128, 4, TT], BF16)
            gt = sb("gtC", [128, 24, TT], BF16)
            mg = sb("mgC", [128, 8, TT], BF16)
            t1 = [sb(f"t1C{i}", [128, TT]) for i in range(2)]
            t2 = [sb(f"t2C{i}", [128, TT]) for i in range(2)]
            t3 = [sb(f"t3C{i}", [128, TT]) for i in range(2)]
            self.load_w_bf16(PA, self.proj_a[l], "PA", 1)
            self.load_w_bf16(PB, self.proj_b[l], "PB", 1)
            self.load_w_bf16(PC, self.proj_c[l], "PC", 1)
            self.load_w_bf16(WO, self.w_out[l], "WO", 1)
            for tt in range(NTILE):
                c0 = tt * TT
                j = 0 if tt == 0 else 1
                for k in range(8):
                    fw.dma(xt[:, k, :], self.xT[k * 128:(k + 1) * 128, c0:c0 + TT], reads=[("xT", k)], writes=["xt"])
                for k in range(2):
                    fw.dma(ya[:, k, :], self.yaT[k * 128:(k + 1) * 128, c0:c0 + TT], reads=[("yaT", k)], writes=["ya"])
                    fw.dma(yb[:, k, :], self.ybT[k * 128:(k + 1) * 128, c0:c0 + TT], reads=[("ybT", k)], writes=["yb"])
                for k in range(4):
                    fw.dma(yc[:, k, :], self.ycT[k * 128:(k + 1) * 128, c0:c0 + TT], reads=[("ycT", k)], writes=["yc"])
                for m in range(24):
                    fw.dma(gt[:, m, :], self.gT[m * 128:(m + 1) * 128, c0:c0 + TT], reads=[("gT", m)], writes=["gt"])
                for mf in range(8):
                    par = mf % 2
                    pa, pb_, pc = self.pb[par * 3 + 0], self.pb[par * 3 + 1], self.pb[par * 3 + 2]
                    ka, kb, kc = ("pb", par * 3), ("pb", par * 3 + 1), ("pb", par * 3 + 2)
                    cs = slice(mf * 128, (mf + 1) * 128)
                    for k in range(2):
                        fw.op(fw.pe, lambda t, k=k: t.matmul(pa[:], lhsT=PA[:, k, cs], rhs=ya[:, k, :], start=(k == 0), stop=(k == 1)),
                              reads=["PA", "ya"], writes=[ka], inc=(k == 1))
                    for k in range(2):
                        fw.op(fw.pe, lambda t, k=k: t.matmul(pb_[:], lhsT=PB[:, k, cs], rhs=yb[:, k, :], start=(k == 0), stop=(k == 1)),
                              reads=["PB", "yb"], writes=[kb], inc=(k == 1))
                    for k in range(4):
                        fw.op(fw.pe, lambda t, k=k: t.matmul(pc[:], lhsT=PC[:, k, cs], rhs=yc[:, k, :], start=(k == 0), stop=(k == 3)),
                              reads=["PC", "yc"], writes=[kc], inc=(k == 3))
                    a_, b_, c_ = t1[par], t2[par], t3[par]
                    fw.op(fw.dve, lambda v: v.tensor_tensor(out=a_[:], in0=pa[:], in1=gt[:, mf, :], op=ALU.mult), reads=[ka, "gt"], writes=[("t1", par)])
                    fw.op(fw.dve, lambda v: v.tensor_tensor(out=b_[:], in0=pb_[:], in1=gt[:, 8 + mf, :], op=ALU.mult), reads=[kb, "gt"], writes=[("t2", par)])
                    fw.op(fw.dve, lambda v: v.tensor_tensor(out=c_[:], in0=pc[:], in1=gt[:, 16 + mf, :], op=ALU.mult), reads=[kc, "gt"], writes=[("t3", par)])
                    fw.op(fw.pool, lambda g: g.tensor_tensor(out=a_[:], in0=a_[:], in1=b_[:], op=ALU.add), reads=[("t1", par), ("t2", par)], writes=[("t1", par)])
                    fw.op(fw.pool, lambda g: g.tensor_tensor(out=mg[:, mf, :], in0=a_[:], in1=c_[:], op=ALU.add), reads=[("t1", par), ("t3", par)], writes=["mg"])
                for mf in range(8):
                    ps, pk = self.pb[6 + mf % 2], ("pb", 6 + mf % 2)
                    for k in range(8):
                        fw.op(fw.pe, lambda t, k=k: t.matmul(ps[:], lhsT=WO[:, k, mf * 128:(mf + 1) * 128], rhs=mg[:, k, :], start=(k == 0), stop=(k == 7)),
                              reads=["WO", "mg"], writes=[pk], inc=(k == 7))
                    fw.op(fw.dve, lambda v: v.scalar_tensor_tensor(out=xt[:, mf, :], in0=ps[:], scalar=self.modT[:, 16 + mf, j:j + 1], in1=xt[:, mf, :],
                                                                   op0=ALU.mult, op1=ALU.add), reads=[pk, "modT", "xt"], writes=["xt"])
                for k in range(8):
                    fw.dma(self.xT[k * 128:(k + 1) * 128, c0:c0 + TT], xt[:, k, :], reads=["xt"], writes=[("xT", k)])

    def phase_C2(self, l):
        nc, fw = self.nc, self.fw
        with ExitStack() as es:
            sb = lambda n, s, d=F32: es.enter_context(nc.sbuf_tensor(self.uniq(n), s, d))
            WG = sb("WG", [128, 8, FF], BF16)
            WU = sb("WU", [128, 8, FF], BF16)
            xtb = [sb(f"xtD{i}", [128, 8, TT]) for i in range(2)]
            xsqb = [sb(f"xsqD{i}", [128, 8, TT]) for i in range(2)]
            rinb = [sb(f"rinD{i}", [128, TT]) for i in range(2)]
            hTb = [sb(f"hTD{i}", [128, 8, TT], BF16) for i in range(2)]

            def prep(tt):
                bi = tt % 2
                for k in range(8):
                    fw.dma(xtb[bi][:, k, :], self.xT[k * 128:(k + 1) * 128, tt * TT:(tt + 1) * TT], reads=[("xT", k)], writes=[f"xt{bi}"])
                self.norm_mod(es, xtb[bi], self.a2, 24, 0 if tt == 0 else 1, hTb[bi], "D", bufs=(xsqb[bi], rinb[bi]), sfx=str(bi))
            sl = [sb(f"slD{i}", [128, TT]) for i in range(2)]
            ao = [sb(f"aoD{i}", [128, TT], BF16) for i in range(3)]
            self.load_w_bf16(WG, self.w_gate[l], "WG", 2)
            self.load_w_bf16(WU, self.w_up[l], "WU", 2)
            prep(0)
            for tt in range(NTILE):
                c0 = tt * TT
                j = 0 if tt == 0 else 1
                if tt + 1 < NTILE:
                    prep(tt + 1)
                hT = hTb[tt % 2]
                hk = f"hT{tt % 2}"
                for jj in range(NJ):
                    par = jj % 3
                    pg, pu = self.pb[par * 2], self.pb[par * 2 + 1]
                    kg, ku = ("pb", par * 2), ("pb", par * 2 + 1)
                    cs = slice(jj * 128, (jj + 1) * 128)
                    for k in range(8):
                        fw.op(fw.pe, lambda t, k=k: t.matmul(pg[:], lhsT=WG[:, k, cs], rhs=hT[:, k, :], start=(k == 0), stop=(k == 7)),
                              reads=["WG", hk], writes=[kg], inc=(k == 7))
                    for k in range(8):
                        fw.op(fw.pe, lambda t, k=k: t.matmul(pu[:], lhsT=WU[:, k, cs], rhs=hT[:, k, :], start=(k == 0), stop=(k == 7)),
                              reads=["WU", hk], writes=[ku], inc=(k == 7))
                    s_ = sl[jj % 2]; sk = ("sl", jj % 2)
                    a_ = ao[jj % 3]; ak = ("ao", jj % 3)
                    fw.op(fw.act, lambda a: a.activation(out=s_[:], in_=pg[:], func=AF.Silu), reads=[kg], writes=[sk])
                    fw.op(fw.dve, lambda v: v.tensor_tensor(out=a_[:], in0=pu[:], in1=s_[:], op=ALU.mult), reads=[ku, sk], writes=[ak])
                    fw.dma(self.aT[jj * 128:(jj + 1) * 128, c0:c0 + TT], a_[:], reads=[ak], writes=[("aT", jj)])

    def phase_C3(self, l):
        nc, fw = self.nc, self.fw
        with ExitStack() as es:
            sb = lambda n, s, d=F32: es.enter_context(nc.sbuf_tensor(self.uniq(n), s, d))
            WD = sb("WD", [128, NJ, D], BF16)
            xt = sb("xtE", [128, 8, TT])
            at = sb("atE", [128, NJ, TT], BF16)
            self.load_w_bf16(WD, self.w_down[l], "WD", 1)
            for tt in range(NTILE):
                c0 = tt * TT
                j = 0 if tt == 0 else 1
                for k in range(8):
                    fw.dma(xt[:, k, :], self.xT[k * 128:(k + 1) * 128, c0:c0 + TT], reads=[("xT", k)], writes=["xt"])
                for jj in range(NJ):
                    fw.dma(at[:, jj, :], self.aT[jj * 128:(jj + 1) * 128, c0:c0 + TT], reads=[("aT", jj)], writes=["at"])
                for mf in range(8):
                    ps, pk = self.pb[mf % 4], ("pb", mf % 4)
                    for jj in range(NJ):
                        fw.op(fw.pe, lambda t, jj=jj: t.matmul(ps[:], lhsT=WD[:, jj, mf * 128:(mf + 1) * 128], rhs=at[:, jj, :], start=(jj == 0), stop=(jj == NJ - 1)),
                              reads=["WD", "at"], writes=[pk], inc=(jj == NJ - 1))
                    fw.op(fw.dve, lambda v: v.scalar_tensor_tensor(out=xt[:, mf, :], in0=ps[:], scalar=self.modT[:, 40 + mf, j:j + 1], in1=xt[:, mf, :],
                                                                   op0=ALU.mult, op1=ALU.add), reads=[pk, "modT", "xt"], writes=["xt"])
                for k in range(8):
                    fw.dma(self.xT[k * 128:(k + 1) * 128, c0:c0 + TT], xt[:, k, :], reads=["xt"], writes=[("xT", k)])

    def phase_final(self):
        nc, fw = self.nc, self.fw
        with ExitStack() as es:
            sb = lambda n, s, d=F32: es.enter_context(nc.sbuf_tensor(self.uniq(n), s, d))
            xt = sb("xtF", [128, 8, TT])
            self._xsq = sb("xsqF", [128, 8, TT])
            self._rin = sb("rinF", [128, TT])
            hT = sb("hTF", [128, 8, TT])
            gf = sb("gfin", [128, 8])
            fw.dma(gf[:], self.final_g[:, :], writes=["gfin"])
            for tt in range(NTILE):
                c0 = tt * TT
                for k in range(8):
                    fw.dma(xt[:, k, :], self.xT[k * 128:(k + 1) * 128, c0:c0 + TT], reads=[("xT", k)], writes=["xt"])
                self.norm_mod(es, xt, gf, None, 0, hT, "F")
                for k in range(8):
                    fw.dma(self.yT[k * 128:(k + 1) * 128, c0:c0 + TT], hT[:, k, :], reads=["hT"], is_output=True)


_SHARED_KEYS_PER_CORE = ("xT_in", "cT", "cache_k", "cache_v", "s5_h0")


def host_inputs(inputs, core, depth=DEPTH, shared=None):
    if shared is not None and shared.get("ready"):
        return _host_core_only(inputs, core, depth, shared)
    m = _host_inputs_full(inputs, core, depth)
    if shared is not None:
        shared.update({k: v for k, v in m.items() if k not in _SHARED_KEYS_PER_CORE})
        shared["ready"] = True
    return m


def _host_core_only(inputs, core, depth, shared):
    f = lambda a: np.ascontiguousarray(np.asarray(a, dtype=np.float32))
    xp = np.asarray(inputs["x_prompt"])[2 * core:2 * core + 2].reshape(NP, D)
    b = core % 2
    xs = np.asarray(inputs["x_sample"])[b]
    xT = np.concatenate([xp, xs], 0).T
    cT = np.stack([np.asarray(inputs["c_ctx"]), np.asarray(inputs["c"])[b]], 1)

    def pair(a):
        return a.reshape(depth, 2, 8, 2, 64).transpose(0, 3, 4, 1, 2).reshape(depth, 128, 16)
    m = {k: v for k, v in shared.items() if k != "ready"}
    m["xT_in"] = f(xT)
    m["cT"] = f(cT.reshape(8, 128, 2).transpose(1, 0, 2))
    m["cache_k"] = f(np.asarray(inputs["cache_k"])[b, :depth].reshape(depth, 512, 128))
    m["cache_v"] = f(np.asarray(inputs["cache_v"])[b, :depth].reshape(depth, 512, 128))
    m["s5_h0"] = f(np.stack([pair(np.asarray(inputs["state_s5_re"])[b, :depth]), pair(np.asarray(inputs["state_s5_im"])[b, :depth])], 2))
    return m


def _host_inputs_full(inputs, core, depth=DEPTH):
    f = lambda a: np.ascontiguousarray(np.asarray(a, dtype=np.float32))
    xp = np.asarray(inputs["x_prompt"])[2 * core:2 * core + 2].reshape(NP, D)
    b = core % 2
    xs = np.asarray(inputs["x_sample"])[b]
    xT = np.concatenate([xp, xs], 0).T
    cT = np.stack([np.asarray(inputs["c_ctx"]), np.asarray(inputs["c"])[b]], 1)
    perm = _perm_cols()
    w_in = np.asarray(inputs["w_in"])[:depth]
    w_ext = np.concatenate([w_in, w_in[:, :, perm]], axis=2)
    cos2, sin2 = _rope_tables()
    def pair(a):
        return a.reshape(depth, 2, 8, 2, 64).transpose(0, 3, 4, 1, 2).reshape(depth, 128, 16)
    lre = pair(np.asarray(inputs["s5_lam_re"])[:depth]); lim = pair(np.asarray(inputs["s5_lam_im"])[:depth])
    lst = pair(np.broadcast_to(np.asarray(inputs["s5_log_step"])[:depth][..., None], (depth, 2, 16, 64)))
    s5_sc = np.stack([lre, lim, lst], 2)
    s5_h0 = np.stack([pair(np.asarray(inputs["state_s5_re"])[b, :depth]), pair(np.asarray(inputs["state_s5_im"])[b, :depth])], 2)
    s5_B = np.zeros((depth, 2, 8, 2, 128, 128), np.float32); s5_C = np.zeros((depth, 2, 8, 2, 128, 128), np.float32)
    bre = np.asarray(inputs["s5_b_re"])[:depth]; bim = np.asarray(inputs["s5_b_im"])[:depth]
    cre = np.asarray(inputs["s5_c_re"])[:depth]; cim = np.asarray(inputs["s5_c_im"])[:depth]
    for g_ in range(16):
        pr_, e_, gl_ = g_ // 2, g_ % 2, g_ % 8
        s5_B[:, :, pr_, 0, e_ * 64:(e_ + 1) * 64, gl_ * 16:(gl_ + 1) * 16] = bre[:, :, g_]
        s5_B[:, :, pr_, 1, e_ * 64:(e_ + 1) * 64, gl_ * 16:(gl_ + 1) * 16] = bim[:, :, g_]
        s5_C[:, :, pr_, 0, e_ * 64:(e_ + 1) * 64, gl_ * 16:(gl_ + 1) * 16] = cre[:, :, g_].transpose(0, 1, 3, 2)
        s5_C[:, :, pr_, 1, e_ * 64:(e_ + 1) * 64, gl_ * 16:(gl_ + 1) * 16] = cim[:, :, g_].transpose(0, 1, 3, 2)
    pk = lambda a: a.reshape(a.shape[0], -1, 128).transpose(0, 2, 1)
    import ml_dtypes
    bf = lambda a: np.ascontiguousarray(np.asarray(a, dtype=np.float32).astype(ml_dtypes.bfloat16))

    def zp(a, r, c):
        o = np.zeros(a.shape[:-2] + (r, c), np.float32)
        o[..., :a.shape[-2], :a.shape[-1]] = a
        return o
    m = {
        "xT_in": f(xT), "cT": f(cT.reshape(8, 128, 2).transpose(1, 0, 2)),
        "ada_w": f(np.asarray(inputs["ada_w"])[:depth]), "ada_b": f(pk(np.asarray(inputs["ada_b"])[:depth])),
        "norm1_g": f(pk(np.asarray(inputs["norm1_g"])[:depth])), "norm2_g": f(pk(np.asarray(inputs["norm2_g"])[:depth])),
        "final_g": f(pk(np.asarray(inputs["final_norm_g"])[None])[0]),
        "w_in": f(w_ext),
        "proj_a": f(np.asarray(inputs["proj_a"])[:depth]), "proj_b": f(np.asarray(inputs["proj_b"])[:depth]),
        "proj_c": f(np.asarray(inputs["proj_c"])[:depth]), "w_out": f(np.asarray(inputs["w_out"])[:depth]),
        "w_gate": f(np.asarray(inputs["ffn_w_gate"])[:depth]), "w_up": f(np.asarray(inputs["ffn_w_up"])[:depth]),
        "w_down": f(np.asarray(inputs["ffn_w_down"])[:depth]),
        "rope_cos": cos2, "rope_sin": sin2,
        "hy_GB": bf(_hy_gb()),
        "hy_T256": bf(_hy_t256()),
        "hy_w1": f(zp(np.asarray(inputs["hy_pos_w1"])[:depth], 128, 128)),
        "hy_w2": f(zp(np.asarray(inputs["hy_pos_w2"])[:depth], 128, 128)),
        "hy_w3": f(zp(np.asarray(inputs["hy_pos_w3"])[:depth], 128, 1024)),
        "hy_vec": f(zp(np.stack([np.asarray(inputs["hy_pos_b1"])[:depth], np.asarray(inputs["hy_pos_b2"])[:depth],
                                 np.asarray(inputs["hy_sin_freq"])[:depth]], -1), 128, 3)),
        "hy_decay": f(np.asarray(inputs["hy_decay"])[:depth]),
        "hy_cw": f(np.concatenate([np.asarray(inputs["hy_conv_w"])[:depth].transpose(0, 2, 1),
                                   np.asarray(inputs["hy_conv_b"])[:depth][:, :, None]], -1).reshape(depth, 6, 128, 4).transpose(0, 2, 1, 3)),
        "hy_skip": f(np.asarray(inputs["hy_skip"])[:depth].reshape(depth, 2, 2, 128).transpose(0, 3, 1, 2)),
        "s5_sc": f(s5_sc), "s5_B": f(s5_B), "s5_C": f(s5_C), "s5_h0": f(s5_h0),
        "s5_skip": f(np.asarray(inputs["s5_skip"])[:depth].reshape(depth, 2, 128).transpose(0, 2, 1)),
        "s5_gw": f(np.asarray(inputs["s5_glu_w"])[:depth]),
        "s5_gb": f(np.asarray(inputs["s5_glu_b"])[:depth].reshape(depth, 4, 128).transpose(0, 2, 1)),
        "s5_idx": f(np.broadcast_to(np.arange(513, dtype=np.float32), (128, 513))),
        "cache_k": f(np.asarray(inputs["cache_k"])[b, :depth].reshape(depth, 512, 128)),
        "cache_v": f(np.asarray(inputs["cache_v"])[b, :depth].reshape(depth, 512, 128)),
        "sink": f(np.asarray(inputs["attn_sink"])[:depth]),
        "mask_prev": f(np.tril(np.ones((128, 128)))),
        "mask_next": f(np.triu(np.ones((128, 128)))),
    }
    for L_ in (256, 4096):
        FA, FAi = _hy_tables(L_)
        fT, negt = _hy_feat(L_)
        m[f"hy_FA{L_}"] = bf(FA); m[f"hy_FAi{L_}"] = bf(FAi)
        m[f"hy_featT{L_}"] = f(fT); m[f"hy_negt{L_}"] = f(negt)
    return m


_PROG = {}


def get_prog(depth=DEPTH, mixers=("attn", "hyena", "s5"), taps=()):
    key = (depth, tuple(mixers), tuple(taps))
    if key not in _PROG:
        _PROG[key] = Prog(depth, mixers, taps)
    return _PROG[key]


def kernel(**inputs):
    prog = get_prog()
    in_maps = []
    shared = {}
    for c in range(8):
        m = host_inputs(inputs, c, DEPTH, shared)
        in_maps.append({k: m[k] for k in prog.din})
    res = run_bass_kernel_spmd(prog.nc, in_maps, core_ids=list(range(8)))
    R = res.results
    B, S = 16, 256
    y_prompt = np.zeros((B, S, D), np.float32)
    y_sample = np.zeros((2, LS, D), np.float32)
    nk_ = np.zeros((B, DEPTH, S, 2, 64), np.float32)
    nv_ = np.zeros((B, DEPTH, S, 2, 64), np.float32)
    sre = np.zeros((B, DEPTH, 2, 16, 64), np.float32)
    sim = np.zeros((B, DEPTH, 2, 16, 64), np.float32)
    for c in range(8):
        yT = R[c]["yT"]
        y_prompt[2 * c:2 * c + 2] = yT[:, :NP].T.reshape(2, S, D)
        if c < 2:
            y_sample[c] = yT[:, NP:].T
        ok = R[c]["ok"].reshape(DEPTH, 2, S, 2, 64)
        ov = R[c]["ov"].reshape(DEPTH, 2, S, 2, 64)
        nk_[2 * c:2 * c + 2] = ok.transpose(1, 0, 2, 3, 4)
        nv_[2 * c:2 * c + 2] = ov.transpose(1, 0, 2, 3, 4)
        osf = R[c]["osf"]
        sre[2 * c:2 * c + 2] = osf[:, :, 0].transpose(1, 0, 2, 3, 4)
        sim[2 * c:2 * c + 2] = osf[:, :, 1].transpose(1, 0, 2, 3, 4)
    return (y_prompt, y_sample, nk_, nv_, sre, sim)
```

```python
from contextlib import ExitStack
import math
import numpy as np
import concourse.bass as bass
import concourse.mybir as mybir
from concourse.bass_utils import run_bass_kernel_spmd

F32 = mybir.dt.float32
BF16 = mybir.dt.bfloat16
I32 = mybir.dt.int32
AF = mybir.ActivationFunctionType
ALU = mybir.AluOpType
AX = mybir.AxisListType

D = 1024
DEPTH = 4
NP = 512
LS = 4096
NT = NP + LS
TT = 512
NTILE = NT // TT
FF = 2816
NJ = FF // 128
WIN = 4864
WEXT = WIN + 640
EPS = 1e-6


class _Eng:
    def __init__(self, name, h, sem):
        self.name, self.h, self.sem = name, h, sem
        self.count = 0
        self.seen = {}


class _Buf:
    __slots__ = ("w", "r")

    def __init__(self):
        self.w = None
        self.r = []


class FW:
    NDMA = 48

    def __init__(self, nc, es):
        self.nc = nc
        mk = lambda n: es.enter_context(nc.semaphore(n))
        self.sync = _Eng("sync", nc.sync, mk("e_sync"))
        self.act = _Eng("act", nc.scalar, mk("e_act"))
        self.dve = _Eng("dve", nc.vector, mk("e_dve"))
        self.pool = _Eng("pool", nc.gpsimd, mk("e_pool"))
        self.pe = _Eng("pe", nc.tensor, mk("e_pe"))
        self.engs = [self.sync, self.act, self.dve, self.pool, self.pe]
        self.dsems = [mk(f"d{i}") for i in range(self.NDMA)]
        self.dcount = [0] * self.NDMA
        self.dnext = 0
        self.NSW = 16
        self.dnext_sw = 0
        self.bufs = {}
        self.out_events = []
        self.n_inst = 0

    def _rec(self, key):
        b = self.bufs.get(key)
        if b is None:
            b = self.bufs[key] = _Buf()
        return b

    def _deps(self, reads, writes):
        deps = []
        for k in reads:
            b = self._rec(k)
            if b.w is not None:
                deps.append(b.w)
        for k in writes:
            b = self._rec(k)
            if b.w is not None:
                deps.append(b.w)
            deps.extend(b.r)
        return deps

    def _wait(self, eng, deps):
        need = {}
        for (sem, val) in deps:
            i = id(sem)
            if need.get(i, (None, 0))[1] < val:
                need[i] = (sem, val)
        for i, (sem, val) in need.items():
            if eng.seen.get(i, 0) >= val:
                continue
            eng.h.wait_ge(sem, val)
            eng.seen[i] = val

    def _commit(self, ev, reads, writes):
        for k in reads:
            b = self._rec(k)
            for i_, (s_, v_) in enumerate(b.r):
                if s_ is ev[0]:
                    if v_ < ev[1]:
                        b.r[i_] = ev
                    break
            else:
                b.r.append(ev)
        for k in writes:
            b = self._rec(k)
            b.w = ev
            b.r = []

    def op(self, eng, fn, reads=(), writes=(), inc=True):
        pr = [k for k in reads if isinstance(k, tuple) and k[0] == "pb"]
        if pr:
            reads = [k for k in reads if k not in pr]
            writes = list(writes) + pr
        deps = self._deps(reads, writes)
        if eng is self.pe:
            deps = [d for d in deps if d[0] is not eng.sem]
        self._wait(eng, deps)
        ins = fn(eng.h)
        if inc:
            eng.count += 1
            ins.then_inc(eng.sem, 1)
            ev = (eng.sem, eng.count)
        else:
            ev = (eng.sem, eng.count + 1)
        self._commit(ev, reads, writes)
        self.n_inst += 1
        return ins

    def dma(self, out, in_, reads=(), writes=(), q=None, is_output=False, **kw):
        q = q or self.sync
        if q is self.pool:
            i = self.NDMA - self.NSW + self.dnext_sw
            self.dnext_sw = (self.dnext_sw + 1) % self.NSW
        else:
            i = self.dnext
            self.dnext = (self.dnext + 1) % (self.NDMA - self.NSW)
        sem = self.dsems[i]
        deps = self._deps(reads, writes)
        if self.dcount[i] > 0:
            deps.append((sem, self.dcount[i]))
        self._wait(q, deps)
        ins = q.h.dma_start(out=out, in_=in_, **kw)
        self.dcount[i] += 16
        ins.then_inc(sem, 16)
        ev = (sem, self.dcount[i])
        self._commit(ev, reads, writes)
        if is_output:
            self.out_events.append(ev)
        self.n_inst += 1
        return ins

    def barrier(self):
        evs = [(e.sem, e.count) for e in self.engs if e.count > 0]
        evs += [(self.dsems[i], self.dcount[i]) for i in range(self.NDMA) if self.dcount[i] > 0]
        for e in self.engs:
            self._wait(e, evs)
        self.bufs = {}

    def finish(self):
        evs = list(self.out_events)
        evs += [(self.dsems[i], self.dcount[i]) for i in range(self.NDMA) if self.dcount[i] > 0]
        evs += [(e.sem, e.count) for e in self.engs if e.count > 0 and e is not self.sync]
        self._wait(self.sync, evs)


def _perm_cols():
    def partner(d):
        return d + 16 if (d % 32) < 16 else d - 16
    SPLIT_Q, SPLIT_K = 1024, 1536
    pq = [SPLIT_Q + h * 64 + partner(d) for h in range(8) for d in range(64)]
    pk = [SPLIT_K + g * 64 + partner(d) for g in range(2) for d in range(64)]
    return np.array(pq + pk, dtype=np.int64)


def _rope_tables():
    t = np.arange(LS)
    row = (t // 64).astype(np.float64)
    col = (t % 64).astype(np.float64)
    inv = 10000.0 ** (-np.arange(16, dtype=np.float64) / 16)
    cos = np.zeros((64, LS)); sin = np.zeros((64, LS))
    for d in range(64):
        pos = row if d < 32 else col
        ang = pos * inv[d % 16]
        cos[d] = np.cos(ang)
        sin[d] = np.sin(ang) * (-1.0 if (d % 32) < 16 else 1.0)
    cos2 = np.concatenate([cos, cos], 0).astype(np.float32)
    sin2 = np.concatenate([sin, sin], 0).astype(np.float32)
    return cos2, sin2


def _hy_tables(L):
    NH, KA, N2 = L // 128, L // 64, 2 * L
    n_hi = np.arange(128)[:, None, None]; n_lo = np.arange(128)[None, :, None]
    FA = np.zeros((128, 128, 128))
    ka = np.arange(KA)[None, None, :]
    th = 2 * np.pi * (n_lo + 128 * n_hi) * (ka + 0.5) / N2
    valid = (n_hi < NH)
    FA[:, :, :KA] = np.cos(th) * valid
    FA[:, :, 64:64 + KA] = -np.sin(th) * valid
    FAi = np.zeros((128, 128, NH))
    ka2 = np.arange(KA)[:, None, None]; nl = np.arange(128)[None, :, None]; nh = np.arange(NH)[None, None, :]
    th2 = 2 * np.pi * (nl + 128 * nh) * (ka2 + 0.5) / N2
    FAi[:KA] = np.cos(th2) / L
    FAi[64:64 + KA] = -np.sin(th2) / L
    return FA, FAi


def _hy_gb():
    nb = np.arange(128)[:, None]; kb = np.arange(64)[None, :]
    C = np.cos(2 * np.pi * nb * kb / 128); S = np.sin(2 * np.pi * nb * kb / 128)
    cat = lambda a, b: np.concatenate([a, b], 1)
    lst = [cat(C, -S), cat(S, C), cat(-S, C), cat(C, S),
           cat(C, C), cat(S, S),
           cat(S, -S), cat(-C, C), cat(-S, S), cat(C, -C),
           np.concatenate([C.T, -S.T], 0), np.concatenate([S.T, C.T], 0)]
    return np.stack(lst, 1)


def _hy_t256():
    L = 256
    n = (np.arange(128)[:, None, None, None] + 128 * np.arange(2)[None, :, None, None])
    k = np.arange(4)[None, None, :, None] + 4 * np.arange(64)[None, None, None, :]
    th = 2 * np.pi * n * (k + 0.5) / 512.0
    C, S = np.cos(th), np.sin(th)
    cat = lambda a, b: np.concatenate([a, b], -1).reshape(128, 1024)
    T = np.stack([cat(C, -S), cat(-S, C), cat(C, C), cat(S, -S), cat(-S, S)], 1)
    kk = np.arange(4)[None, :, None] + 4 * np.arange(64)[:, None, None]
    nn = np.arange(256)[None, None, :]
    th2 = 2 * np.pi * nn * (kk + 0.5) / 512.0
    Ti = np.concatenate([np.cos(th2) / L, -np.sin(th2) / L], 0).reshape(128, 1024)
    return np.concatenate([T, Ti[:, None, :]], 1)


def _hy_feat(L):
    t = np.linspace(0.0, 1.0, L)
    w = 2.0 * np.pi * np.arange(L) / L
    bands = np.linspace(1e-4, 15, 16)
    ang = w[:, None] * bands[None, :]
    feat = np.concatenate([t[:, None], np.cos(ang), -np.sin(ang)], -1)
    fT = np.zeros((128, L)); fT[:33] = feat.T
    negt = -t.reshape(L // 128, 128).T
    return fT, negt


class Prog:
    def __init__(self, depth=DEPTH, mixers=("attn", "hyena", "s5"), taps=(), stop=None):
        self.stop = stop
        self.depth = depth
        self.mixers = mixers
        self.taps = taps
        self.nc = bass.Bass("TRN2", target_bir_lowering=False)
        self.din = {}
        self.dout = {}
        self.build()

    def uniq(self, n):
        self._uid = getattr(self, "_uid", 0) + 1
        return f"{n}_{self._uid}"

    def inp(self, name, shape, dt=F32):
        t = self.nc.dram_tensor(name, list(shape), dt, kind="ExternalInput").ap()
        self.din[name] = (tuple(shape), dt)
        return t

    def outp(self, name, shape, dt=F32):
        t = self.nc.dram_tensor(name, list(shape), dt, kind="ExternalOutput").ap()
        self.dout[name] = (tuple(shape), dt)
        return t

    def scratch(self, name, shape, dt=F32):
        return self.nc.dram_tensor(name, list(shape), dt, kind="Internal").ap()

    def build(self):
        nc = self.nc
        L = self.depth
        self.xT_in = self.inp("xT_in", [D, NT])
        self.cT = self.inp("cT", [128, 8, 2])
        self.ada_w = self.inp("ada_w", [L, D, 6 * D])
        self.ada_b = self.inp("ada_b", [L, 128, 48])
        self.norm1_g = self.inp("norm1_g", [L, 128, 8])
        self.norm2_g = self.inp("norm2_g", [L, 128, 8])
        self.final_g = self.inp("final_g", [128, 8])
        self.w_in = self.inp("w_in", [L, D, WEXT])
        self.proj_a = self.inp("proj_a", [L, 256, D])
        self.proj_b = self.inp("proj_b", [L, 256, D])
        self.proj_c = self.inp("proj_c", [L, 512, D])
        self.w_out = self.inp("w_out", [L, D, D])
        self.w_gate = self.inp("w_gate", [L, D, FF])
        self.w_up = self.inp("w_up", [L, D, FF])
        self.w_down = self.inp("w_down", [L, FF, D])
        self.rope_cos = self.inp("rope_cos", [128, LS])
        self.rope_sin = self.inp("rope_sin", [128, LS])
        self.hy_FA = {256: self.inp("hy_FA256", [128, 128, 128], BF16), 4096: self.inp("hy_FA4096", [128, 128, 128], BF16)}
        self.hy_FAi = {256: self.inp("hy_FAi256", [128, 128, 2], BF16), 4096: self.inp("hy_FAi4096", [128, 128, 32], BF16)}
        self.hy_GB = self.inp("hy_GB", [128, 12, 128], BF16)
        self.hy_T256 = self.inp("hy_T256", [128, 6, 1024], BF16)
        self.hy_featT = {256: self.inp("hy_featT256", [128, 256]), 4096: self.inp("hy_featT4096", [128, 4096])}
        self.hy_negt = {256: self.inp("hy_negt256", [128, 2]), 4096: self.inp("hy_negt4096", [128, 32])}
        self.hy_w1 = self.inp("hy_w1", [L, 128, 128])
        self.hy_w2 = self.inp("hy_w2", [L, 128, 128])
        self.hy_w3 = self.inp("hy_w3", [L, 128, 1024])
        self.hy_vec = self.inp("hy_vec", [L, 128, 3])
        self.hy_decay = self.inp("hy_decay", [L, 1024])
        self.hy_cw = self.inp("hy_cw", [L, 128, 6, 4])
        self.hy_skip = self.inp("hy_skip", [L, 128, 2, 2])
        self.s5_sc = self.inp("s5_sc", [L, 128, 3, 16])
        self.s5_B = self.inp("s5_B", [L, 2, 8, 2, 128, 128])
        self.s5_C = self.inp("s5_C", [L, 2, 8, 2, 128, 128])
        self.s5_h0 = self.inp("s5_h0", [L, 128, 2, 16])
        self.s5_skip = self.inp("s5_skip", [L, 128, 2])
        self.s5_gw = self.inp("s5_gw", [L, 256, 512])
        self.s5_gb = self.inp("s5_gb", [L, 128, 4])
        self.s5_idx = self.inp("s5_idx", [128, 513])
        self.cache_k = self.inp("cache_k", [L, 512, 128])
        self.cache_v = self.inp("cache_v", [L, 512, 128])
        self.sink = self.inp("sink", [L, 8])
        self.mask_prev = self.inp("mask_prev", [128, 128])
        self.mask_next = self.inp("mask_next", [128, 128])
        self.yT = self.outp("yT", [D, NT])
        self.ok = self.outp("ok", [L, NP, 128])
        self.ov = self.outp("ov", [L, NP, 128])
        self.osf = self.outp("osf", [L, 2, 2, 2, 16, 64])
        self.xT = self.scratch("xT", [D, NT])
        self.zaT = self.scratch("zaT", [768, NT])
        self.zbT = self.scratch("zbT", [256, NT], BF16)
        self.gT = self.scratch("gT", [3072, NT], BF16)
        self.qA = self.scratch("qA", [8, 65, NT], BF16)
        self.kA = self.scratch("kA", [2, 65, NT], BF16)
        self.vtok = self.scratch("vtok", [NT, 128], BF16)
        self.nk = self.scratch("nk", [NT, 2])
        self.yaT = self.scratch("yaT", [256, NT], BF16)
        self.ybT = self.scratch("ybT", [256, NT], BF16)
        self.ycT = self.scratch("ycT", [512, NT], BF16)
        self.aT = self.scratch("aT", [FF, NT], BF16)
        self.ugT = self.scratch("ugT", [512, NT])
        self.hzT = self.scratch("hzT", [256, NT])
        self.hyT = self.scratch("hyT", [256, NT])
        self.ztok = self.scratch("ztok", [NT, 256], BF16)
        self.hraw = {256: self.scratch("hraw256", [256, 1024]), 4096: self.scratch("hraw4096", [4096, 1024])}
        self.ftok = {256: self.scratch("ftok256", [256, 1024], BF16), 4096: self.scratch("ftok4096", [4096, 1024], BF16)}
        self.Ascr = self.scratch("Ascr", [128, 128, 512], BF16)
        self.Bscr = self.scratch("Bscr", [128, 2, 64, 256], BF16)
        self.Ksp = {256: self.scratch("Ksp256", [2, 2, 128, 4, 256]), 4096: self.scratch("Ksp4096", [2, 2, 128, 64, 256])}
        self.tapo = {}
        for name, shape in self.taps:
            self.tapo[name] = self.outp("tap_" + name, shape)

        with ExitStack() as es:
            self.fw = fw = FW(nc, es)
            self.pb = [es.enter_context(nc.psum_tensor(f"pb{i}", [128, 512], F32)) for i in range(8)]
            sb = lambda n, s, d=F32: es.enter_context(nc.sbuf_tensor(self.uniq(n), s, d))
            self.ones32 = sb("ones32", [128, 128])
            self.ident = sb("ident", [128, 128])
            self.blk2 = sb("blk2", [128, 128])
            self.sc = sb("silu_c", [128, 8, 2])
            self.modT = sb("modT", [128, 48, 2])
            self.a1 = sb("a1", [128, 8, 2])
            self.a2 = sb("a2", [128, 8, 2])
            self.g1 = sb("g1", [128, 8])
            self.g2 = sb("g2", [128, 8])
            self.adab = sb("adab", [128, 48])
            self.epsc = sb("epsc", [128, 1])
            self.init_consts()
            for k in range(8):
                fw.dma(self.xT[k * 128:(k + 1) * 128, :], self.xT_in[k * 128:(k + 1) * 128, :], writes=[("xT", k)])
            fw.barrier()
            phases = [("mod", self.phase_mod), ("A", self.phase_A), ("B", self.phase_B), ("C1", self.phase_C1),
                      ("C2", self.phase_C2), ("C3", self.phase_C3)]
            done = False
            for l in range(L):
                for nm, ph in phases:
                    ph(l)
                    fw.barrier()
                    if self.stop == nm:
                        done = True
                        break
                if done:
                    break
            self.phase_final()
            fw.finish()

    def init_consts(self):
        nc, fw = self.nc, self.fw
        fw.op(fw.pool, lambda g: g.memset(self.ones32[:], 1.0), writes=["ones32"])
        fw.op(fw.pool, lambda g: g.memset(self.ident[:], 0.0), writes=["ident"])
        fw.op(fw.pool, lambda g: g.affine_select(out=self.ident[:], in_=self.ident[:], pattern=[[-1, 128]],
                                                 compare_op=ALU.not_equal, fill=1.0, base=0, channel_multiplier=1),
              reads=["ident"], writes=["ident"])
        fw.op(fw.pool, lambda g: g.memset(self.blk2[:], 0.0), writes=["blk2"])
        fw.op(fw.pool, lambda g: g.memset(self.blk2[0:64, 0:1], 1.0), reads=["blk2"], writes=["blk2"])
        fw.op(fw.pool, lambda g: g.memset(self.blk2[64:128, 1:2], 1.0), reads=["blk2"], writes=["blk2"])
        fw.op(fw.pool, lambda g: g.memset(self.epsc[:], EPS), writes=["epsc"])
        fw.dma(self.sc[:], self.cT[:, :, :], writes=["sc"])
        fw.op(fw.act, lambda a: a.activation(out=self.sc[:], in_=self.sc[:], func=AF.Silu), reads=["sc"], writes=["sc"])
        es2 = ExitStack()
        with es2:
            onesb = es2.enter_context(nc.sbuf_tensor(self.uniq("onesb"), [1, NT], BF16))
            fw.op(fw.pool, lambda g: g.memset(onesb[:], 1.0), writes=["onesb"])
            for g_ in range(2):
                fw.dma(self.kA[g_, 64:65, :], onesb[:], reads=["onesb"], writes=[("kA", g_)])
            fw.barrier()

    def phase_mod(self, l):
        nc, fw = self.nc, self.fw
        with ExitStack() as es:
            wblk = [es.enter_context(nc.sbuf_tensor(self.uniq(f"adaw{i}"), [128, 8, 1024], F32)) for i in range(2)]
            fw.dma(self.adab[:], self.ada_b[l], writes=["adab"])
            fw.dma(self.g1[:], self.norm1_g[l], writes=["g1"])
            fw.dma(self.g2[:], self.norm2_g[l], writes=["g2"])
            for blk in range(6):
                wb = wblk[blk % 2]
                key = ("adaw", blk % 2)
                for k in range(8):
                    fw.dma(wb[:, k, :], self.ada_w[l, k * 128:(k + 1) * 128, blk * 1024:(blk + 1) * 1024], writes=[key])
                for mm in range(8):
                    m = blk * 8 + mm
                    ps = self.pb[m % 4]
                    pk = ("pb", m % 4)
                    for k in range(8):
                        fw.op(fw.pe, lambda t, k=k, mm=mm, ps=ps, wb=wb: t.matmul(ps[:, 0:2], lhsT=wb[:, k, mm * 128:(mm + 1) * 128], rhs=self.sc[:, k, :],
                                                                                 start=(k == 0), stop=(k == 7)),
                              reads=[key, "sc"], writes=[pk], inc=(k == 7))
                    fw.op(fw.act, lambda a, m=m, ps=ps: a.activation(out=self.modT[:, m, :], in_=ps[:, 0:2], func=AF.Identity,
                                                                     bias=self.adab[:, m:m + 1], scale=1.0),
                          reads=[pk, "adab"], writes=["modT"])
            for (a_t, g_t, base, nm) in ((self.a1, self.g1, 8, "a1"), (self.a2, self.g2, 32, "a2")):
                for j in range(2):
                    fw.op(fw.dve, lambda v, a_t=a_t, g_t=g_t, base=base, j=j: v.scalar_tensor_tensor(
                        out=a_t[:, :, j], in0=self.modT[:, base:base + 8, j], scalar=1.0, in1=g_t[:, :], op0=ALU.add, op1=ALU.mult),
                        reads=["modT", "g1", "g2"], writes=[nm])

    def norm_mod(self, es, xt, a_t, shift_base, j, hT, tagsfx, bufs=None, sfx=""):
        nc, fw = self.nc, self.fw
        xsq, rin = bufs if bufs is not None else (self._xsq, self._rin)
        kxt, kxs, kri, khT = "xt" + sfx, "xsq" + sfx, "rin" + sfx, "hT" + sfx
        ss = self.pb[7]
        fw.op(fw.act, lambda a: a.activation(out=xsq[:], in_=xt[:], func=AF.Square), reads=[kxt], writes=[kxs])
        for k in range(8):
            fw.op(fw.pe, lambda t, k=k: t.matmul(ss[:], lhsT=self.ones32[:], rhs=xsq[:, k, :], start=(k == 0), stop=(k == 7)),
                  reads=[kxs, "ones32"], writes=[("pb", 7)], inc=(k == 7))
        fw.op(fw.act, lambda a: a.activation(out=rin[:], in_=ss[:], func=AF.Sqrt, scale=1.0 / D, bias=self.epsc[:, 0:1]),
              reads=[("pb", 7), "epsc"], writes=[kri])
        fw.op(fw.dve, lambda v: v.reciprocal(out=rin[:], in_=rin[:]), reads=[kri], writes=[kri])
        fw.op(fw.dve, lambda v: v.tensor_tensor(out=xsq[:], in0=xt[:], in1=rin[:].unsqueeze(1).to_broadcast([128, 8, TT]), op=ALU.mult),
              reads=[kxt, kri, kxs], writes=[kxs])
        for k in range(8):
            if shift_base is None:
                fw.op(fw.act, lambda a, k=k: a.activation(out=hT[:, k, :], in_=xsq[:, k, :], func=AF.Identity, scale=a_t[:, k:k + 1]),
                      reads=[kxs, "gfin"], writes=[khT])
            else:
                fw.op(fw.act, lambda a, k=k: a.activation(out=hT[:, k, :], in_=xsq[:, k, :], func=AF.Identity,
                                                          scale=a_t[:, k, j:j + 1], bias=self.modT[:, shift_base + k, j:j + 1]),
                      reads=[kxs, "a1", "a2", "modT"], writes=[khT])

    def load_w_bf16(self, dst, src, key, nchunk):
        fw = self.fw
        K = dst.shape[1]
        N = dst.shape[2]
        cw = N // nchunk
        assert cw <= 2048 and cw * nchunk == N
        for k in range(K):
            for c in range(nchunk):
                fw.dma(dst[:, k, c * cw:(c + 1) * cw], src[k * 128:(k + 1) * 128, c * cw:(c + 1) * cw], writes=[key], q=fw.pool)

    def phase_A(self, l):
        nc, fw = self.nc, self.fw
        with ExitStack() as es:
            sb = lambda n, s, d=F32: es.enter_context(nc.sbuf_tensor(self.uniq(n), s, d))
            WinB = sb("WinB", [128, 8, WEXT], BF16)
            xtb = [sb(f"xtA{i}", [128, 8, TT]) for i in range(2)]
            xsqb = [sb(f"xsqA{i}", [128, 8, TT]) for i in range(2)]
            rinb = [sb(f"rinA{i}", [128, TT]) for i in range(2)]
            hTb = [sb(f"hTA{i}", [128, 8, TT], BF16) for i in range(2)]
            cosT = sb("cosT", [128, TT])
            sinT = sb("sinT", [128, TT])
            NST = 4
            st32 = [sb(f"st32_{i}", [128, TT]) for i in range(NST)]
            st16 = [sb(f"st16_{i}", [128, TT], BF16) for i in range(NST)]
            tmp32 = [sb(f"tmp32_{i}", [128, TT]) for i in range(2)]
            nrm = [sb(f"nrm_{i}", [2, TT], BF16) for i in range(2)]
            kv32 = [sb(f"kv32_{i}", [128, 256]) for i in range(2)]
            v16 = [sb(f"v16_{i}", [128, 128], BF16) for i in range(2)]
            nkt = [sb(f"nkt_{i}", [128, 2]) for i in range(2)]
            junk = sb("junkA", [128, 64])
            self.load_w_bf16(WinB, self.w_in[l], "WinB", 4)
            cnt = {"s32": 0, "s16": 0, "ps": 0, "t32": 0, "nrm": 0, "kv": 0}

            def next_ps():
                i = cnt["ps"] % 6
                cnt["ps"] += 1
                return self.pb[i], ("pb", i)

            cur = {"hT": None, "hk": None}

            def mm_tile(m, ps, pk):
                hT, hk = cur["hT"], cur["hk"]
                for k in range(8):
                    fw.op(fw.pe, lambda t, k=k: t.matmul(ps[:], lhsT=WinB[:, k, m * 128:(m + 1) * 128], rhs=hT[:, k, :], start=(k == 0), stop=(k == 7)),
                          reads=["WinB", hk], writes=[pk], inc=(k == 7))

            def prep(tt):
                bi = tt % 2
                for k in range(8):
                    fw.dma(xtb[bi][:, k, :], self.xT[k * 128:(k + 1) * 128, tt * TT:(tt + 1) * TT], reads=[("xT", k)], writes=[f"xt{bi}"])
                self.norm_mod(es, xtb[bi], self.a1, 0, 0 if tt == 0 else 1, hTb[bi], "A", bufs=(xsqb[bi], rinb[bi]), sfx=str(bi))

            import os
            prep(0)
            for tt in range(int(os.environ.get('DBGT', NTILE))):
                c0 = tt * TT
                j = 0 if tt == 0 else 1
                if tt + 1 < NTILE:
                    prep(tt + 1)
                hT = hTb[tt % 2]
                cur["hT"], cur["hk"] = hT, f"hT{tt % 2}"
                if j == 1:
                    p0 = c0 - NP
                    fw.dma(cosT[:], self.rope_cos[:, p0:p0 + TT], writes=["cosT"])
                    fw.dma(sinT[:], self.rope_sin[:, p0:p0 + TT], writes=["sinT"])
                evac_flip = 0
                import os
                lvl = int(os.environ.get('DBGA', '9'))
                mlist = list(range(0, 13)) + list(range(14, 38))
                if lvl == 0: mlist = list(range(0, 6))
                if lvl == 1: mlist = list(range(0, 8))
                if lvl == 2: mlist = list(range(0, 13))
                for m in mlist:
                    ps, pk = next_ps()
                    mm_tile(m, ps, pk)
                    eng = fw.act if (evac_flip % 2 == 0) else fw.dve
                    evac_flip += 1
                    if m < 6:
                        i = cnt["s32"] % NST; cnt["s32"] += 1
                        s = st32[i]; sk = ("st32", i)
                        if eng is fw.act:
                            fw.op(eng, lambda a, s=s, ps=ps: a.copy(out=s[:], in_=ps[:]), reads=[pk], writes=[sk])
                        else:
                            fw.op(eng, lambda v, s=s, ps=ps: v.tensor_copy(out=s[:], in_=ps[:]), reads=[pk], writes=[sk])
                        fw.dma(self.zaT[m * 128:(m + 1) * 128, c0:c0 + TT], s[:], reads=[sk], writes=[("zaT", m)])
                    elif m < 8:
                        i = cnt["s16"] % NST; cnt["s16"] += 1
                        s = st16[i]; sk = ("st16", i)
                        fw.op(fw.act, lambda a, s=s, ps=ps: a.copy(out=s[:], in_=ps[:]), reads=[pk], writes=[sk])
                        fw.dma(self.zbT[(m - 6) * 128:(m - 5) * 128, c0:c0 + TT], s[:], reads=[sk], writes=[("zbT", m - 6)])
                    elif m < 13:
                        i = cnt["s16"] % NST; cnt["s16"] += 1
                        s = st16[i]; sk = ("st16", i)
                        ti = cnt["t32"] % 2; cnt["t32"] += 1
                        t32 = tmp32[ti]; tk = ("tmp32", ti)
                        if m < 12 and os.environ.get('DBGQ') != '1':
                            fw.op(fw.act, lambda a, t32=t32, ps=ps: a.activation(out=t32[:], in_=ps[:], func=AF.Square), reads=[pk], writes=[tk])
                            pn, pnk = self.pb[6], ("pb", 6)
                            fw.op(fw.pe, lambda t, t32=t32: t.matmul(pn[:, :], lhsT=self.blk2[:], rhs=t32[:], start=True, stop=True),
                                  reads=[tk, "blk2"], writes=[pnk])
                            ni = cnt["nrm"] % 2; cnt["nrm"] += 1
                            nr = nrm[ni]; nrk = ("nrm", ni)
                            if os.environ.get('DBGQ') != '2':
                                fw.op(fw.act, lambda a, nr=nr: a.mul(out=nr[:], in_=pn[0:2, :], mul=-0.5), reads=[pnk], writes=[nrk])
                            for hh in range(2 if os.environ.get('DBGQ') not in ('2', '3') else 0):
                                h = 2 * (m - 8) + hh
                                fw.dma(self.qA[h, 64:65, c0:c0 + TT], nr[hh:hh + 1, :], reads=[nrk], writes=[("qA", h)])
                        if j == 0:
                            fw.op(fw.dve, lambda v, s=s, ps=ps: v.tensor_copy(out=s[:], in_=ps[:]), reads=[pk], writes=[sk])
                        else:
                            mp = 38 + (m - 8)
                            ps2, pk2 = next_ps()
                            mm_tile(mp, ps2, pk2)
                            fw.op(fw.dve, lambda v, t32=t32, ps=ps: v.tensor_tensor(out=t32[:], in0=ps[:], in1=cosT[:], op=ALU.mult),
                                  reads=[pk, "cosT", tk], writes=[tk])
                            i2 = cnt["s32"] % NST; cnt["s32"] += 1
                            s2 = st32[i2]; s2k = ("st32", i2)
                            fw.op(fw.dve, lambda v, s2=s2, ps2=ps2: v.tensor_tensor(out=s2[:], in0=ps2[:], in1=sinT[:], op=ALU.mult),
                                  reads=[pk2, "sinT"], writes=[s2k])
                            fw.op(fw.pool, lambda g, s=s, t32=t32, s2=s2: g.tensor_tensor(out=s[:], in0=t32[:], in1=s2[:], op=ALU.add),
                                  reads=[tk, s2k], writes=[sk])
                        for hh in range(2):
                            if m < 12:
                                h = 2 * (m - 8) + hh
                                fw.dma(self.qA[h, 0:64, c0:c0 + TT], s[hh * 64:(hh + 1) * 64, :], reads=[sk], writes=[("qA", h)])
                            else:
                                fw.dma(self.kA[hh, 0:64, c0:c0 + TT], s[hh * 64:(hh + 1) * 64, :], reads=[sk], writes=[("kA", hh)])
                    else:
                        i = cnt["s16"] % NST; cnt["s16"] += 1
                        s = st16[i]; sk = ("st16", i)
                        fw.op(fw.act, lambda a, s=s, ps=ps: a.activation(out=s[:], in_=ps[:], func=AF.Sigmoid), reads=[pk], writes=[sk])
                        fw.dma(self.gT[(m - 14) * 128:(m - 13) * 128, c0:c0 + TT], s[:], reads=[sk], writes=[("gT", m - 14)])
                for ts in range(4 if lvl >= 4 else 0):
                    ps, pk = next_ps()
                    for k in range(8):
                        fw.op(fw.pe, lambda t, k=k, ts=ts, ps=ps: t.matmul(ps[:, 0:256], lhsT=hT[:, k, ts * 128:(ts + 1) * 128], rhs=WinB[:, k, 1536:1792],
                                                                         start=(k == 0), stop=(k == 7)),
                              reads=["WinB", cur["hk"]], writes=[pk], inc=(k == 7))
                    i = cnt["kv"] % 2; cnt["kv"] += 1
                    r0 = c0 + ts * 128
                    fw.op(fw.dve, lambda v, i=i, ps=ps: v.tensor_copy(out=kv32[i][:], in_=ps[:, 0:256]), reads=[pk], writes=[("kv32", i)])
                    fw.op(fw.act, lambda a, i=i, ps=ps: a.copy(out=v16[i][:], in_=ps[:, 128:256]), reads=[pk], writes=[("v16", i)])
                    for g_ in range(2):
                        fw.op(fw.act, lambda a, i=i, g_=g_: a.activation(out=junk[:], in_=kv32[i][:, g_ * 64:(g_ + 1) * 64], func=AF.Square,
                                                                          accum_out=nkt[i][:, g_:g_ + 1]),
                              reads=[("kv32", i)], writes=["junkA", ("nkt", i)])
                    fw.op(fw.dve, lambda v, i=i: v.tensor_scalar(out=nkt[i][:], in0=nkt[i][:], scalar1=1.0 / 16.0, scalar2=None, op0=ALU.mult),
                          reads=[("nkt", i)], writes=[("nkt", i)])
                    fw.dma(self.nk[r0:r0 + 128, :], nkt[i][:], reads=[("nkt", i)], writes=["nk"])
                    fw.dma(self.vtok[r0:r0 + 128, :], v16[i][:], reads=[("v16", i)], writes=["vtok"])
                    if tt == 0:
                        fw.dma(self.ok[l, r0:r0 + 128, :], kv32[i][:, 0:128], reads=[("kv32", i)], is_output=True)
                        fw.dma(self.ov[l, r0:r0 + 128, :], kv32[i][:, 128:256], reads=[("kv32", i)], is_output=True)

    def phase_B(self, l):
        nc, fw = self.nc, self.fw
        with ExitStack() as es:
            sb = lambda n, s, d=F32: es.enter_context(nc.sbuf_tensor(self.uniq(n), s, d))
            t32 = sb("bp32", [128, NT])
            t16 = sb("bp16", [128, NT], BF16)
            z16 = sb("bpz", [128, NT], BF16)
            if "hyena" not in self.mixers:
                for m in range(2):
                    fw.dma(t32[:], self.zaT[m * 128:(m + 1) * 128, :], reads=[("zaT", m)], writes=["bp32"])
                    fw.op(fw.dve, lambda v: v.tensor_copy(out=t16[:], in_=t32[:]), reads=["bp32"], writes=["bp16"])
                    fw.dma(self.yaT[m * 128:(m + 1) * 128, :], t16[:], reads=["bp16"], writes=[("yaT", m)])
            if "s5" not in self.mixers:
                for m in range(2):
                    fw.dma(t16[:], self.zbT[m * 128:(m + 1) * 128, :], reads=[("zbT", m)], writes=["bp16"])
                    fw.dma(self.ybT[m * 128:(m + 1) * 128, :], t16[:], reads=["bp16"], writes=[("ybT", m)])
            if "attn" not in self.mixers:
                fw.op(fw.pool, lambda g: g.memset(z16[:], 0.0), writes=["bpz"])
                for m in range(4):
                    fw.dma(self.ycT[m * 128:(m + 1) * 128, :], z16[:], reads=["bpz"], writes=[("ycT", m)])
        if "attn" in self.mixers:
            fw.barrier()
            self.attention(l)
        if "hyena" in self.mixers:
            fw.barrier()
            self.hyena(l)
        if "s5" in self.mixers:
            fw.barrier()
            self.s5(l)

    def attention(self, l):
        nc, fw = self.nc, self.fw
        with ExitStack() as es:
            sb = lambda n, s, d=F32: es.enter_context(nc.sbuf_tensor(self.uniq(n), s, d))
            NKT = NT // 128
            kAs = sb("kAs", [65, NT], BF16)
            qAs = sb("qAs", [65, 4, NT], BF16)
            vt = sb("vt", [128, NKT, 128], BF16)
            nkS = sb("nkS", [128, NKT, 2])
            en = sb("enS", [128, NKT, 2])
            nnk = sb("nnkS", [128, NKT, 2])
            Vp = sb("Vp", [128, NKT, 65], BF16)
            ck32 = sb("ck32", [128, 4, 128])
            cv32 = sb("cv32", [128, 4, 128])
            kcA = sb("kcA", [65, 2, 512], BF16)
            nkc = sb("nkc", [128, 4, 2])
            enc = sb("enc", [128, 4, 2])
            nnkc = sb("nnkc", [128, 4, 2])
            Vpc = sb("Vpc", [128, 2, 4, 65], BF16)
            junk = sb("junkB", [128, 64])
            sinkt = sb("sinkt", [65, 8])
            mprev = sb("mprev", [128, 128], BF16)
            mnext = sb("mnext", [128, 128], BF16)
            m32 = sb("m32", [128, 128])
            sel64 = sb("sel64", [128, 128])
            Rrow = sb("Rrow", [128, 1024])
            PT = [sb(f"PT{i}", [128, 1024], BF16) for i in range(2)]
            sk_t = sb("sk_t", [65, 1024])
            bcs = sb("bcs", [64, 1024])
            ycS = sb("ycS", [64, 4, NT], BF16)

            fw.dma(sinkt[:], self.sink[l:l + 1, :].partition_broadcast(65), writes=["sinkt"])
            fw.dma(m32[:], self.mask_prev[:, :], writes=["m32"])
            fw.op(fw.dve, lambda v: v.tensor_copy(out=mprev[:], in_=m32[:]), reads=["m32"], writes=["mprev"])
            fw.dma(m32[:], self.mask_next[:, :], reads=[], writes=["m32"])
            fw.op(fw.dve, lambda v: v.tensor_copy(out=mnext[:], in_=m32[:]), reads=["m32"], writes=["mnext"])
            fw.op(fw.pool, lambda g: g.memset(sel64[:], 0.0), writes=["sel64"])
            fw.op(fw.pool, lambda g: g.memset(sel64[64:65, :], 1.0), reads=["sel64"], writes=["sel64"])
            fw.op(fw.pool, lambda g: g.memset(Rrow[:], 0.0), writes=["Rrow"])
            fw.dma(vt[:], self.vtok.rearrange("(t p) c -> p t c", p=128), reads=["vtok"], writes=["vt"])
            fw.dma(nkS[:], self.nk.rearrange("(t p) c -> p t c", p=128), reads=["nk"], writes=["nkS"])
            fw.op(fw.act, lambda a: a.activation(out=en[:], in_=nkS[:], func=AF.Exp), reads=["nkS"], writes=["en"])
            fw.op(fw.dve, lambda v: v.tensor_scalar(out=nnk[:], in0=nkS[:], scalar1=-1.0, scalar2=None, op0=ALU.mult), reads=["nkS"], writes=["nnk"])
            fw.dma(ck32[:], self.cache_k[l].rearrange("(t p) c -> p t c", p=128), writes=["ck32"])
            fw.dma(cv32[:], self.cache_v[l].rearrange("(t p) c -> p t c", p=128), writes=["cv32"])
            fw.op(fw.pool, lambda g: g.memset(kcA[64:65, :, :], 1.0), writes=["kcA"])
            for t in range(4):
                for g_ in range(2):
                    fw.op(fw.act, lambda a, t=t, g_=g_: a.activation(out=junk[:], in_=ck32[:, t, g_ * 64:(g_ + 1) * 64], func=AF.Square,
                                                                      accum_out=nkc[:, t, g_:g_ + 1]), reads=["ck32"], writes=["junkB", "nkc"])
                pt, ptk = self.pb[5], ("pb", 5)
                fw.op(fw.pe, lambda t_, t=t: t_.transpose(pt[:, 0:128], ck32[:, t, :], self.ident[:]), reads=["ck32", "ident"], writes=[ptk])
                for g_ in range(2):
                    fw.op(fw.dve, lambda v, t=t, g_=g_: v.tensor_copy(out=kcA[0:64, g_, t * 128:(t + 1) * 128], in_=pt[g_ * 64:(g_ + 1) * 64, 0:128]),
                          reads=[ptk], writes=["kcA"])
            fw.op(fw.dve, lambda v: v.tensor_scalar(out=nkc[:], in0=nkc[:], scalar1=1.0 / 16.0, scalar2=None, op0=ALU.mult), reads=["nkc"], writes=["nkc"])
            fw.op(fw.act, lambda a: a.activation(out=enc[:], in_=nkc[:], func=AF.Exp), reads=["nkc"], writes=["enc"])
            fw.op(fw.dve, lambda v: v.tensor_scalar(out=nnkc[:], in0=nkc[:], scalar1=-1.0, scalar2=None, op0=ALU.mult), reads=["nkc"], writes=["nnkc"])

            for g_ in range(2):
                fw.dma(kAs[:], self.kA[g_], reads=[("kA", g_)], writes=["kAs"])
                for hh in range(4):
                    fw.dma(qAs[:, hh, :], self.qA[4 * g_ + hh], reads=[("qA", 4 * g_ + hh)], writes=["qAs"])
                for t in range(NKT):
                    fw.op(fw.dve, lambda v, t=t: v.tensor_scalar(out=Vp[:, t, 0:64], in0=vt[:, t, g_ * 64:(g_ + 1) * 64], scalar1=en[:, t, g_:g_ + 1],
                                                                  scalar2=None, op0=ALU.mult), reads=["vt", "en"], writes=["Vp"])
                fw.op(fw.dve, lambda v: v.tensor_copy(out=Vp[:, :, 64], in_=en[:, :, g_]), reads=["en"], writes=["Vp"])
                for t in range(4):
                    fw.op(fw.dve, lambda v, t=t: v.tensor_scalar(out=Vpc[:, g_, t, 0:64], in0=cv32[:, t, g_ * 64:(g_ + 1) * 64], scalar1=enc[:, t, g_:g_ + 1],
                                                                  scalar2=None, op0=ALU.mult), reads=["cv32", "enc"], writes=["Vpc"])
                fw.op(fw.dve, lambda v: v.tensor_copy(out=Vpc[:, g_, :, 64], in_=enc[:, :, g_]), reads=["enc"], writes=["Vpc"])

                groups = []
                for s_ in range(2):
                    groups.append((s_ * 256, 256, [("tok", 2 * s_), ("tok", 2 * s_ + 1)]))
                for qb in range(32):
                    kts = [("cache", t) for t in range(4)]
                    if qb > 0:
                        kts.append(("prev", 4 + qb - 1))
                    kts.append(("tok", 4 + qb))
                    if qb < 31:
                        kts.append(("next", 4 + qb + 1))
                    groups.append((NP + qb * 128, 128, kts))
                for gi, (q0, nq, kts) in enumerate(groups):
                    ncol = 4 * nq
                    nb = ncol // 512
                    po = [self.pb[2 + 2 * (gi % 2) + b_] for b_ in range(nb)]
                    pok = [("pb", 2 + 2 * (gi % 2) + b_) for b_ in range(nb)]
                    for ki, (kind, t) in enumerate(kts):
                        pt_ = PT[ki % 2]; ptk_ = ("PT", ki % 2)
                        pss = [self.pb[0], self.pb[1]]; pssk = [("pb", 0), ("pb", 1)]
                        if nb == 1:
                            pss = [self.pb[ki % 2]]; pssk = [("pb", ki % 2)]
                        if kind == "cache":
                            lhs = kcA[0:65, g_, t * 128:(t + 1) * 128]
                            bias = nnkc[:, t, g_:g_ + 1]
                            vprime = Vpc[:, g_, t, :]
                            rk = ["kcA"]; bk = "nnkc"; vk = "Vpc"
                        else:
                            lhs = kAs[0:65, t * 128:(t + 1) * 128]
                            bias = nnk[:, t, g_:g_ + 1]
                            vprime = Vp[:, t, :]
                            rk = ["kAs"]; bk = "nnk"; vk = "Vp"
                        for hh in range(4):
                            col = hh * nq
                            b_ = col // 512
                            fw.op(fw.pe, lambda t_, hh=hh, col=col, b_=b_: t_.matmul(pss[b_][:, col % 512:col % 512 + nq], lhsT=lhs, rhs=qAs[0:65, hh, q0:q0 + nq],
                                                                                   start=True, stop=True),
                                  reads=rk + ["qAs"], writes=[pssk[b_]], inc=(hh % (4 // nb) == (4 // nb) - 1))
                        for b_ in range(nb):
                            fw.op(fw.act, lambda a, b_=b_: a.activation(out=pt_[:, b_ * 512:(b_ + 1) * 512], in_=pss[b_][:], func=AF.Exp, scale=0.125, bias=bias),
                                  reads=[pssk[b_], bk], writes=[ptk_])
                        if kind in ("prev", "next"):
                            mk_ = mprev if kind == "prev" else mnext
                            fw.op(fw.pool, lambda g: g.tensor_tensor(out=pt_[:, 0:512].rearrange("p (h q) -> p h q", h=4),
                                                                     in0=pt_[:, 0:512].rearrange("p (h q) -> p h q", h=4),
                                                                     in1=mk_[:].unsqueeze(1).to_broadcast([128, 4, 128]), op=ALU.mult),
                                  reads=[ptk_, "mprev", "mnext"], writes=[ptk_])
                        for b_ in range(nb):
                            fw.op(fw.pe, lambda t_, b_=b_: t_.matmul(po[b_][0:65, :], lhsT=vprime[:, 0:65], rhs=pt_[:, b_ * 512:(b_ + 1) * 512],
                                                                     start=(ki == 0), stop=(ki == len(kts) - 1)),
                                  reads=[vk, ptk_], writes=[pok[b_]], inc=(ki == len(kts) - 1))
                    for hh in range(4):
                        h = 4 * g_ + hh
                        fw.op(fw.act, lambda a, hh=hh, h=h: a.activation(out=sk_t[64:65, hh * nq:(hh + 1) * nq], in_=qAs[64:65, hh, q0:q0 + nq], func=AF.Exp,
                                                                          scale=0.125, bias=sinkt[64:65, h:h + 1]), reads=["qAs", "sinkt"], writes=["sk_t"])
                    for b_ in range(nb):
                        cs = slice(b_ * 512, (b_ + 1) * 512)
                        fw.op(fw.dve, lambda v, b_=b_, cs=cs: v.tensor_tensor(out=sk_t[64:65, cs], in0=po[b_][64:65, :], in1=sk_t[64:65, cs], op=ALU.add),
                              reads=[pok[b_], "sk_t"], writes=["sk_t"])
                        fw.op(fw.dve, lambda v, cs=cs: v.reciprocal(out=Rrow[64:65, cs], in_=sk_t[64:65, cs]), reads=["sk_t"], writes=["Rrow"])
                        pbc, pbck = self.pb[6 + b_], ("pb", 6 + b_)
                        fw.op(fw.pe, lambda t_, cs=cs, pbc=pbc: t_.matmul(pbc[:], lhsT=sel64[:], rhs=Rrow[:, cs], start=True, stop=True),
                              reads=["sel64", "Rrow"], writes=[pbck])
                        fw.op(fw.act, lambda a, cs=cs, pbc=pbc: a.copy(out=bcs[:, cs], in_=pbc[0:64, :]), reads=[pbck], writes=["bcs"])
                        for hh in range(4 // nb):
                            hg = b_ * (4 // nb) + hh
                            fw.op(fw.dve, lambda v, hh=hh, hg=hg, b_=b_: v.tensor_tensor(out=ycS[:, hg, q0:q0 + nq], in0=po[b_][0:64, hh * nq:(hh + 1) * nq],
                                                                                         in1=bcs[:, b_ * 512 + hh * nq:b_ * 512 + (hh + 1) * nq], op=ALU.mult),
                                  reads=[pok[b_], "bcs"], writes=["ycS"])
                for hh in range(4):
                    h = 4 * g_ + hh
                    fw.dma(self.ycT[h * 64:(h + 1) * 64, :], ycS[:, hh, :], reads=["ycS"], writes=[("ycT", h // 2)])

    HSEG = [(0, 256), (256, 256), (NP, LS)]

    def hyena(self, l):
        nc, fw = self.nc, self.fw
        if not hasattr(self, "hzT"):
            pass
        with ExitStack() as es:
            sb = lambda n, s, d=F32: es.enter_context(nc.sbuf_tensor(self.uniq(n), s, d))
            za = sb("hy_za", [128, NT]); U = sb("hy_U", [128, NT]); cw = sb("hy_cwS", [128, 6, 4])
            fw.dma(cw[:], self.hy_cw[l], writes=["hy_cwS"])
            for m in range(6):
                fw.dma(za[:], self.zaT[m * 128:(m + 1) * 128, :], reads=[("zaT", m)], writes=["hy_za"])
                fw.op(fw.act, lambda a: a.activation(out=U[:], in_=za[:], func=AF.Identity, scale=cw[:, m, 1:2], bias=cw[:, m, 3:4]),
                      reads=["hy_za", "hy_cwS"], writes=["hy_U"])
                for (t0, Ls) in self.HSEG:
                    fw.op(fw.dve, lambda v: v.scalar_tensor_tensor(out=U[:, t0 + 1:t0 + Ls], in0=za[:, t0:t0 + Ls - 1], scalar=cw[:, m, 0:1],
                                                                   in1=U[:, t0 + 1:t0 + Ls], op0=ALU.mult, op1=ALU.add),
                          reads=["hy_za", "hy_cwS", "hy_U"], writes=["hy_U"])
                    fw.op(fw.dve, lambda v: v.scalar_tensor_tensor(out=U[:, t0:t0 + Ls - 1], in0=za[:, t0 + 1:t0 + Ls], scalar=cw[:, m, 2:3],
                                                                   in1=U[:, t0:t0 + Ls - 1], op0=ALU.mult, op1=ALU.add),
                          reads=["hy_za", "hy_cwS", "hy_U"], writes=["hy_U"])
                if m < 2:
                    fw.dma(self.hzT[m * 128:(m + 1) * 128, :], U[:], reads=["hy_U"], writes=[("hzT", m)])
                else:
                    fw.dma(self.ugT[(m - 2) * 128:(m - 1) * 128, :], U[:], reads=["hy_U"], writes=[("ugT", m - 2)])
        fw.barrier()
        for L_ in (256, 4096):
            self.hy_filters(l, L_)
            fw.barrier()
        for o in range(2):
            self.hy_to_tok()
            fw.barrier()
            self.hy_conv256(o)
            fw.barrier()
            self.hy_conv(o, NP, LS)
            fw.barrier()
            with ExitStack() as es:
                sb = lambda n, s, d=F32: es.enter_context(nc.sbuf_tensor(self.uniq(n), s, d))
                z = sb("hy_z", [128, NT]); y = sb("hy_y", [128, NT]); g = sb("hy_g", [128, NT]); skp = sb("hy_skp", [128, 2, 2])
                z16 = sb("hy_z16", [128, NT], BF16)
                fw.dma(skp[:], self.hy_skip[l], writes=["hy_skp"])
                for ch in range(2):
                    fw.dma(z[:], self.hzT[ch * 128:(ch + 1) * 128, :], reads=[("hzT", ch)], writes=["hy_z"])
                    fw.dma(y[:], self.hyT[ch * 128:(ch + 1) * 128, :], reads=[("hyT", ch)], writes=["hy_y"])
                    fw.dma(g[:], self.ugT[(2 * o + ch) * 128:(2 * o + ch + 1) * 128, :], reads=[("ugT", 2 * o + ch)], writes=["hy_g"])
                    fw.op(fw.dve, lambda v: v.scalar_tensor_tensor(out=y[:], in0=z[:], scalar=skp[:, o, ch:ch + 1], in1=y[:], op0=ALU.mult, op1=ALU.add),
                          reads=["hy_z", "hy_skp", "hy_y"], writes=["hy_y"])
                    if o == 0:
                        fw.op(fw.pool, lambda g_: g_.tensor_tensor(out=z[:], in0=y[:], in1=g[:], op=ALU.mult), reads=["hy_y", "hy_g"], writes=["hy_z"])
                        fw.dma(self.hzT[ch * 128:(ch + 1) * 128, :], z[:], reads=["hy_z"], writes=[("hzT", ch)])
                    else:
                        fw.op(fw.pool, lambda g_: g_.tensor_tensor(out=z16[:], in0=y[:], in1=g[:], op=ALU.mult), reads=["hy_y", "hy_g"], writes=["hy_z16"])
                        fw.dma(self.yaT[ch * 128:(ch + 1) * 128, :], z16[:], reads=["hy_z16"], writes=[("yaT", ch)])
            fw.barrier()

    def hy_to_tok(self):
        nc, fw = self.nc, self.fw
        with ExitStack() as es:
            sb = lambda n, s, d=F32: es.enter_context(nc.sbuf_tensor(self.uniq(n), s, d))
            z = sb("ht_z", [128, 2, NT])
            st = [sb(f"ht_st{i}", [128, 256], BF16) for i in range(3)]
            for ch in range(2):
                fw.dma(z[:, ch, :], self.hzT[ch * 128:(ch + 1) * 128, :], reads=[("hzT", ch)], writes=["ht_z"])
            for t in range(NT // 128):
                ps, pk = self.pb[t % 4], ("pb", t % 4)
                for ch in range(2):
                    fw.op(fw.pe, lambda t_, ch=ch: t_.transpose(ps[:, ch * 128:(ch + 1) * 128], z[:, ch, t * 128:(t + 1) * 128], self.ident[:]),
                          reads=["ht_z", "ident"], writes=[pk], inc=(ch == 1))
                s_ = st[t % 3]; sk = ("ht_st", t % 3)
                if t % 2 == 0:
                    fw.op(fw.dve, lambda v: v.tensor_copy(out=s_[:], in_=ps[:, 0:256]), reads=[pk], writes=[sk])
                else:
                    fw.op(fw.act, lambda a: a.activation(out=s_[:], in_=ps[:, 0:256], func=AF.Identity), reads=[pk], writes=[sk])
                fw.dma(self.ztok[t * 128:(t + 1) * 128, :], s_[:], reads=[sk], writes=["ztok"])

    def hy_stageA(self, es, L_, src, C, FAt, xt, stg):
        nc, fw = self.nc, self.fw
        NH = L_ // 128
        srcv = src.rearrange("(h l) c -> h l c", l=128)
        for ch in range(8):
            fw.dma(xt[0:NH, :, 0:C], srcv[:, ch * 16:(ch + 1) * 16, :], reads=["hy_src"], writes=["hy_xt"])
            sg = stg[ch % 2]; sgk = ("hy_stg", ch % 2)
            for i in range(16):
                n_lo = ch * 16 + i
                ps, pk = self.pb[i % 4], ("pb", i % 4)
                fw.op(fw.pe, lambda t_: t_.matmul(ps[:, 0:C], lhsT=FAt[:, n_lo, :], rhs=xt[:, i, 0:C], start=True, stop=True),
                      reads=["hy_FAt", "hy_xt"], writes=[pk])
                if i % 2 == 0:
                    fw.op(fw.dve, lambda v: v.tensor_copy(out=sg[:, i, 0:C], in_=ps[:, 0:C]), reads=[pk], writes=[sgk])
                else:
                    fw.op(fw.act, lambda a: a.activation(out=sg[:, i, 0:C], in_=ps[:, 0:C], func=AF.Identity), reads=[pk], writes=[sgk])
            fw.dma(self.Ascr[:, ch * 16:(ch + 1) * 16, 0:C], sg[:, :, 0:C], reads=[sgk], writes=["Ascr"])

    def hy_filters(self, l, L_):
        nc, fw = self.nc, self.fw
        NJT = L_ // 128
        KA = L_ // 64
        cwid = min(512, L_)
        TWO_PI = 2.0 * math.pi
        with ExitStack() as es:
            sb = lambda n, s, d=F32: es.enter_context(nc.sbuf_tensor(self.uniq(n), s, d))
            featT = sb("hf_feat", [128, L_]); negt = sb("hf_negt", [128, NJT])
            w1 = sb("hf_w1", [128, 128]); w2 = sb("hf_w2", [128, 128]); w3 = sb("hf_w3", [128, 1024])
            vec = sb("hf_vec", [128, 3]); fb = sb("hf_fb", [128, 2]); dabs = sb("hf_dabs", [128, 1024])
            h1T = sb("hf_h1T", [128, L_]); h2T = sb("hf_h2T", [128, L_])
            yt = sb("hf_yt", [128, 512]); ki = sb("hf_ki", [128, 512], I32); kf = sb("hf_kf", [128, 512])
            fw.dma(featT[:], self.hy_featT[L_][:, :], writes=["hf_feat"])
            fw.dma(negt[:], self.hy_negt[L_][:, :], writes=["hf_negt"])
            fw.dma(w1[:], self.hy_w1[l], writes=["hf_w1"]); fw.dma(w2[:], self.hy_w2[l], writes=["hf_w2"]); fw.dma(w3[:], self.hy_w3[l], writes=["hf_w3"])
            fw.dma(vec[:], self.hy_vec[l], writes=["hf_vec"])
            fw.dma(dabs[:], self.hy_decay[l:l + 1, :].partition_broadcast(128), writes=["hf_dabs"])
            fw.op(fw.act, lambda a: a.activation(out=dabs[:], in_=dabs[:], func=AF.Abs), reads=["hf_dabs"], writes=["hf_dabs"])
            fw.op(fw.dve, lambda v: v.tensor_tensor(out=fb[:], in0=vec[:, 0:2], in1=vec[:, 2:3].to_broadcast([128, 2]), op=ALU.mult), reads=["hf_vec"], writes=["hf_fb"])

            def sin_layer(srcT, W, col, dstT, sk, wk, dk):
                for ch in range(L_ // cwid):
                    cs = slice(ch * cwid, (ch + 1) * cwid)
                    ps, pk = self.pb[ch % 2], ("pb", ch % 2)
                    fw.op(fw.pe, lambda t_: t_.matmul(ps[:, 0:cwid], lhsT=W[:], rhs=srcT[:, cs], start=True, stop=True), reads=[sk, wk], writes=[pk])
                    fw.op(fw.act, lambda a: a.activation(out=yt[:, 0:cwid], in_=ps[:, 0:cwid], func=AF.Identity, scale=vec[:, 2:3], bias=fb[:, col:col + 1]),
                          reads=[pk, "hf_vec", "hf_fb"], writes=["hf_yt"])
                    fw.op(fw.dve, lambda v: v.tensor_scalar(out=ki[:, 0:cwid], in0=yt[:, 0:cwid], scalar1=1.0 / TWO_PI, scalar2=None, op0=ALU.mult),
                          reads=["hf_yt"], writes=["hf_ki"])
                    fw.op(fw.dve, lambda v: v.tensor_copy(out=kf[:, 0:cwid], in_=ki[:, 0:cwid]), reads=["hf_ki"], writes=["hf_kf"])
                    fw.op(fw.dve, lambda v: v.scalar_tensor_tensor(out=yt[:, 0:cwid], in0=kf[:, 0:cwid], scalar=-TWO_PI, in1=yt[:, 0:cwid], op0=ALU.mult, op1=ALU.add),
                          reads=["hf_kf", "hf_yt"], writes=["hf_yt"])
                    fw.op(fw.dve, lambda v: v.tensor_scalar(out=yt[:, 0:cwid], in0=yt[:, 0:cwid], scalar1=3.141592, scalar2=-3.141592, op0=ALU.min, op1=ALU.max),
                          reads=["hf_yt"], writes=["hf_yt"])
                    fw.op(fw.act, lambda a: a.activation(out=dstT[:, cs], in_=yt[:, 0:cwid], func=AF.Sin), reads=["hf_yt"], writes=[dk])

            sin_layer(featT, w1, 0, h1T, "hf_feat", "hf_w1", "hf_h1T")
            sin_layer(h1T, w2, 1, h2T, "hf_h1T", "hf_w2", "hf_h2T")

            hrow = [sb(f"hf_hrow{i}", [128, 1024]) for i in range(2)]
            habs = sb("hf_habs", [128, 1024]); Et = sb("hf_E", [128, 512])
            pS = [self.pb[6], self.pb[7]]; pSk = [("pb", 6), ("pb", 7)]
            for j in range(NJT):
                hr = hrow[j % 2]; hk = ("hf_hrow", j % 2)
                for half in range(2):
                    cs = slice(half * 512, (half + 1) * 512)
                    ps, pk = self.pb[2 + half], ("pb", 2 + half)
                    fw.op(fw.pe, lambda t_: t_.matmul(ps[:], lhsT=h2T[:, j * 128:(j + 1) * 128], rhs=w3[:, cs], start=True, stop=True),
                          reads=["hf_h2T", "hf_w3"], writes=[pk])
                    fw.op(fw.act, lambda a: a.activation(out=Et[:], in_=dabs[:, cs], func=AF.Exp, scale=negt[:, j:j + 1]), reads=["hf_dabs", "hf_negt"], writes=["hf_E"])
                    fw.op(fw.dve, lambda v: v.tensor_tensor(out=hr[:, cs], in0=ps[:], in1=Et[:], op=ALU.mult), reads=[pk, "hf_E"], writes=[hk])
                if j == 0:
                    fw.op(fw.pool, lambda g: g.memset(hr[0:1, 256:512], 0.0), reads=[hk], writes=[hk])
                    fw.op(fw.pool, lambda g: g.memset(hr[0:1, 768:1024], 0.0), reads=[hk], writes=[hk])
                fw.op(fw.act, lambda a: a.activation(out=habs[:], in_=hr[:], func=AF.Abs), reads=[hk], writes=["hf_habs"])
                for half in range(2):
                    fw.op(fw.pe, lambda t_: t_.matmul(pS[half][:], lhsT=self.ones32[:], rhs=habs[:, half * 512:(half + 1) * 512], start=(j == 0), stop=(j == NJT - 1)),
                          reads=["hf_habs", "ones32"], writes=[pSk[half]])
                fw.dma(self.hraw[L_][j * 128:(j + 1) * 128, :], hr[:], reads=[hk], writes=["hraw"])
            Ssb = sb("hf_S", [128, 1024]); rn = sb("hf_rn", [128, 2, 256]); rnb = sb("hf_rnb", [128, 1024])
            for half in range(2):
                fw.op(fw.act, lambda a: a.copy(out=Ssb[:, half * 512:(half + 1) * 512], in_=pS[half][:]), reads=[pSk[half]], writes=["hf_S"])
            Sv = Ssb[:].rearrange("p (o d c) -> p o d c", o=2, d=2)
            fw.op(fw.dve, lambda v: v.tensor_tensor(out=rn[:], in0=Sv[:, :, 0, :], in1=Sv[:, :, 1, :], op=ALU.add), reads=["hf_S"], writes=["hf_rn"])
            fw.op(fw.dve, lambda v: v.reciprocal(out=rn[:], in_=rn[:]), reads=["hf_rn"], writes=["hf_rn"])
            rv = rnb[:].rearrange("p (o d c) -> p o d c", o=2, d=2)
            for d_ in range(2):
                fw.op(fw.dve, lambda v: v.tensor_copy(out=rv[:, :, d_, :], in_=rn[:]), reads=["hf_rn"], writes=["hf_rnb"])
            hb16 = [sb(f"hf_hb16{i}", [128, 1024], BF16) for i in range(2)]
            for j in range(NJT):
                hr = hrow[j % 2]; hk = ("hf_hrow", j % 2)
                fw.dma(hr[:], self.hraw[L_][j * 128:(j + 1) * 128, :], reads=["hraw"], writes=[hk])
                fw.op(fw.dve, lambda v: v.tensor_tensor(out=hb16[j % 2][:], in0=hr[:], in1=rnb[:], op=ALU.mult), reads=[hk, "hf_rnb"], writes=[("hf_hb16", j % 2)])
                fw.dma(self.ftok[L_][j * 128:(j + 1) * 128, :], hb16[j % 2][:], reads=[("hf_hb16", j % 2)], writes=["ftok"])
        fw.barrier()
        if L_ == 256:
            self.hy_spectra256()
            fw.barrier()
            return
        with ExitStack() as es:
            sb = lambda n, s, d=F32: es.enter_context(nc.sbuf_tensor(self.uniq(n), s, d))
            FAt = sb("hy_FAt", [128, 128, 128], BF16); GB = sb("hy_GBs", [128, 12, 128], BF16)
            xt = sb("hy_xt", [128, 16, 512], BF16); stg = [sb(f"hy_stg{i}", [128, 16, 512], BF16) for i in range(2)]
            A2 = sb("hy_A2", [128, 2, 4, 512], BF16)
            kst = [sb(f"hy_kst{i}", [128, 2, 4, 256]) for i in range(2)]
            fw.dma(FAt[:], self.hy_FA[L_][:, :, :], writes=["hy_FAt"])
            fw.dma(GB[:], self.hy_GB[:, :, :], writes=["hy_GBs"])
            fw.op(fw.pool, lambda g: g.memset(xt[:], 0.0), writes=["hy_xt"])
            for o in range(2):
                self.hy_stageA(es, L_, self.ftok[L_][:, o * 512:(o + 1) * 512], 512, FAt, xt, stg)
                nkc = min(4, KA)
                for kc in range(KA // nkc):
                    for r in range(2):
                        fw.dma(A2[:, r, 0:nkc, :], self.Ascr[r * 64 + kc * nkc:r * 64 + (kc + 1) * nkc, :, :].rearrange("k l c -> l k c"),
                               reads=["Ascr"], writes=["hy_A2"])
                    ks = kst[kc % 2]; ksk = ("hy_kst", kc % 2)
                    for kk in range(nkc):
                        pa, pak = self.pb[(2 * kk) % 8], ("pb", (2 * kk) % 8)
                        pb_, pbk = self.pb[(2 * kk + 1) % 8], ("pb", (2 * kk + 1) % 8)
                        ops = [(0, 0, 4), (1, 0, 5), (0, 256, 4), (1, 256, 5)]
                        for ii, (r, c0_, gi) in enumerate(ops):
                            fw.op(fw.pe, lambda t_: t_.matmul(pa[:, 0:256], lhsT=GB[:, gi, :], rhs=A2[:, r, kk, c0_:c0_ + 256], start=(ii == 0), stop=(ii == 3)),
                                  reads=["hy_GBs", "hy_A2"], writes=[pak], inc=(ii == 3))
                        ops = [(0, 0, 6), (1, 0, 7), (0, 256, 8), (1, 256, 9)]
                        for ii, (r, c0_, gi) in enumerate(ops):
                            fw.op(fw.pe, lambda t_: t_.matmul(pb_[:, 0:256], lhsT=GB[:, gi, :], rhs=A2[:, r, kk, c0_:c0_ + 256], start=(ii == 0), stop=(ii == 3)),
                                  reads=["hy_GBs", "hy_A2"], writes=[pbk], inc=(ii == 3))
                        fw.op(fw.act, lambda a: a.copy(out=ks[:, 0, kk, :], in_=pa[:, 0:256]), reads=[pak], writes=[ksk])
                        fw.op(fw.dve, lambda v: v.tensor_copy(out=ks[:, 1, kk, :], in_=pb_[:, 0:256]), reads=[pbk], writes=[ksk])
                    for ab in range(2):
                        fw.dma(self.Ksp[L_][o, ab, :, kc * nkc:(kc + 1) * nkc, :], ks[:, ab, 0:nkc, :], reads=[ksk], writes=["Ksp"])
                fw.barrier()

    def hy_spectra256(self):
        nc, fw = self.nc, self.fw
        with ExitStack() as es:
            sb = lambda n, s, d=F32: es.enter_context(nc.sbuf_tensor(self.uniq(n), s, d))
            T = sb("hy_T", [128, 6, 1024], BF16)
            ft = sb("hy_ft", [128, 2, 1024], BF16)
            ks = sb("hy_ks", [128, 2, 2, 4, 256])
            fw.dma(T[:], self.hy_T256[:, :, :], writes=["hy_T"])
            fw.dma(ft[:], self.ftok[256].rearrange("(h p) c -> p h c", p=128), reads=["ftok"], writes=["hy_ft"])
            Tv = lambda var, nh, ka: T[:, var, (nh * 4 + ka) * 128:(nh * 4 + ka + 1) * 128]
            for o in range(2):
                for ka in range(4):
                    pa, pak = self.pb[(2 * ka) % 8], ("pb", (2 * ka) % 8)
                    pb_, pbk = self.pb[(2 * ka + 1) % 8], ("pb", (2 * ka + 1) % 8)
                    seq = [(2, nh, 0) for nh in range(2)] + [(2, nh, 256) for nh in range(2)]
                    for ii, (var, nh, c0_) in enumerate(seq):
                        fw.op(fw.pe, lambda t_: t_.matmul(pa[:, 0:256], lhsT=Tv(var, nh, ka), rhs=ft[:, nh, o * 512 + c0_:o * 512 + c0_ + 256], start=(ii == 0), stop=(ii == 3)),
                              reads=["hy_T", "hy_ft"], writes=[pak], inc=(ii == 3))
                    seq = [(3, nh, 0) for nh in range(2)] + [(4, nh, 256) for nh in range(2)]
                    for ii, (var, nh, c0_) in enumerate(seq):
                        fw.op(fw.pe, lambda t_: t_.matmul(pb_[:, 0:256], lhsT=Tv(var, nh, ka), rhs=ft[:, nh, o * 512 + c0_:o * 512 + c0_ + 256], start=(ii == 0), stop=(ii == 3)),
                              reads=["hy_T", "hy_ft"], writes=[pbk], inc=(ii == 3))
                    fw.op(fw.act, lambda a: a.copy(out=ks[:, o, 0, ka, :], in_=pa[:, 0:256]), reads=[pak], writes=["hy_ks"])
                    fw.op(fw.dve, lambda v: v.tensor_copy(out=ks[:, o, 1, ka, :], in_=pb_[:, 0:256]), reads=[pbk], writes=["hy_ks"])
                for ab in range(2):
                    fw.dma(self.Ksp[256][o, ab], ks[:, o, ab], reads=["hy_ks"], writes=["Ksp"])

    def hy_conv256(self, o):
        nc, fw = self.nc, self.fw
        with ExitStack() as es:
            sb = lambda n, s, d=F32: es.enter_context(nc.sbuf_tensor(self.uniq(n), s, d))
            T = sb("hy_T", [128, 6, 1024], BF16)
            xt = sb("hy_x256", [128, 4, 256], BF16)
            Kab = sb("hy_K256", [128, 2, 4, 256])
            Y = sb("hy_Y256", [128, 2, 4, 256], BF16)
            t1 = [sb(f"hy_t1{i}", [128, 256]) for i in range(2)]; t2 = [sb(f"hy_t2{i}", [128, 256]) for i in range(2)]
            y32 = sb("hy_y256", [128, 2, 512])
            fw.dma(T[:], self.hy_T256[:, :, :], writes=["hy_T"])
            fw.dma(xt[:], self.ztok[0:512, :].rearrange("(h p) c -> p h c", p=128), reads=["ztok"], writes=["hy_x256"])
            for ab in range(2):
                fw.dma(Kab[:, ab], self.Ksp[256][o, ab], reads=["Ksp"], writes=["hy_K256"])
            Tv = lambda var, nh, ka: T[:, var, (nh * 4 + ka) * 128:(nh * 4 + ka + 1) * 128]
            for sq in range(2):
                for ka in range(4):
                    i_ = (sq * 4 + ka) % 2
                    pa, pak = self.pb[2 * i_], ("pb", 2 * i_)
                    pw, pwk = self.pb[2 * i_ + 1], ("pb", 2 * i_ + 1)
                    for nh in range(2):
                        fw.op(fw.pe, lambda t_: t_.matmul(pa[:, 0:256], lhsT=Tv(0, nh, ka), rhs=xt[:, 2 * sq + nh, :], start=(nh == 0), stop=(nh == 1)),
                              reads=["hy_T", "hy_x256"], writes=[pak], inc=(nh == 1))
                    for nh in range(2):
                        fw.op(fw.pe, lambda t_: t_.matmul(pw[:, 0:256], lhsT=Tv(1, nh, ka), rhs=xt[:, 2 * sq + nh, :], start=(nh == 0), stop=(nh == 1)),
                              reads=["hy_T", "hy_x256"], writes=[pwk], inc=(nh == 1))
                    a_, b_ = t1[i_], t2[i_]
                    fw.op(fw.dve, lambda v: v.tensor_tensor(out=a_[:], in0=pa[:, 0:256], in1=Kab[:, 0, ka, :], op=ALU.mult), reads=[pak, "hy_K256"], writes=[("hy_t1", i_)])
                    fw.op(fw.dve, lambda v: v.tensor_tensor(out=b_[:], in0=pw[:, 0:256], in1=Kab[:, 1, ka, :], op=ALU.mult), reads=[pwk, "hy_K256"], writes=[("hy_t2", i_)])
                    fw.op(fw.pool, lambda g: g.tensor_tensor(out=Y[:, sq, ka, :], in0=a_[:], in1=b_[:], op=ALU.add), reads=[("hy_t1", i_), ("hy_t2", i_)], writes=["hy_Y256"])
                for chalf in range(2):
                    ps, pk = self.pb[4 + (2 * sq + chalf) % 4], ("pb", 4 + (2 * sq + chalf) % 4)
                    for ka in range(4):
                        fw.op(fw.pe, lambda t_: t_.matmul(ps[:, 0:256], lhsT=Y[:, sq, ka, chalf * 128:(chalf + 1) * 128], rhs=T[:, 5, ka * 256:(ka + 1) * 256], start=(ka == 0), stop=(ka == 3)),
                              reads=["hy_Y256", "hy_T"], writes=[pk], inc=(ka == 3))
                    if chalf == 0:
                        fw.op(fw.dve, lambda v: v.tensor_copy(out=y32[:, chalf, sq * 256:(sq + 1) * 256], in_=ps[:, 0:256]), reads=[pk], writes=["hy_y256"])
                    else:
                        fw.op(fw.act, lambda a: a.copy(out=y32[:, chalf, sq * 256:(sq + 1) * 256], in_=ps[:, 0:256]), reads=[pk], writes=["hy_y256"])
            for chalf in range(2):
                fw.dma(self.hyT[chalf * 128:(chalf + 1) * 128, 0:512], y32[:, chalf, :], reads=["hy_y256"], writes=[("hyT", chalf)])

    def hy_conv(self, o, t0, L_):
        nc, fw = self.nc, self.fw
        NH = L_ // 128
        KA = L_ // 64
        with ExitStack() as es:
            sb = lambda n, s, d=F32: es.enter_context(nc.sbuf_tensor(self.uniq(n), s, d))
            FAt = sb("hy_FAt", [128, 128, 128], BF16); GB = sb("hy_GBs", [128, 12, 128], BF16)
            FAi = sb("hy_FAi", [128, 128, NH], BF16)
            xt = sb("hy_xt", [128, 16, 256], BF16); stg = [sb(f"hy_stg{i}", [128, 16, 256], BF16) for i in range(2)]
            A2 = sb("hy_A2", [128, 2, 8, 256], BF16)
            Kab = sb("hy_Kab", [128, 2, 8, 256])
            Y = sb("hy_Y", [128, KA, 256], BF16)
            t1 = [sb(f"hy_t1{i}", [128, 256]) for i in range(2)]; t2 = [sb(f"hy_t2{i}", [128, 256]) for i in range(2)]
            stB = [sb(f"hy_stB{i}", [128, 2, 8, 256], BF16) for i in range(2)]
            Bq2 = sb("hy_Bq2", [128, 16, 256], BF16)
            y32 = sb("hy_y32", [128, 2, L_])
            fw.dma(FAt[:], self.hy_FA[L_][:, :, :], writes=["hy_FAt"])
            fw.dma(FAi[:], self.hy_FAi[L_][:, :, :], writes=["hy_FAi"])
            fw.dma(GB[:], self.hy_GB[:, :, :], writes=["hy_GBs"])
            fw.op(fw.pool, lambda g: g.memset(xt[:], 0.0), writes=["hy_xt"])
            fw.op(fw.pool, lambda g: g.memset(Bq2[:], 0.0), writes=["hy_Bq2"])
            self.hy_stageA(es, L_, self.ztok[t0:t0 + L_, :], 256, FAt, xt, stg)
            nkc = min(8, KA)
            for kc in range(KA // nkc):
                for r in range(2):
                    fw.dma(A2[:, r, 0:nkc, :], self.Ascr[r * 64 + kc * nkc:r * 64 + (kc + 1) * nkc, :, 0:256].rearrange("k l c -> l k c"),
                           reads=["Ascr"], writes=["hy_A2"])
                    fw.dma(Kab[:, r, 0:nkc, :], self.Ksp[L_][o, r, :, kc * nkc:(kc + 1) * nkc, :], reads=["Ksp"], writes=["hy_Kab"])
                for kk in range(nkc):
                    ka = kc * nkc + kk
                    pa, pak = self.pb[(2 * kk) % 4], ("pb", (2 * kk) % 4)
                    pw, pwk = self.pb[(2 * kk + 1) % 4], ("pb", (2 * kk + 1) % 4)
                    for ii, (r, gi) in enumerate([(0, 0), (1, 1)]):
                        fw.op(fw.pe, lambda t_: t_.matmul(pa[:, 0:256], lhsT=GB[:, gi, :], rhs=A2[:, r, kk, :], start=(ii == 0), stop=(ii == 1)),
                              reads=["hy_GBs", "hy_A2"], writes=[pak], inc=(ii == 1))
                    for ii, (r, gi) in enumerate([(0, 2), (1, 3)]):
                        fw.op(fw.pe, lambda t_: t_.matmul(pw[:, 0:256], lhsT=GB[:, gi, :], rhs=A2[:, r, kk, :], start=(ii == 0), stop=(ii == 1)),
                              reads=["hy_GBs", "hy_A2"], writes=[pwk], inc=(ii == 1))
                    a_, b_ = t1[kk % 2], t2[kk % 2]
                    fw.op(fw.dve, lambda v: v.tensor_tensor(out=a_[:], in0=pa[:, 0:256], in1=Kab[:, 0, kk, :], op=ALU.mult), reads=[pak, "hy_Kab"], writes=[("hy_t1", kk % 2)])
                    fw.op(fw.dve, lambda v: v.tensor_tensor(out=b_[:], in0=pw[:, 0:256], in1=Kab[:, 1, kk, :], op=ALU.mult), reads=[pwk, "hy_Kab"], writes=[("hy_t2", kk % 2)])
                    fw.op(fw.pool, lambda g: g.tensor_tensor(out=Y[:, ka, :], in0=a_[:], in1=b_[:], op=ALU.add), reads=[("hy_t1", kk % 2), ("hy_t2", kk % 2)], writes=["hy_Y"])
            for kc in range(KA // nkc):
                sB = stB[kc % 2]; sBk = ("hy_stB", kc % 2)
                for kk in range(nkc):
                    ka = kc * nkc + kk
                    for r in range(2):
                        ps, pk = self.pb[4 + (2 * kk + r) % 4], ("pb", 4 + (2 * kk + r) % 4)
                        fw.op(fw.pe, lambda t_: t_.matmul(ps[:, 0:256], lhsT=GB[:, 10 + r, :], rhs=Y[:, ka, :], start=True, stop=True), reads=["hy_GBs", "hy_Y"], writes=[pk])
                        if r == 0:
                            fw.op(fw.dve, lambda v: v.tensor_copy(out=sB[:, r, kk, :], in_=ps[:, 0:256]), reads=[pk], writes=[sBk])
                        else:
                            fw.op(fw.act, lambda a: a.activation(out=sB[:, r, kk, :], in_=ps[:, 0:256], func=AF.Identity), reads=[pk], writes=[sBk])
                for r in range(2):
                    fw.dma(self.Bscr[:, r, kc * nkc:(kc + 1) * nkc, :], sB[:, r, 0:nkc, :], reads=[sBk], writes=["Bscr"])
            for ch in range(8):
                for r in range(2):
                    fw.dma(Bq2[r * 64:r * 64 + KA, :, :], self.Bscr[ch * 16:(ch + 1) * 16, r, 0:KA, :].rearrange("l k c -> k l c"), reads=["Bscr"], writes=["hy_Bq2"])
                for chalf in range(2):
                    ps, pk = self.pb[(2 * ch + chalf) % 4], ("pb", (2 * ch + chalf) % 4)
                    for i in range(16):
                        n_lo = ch * 16 + i
                        fw.op(fw.pe, lambda t_: t_.matmul(ps[:, i * NH:(i + 1) * NH], lhsT=Bq2[:, i, chalf * 128:(chalf + 1) * 128], rhs=FAi[:, n_lo, :], start=True, stop=True),
                              reads=["hy_Bq2", "hy_FAi"], writes=[pk], inc=(i == 15))
                    ov_ = y32[:, chalf, :].rearrange("p (h l) -> p l h", l=128)[:, ch * 16:(ch + 1) * 16, :]
                    iv_ = ps[:, 0:16 * NH].rearrange("p (j h) -> p j h", h=NH)
                    if chalf == 0:
                        fw.op(fw.dve, lambda v: v.tensor_copy(out=ov_, in_=iv_), reads=[pk], writes=["hy_y32"])
                    else:
                        fw.op(fw.act, lambda a: a.activation(out=ov_, in_=iv_, func=AF.Identity), reads=[pk], writes=["hy_y32"])
            for chalf in range(2):
                fw.dma(self.hyT[chalf * 128:(chalf + 1) * 128, t0:t0 + L_], y32[:, chalf, :], reads=["hy_y32"], writes=[("hyT", chalf)])

    def s5(self, l):
        nc, fw = self.nc, self.fw
        NA = NT // 8
        TWO_PI = 2.0 * math.pi
        with ExitStack() as es:
            sb = lambda n, s, d=F32: es.enter_context(nc.sbuf_tensor(self.uniq(n), s, d))
            zb = sb("s5_zb", [128, 2, NT], BF16)
            CQ = sb("s5_CQ", [128, 16, 2, 8, 128], BF16)
            BD = sb("s5_BD", [128, 4, 8, 128], BF16)
            HM = sb("s5_HM", [128, 16, 2, NA], BF16)
            sc = sb("s5_scS", [128, 3, 16]); h0 = sb("s5_h0S", [128, 2, 16]); skp = sb("s5_skS", [128, 2])
            idx = sb("s5_idxS", [128, 513])
            W = {}
            for nm in ("st", "ar", "th", "rho", "c1", "s1", "pr1", "pi1", "nr", "dd", "fr", "fi", "t1", "t2", "rho8", "phr", "th2"):
                W[nm] = sb("s5w_" + nm, [128, 16])
            wi = sb("s5w_i", [128, 16], I32)
            PWr = sb("s5_PWr", [128, 9, 16]); PWi = sb("s5_PWi", [128, 9, 16]); NPWi = sb("s5_NPWi", [128, 9, 16])
            for h in range(2):
                fw.dma(zb[:, h, :], self.zbT[h * 128:(h + 1) * 128, :], reads=[("zbT", h)], writes=["s5_zb"])
            fw.dma(sc[:], self.s5_sc[l], writes=["s5_sc"]); fw.dma(h0[:], self.s5_h0[l], writes=["s5_h0"])
            fw.dma(skp[:], self.s5_skip[l], writes=["s5_skp"]); fw.dma(idx[:], self.s5_idx[:, :], writes=["s5_idx"])
            K = "s5_small"
            dv = lambda fn, r=(K, "s5_sc"), w=(K,): fw.op(fw.dve, fn, reads=list(r), writes=list(w))
            ac = lambda fn, r=(K, "s5_sc"), w=(K,): fw.op(fw.act, fn, reads=list(r), writes=list(w))
            lre, lim, lst = sc[:, 0, :], sc[:, 1, :], sc[:, 2, :]

            def reduce_sin(dst, src, tmpf, tmpi):
                dv(lambda v: v.tensor_scalar(out=tmpi, in0=src, scalar1=1.0 / TWO_PI, scalar2=None, op0=ALU.mult))
                dv(lambda v: v.tensor_copy(out=tmpf, in_=tmpi))
                dv(lambda v: v.scalar_tensor_tensor(out=src, in0=tmpf, scalar=-TWO_PI, in1=src, op0=ALU.mult, op1=ALU.add))
                dv(lambda v: v.tensor_scalar(out=src, in0=src, scalar1=3.141592, scalar2=-3.141592, op0=ALU.min, op1=ALU.max))
                ac(lambda a: a.activation(out=dst, in_=src, func=AF.Sin))

            ac(lambda a: a.activation(out=W["st"][:], in_=lst, func=AF.Exp))
            dv(lambda v: v.tensor_tensor(out=W["ar"][:], in0=lre, in1=W["st"][:], op=ALU.mult))
            dv(lambda v: v.tensor_tensor(out=W["th"][:], in0=lim, in1=W["st"][:], op=ALU.mult))
            ac(lambda a: a.activation(out=W["rho"][:], in_=W["ar"][:], func=AF.Exp))
            ac(lambda a: a.activation(out=W["rho8"][:], in_=W["ar"][:], func=AF.Exp, scale=8.0))
            dv(lambda v: v.tensor_scalar(out=W["phr"][:], in0=W["th"][:], scalar1=8.0, scalar2=None, op0=ALU.mult))
            dv(lambda v: v.tensor_scalar(out=W["th2"][:], in0=W["th"][:], scalar1=math.pi / 2, scalar2=None, op0=ALU.add))
            reduce_sin(W["s1"][:], W["th"][:], W["t1"][:], wi[:])
            reduce_sin(W["c1"][:], W["th2"][:], W["t1"][:], wi[:])
            dv(lambda v: v.tensor_scalar(out=wi[:], in0=W["phr"][:], scalar1=1.0 / TWO_PI, scalar2=None, op0=ALU.mult))
            dv(lambda v: v.tensor_copy(out=W["t1"][:], in_=wi[:]))
            dv(lambda v: v.scalar_tensor_tensor(out=W["phr"][:], in0=W["t1"][:], scalar=-TWO_PI, in1=W["phr"][:], op0=ALU.mult, op1=ALU.add))
            dv(lambda v: v.tensor_tensor(out=W["pr1"][:], in0=W["rho"][:], in1=W["c1"][:], op=ALU.mult))
            dv(lambda v: v.tensor_tensor(out=W["pi1"][:], in0=W["rho"][:], in1=W["s1"][:], op=ALU.mult))
            dv(lambda v: v.tensor_scalar(out=W["nr"][:], in0=W["pr1"][:], scalar1=-1.0, scalar2=None, op0=ALU.add))
            dv(lambda v: v.tensor_tensor(out=W["dd"][:], in0=lre, in1=lre, op=ALU.mult))
            dv(lambda v: v.tensor_tensor(out=W["t1"][:], in0=lim, in1=lim, op=ALU.mult))
            dv(lambda v: v.tensor_tensor(out=W["dd"][:], in0=W["dd"][:], in1=W["t1"][:], op=ALU.add))
            dv(lambda v: v.reciprocal(out=W["dd"][:], in_=W["dd"][:]))
            dv(lambda v: v.tensor_tensor(out=W["t1"][:], in0=W["nr"][:], in1=lre, op=ALU.mult))
            dv(lambda v: v.tensor_tensor(out=W["t2"][:], in0=W["pi1"][:], in1=lim, op=ALU.mult))
            dv(lambda v: v.tensor_tensor(out=W["t1"][:], in0=W["t1"][:], in1=W["t2"][:], op=ALU.add))
            dv(lambda v: v.tensor_tensor(out=W["fr"][:], in0=W["t1"][:], in1=W["dd"][:], op=ALU.mult))
            dv(lambda v: v.tensor_tensor(out=W["t1"][:], in0=W["pi1"][:], in1=lre, op=ALU.mult))
            dv(lambda v: v.tensor_tensor(out=W["t2"][:], in0=W["nr"][:], in1=lim, op=ALU.mult))
            dv(lambda v: v.tensor_tensor(out=W["t1"][:], in0=W["t1"][:], in1=W["t2"][:], op=ALU.subtract))
            dv(lambda v: v.tensor_tensor(out=W["fi"][:], in0=W["t1"][:], in1=W["dd"][:], op=ALU.mult))
            dv(lambda v: v.memset(PWr[:, 0, :], 1.0)); dv(lambda v: v.memset(PWi[:, 0, :], 0.0))
            for j in range(8):
                dv(lambda v: v.tensor_tensor(out=W["t1"][:], in0=PWr[:, j, :], in1=W["pr1"][:], op=ALU.mult))
                dv(lambda v: v.tensor_tensor(out=W["t2"][:], in0=PWi[:, j, :], in1=W["pi1"][:], op=ALU.mult))
                dv(lambda v: v.tensor_tensor(out=PWr[:, j + 1, :], in0=W["t1"][:], in1=W["t2"][:], op=ALU.subtract))
                dv(lambda v: v.tensor_tensor(out=W["t1"][:], in0=PWr[:, j, :], in1=W["pi1"][:], op=ALU.mult))
                dv(lambda v: v.tensor_tensor(out=W["t2"][:], in0=PWi[:, j, :], in1=W["pr1"][:], op=ALU.mult))
                dv(lambda v: v.tensor_tensor(out=PWi[:, j + 1, :], in0=W["t1"][:], in1=W["t2"][:], op=ALU.add))
            dv(lambda v: v.tensor_scalar(out=NPWi[:], in0=PWi[:], scalar1=-1.0, scalar2=None, op0=ALU.mult))

            def ucols(d, h, r, a_lo, n):
                if d == 0:
                    s0 = r + 8 * a_lo
                    return zb[:, h, s0:s0 + 8 * (n - 1) + 1:8]
                s0 = NT - 1 - r - 8 * a_lo
                e0 = s0 - 8 * n
                return zb[:, h, s0:e0:-8] if e0 >= 0 else zb[:, h, s0::-8]

            with ExitStack() as es2:
                sb2 = lambda n, s, d=F32: es2.enter_context(nc.sbuf_tensor(self.uniq(n), s, d))
                Bm = [sb2(f"s5_Bm{i}", [128, 2, 128]) for i in range(2)]
                Cm = [sb2(f"s5_Cm{i}", [128, 2, 128]) for i in range(2)]
                nCi = [sb2(f"s5_nCi{i}", [128, 128]) for i in range(2)]
                Bb = sb2("s5_Bb", [128, 2, 128]); tA = sb2("s5_tA", [128, 128]); tB = sb2("s5_tB", [128, 128])
                X1 = sb2("s5_X1", [128, 8, 128]); X2 = sb2("s5_X2", [128, 8, 128]); X3 = X1; X4 = X2
                Pm = [sb2("s5_Pm0", [128, 2, 8, 128])] * 2
                WV = [sb2("s5_WV0", [128, 2, 8, 128], BF16)] * 2
                Vs = sb2("s5_Vs", [128, 2, NA])
                Tt = sb2("s5_Tt", [128, 2, 513]); ti = sb2("s5_ti", [128, 2, 513], I32); tf = sb2("s5_tf", [128, 2, 513])
                TA = sb2("s5_TA", [128, 2, NA]); TH = sb2("s5_TH", [128, 2, NA + 3])
                Wr = sb2("s5_Wr", [128, 2, NA]); w1 = sb2("s5_w1", [128, NA + 3]); w2 = sb2("s5_w2", [128, NA + 3])
                Gm = sb2("s5_Gm", [128, 2, NA + 3]); Hf = sb2("s5_Hf", [128, 2, NA + 3])
                for d in range(2):
                    if d == 0:
                        segs = [(0, 32, 0), (32, 32, 1), (64, 512, 2)]
                    else:
                        segs = [(0, 512, 2), (512, 32, 1), (544, 32, 0)]
                    for pr in range(8):
                        col = d * 8 + pr
                        h = pr // 4
                        q = pr % 2
                        Bk, Ck, Pk, WVk = ("s5_Bm", q), ("s5_Cm", q), ("s5_Pm", 0), ("s5_WV", 0)
                        for ri in range(2):
                            fw.dma(Bm[q][:, ri, :], self.s5_B[l, d, pr, ri], writes=[Bk])
                            fw.dma(Cm[q][:, ri, :], self.s5_C[l, d, pr, ri], writes=[Ck])
                        fr, fi = W["fr"][:, col:col + 1], W["fi"][:, col:col + 1]
                        fw.op(fw.dve, lambda v: v.tensor_scalar(out=tA[:], in0=Bm[q][:, 1, :], scalar1=fi, scalar2=None, op0=ALU.mult), reads=[Bk, K], writes=["s5_tA"])
                        fw.op(fw.dve, lambda v: v.scalar_tensor_tensor(out=Bb[:, 0, :], in0=Bm[q][:, 0, :], scalar=fr, in1=tA[:], op0=ALU.mult, op1=ALU.subtract),
                              reads=[Bk, K, "s5_tA"], writes=["s5_Bb"])
                        fw.op(fw.dve, lambda v: v.tensor_scalar(out=tB[:], in0=Bm[q][:, 0, :], scalar1=fi, scalar2=None, op0=ALU.mult), reads=[Bk, K], writes=["s5_tB"])
                        fw.op(fw.dve, lambda v: v.scalar_tensor_tensor(out=Bb[:, 1, :], in0=Bm[q][:, 1, :], scalar=fr, in1=tB[:], op0=ALU.mult, op1=ALU.add),
                              reads=[Bk, K, "s5_tB"], writes=["s5_Bb"])
                        fw.op(fw.pool, lambda g: g.tensor_scalar(out=nCi[q][:], in0=Cm[q][:, 1, :], scalar1=-1.0, scalar2=None, op0=ALU.mult), reads=[Ck], writes=[("s5_nCi", q)])
                        bc3 = lambda m2: m2.unsqueeze(1).to_broadcast([128, 8, 128])
                        pw3 = lambda tab, j0: tab[:, j0:j0 + 8, col].unsqueeze(2).to_broadcast([128, 8, 128])
                        fw.op(fw.dve, lambda v: v.tensor_tensor(out=X1[:], in0=bc3(Bb[:, 0, :]), in1=pw3(PWr, 0), op=ALU.mult), reads=["s5_Bb", K], writes=["s5_X1"])
                        fw.op(fw.dve, lambda v: v.tensor_tensor(out=X2[:], in0=bc3(Bb[:, 1, :]), in1=pw3(PWi, 0), op=ALU.mult), reads=["s5_Bb", K], writes=["s5_X2"])
                        fw.op(fw.pool, lambda g: g.tensor_tensor(out=Pm[q][:, 0, :, :], in0=X1[:], in1=X2[:], op=ALU.subtract), reads=["s5_X1", "s5_X2"], writes=[Pk])
                        fw.op(fw.dve, lambda v: v.tensor_tensor(out=X3[:], in0=bc3(Bb[:, 1, :]), in1=pw3(PWr, 0), op=ALU.mult), reads=["s5_Bb", K], writes=["s5_X1"])
                        fw.op(fw.dve, lambda v: v.tensor_tensor(out=X4[:], in0=bc3(Bb[:, 0, :]), in1=pw3(PWi, 0), op=ALU.mult), reads=["s5_Bb", K], writes=["s5_X2"])
                        fw.op(fw.pool, lambda g: g.tensor_tensor(out=Pm[q][:, 1, :, :], in0=X3[:], in1=X4[:], op=ALU.add), reads=["s5_X1", "s5_X2"], writes=[Pk])
                        fw.op(fw.dve, lambda v: v.tensor_tensor(out=X1[:], in0=bc3(Cm[q][:, 0, :]), in1=pw3(PWr, 1), op=ALU.mult), reads=[Ck, K, "s5_X1"], writes=["s5_X1"])
                        fw.op(fw.dve, lambda v: v.tensor_tensor(out=X2[:], in0=bc3(Cm[q][:, 1, :]), in1=pw3(PWi, 1), op=ALU.mult), reads=[Ck, K, "s5_X2"], writes=["s5_X2"])
                        fw.op(fw.pool, lambda g: g.tensor_tensor(out=CQ[:, col, 0, :, :], in0=X1[:], in1=X2[:], op=ALU.subtract), reads=["s5_X1", "s5_X2"], writes=["s5_CQ"])
                        fw.op(fw.dve, lambda v: v.tensor_tensor(out=X3[:], in0=bc3(Cm[q][:, 0, :]), in1=pw3(NPWi, 1), op=ALU.mult), reads=[Ck, K, "s5_X1"], writes=["s5_X1"])
                        fw.op(fw.dve, lambda v: v.tensor_tensor(out=X4[:], in0=bc3(Cm[q][:, 1, :]), in1=pw3(PWr, 1), op=ALU.mult), reads=[Ck, K, "s5_X2"], writes=["s5_X2"])
                        fw.op(fw.pool, lambda g: g.tensor_tensor(out=CQ[:, col, 1, :, :], in0=X3[:], in1=X4[:], op=ALU.subtract), reads=["s5_X1", "s5_X2"], writes=["s5_CQ"])
                        for ri in range(2):
                            for half in range(2):
                                ps, pk = self.pb[half], ("pb", half)
                                for jj in range(4):
                                    rp = half * 4 + jj
                                    fw.op(fw.pe, lambda t_: t_.transpose(ps[:, jj * 128:(jj + 1) * 128], Pm[q][:, ri, 7 - rp, :], self.ident[:]),
                                          reads=[Pk, "ident"], writes=[pk], inc=(jj == 3))
                                fw.op(fw.act, lambda a: a.activation(out=WV[q][:, ri, half * 4:(half + 1) * 4, :], in_=ps[:].rearrange("p (j c) -> p j c", j=4), func=AF.Identity),
                                      reads=[pk], writes=[WVk])
                        for j in range(8):
                            ps, pk = self.pb[6 + j // 4], ("pb", 6 + j // 4)
                            first = (pr % 4 == 0) and (j % 4 == 0)
                            last = (pr % 4 == 3) and (j % 4 == 3)
                            fw.op(fw.pe, lambda t_: t_.matmul(ps[:, (j % 4) * 128:(j % 4 + 1) * 128], lhsT=Pm[q][:, 0, j, :], rhs=Cm[q][:, 0, :], start=first, stop=False,
                                                             skip_group_check=True), reads=[Pk, Ck], writes=[pk], inc=False)
                            fw.op(fw.pe, lambda t_: t_.matmul(ps[:, (j % 4) * 128:(j % 4 + 1) * 128], lhsT=Pm[q][:, 1, j, :], rhs=nCi[q][:], start=False, stop=last,
                                                             skip_group_check=True), reads=[Pk, ("s5_nCi", q)], writes=[pk], inc=(j % 4 == 3))
                        if pr % 4 == 3:
                            for hb in range(2):
                                fw.op(fw.act, lambda a: a.activation(out=BD[:, d * 2 + h, hb * 4:(hb + 1) * 4, :], in_=self.pb[6 + hb][:].rearrange("p (j c) -> p j c", j=4),
                                                                      func=AF.Identity), reads=[("pb", 6 + hb)], writes=["s5_BD"])
                        for ri in range(2):
                            for (b0, nb_, bank) in ((0, 512, 2 + 2 * ri), (512, 64, 3 + 2 * ri)):
                                ps, pk = self.pb[bank], ("pb", bank)
                                for rp in range(8):
                                    fw.op(fw.pe, lambda t_: t_.matmul(ps[:, 0:nb_], lhsT=WV[q][:, ri, rp, :], rhs=ucols(d, h, rp, b0, nb_), start=(rp == 0), stop=(rp == 7)),
                                          reads=[WVk, "s5_zb"], writes=[pk], inc=(rp == 7))
                                if ri == 0:
                                    fw.op(fw.act, lambda a: a.copy(out=Vs[:, ri, b0:b0 + nb_], in_=ps[:, 0:nb_]), reads=[pk], writes=["s5_Vs"])
                                else:
                                    fw.op(fw.dve, lambda v: v.tensor_copy(out=Vs[:, ri, b0:b0 + nb_], in_=ps[:, 0:nb_]), reads=[pk], writes=["s5_Vs"])
                        fw.op(fw.dve, lambda v: v.tensor_scalar(out=Tt[:, 1, :], in0=idx[:], scalar1=W["phr"][:, col:col + 1], scalar2=None, op0=ALU.mult), reads=["s5_idx", K], writes=["s5_Tt"])
                        fw.op(fw.dve, lambda v: v.tensor_scalar(out=Tt[:, 0, :], in0=Tt[:, 1, :], scalar1=math.pi / 2, scalar2=None, op0=ALU.add), reads=["s5_Tt"], writes=["s5_Tt"])
                        fw.op(fw.dve, lambda v: v.tensor_scalar(out=ti[:], in0=Tt[:], scalar1=1.0 / TWO_PI, scalar2=None, op0=ALU.mult), reads=["s5_Tt"], writes=["s5_ti"])
                        fw.op(fw.dve, lambda v: v.tensor_copy(out=tf[:], in_=ti[:]), reads=["s5_ti"], writes=["s5_tf"])
                        fw.op(fw.dve, lambda v: v.scalar_tensor_tensor(out=Tt[:], in0=tf[:], scalar=-TWO_PI, in1=Tt[:], op0=ALU.mult, op1=ALU.add), reads=["s5_tf", "s5_Tt"], writes=["s5_Tt"])
                        fw.op(fw.dve, lambda v: v.tensor_scalar(out=Tt[:], in0=Tt[:], scalar1=3.141592, scalar2=-3.141592, op0=ALU.min, op1=ALU.max), reads=["s5_Tt"], writes=["s5_Tt"])
                        fw.op(fw.act, lambda a: a.activation(out=Tt[:], in_=Tt[:], func=AF.Sin), reads=["s5_Tt"], writes=["s5_Tt"])
                        hoff = 0
                        for (a0, A1, sid) in segs:
                            fw.op(fw.act, lambda a: a.copy(out=TA[:, :, a0:a0 + A1], in_=Tt[:, :, 1:A1 + 1]), reads=["s5_Tt"], writes=["s5_TA"])
                            fw.op(fw.pool, lambda g: g.tensor_copy(out=TH[:, :, hoff:hoff + A1 + 1], in_=Tt[:, :, 0:A1 + 1]), reads=["s5_Tt"], writes=["s5_TH"])
                            hoff += A1 + 1
                        fw.op(fw.dve, lambda v: v.tensor_tensor(out=w1[:, 0:NA], in0=Vs[:, 0, :], in1=TA[:, 0, :], op=ALU.mult), reads=["s5_Vs", "s5_TA"], writes=["s5_w1"])
                        fw.op(fw.dve, lambda v: v.tensor_tensor(out=w2[:, 0:NA], in0=Vs[:, 1, :], in1=TA[:, 1, :], op=ALU.mult), reads=["s5_Vs", "s5_TA"], writes=["s5_w2"])
                        fw.op(fw.pool, lambda g: g.tensor_tensor(out=Wr[:, 0, :], in0=w1[:, 0:NA], in1=w2[:, 0:NA], op=ALU.add), reads=["s5_w1", "s5_w2"], writes=["s5_Wr"])
                        fw.op(fw.dve, lambda v: v.tensor_tensor(out=w1[:, 0:NA], in0=Vs[:, 1, :], in1=TA[:, 0, :], op=ALU.mult), reads=["s5_Vs", "s5_TA", "s5_w1"], writes=["s5_w1"])
                        fw.op(fw.dve, lambda v: v.tensor_tensor(out=w2[:, 0:NA], in0=Vs[:, 0, :], in1=TA[:, 1, :], op=ALU.mult), reads=["s5_Vs", "s5_TA", "s5_w2"], writes=["s5_w2"])
                        fw.op(fw.pool, lambda g: g.tensor_tensor(out=Wr[:, 1, :], in0=w1[:, 0:NA], in1=w2[:, 0:NA], op=ALU.subtract), reads=["s5_w1", "s5_w2"], writes=["s5_Wr"])
                        hoff = 0
                        for (a0, A1, sid) in segs:
                            for ri in range(2):
                                if sid == 2:
                                    fw.op(fw.act, lambda a: a.copy(out=Gm[:, ri, hoff:hoff + 1], in_=h0[:, ri, col:col + 1]), reads=["s5_h0"], writes=["s5_Gm"])
                                    init = h0[:, ri, col:col + 1]
                                else:
                                    fw.op(fw.pool, lambda g: g.memset(Gm[:, ri, hoff:hoff + 1], 0.0), writes=["s5_Gm"])
                                    init = 0.0
                                fw.op(fw.dve, lambda v: v.tensor_tensor_scan(out=Gm[:, ri, hoff + 1:hoff + 1 + A1], data0=W["rho8"][:, col:col + 1].to_broadcast([128, A1]),
                                                                               data1=Wr[:, ri, a0:a0 + A1], initial=init, op0=ALU.mult, op1=ALU.add),
                                      reads=["s5_Wr", K, "s5_h0"], writes=["s5_Gm"])
                            hoff += A1 + 1
                        NH_ = NA + 3
                        fw.op(fw.dve, lambda v: v.tensor_tensor(out=w1[:], in0=Gm[:, 0, :], in1=TH[:, 0, :], op=ALU.mult), reads=["s5_Gm", "s5_TH", "s5_w1"], writes=["s5_w1"])
                        fw.op(fw.dve, lambda v: v.tensor_tensor(out=w2[:], in0=Gm[:, 1, :], in1=TH[:, 1, :], op=ALU.mult), reads=["s5_Gm", "s5_TH", "s5_w2"], writes=["s5_w2"])
                        fw.op(fw.pool, lambda g: g.tensor_tensor(out=Hf[:, 0, :], in0=w1[:], in1=w2[:], op=ALU.subtract), reads=["s5_w1", "s5_w2"], writes=["s5_Hf"])
                        fw.op(fw.dve, lambda v: v.tensor_tensor(out=w1[:], in0=Gm[:, 0, :], in1=TH[:, 1, :], op=ALU.mult), reads=["s5_Gm", "s5_TH", "s5_w1"], writes=["s5_w1"])
                        fw.op(fw.dve, lambda v: v.tensor_tensor(out=w2[:], in0=Gm[:, 1, :], in1=TH[:, 0, :], op=ALU.mult), reads=["s5_Gm", "s5_TH", "s5_w2"], writes=["s5_w2"])
                        fw.op(fw.pool, lambda g: g.tensor_tensor(out=Hf[:, 1, :], in0=w1[:], in1=w2[:], op=ALU.add), reads=["s5_w1", "s5_w2"], writes=["s5_Hf"])
                        hoff = 0
                        for (a0, A1, sid) in segs:
                            fw.op(fw.act, lambda a: a.activation(out=HM[:, col, :, a0:a0 + A1], in_=Hf[:, :, hoff:hoff + A1], func=AF.Identity), reads=["s5_Hf"], writes=["s5_HM"])
                            if sid < 2:
                                for ri in range(2):
                                    fw.dma(self.osf[l, sid, ri, d, 2 * pr:2 * pr + 2, :].rearrange("e (n o) -> (e n) o", o=1), Hf[:, ri, hoff + A1:hoff + A1 + 1],
                                           reads=["s5_Hf"], is_output=True)
                            hoff += A1 + 1
            fw.barrier()
            with ExitStack() as es3:
                sb3 = lambda n, s, d=F32: es3.enter_context(nc.sbuf_tensor(self.uniq(n), s, d))
                GW = sb3("s5_GW", [128, 2, 512], BF16); gb = sb3("s5_gbS", [128, 4])
                y16 = [sb3(f"s5_y16{i}", [128, 2, 512], BF16) for i in range(2)]
                sg = [sb3(f"s5_sg{i}", [128, 512]) for i in range(2)]
                yo = [sb3(f"s5_yo{i}", [128, 512], BF16) for i in range(2)]
                self.load_w_bf16(GW, self.s5_gw[l], "s5_GW", 1)
                fw.dma(gb[:], self.s5_gb[l], writes=["s5_gb"])
                for blk in range(NT // 512):
                    c0 = blk * 512
                    yb_ = y16[blk % 2]; ybk = ("s5_y16", blk % 2)
                    for h in range(2):
                        ps, pk = self.pb[h], ("pb", h)
                        first = True
                        for d in range(2):
                            a_lo = blk * 64 if d == 0 else NA - 64 * (blk + 1)
                            for r in range(8):
                                if d == 0:
                                    oap = ps[:, r:r + 8 * 63 + 1:8]
                                else:
                                    s0 = 511 - r
                                    oap = ps[:, s0:(7 - r) - 1:-8] if (7 - r) - 1 >= 0 else ps[:, s0::-8]
                                for r2 in range(r + 1):
                                    fw.op(fw.pe, lambda t_: t_.matmul(oap, lhsT=BD[:, d * 2 + h, r - r2, :], rhs=ucols(d, h, r2, a_lo, 64), start=first, stop=False,
                                                                     skip_group_check=True), reads=["s5_BD", "s5_zb"], writes=[pk], inc=False)
                                    first = False
                                for pr in range(4 * h, 4 * h + 4):
                                    col = d * 8 + pr
                                    for ri in range(2):
                                        lastmm = (d == 1 and r == 7 and pr == 4 * h + 3 and ri == 1)
                                        fw.op(fw.pe, lambda t_: t_.matmul(oap, lhsT=CQ[:, col, ri, r, :], rhs=HM[:, col, ri, a_lo:a_lo + 64], start=False, stop=lastmm,
                                                                         skip_group_check=True), reads=["s5_CQ", "s5_HM"], writes=[pk], inc=lastmm)
                        fw.op(fw.dve, lambda v: v.scalar_tensor_tensor(out=yb_[:, h, :], in0=zb[:, h, c0:c0 + 512], scalar=skp[:, h:h + 1], in1=ps[:], op0=ALU.mult, op1=ALU.add),
                              reads=["s5_zb", "s5_skp", pk], writes=[ybk])
                    for mt in range(2):
                        pa, pak = self.pb[2 + mt], ("pb", 2 + mt)
                        pg, pgk = self.pb[4 + mt], ("pb", 4 + mt)
                        for h in range(2):
                            fw.op(fw.pe, lambda t_: t_.matmul(pa[:], lhsT=GW[:, h, mt * 128:(mt + 1) * 128], rhs=yb_[:, h, :], start=(h == 0), stop=(h == 1)),
                                  reads=["s5_GW", ybk], writes=[pak], inc=(h == 1))
                        for h in range(2):
                            fw.op(fw.pe, lambda t_: t_.matmul(pg[:], lhsT=GW[:, h, (2 + mt) * 128:(3 + mt) * 128], rhs=yb_[:, h, :], start=(h == 0), stop=(h == 1)),
                                  reads=["s5_GW", ybk], writes=[pgk], inc=(h == 1))
                        fw.op(fw.act, lambda a: a.activation(out=sg[mt][:], in_=pg[:], func=AF.Sigmoid, bias=gb[:, 2 + mt:3 + mt], scale=1.0), reads=[pgk, "s5_gb"], writes=[("s5_sg", mt)])
                        fw.op(fw.dve, lambda v: v.scalar_tensor_tensor(out=yo[mt][:], in0=pa[:], scalar=gb[:, mt:mt + 1], in1=sg[mt][:], op0=ALU.add, op1=ALU.mult),
                              reads=[pak, "s5_gb", ("s5_sg", mt)], writes=[("s5_yo", mt)])
                        fw.dma(self.ybT[mt * 128:(mt + 1) * 128, c0:c0 + 512], yo[mt][:], reads=[("s5_yo", mt)], writes=[("ybT", mt)])

    def phase_C1(self, l):
        nc, fw = self.nc, self.fw
        with ExitStack() as es:
            sb = lambda n, s, d=F32: es.enter_context(nc.sbuf_tensor(self.uniq(n), s, d))
            PA = sb("PA", [128, 2, D], BF16)
            PB = sb("PB", [128, 2, D], BF16)
            PC = sb("PC", [128, 4, D], BF16)
            WO = sb("WO", [128, 8, D], BF16)
            xt2 = [sb(f"xtC{i}", [128, 8, TT]) for i in range(2)]
            ya2 = [sb(f"yaC{i}", [128, 2, TT], BF16) for i in range(2)]
            yb2 = [sb(f"ybC{i}", [128, 2, TT], BF16) for i in range(2)]
            yc2 = [sb(f"ycC{i}", [128, 4, TT], BF16) for i in range(2)]
            gt2 = [sb(f"gtC{i}", [128, 24, TT], BF16) for i in range(2)]
            mg = sb("mgC", [128, 8, TT], BF16)
            t1 = [sb(f"t1C{i}", [128, TT]) for i in range(2)]
            t2 = [sb(f"t2C{i}", [128, TT]) for i in range(2)]
            t3 = [sb(f"t3C{i}", [128, TT]) for i in range(2)]
            self.load_w_bf16(PA, self.proj_a[l], "PA", 1)
            self.load_w_bf16(PB, self.proj_b[l], "PB", 1)
            self.load_w_bf16(PC, self.proj_c[l], "PC", 1)
            self.load_w_bf16(WO, self.w_out[l], "WO", 1)
            def loadC1(tt):
                bi = tt % 2
                c0 = tt * TT
                for k in range(8):
                    fw.dma(xt2[bi][:, k, :], self.xT[k * 128:(k + 1) * 128, c0:c0 + TT], reads=[("xT", k)], writes=[f"xt{bi}"])
                for k in range(2):
                    fw.dma(ya2[bi][:, k, :], self.yaT[k * 128:(k + 1) * 128, c0:c0 + TT], reads=[("yaT", k)], writes=[f"ya{bi}"])
                    fw.dma(yb2[bi][:, k, :], self.ybT[k * 128:(k + 1) * 128, c0:c0 + TT], reads=[("ybT", k)], writes=[f"yb{bi}"])
                for k in range(4):
                    fw.dma(yc2[bi][:, k, :], self.ycT[k * 128:(k + 1) * 128, c0:c0 + TT], reads=[("ycT", k)], writes=[f"yc{bi}"])
                for m in range(24):
                    fw.dma(gt2[bi][:, m, :], self.gT[m * 128:(m + 1) * 128, c0:c0 + TT], reads=[("gT", m)], writes=[f"gt{bi}"])

            loadC1(0)
            for tt in range(NTILE):
                c0 = tt * TT
                j = 0 if tt == 0 else 1
                if tt + 1 < NTILE:
                    loadC1(tt + 1)
                bi = tt % 2
                xt, ya, yb, yc, gt = xt2[bi], ya2[bi], yb2[bi], yc2[bi], gt2[bi]
                kxt, kya, kyb, kyc, kgt = f"xt{bi}", f"ya{bi}", f"yb{bi}", f"yc{bi}", f"gt{bi}"
                for mf in range(8):
                    par = mf % 2
                    pa, pb_, pc = self.pb[par * 3 + 0], self.pb[par * 3 + 1], self.pb[par * 3 + 2]
                    ka, kb, kc = ("pb", par * 3), ("pb", par * 3 + 1), ("pb", par * 3 + 2)
                    cs = slice(mf * 128, (mf + 1) * 128)
                    for k in range(2):
                        fw.op(fw.pe, lambda t, k=k: t.matmul(pa[:], lhsT=PA[:, k, cs], rhs=ya[:, k, :], start=(k == 0), stop=(k == 1)),
                              reads=["PA", kya], writes=[ka], inc=(k == 1))
                    for k in range(2):
                        fw.op(fw.pe, lambda t, k=k: t.matmul(pb_[:], lhsT=PB[:, k, cs], rhs=yb[:, k, :], start=(k == 0), stop=(k == 1)),
                              reads=["PB", kyb], writes=[kb], inc=(k == 1))
                    for k in range(4):
                        fw.op(fw.pe, lambda t, k=k: t.matmul(pc[:], lhsT=PC[:, k, cs], rhs=yc[:, k, :], start=(k == 0), stop=(k == 3)),
                              reads=["PC", kyc], writes=[kc], inc=(k == 3))
                    a_, b_, c_ = t1[par], t2[par], t3[par]
                    fw.op(fw.dve, lambda v: v.tensor_tensor(out=a_[:], in0=pa[:], in1=gt[:, mf, :], op=ALU.mult), reads=[ka, kgt], writes=[("t1", par)])
                    fw.op(fw.dve, lambda v: v.tensor_tensor(out=b_[:], in0=pb_[:], in1=gt[:, 8 + mf, :], op=ALU.mult), reads=[kb, kgt], writes=[("t2", par)])
                    fw.op(fw.dve, lambda v: v.tensor_tensor(out=c_[:], in0=pc[:], in1=gt[:, 16 + mf, :], op=ALU.mult), reads=[kc, kgt], writes=[("t3", par)])
                    fw.op(fw.pool, lambda g: g.tensor_tensor(out=a_[:], in0=a_[:], in1=b_[:], op=ALU.add), reads=[("t1", par), ("t2", par)], writes=[("t1", par)])
                    fw.op(fw.pool, lambda g: g.tensor_tensor(out=mg[:, mf, :], in0=a_[:], in1=c_[:], op=ALU.add), reads=[("t1", par), ("t3", par)], writes=["mg"])
                for mf in range(8):
                    ps, pk = self.pb[6 + mf % 2], ("pb", 6 + mf % 2)
                    for k in range(8):
                        fw.op(fw.pe, lambda t, k=k: t.matmul(ps[:], lhsT=WO[:, k, mf * 128:(mf + 1) * 128], rhs=mg[:, k, :], start=(k == 0), stop=(k == 7)),
                              reads=["WO", "mg"], writes=[pk], inc=(k == 7))
                    fw.op(fw.dve, lambda v: v.scalar_tensor_tensor(out=xt[:, mf, :], in0=ps[:], scalar=self.modT[:, 16 + mf, j:j + 1], in1=xt[:, mf, :],
                                                                   op0=ALU.mult, op1=ALU.add), reads=[pk, "modT", kxt], writes=[kxt])
                for k in range(8):
                    fw.dma(self.xT[k * 128:(k + 1) * 128, c0:c0 + TT], xt[:, k, :], reads=[kxt], writes=[("xT", k)])

    def phase_C2(self, l):
        nc, fw = self.nc, self.fw
        with ExitStack() as es:
            sb = lambda n, s, d=F32: es.enter_context(nc.sbuf_tensor(self.uniq(n), s, d))
            WG = sb("WG", [128, 8, FF], BF16)
            WU = sb("WU", [128, 8, FF], BF16)
            xtb = [sb(f"xtD{i}", [128, 8, TT]) for i in range(2)]
            xsqb = [sb(f"xsqD{i}", [128, 8, TT]) for i in range(2)]
            rinb = [sb(f"rinD{i}", [128, TT]) for i in range(2)]
            hTb = [sb(f"hTD{i}", [128, 8, TT], BF16) for i in range(2)]

            def prep(tt):
                bi = tt % 2
                for k in range(8):
                    fw.dma(xtb[bi][:, k, :], self.xT[k * 128:(k + 1) * 128, tt * TT:(tt + 1) * TT], reads=[("xT", k)], writes=[f"xt{bi}"])
                self.norm_mod(es, xtb[bi], self.a2, 24, 0 if tt == 0 else 1, hTb[bi], "D", bufs=(xsqb[bi], rinb[bi]), sfx=str(bi))
            sl = [sb(f"slD{i}", [128, TT]) for i in range(2)]
            ao = [sb(f"aoD{i}", [128, TT], BF16) for i in range(3)]
            self.load_w_bf16(WG, self.w_gate[l], "WG", 2)
            self.load_w_bf16(WU, self.w_up[l], "WU", 2)
            prep(0)
            for tt in range(NTILE):
                c0 = tt * TT
                j = 0 if tt == 0 else 1
                if tt + 1 < NTILE:
                    prep(tt + 1)
                hT = hTb[tt % 2]
                hk = f"hT{tt % 2}"
                for jj in range(NJ):
                    par = jj % 3
                    pg, pu = self.pb[par * 2], self.pb[par * 2 + 1]
                    kg, ku = ("pb", par * 2), ("pb", par * 2 + 1)
                    cs = slice(jj * 128, (jj + 1) * 128)
                    for k in range(8):
                        fw.op(fw.pe, lambda t, k=k: t.matmul(pg[:], lhsT=WG[:, k, cs], rhs=hT[:, k, :], start=(k == 0), stop=(k == 7)),
                              reads=["WG", hk], writes=[kg], inc=(k == 7))
                    for k in range(8):
                        fw.op(fw.pe, lambda t, k=k: t.matmul(pu[:], lhsT=WU[:, k, cs], rhs=hT[:, k, :], start=(k == 0), stop=(k == 7)),
                              reads=["WU", hk], writes=[ku], inc=(k == 7))
                    s_ = sl[jj % 2]; sk = ("sl", jj % 2)
                    a_ = ao[jj % 3]; ak = ("ao", jj % 3)
                    fw.op(fw.act, lambda a: a.activation(out=s_[:], in_=pg[:], func=AF.Silu), reads=[kg], writes=[sk])
                    fw.op(fw.dve, lambda v: v.tensor_tensor(out=a_[:], in0=pu[:], in1=s_[:], op=ALU.mult), reads=[ku, sk], writes=[ak])
                    fw.dma(self.aT[jj * 128:(jj + 1) * 128, c0:c0 + TT], a_[:], reads=[ak], writes=[("aT", jj)])

    def phase_C3(self, l):
        nc, fw = self.nc, self.fw
        with ExitStack() as es:
            sb = lambda n, s, d=F32: es.enter_context(nc.sbuf_tensor(self.uniq(n), s, d))
            WD = sb("WD", [128, NJ, D], BF16)
            xt2 = [sb(f"xtE{i}", [128, 8, TT]) for i in range(2)]
            at2 = [sb(f"atE{i}", [128, NJ, TT], BF16) for i in range(2)]
            self.load_w_bf16(WD, self.w_down[l], "WD", 1)

            def loadC3(tt):
                bi = tt % 2
                c0 = tt * TT
                for k in range(8):
                    fw.dma(xt2[bi][:, k, :], self.xT[k * 128:(k + 1) * 128, c0:c0 + TT], reads=[("xT", k)], writes=[f"xt{bi}"])
                for jj in range(NJ):
                    fw.dma(at2[bi][:, jj, :], self.aT[jj * 128:(jj + 1) * 128, c0:c0 + TT], reads=[("aT", jj)], writes=[f"at{bi}"])

            loadC3(0)
            for tt in range(NTILE):
                c0 = tt * TT
                j = 0 if tt == 0 else 1
                if tt + 1 < NTILE:
                    loadC3(tt + 1)
                bi = tt % 2
                xt, at = xt2[bi], at2[bi]
                kxt, kat = f"xt{bi}", f"at{bi}"
                for mf in range(8):
                    ps, pk = self.pb[mf % 4], ("pb", mf % 4)
                    for jj in range(NJ):
                        fw.op(fw.pe, lambda t, jj=jj: t.matmul(ps[:], lhsT=WD[:, jj, mf * 128:(mf + 1) * 128], rhs=at[:, jj, :], start=(jj == 0), stop=(jj == NJ - 1)),
                              reads=["WD", kat], writes=[pk], inc=(jj == NJ - 1))
                    fw.op(fw.dve, lambda v: v.scalar_tensor_tensor(out=xt[:, mf, :], in0=ps[:], scalar=self.modT[:, 40 + mf, j:j + 1], in1=xt[:, mf, :],
                                                                   op0=ALU.mult, op1=ALU.add), reads=[pk, "modT", kxt], writes=[kxt])
                for k in range(8):
                    fw.dma(self.xT[k * 128:(k + 1) * 128, c0:c0 + TT], xt[:, k, :], reads=[kxt], writes=[("xT", k)])

    def phase_final(self):
        nc, fw = self.nc, self.fw
        with ExitStack() as es:
            sb = lambda n, s, d=F32: es.enter_context(nc.sbuf_tensor(self.uniq(n), s, d))
            xt = sb("xtF", [128, 8, TT])
            self._xsq = sb("xsqF", [128, 8, TT])
            self._rin = sb("rinF", [128, TT])
            hT = sb("hTF", [128, 8, TT])
            gf = sb("gfin", [128, 8])
            fw.dma(gf[:], self.final_g[:, :], writes=["gfin"])
            for tt in range(NTILE):
                c0 = tt * TT
                for k in range(8):
                    fw.dma(xt[:, k, :], self.xT[k * 128:(k + 1) * 128, c0:c0 + TT], reads=[("xT", k)], writes=["xt"])
                self.norm_mod(es, xt, gf, None, 0, hT, "F")
                for k in range(8):
                    fw.dma(self.yT[k * 128:(k + 1) * 128, c0:c0 + TT], hT[:, k, :], reads=["hT"], is_output=True)


_SHARED_KEYS_PER_CORE = ("xT_in", "cT", "cache_k", "cache_v", "s5_h0")


def host_inputs(inputs, core, depth=DEPTH, shared=None):
    if shared is not None and shared.get("ready"):
        return _host_core_only(inputs, core, depth, shared)
    m = _host_inputs_full(inputs, core, depth)
    if shared is not None:
        shared.update({k: v for k, v in m.items() if k not in _SHARED_KEYS_PER_CORE})
        shared["ready"] = True
    return m


def _host_core_only(inputs, core, depth, shared):
    f = lambda a: np.ascontiguousarray(np.asarray(a, dtype=np.float32))
    xp = np.asarray(inputs["x_prompt"])[2 * core:2 * core + 2].reshape(NP, D)
    b = core % 2
    xs = np.asarray(inputs["x_sample"])[b]
    xT = np.concatenate([xp, xs], 0).T
    cT = np.stack([np.asarray(inputs["c_ctx"]), np.asarray(inputs["c"])[b]], 1)

    def pair(a):
        return a.reshape(depth, 2, 8, 2, 64).transpose(0, 3, 4, 1, 2).reshape(depth, 128, 16)
    m = {k: v for k, v in shared.items() if k != "ready"}
    m["xT_in"] = f(xT)
    m["cT"] = f(cT.reshape(8, 128, 2).transpose(1, 0, 2))
    m["cache_k"] = f(np.asarray(inputs["cache_k"])[b, :depth].reshape(depth, 512, 128))
    m["cache_v"] = f(np.asarray(inputs["cache_v"])[b, :depth].reshape(depth, 512, 128))
    m["s5_h0"] = f(np.stack([pair(np.asarray(inputs["state_s5_re"])[b, :depth]), pair(np.asarray(inputs["state_s5_im"])[b, :depth])], 2))
    return m


def _host_inputs_full(inputs, core, depth=DEPTH):
    f = lambda a: np.ascontiguousarray(np.asarray(a, dtype=np.float32))
    xp = np.asarray(inputs["x_prompt"])[2 * core:2 * core + 2].reshape(NP, D)
    b = core % 2
    xs = np.asarray(inputs["x_sample"])[b]
    xT = np.concatenate([xp, xs], 0).T
    cT = np.stack([np.asarray(inputs["c_ctx"]), np.asarray(inputs["c"])[b]], 1)
    perm = _perm_cols()
    w_in = np.asarray(inputs["w_in"])[:depth]
    w_ext = np.concatenate([w_in, w_in[:, :, perm]], axis=2)
    cos2, sin2 = _rope_tables()
    def pair(a):
        return a.reshape(depth, 2, 8, 2, 64).transpose(0, 3, 4, 1, 2).reshape(depth, 128, 16)
    lre = pair(np.asarray(inputs["s5_lam_re"])[:depth]); lim = pair(np.asarray(inputs["s5_lam_im"])[:depth])
    lst = pair(np.broadcast_to(np.asarray(inputs["s5_log_step"])[:depth][..., None], (depth, 2, 16, 64)))
    s5_sc = np.stack([lre, lim, lst], 2)
    s5_h0 = np.stack([pair(np.asarray(inputs["state_s5_re"])[b, :depth]), pair(np.asarray(inputs["state_s5_im"])[b, :depth])], 2)
    s5_B = np.zeros((depth, 2, 8, 2, 128, 128), np.float32); s5_C = np.zeros((depth, 2, 8, 2, 128, 128), np.float32)
    bre = np.asarray(inputs["s5_b_re"])[:depth]; bim = np.asarray(inputs["s5_b_im"])[:depth]
    cre = np.asarray(inputs["s5_c_re"])[:depth]; cim = np.asarray(inputs["s5_c_im"])[:depth]
    for g_ in range(16):
        pr_, e_, gl_ = g_ // 2, g_ % 2, g_ % 8
        s5_B[:, :, pr_, 0, e_ * 64:(e_ + 1) * 64, gl_ * 16:(gl_ + 1) * 16] = bre[:, :, g_]
        s5_B[:, :, pr_, 1, e_ * 64:(e_ + 1) * 64, gl_ * 16:(gl_ + 1) * 16] = bim[:, :, g_]
        s5_C[:, :, pr_, 0, e_ * 64:(e_ + 1) * 64, gl_ * 16:(gl_ + 1) * 16] = cre[:, :, g_].transpose(0, 1, 3, 2)
        s5_C[:, :, pr_, 1, e_ * 64:(e_ + 1) * 64, gl_ * 16:(gl_ + 1) * 16] = cim[:, :, g_].transpose(0, 1, 3, 2)
    pk = lambda a: a.reshape(a.shape[0], -1, 128).transpose(0, 2, 1)
    import ml_dtypes
    bf = lambda a: np.ascontiguousarray(np.asarray(a, dtype=np.float32).astype(ml_dtypes.bfloat16))

    def zp(a, r, c):
        o = np.zeros(a.shape[:-2] + (r, c), np.float32)
        o[..., :a.shape[-2], :a.shape[-1]] = a
        return o
    m = {
        "xT_in": f(xT), "cT": f(cT.reshape(8, 128, 2).transpose(1, 0, 2)),
        "ada_w": f(np.asarray(inputs["ada_w"])[:depth]), "ada_b": f(pk(np.asarray(inputs["ada_b"])[:depth])),
        "norm1_g": f(pk(np.asarray(inputs["norm1_g"])[:depth])), "norm2_g": f(pk(np.asarray(inputs["norm2_g"])[:depth])),
        "final_g": f(pk(np.asarray(inputs["final_norm_g"])[None])[0]),
        "w_in": f(w_ext),
        "proj_a": f(np.asarray(inputs["proj_a"])[:depth]), "proj_b": f(np.asarray(inputs["proj_b"])[:depth]),
        "proj_c": f(np.asarray(inputs["proj_c"])[:depth]), "w_out": f(np.asarray(inputs["w_out"])[:depth]),
        "w_gate": f(np.asarray(inputs["ffn_w_gate"])[:depth]), "w_up": f(np.asarray(inputs["ffn_w_up"])[:depth]),
        "w_down": f(np.asarray(inputs["ffn_w_down"])[:depth]),
        "rope_cos": cos2, "rope_sin": sin2,
        "hy_GB": bf(_hy_gb()),
        "hy_T256": bf(_hy_t256()),
        "hy_w1": f(zp(np.asarray(inputs["hy_pos_w1"])[:depth], 128, 128)),
        "hy_w2": f(zp(np.asarray(inputs["hy_pos_w2"])[:depth], 128, 128)),
        "hy_w3": f(zp(np.asarray(inputs["hy_pos_w3"])[:depth], 128, 1024)),
        "hy_vec": f(zp(np.stack([np.asarray(inputs["hy_pos_b1"])[:depth], np.asarray(inputs["hy_pos_b2"])[:depth],
                                 np.asarray(inputs["hy_sin_freq"])[:depth]], -1), 128, 3)),
        "hy_decay": f(np.asarray(inputs["hy_decay"])[:depth]),
        "hy_cw": f(np.concatenate([np.asarray(inputs["hy_conv_w"])[:depth].transpose(0, 2, 1),
                                   np.asarray(inputs["hy_conv_b"])[:depth][:, :, None]], -1).reshape(depth, 6, 128, 4).transpose(0, 2, 1, 3)),
        "hy_skip": f(np.asarray(inputs["hy_skip"])[:depth].reshape(depth, 2, 2, 128).transpose(0, 3, 1, 2)),
        "s5_sc": f(s5_sc), "s5_B": f(s5_B), "s5_C": f(s5_C), "s5_h0": f(s5_h0),
        "s5_skip": f(np.asarray(inputs["s5_skip"])[:depth].reshape(depth, 2, 128).transpose(0, 2, 1)),
        "s5_gw": f(np.asarray(inputs["s5_glu_w"])[:depth]),
        "s5_gb": f(np.asarray(inputs["s5_glu_b"])[:depth].reshape(depth, 4, 128).transpose(0, 2, 1)),
        "s5_idx": f(np.broadcast_to(np.arange(513, dtype=np.float32), (128, 513))),
        "cache_k": f(np.asarray(inputs["cache_k"])[b, :depth].reshape(depth, 512, 128)),
        "cache_v": f(np.asarray(inputs["cache_v"])[b, :depth].reshape(depth, 512, 128)),
        "sink": f(np.asarray(inputs["attn_sink"])[:depth]),
        "mask_prev": f(np.tril(np.ones((128, 128)))),
        "mask_next": f(np.triu(np.ones((128, 128)))),
    }
    for L_ in (256, 4096):
        FA, FAi = _hy_tables(L_)
        fT, negt = _hy_feat(L_)
        m[f"hy_FA{L_}"] = bf(FA); m[f"hy_FAi{L_}"] = bf(FAi)
        m[f"hy_featT{L_}"] = f(fT); m[f"hy_negt{L_}"] = f(negt)
    return m


_PROG = {}


def get_prog(depth=DEPTH, mixers=("attn", "hyena", "s5"), taps=()):
    key = (depth, tuple(mixers), tuple(taps))
    if key not in _PROG:
        _PROG[key] = Prog(depth, mixers, taps)
    return _PROG[key]


def kernel(**inputs):
    prog = get_prog()
    in_maps = []
    shared = {}
    for c in range(8):
        m = host_inputs(inputs, c, DEPTH, shared)
        in_maps.append({k: m[k] for k in prog.din})
    res = run_bass_kernel_spmd(prog.nc, in_maps, core_ids=list(range(8)))
    R = res.results
    B, S = 16, 256
    y_prompt = np.zeros((B, S, D), np.float32)
    y_sample = np.zeros((2, LS, D), np.float32)
    nk_ = np.zeros((B, DEPTH, S, 2, 64), np.float32)
    nv_ = np.zeros((B, DEPTH, S, 2, 64), np.float32)
    sre = np.zeros((B, DEPTH, 2, 16, 64), np.float32)
    sim = np.zeros((B, DEPTH, 2, 16, 64), np.float32)
    for c in range(8):
        yT = R[c]["yT"]
        y_prompt[2 * c:2 * c + 2] = yT[:, :NP].T.reshape(2, S, D)
        if c < 2:
            y_sample[c] = yT[:, NP:].T
        ok = R[c]["ok"].reshape(DEPTH, 2, S, 2, 64)
        ov = R[c]["ov"].reshape(DEPTH, 2, S, 2, 64)
        nk_[2 * c:2 * c + 2] = ok.transpose(1, 0, 2, 3, 4)
        nv_[2 * c:2 * c + 2] = ov.transpose(1, 0, 2, 3, 4)
        osf = R[c]["osf"]
        sre[2 * c:2 * c + 2] = osf[:, :, 0].transpose(1, 0, 2, 3, 4)
        sim[2 * c:2 * c + 2] = osf[:, :, 1].transpose(1, 0, 2, 3, 4)
    return (y_prompt, y_sample, nk_, nv_, sre, sim)
```

```python
from contextlib import ExitStack
import math
import numpy as np
import concourse.bass as bass
import concourse.mybir as mybir
from concourse.bass_utils import run_bass_kernel_spmd

F32 = mybir.dt.float32
BF16 = mybir.dt.bfloat16
I32 = mybir.dt.int32
AF = mybir.ActivationFunctionType
ALU = mybir.AluOpType
AX = mybir.AxisListType

D = 1024
DEPTH = 4
NP = 512
LS = 4096
NT = NP + LS
TT = 512
NTILE = NT // TT
FF = 2816
NJ = FF // 128
WIN = 4864
WEXT = WIN + 640
EPS = 1e-6


class _Eng:
    def __init__(self, name, h, sem):
        self.name, self.h, self.sem = name, h, sem
        self.count = 0
        self.seen = {}


class _Buf:
    __slots__ = ("w", "r")

    def __init__(self):
        self.w = None
        self.r = []


class FW:
    NDMA = 48

    def __init__(self, nc, es):
        self.nc = nc
        mk = lambda n: es.enter_context(nc.semaphore(n))
        self.sync = _Eng("sync", nc.sync, mk("e_sync"))
        self.act = _Eng("act", nc.scalar, mk("e_act"))
        self.dve = _Eng("dve", nc.vector, mk("e_dve"))
        self.pool = _Eng("pool", nc.gpsimd, mk("e_pool"))
        self.pe = _Eng("pe", nc.tensor, mk("e_pe"))
        self.engs = [self.sync, self.act, self.dve, self.pool, self.pe]
        self.dsems = [mk(f"d{i}") for i in range(self.NDMA)]
        self.dcount = [0] * self.NDMA
        self.dnext = 0
        self.NSW = 16
        self.dnext_sw = 0
        self.bufs = {}
        self.out_events = []
        self.n_inst = 0

    def _rec(self, key):
        b = self.bufs.get(key)
        if b is None:
            b = self.bufs[key] = _Buf()
        return b

    def _deps(self, reads, writes):
        deps = []
        for k in reads:
            b = self._rec(k)
            if b.w is not None:
                deps.append(b.w)
        for k in writes:
            b = self._rec(k)
            if b.w is not None:
                deps.append(b.w)
            deps.extend(b.r)
        return deps

    def _wait(self, eng, deps):
        need = {}
        for (sem, val) in deps:
            i = id(sem)
            if need.get(i, (None, 0))[1] < val:
                need[i] = (sem, val)
        for i, (sem, val) in need.items():
            if eng.seen.get(i, 0) >= val:
                continue
            eng.h.wait_ge(sem, val)
            eng.seen[i] = val

    def _commit(self, ev, reads, writes):
        for k in reads:
            b = self._rec(k)
            for i_, (s_, v_) in enumerate(b.r):
                if s_ is ev[0]:
                    if v_ < ev[1]:
                        b.r[i_] = ev
                    break
            else:
                b.r.append(ev)
        for k in writes:
            b = self._rec(k)
            b.w = ev
            b.r = []

    def op(self, eng, fn, reads=(), writes=(), inc=True):
        pr = [k for k in reads if isinstance(k, tuple) and k[0] == "pb"]
        if pr:
            reads = [k for k in reads if k not in pr]
            writes = list(writes) + pr
        deps = self._deps(reads, writes)
        if eng is self.pe:
            deps = [d for d in deps if d[0] is not eng.sem]
        self._wait(eng, deps)
        ins = fn(eng.h)
        if inc:
            eng.count += 1
            ins.then_inc(eng.sem, 1)
            ev = (eng.sem, eng.count)
        else:
            ev = (eng.sem, eng.count + 1)
        self._commit(ev, reads, writes)
        self.n_inst += 1
        return ins

    def dma(self, out, in_, reads=(), writes=(), q=None, is_output=False, **kw):
        q = q or self.sync
        if q is self.pool:
            i = self.NDMA - self.NSW + self.dnext_sw
            self.dnext_sw = (self.dnext_sw + 1) % self.NSW
        else:
            i = self.dnext
            self.dnext = (self.dnext + 1) % (self.NDMA - self.NSW)
        sem = self.dsems[i]
        deps = self._deps(reads, writes)
        if self.dcount[i] > 0:
            deps.append((sem, self.dcount[i]))
        self._wait(q, deps)
        ins = q.h.dma_start(out=out, in_=in_, **kw)
        self.dcount[i] += 16
        ins.then_inc(sem, 16)
        ev = (sem, self.dcount[i])
        self._commit(ev, reads, writes)
        if is_output:
            self.out_events.append(ev)
        self.n_inst += 1
        return ins

    def barrier(self):
        evs = [(e.sem, e.count) for e in self.engs if e.count > 0]
        evs += [(self.dsems[i], self.dcount[i]) for i in range(self.NDMA) if self.dcount[i] > 0]
        for e in self.engs:
            self._wait(e, evs)
        self.bufs = {}

    def finish(self):
        evs = list(self.out_events)
        evs += [(self.dsems[i], self.dcount[i]) for i in range(self.NDMA) if self.dcount[i] > 0]
        evs += [(e.sem, e.count) for e in self.engs if e.count > 0 and e is not self.sync]
        self._wait(self.sync, evs)


def _perm_cols():
    def partner(d):
        return d + 16 if (d % 32) < 16 else d - 16
    SPLIT_Q, SPLIT_K = 1024, 1536
    pq = [SPLIT_Q + h * 64 + partner(d) for h in range(8) for d in range(64)]
    pk = [SPLIT_K + g * 64 + partner(d) for g in range(2) for d in range(64)]
    return np.array(pq + pk, dtype=np.int64)


def _rope_tables():
    t = np.arange(LS)
    row = (t // 64).astype(np.float64)
    col = (t % 64).astype(np.float64)
    inv = 10000.0 ** (-np.arange(16, dtype=np.float64) / 16)
    cos = np.zeros((64, LS)); sin = np.zeros((64, LS))
    for d in range(64):
        pos = row if d < 32 else col
        ang = pos * inv[d % 16]
        cos[d] = np.cos(ang)
        sin[d] = np.sin(ang) * (-1.0 if (d % 32) < 16 else 1.0)
    cos2 = np.concatenate([cos, cos], 0).astype(np.float32)
    sin2 = np.concatenate([sin, sin], 0).astype(np.float32)
    return cos2, sin2


def _hy_tables(L):
    NH, KA, N2 = L // 128, L // 64, 2 * L
    n_hi = np.arange(128)[:, None, None]; n_lo = np.arange(128)[None, :, None]
    FA = np.zeros((128, 128, 128))
    ka = np.arange(KA)[None, None, :]
    th = 2 * np.pi * (n_lo + 128 * n_hi) * (ka + 0.5) / N2
    valid = (n_hi < NH)
    FA[:, :, :KA] = np.cos(th) * valid
    FA[:, :, 64:64 + KA] = -np.sin(th) * valid
    FAi = np.zeros((128, 128, NH))
    ka2 = np.arange(KA)[:, None, None]; nl = np.arange(128)[None, :, None]; nh = np.arange(NH)[None, None, :]
    th2 = 2 * np.pi * (nl + 128 * nh) * (ka2 + 0.5) / N2
    FAi[:KA] = np.cos(th2) / L
    FAi[64:64 + KA] = -np.sin(th2) / L
    return FA, FAi


def _hy_gb():
    nb = np.arange(128)[:, None]; kb = np.arange(64)[None, :]
    C = np.cos(2 * np.pi * nb * kb / 128); S = np.sin(2 * np.pi * nb * kb / 128)
    cat = lambda a, b: np.concatenate([a, b], 1)
    lst = [cat(C, -S), cat(S, C), cat(-S, C), cat(C, S),
           cat(C, C), cat(S, S),
           cat(S, -S), cat(-C, C), cat(-S, S), cat(C, -C),
           np.concatenate([C.T, -S.T], 0), np.concatenate([S.T, C.T], 0)]
    return np.stack(lst, 1)


def _hy_t256():
    L = 256
    n = (np.arange(128)[:, None, None, None] + 128 * np.arange(2)[None, :, None, None])
    k = np.arange(4)[None, None, :, None] + 4 * np.arange(64)[None, None, None, :]
    th = 2 * np.pi * n * (k + 0.5) / 512.0
    C, S = np.cos(th), np.sin(th)
    cat = lambda a, b: np.concatenate([a, b], -1).reshape(128, 1024)
    T = np.stack([cat(C, -S), cat(-S, C), cat(C, C), cat(S, -S), cat(-S, S)], 1)
    kk = np.arange(4)[None, :, None] + 4 * np.arange(64)[:, None, None]
    nn = np.arange(256)[None, None, :]
    th2 = 2 * np.pi * nn * (kk + 0.5) / 512.0
    Ti = np.concatenate([np.cos(th2) / L, -np.sin(th2) / L], 0).reshape(128, 1024)
    return np.concatenate([T, Ti[:, None, :]], 1)


def _hy_feat(L):
    t = np.linspace(0.0, 1.0, L)
    w = 2.0 * np.pi * np.arange(L) / L
    bands = np.linspace(1e-4, 15, 16)
    ang = w[:, None] * bands[None, :]
    feat = np.concatenate([t[:, None], np.cos(ang), -np.sin(ang)], -1)
    fT = np.zeros((128, L)); fT[:33] = feat.T
    negt = -t.reshape(L // 128, 128).T
    return fT, negt


class Prog:
    def __init__(self, depth=DEPTH, mixers=("attn", "hyena", "s5"), taps=(), stop=None):
        self.stop = stop
        self.depth = depth
        self.mixers = mixers
        self.taps = taps
        self.nc = bass.Bass("TRN2", target_bir_lowering=False)
        self.din = {}
        self.dout = {}
        self.build()

    def uniq(self, n):
        self._uid = getattr(self, "_uid", 0) + 1
        return f"{n}_{self._uid}"

    def inp(self, name, shape, dt=F32):
        t = self.nc.dram_tensor(name, list(shape), dt, kind="ExternalInput").ap()
        self.din[name] = (tuple(shape), dt)
        return t

    def outp(self, name, shape, dt=F32):
        t = self.nc.dram_tensor(name, list(shape), dt, kind="ExternalOutput").ap()
        self.dout[name] = (tuple(shape), dt)
        return t

    def scratch(self, name, shape, dt=F32):
        return self.nc.dram_tensor(name, list(shape), dt, kind="Internal").ap()

    def build(self):
        nc = self.nc
        L = self.depth
        self.xT_in = self.inp("xT_in", [D, NT])
        self.cT = self.inp("cT", [128, 8, 2])
        self.ada_w = self.inp("ada_w", [L, D, 6 * D])
        self.ada_b = self.inp("ada_b", [L, 128, 48])
        self.norm1_g = self.inp("norm1_g", [L, 128, 8])
        self.norm2_g = self.inp("norm2_g", [L, 128, 8])
        self.final_g = self.inp("final_g", [128, 8])
        self.w_in = self.inp("w_in", [L, D, WEXT])
        self.proj_a = self.inp("proj_a", [L, 256, D])
        self.proj_b = self.inp("proj_b", [L, 256, D])
        self.proj_c = self.inp("proj_c", [L, 512, D])
        self.w_out = self.inp("w_out", [L, D, D])
        self.w_gate = self.inp("w_gate", [L, D, FF])
        self.w_up = self.inp("w_up", [L, D, FF])
        self.w_down = self.inp("w_down", [L, FF, D])
        self.rope_cos = self.inp("rope_cos", [128, LS])
        self.rope_sin = self.inp("rope_sin", [128, LS])
        self.hy_FA = {256: self.inp("hy_FA256", [128, 128, 128], BF16), 4096: self.inp("hy_FA4096", [128, 128, 128], BF16)}
        self.hy_FAi = {256: self.inp("hy_FAi256", [128, 128, 2], BF16), 4096: self.inp("hy_FAi4096", [128, 128, 32], BF16)}
        self.hy_GB = self.inp("hy_GB", [128, 12, 128], BF16)
        self.hy_T256 = self.inp("hy_T256", [128, 6, 1024], BF16)
        self.hy_featT = {256: self.inp("hy_featT256", [128, 256]), 4096: self.inp("hy_featT4096", [128, 4096])}
        self.hy_negt = {256: self.inp("hy_negt256", [128, 2]), 4096: self.inp("hy_negt4096", [128, 32])}
        self.hy_w1 = self.inp("hy_w1", [L, 128, 128])
        self.hy_w2 = self.inp("hy_w2", [L, 128, 128])
        self.hy_w3 = self.inp("hy_w3", [L, 128, 1024])
        self.hy_vec = self.inp("hy_vec", [L, 128, 3])
        self.hy_decay = self.inp("hy_decay", [L, 1024])
        self.hy_cw = self.inp("hy_cw", [L, 128, 6, 4])
        self.hy_skip = self.inp("hy_skip", [L, 128, 2, 2])
        self.s5_sc = self.inp("s5_sc", [L, 128, 3, 16])
        self.s5_B = self.inp("s5_B", [L, 2, 8, 2, 128, 128])
        self.s5_C = self.inp("s5_C", [L, 2, 8, 2, 128, 128])
        self.s5_h0 = self.inp("s5_h0", [L, 128, 2, 16])
        self.s5_skip = self.inp("s5_skip", [L, 128, 2])
        self.s5_gw = self.inp("s5_gw", [L, 256, 512])
        self.s5_gb = self.inp("s5_gb", [L, 128, 4])
        self.s5_idx = self.inp("s5_idx", [128, 513])
        self.cache_k = self.inp("cache_k", [L, 512, 128])
        self.cache_v = self.inp("cache_v", [L, 512, 128])
        self.sink = self.inp("sink", [L, 8])
        self.mask_prev = self.inp("mask_prev", [128, 128])
        self.mask_next = self.inp("mask_next", [128, 128])
        self.yT = self.outp("yT", [D, NT])
        self.ok = self.outp("ok", [L, NP, 128])
        self.ov = self.outp("ov", [L, NP, 128])
        self.osf = self.outp("osf", [L, 2, 2, 2, 16, 64])
        self.xT = self.scratch("xT", [D, NT])
        self.zaT = self.scratch("zaT", [768, NT])
        self.zbT = self.scratch("zbT", [256, NT], BF16)
        self.gT = self.scratch("gT", [3072, NT], BF16)
        self.qA = self.scratch("qA", [8, 65, NT], BF16)
        self.kA = self.scratch("kA", [2, 65, NT], BF16)
        self.vtok = self.scratch("vtok", [NT, 128], BF16)
        self.nk = self.scratch("nk", [NT, 2])
        self.yaT = self.scratch("yaT", [256, NT], BF16)
        self.ybT = self.scratch("ybT", [256, NT], BF16)
        self.ycT = self.scratch("ycT", [512, NT], BF16)
        self.aT = self.scratch("aT", [FF, NT], BF16)
        self.ugT = self.scratch("ugT", [512, NT])
        self.hzT = self.scratch("hzT", [256, NT])
        self.hyT = self.scratch("hyT", [256, NT])
        self.ztok = self.scratch("ztok", [NT, 256], BF16)
        self.hraw = {256: self.scratch("hraw256", [256, 1024]), 4096: self.scratch("hraw4096", [4096, 1024])}
        self.ftok = {256: self.scratch("ftok256", [256, 1024], BF16), 4096: self.scratch("ftok4096", [4096, 1024], BF16)}
        self.Ascr = self.scratch("Ascr", [128, 128, 512], BF16)
        self.Bscr = self.scratch("Bscr", [128, 2, 64, 256], BF16)
        self.Ksp = {256: self.scratch("Ksp256", [2, 2, 128, 4, 256]), 4096: self.scratch("Ksp4096", [2, 2, 128, 64, 256])}
        self.tapo = {}
        for name, shape in self.taps:
            self.tapo[name] = self.outp("tap_" + name, shape)

        with ExitStack() as es:
            self.fw = fw = FW(nc, es)
            self.pb = [es.enter_context(nc.psum_tensor(f"pb{i}", [128, 512], F32)) for i in range(8)]
            sb = lambda n, s, d=F32: es.enter_context(nc.sbuf_tensor(self.uniq(n), s, d))
            self.ones32 = sb("ones32", [128, 128])
            self.ident = sb("ident", [128, 128])
            self.blk2 = sb("blk2", [128, 128])
            self.sc = sb("silu_c", [128, 8, 2])
            self.modT = sb("modT", [128, 48, 2])
            self.a1 = sb("a1", [128, 8, 2])
            self.a2 = sb("a2", [128, 8, 2])
            self.g1 = sb("g1", [128, 8])
            self.g2 = sb("g2", [128, 8])
            self.adab = sb("adab", [128, 48])
            self.epsc = sb("epsc", [128, 1])
            self.init_consts()
            for k in range(8):
                fw.dma(self.xT[k * 128:(k + 1) * 128, :], self.xT_in[k * 128:(k + 1) * 128, :], writes=[("xT", k)])
            fw.barrier()
            phases = [("mod", self.phase_mod), ("A", self.phase_A), ("B", self.phase_B), ("C1", self.phase_C1),
                      ("C2", self.phase_C2), ("C3", self.phase_C3)]
            done = False
            for l in range(L):
                for nm, ph in phases:
                    ph(l)
                    fw.barrier()
                    if self.stop == nm:
                        done = True
                        break
                if done:
                    break
            self.phase_final()
            fw.finish()

    def init_consts(self):
        nc, fw = self.nc, self.fw
        fw.op(fw.pool, lambda g: g.memset(self.ones32[:], 1.0), writes=["ones32"])
        fw.op(fw.pool, lambda g: g.memset(self.ident[:], 0.0), writes=["ident"])
        fw.op(fw.pool, lambda g: g.affine_select(out=self.ident[:], in_=self.ident[:], pattern=[[-1, 128]],
                                                 compare_op=ALU.not_equal, fill=1.0, base=0, channel_multiplier=1),
              reads=["ident"], writes=["ident"])
        fw.op(fw.pool, lambda g: g.memset(self.blk2[:], 0.0), writes=["blk2"])
        fw.op(fw.pool, lambda g: g.memset(self.blk2[0:64, 0:1], 1.0), reads=["blk2"], writes=["blk2"])
        fw.op(fw.pool, lambda g: g.memset(self.blk2[64:128, 1:2], 1.0), reads=["blk2"], writes=["blk2"])
        fw.op(fw.pool, lambda g: g.memset(self.epsc[:], EPS), writes=["epsc"])
        fw.dma(self.sc[:], self.cT[:, :, :], writes=["sc"])
        fw.op(fw.act, lambda a: a.activation(out=self.sc[:], in_=self.sc[:], func=AF.Silu), reads=["sc"], writes=["sc"])
        es2 = ExitStack()
        with es2:
            onesb = es2.enter_context(nc.sbuf_tensor(self.uniq("onesb"), [1, NT], BF16))
            fw.op(fw.pool, lambda g: g.memset(onesb[:], 1.0), writes=["onesb"])
            for g_ in range(2):
                fw.dma(self.kA[g_, 64:65, :], onesb[:], reads=["onesb"], writes=[("kA", g_)])
            fw.barrier()

    def phase_mod(self, l):
        nc, fw = self.nc, self.fw
        with ExitStack() as es:
            wblk = [es.enter_context(nc.sbuf_tensor(self.uniq(f"adaw{i}"), [128, 8, 1024], F32)) for i in range(2)]
            fw.dma(self.adab[:], self.ada_b[l], writes=["adab"])
            fw.dma(self.g1[:], self.norm1_g[l], writes=["g1"])
            fw.dma(self.g2[:], self.norm2_g[l], writes=["g2"])
            for blk in range(6):
                wb = wblk[blk % 2]
                key = ("adaw", blk % 2)
                for k in range(8):
                    fw.dma(wb[:, k, :], self.ada_w[l, k * 128:(k + 1) * 128, blk * 1024:(blk + 1) * 1024], writes=[key])
                for mm in range(8):
                    m = blk * 8 + mm
                    ps = self.pb[m % 4]
                    pk = ("pb", m % 4)
                    for k in range(8):
                        fw.op(fw.pe, lambda t, k=k, mm=mm, ps=ps, wb=wb: t.matmul(ps[:, 0:2], lhsT=wb[:, k, mm * 128:(mm + 1) * 128], rhs=self.sc[:, k, :],
                                                                                 start=(k == 0), stop=(k == 7)),
                              reads=[key, "sc"], writes=[pk], inc=(k == 7))
                    fw.op(fw.act, lambda a, m=m, ps=ps: a.activation(out=self.modT[:, m, :], in_=ps[:, 0:2], func=AF.Identity,
                                                                     bias=self.adab[:, m:m + 1], scale=1.0),
                          reads=[pk, "adab"], writes=["modT"])
            for (a_t, g_t, base, nm) in ((self.a1, self.g1, 8, "a1"), (self.a2, self.g2, 32, "a2")):
                for j in range(2):
                    fw.op(fw.dve, lambda v, a_t=a_t, g_t=g_t, base=base, j=j: v.scalar_tensor_tensor(
                        out=a_t[:, :, j], in0=self.modT[:, base:base + 8, j], scalar=1.0, in1=g_t[:, :], op0=ALU.add, op1=ALU.mult),
                        reads=["modT", "g1", "g2"], writes=[nm])

    def norm_mod(self, es, xt, a_t, shift_base, j, hT, tagsfx, bufs=None, sfx=""):
        nc, fw = self.nc, self.fw
        xsq, rin = bufs if bufs is not None else (self._xsq, self._rin)
        kxt, kxs, kri, khT = "xt" + sfx, "xsq" + sfx, "rin" + sfx, "hT" + sfx
        ss = self.pb[7]
        fw.op(fw.act, lambda a: a.activation(out=xsq[:], in_=xt[:], func=AF.Square), reads=[kxt], writes=[kxs])
        for k in range(8):
            fw.op(fw.pe, lambda t, k=k: t.matmul(ss[:], lhsT=self.ones32[:], rhs=xsq[:, k, :], start=(k == 0), stop=(k == 7)),
                  reads=[kxs, "ones32"], writes=[("pb", 7)], inc=(k == 7))
        fw.op(fw.act, lambda a: a.activation(out=rin[:], in_=ss[:], func=AF.Sqrt, scale=1.0 / D, bias=self.epsc[:, 0:1]),
              reads=[("pb", 7), "epsc"], writes=[kri])
        fw.op(fw.dve, lambda v: v.reciprocal(out=rin[:], in_=rin[:]), reads=[kri], writes=[kri])
        fw.op(fw.dve, lambda v: v.tensor_tensor(out=xsq[:], in0=xt[:], in1=rin[:].unsqueeze(1).to_broadcast([128, 8, TT]), op=ALU.mult),
              reads=[kxt, kri, kxs], writes=[kxs])
        for k in range(8):
            if shift_base is None:
                fw.op(fw.act, lambda a, k=k: a.activation(out=hT[:, k, :], in_=xsq[:, k, :], func=AF.Identity, scale=a_t[:, k:k + 1]),
                      reads=[kxs, "gfin"], writes=[khT])
            else:
                fw.op(fw.act, lambda a, k=k: a.activation(out=hT[:, k, :], in_=xsq[:, k, :], func=AF.Identity,
                                                          scale=a_t[:, k, j:j + 1], bias=self.modT[:, shift_base + k, j:j + 1]),
                      reads=[kxs, "a1", "a2", "modT"], writes=[khT])

    def load_w_bf16(self, dst, src, key, nchunk):
        fw = self.fw
        K = dst.shape[1]
        N = dst.shape[2]
        cw = N // nchunk
        assert cw <= 2048 and cw * nchunk == N
        for k in range(K):
            for c in range(nchunk):
                fw.dma(dst[:, k, c * cw:(c + 1) * cw], src[k * 128:(k + 1) * 128, c * cw:(c + 1) * cw], writes=[key], q=fw.pool)

    def phase_A(self, l):
        nc, fw = self.nc, self.fw
        with ExitStack() as es:
            sb = lambda n, s, d=F32: es.enter_context(nc.sbuf_tensor(self.uniq(n), s, d))
            WinB = sb("WinB", [128, 8, WEXT], BF16)
            xtb = [sb(f"xtA{i}", [128, 8, TT]) for i in range(2)]
            xsqb = [sb(f"xsqA{i}", [128, 8, TT]) for i in range(2)]
            rinb = [sb(f"rinA{i}", [128, TT]) for i in range(2)]
            hTb = [sb(f"hTA{i}", [128, 8, TT], BF16) for i in range(2)]
            cosT = sb("cosT", [128, TT])
            sinT = sb("sinT", [128, TT])
            NST = 4
            st32 = [sb(f"st32_{i}", [128, TT]) for i in range(NST)]
            st16 = [sb(f"st16_{i}", [128, TT], BF16) for i in range(NST)]
            tmp32 = [sb(f"tmp32_{i}", [128, TT]) for i in range(2)]
            nrm = [sb(f"nrm_{i}", [2, TT], BF16) for i in range(2)]
            kv32 = [sb(f"kv32_{i}", [128, 256]) for i in range(2)]
            v16 = [sb(f"v16_{i}", [128, 128], BF16) for i in range(2)]
            nkt = [sb(f"nkt_{i}", [128, 2]) for i in range(2)]
            junk = sb("junkA", [128, 64])
            self.load_w_bf16(WinB, self.w_in[l], "WinB", 4)
            cnt = {"s32": 0, "s16": 0, "ps": 0, "t32": 0, "nrm": 0, "kv": 0}

            def next_ps():
                i = cnt["ps"] % 6
                cnt["ps"] += 1
                return self.pb[i], ("pb", i)

            cur = {"hT": None, "hk": None}

            def mm_tile(m, ps, pk):
                hT, hk = cur["hT"], cur["hk"]
                for k in range(8):
                    fw.op(fw.pe, lambda t, k=k: t.matmul(ps[:], lhsT=WinB[:, k, m * 128:(m + 1) * 128], rhs=hT[:, k, :], start=(k == 0), stop=(k == 7)),
                          reads=["WinB", hk], writes=[pk], inc=(k == 7))

            def prep(tt):
                bi = tt % 2
                for k in range(8):
                    fw.dma(xtb[bi][:, k, :], self.xT[k * 128:(k + 1) * 128, tt * TT:(tt + 1) * TT], reads=[("xT", k)], writes=[f"xt{bi}"])
                self.norm_mod(es, xtb[bi], self.a1, 0, 0 if tt == 0 else 1, hTb[bi], "A", bufs=(xsqb[bi], rinb[bi]), sfx=str(bi))

            import os
            prep(0)
            for tt in range(int(os.environ.get('DBGT', NTILE))):
                c0 = tt * TT
                j = 0 if tt == 0 else 1
                if tt + 1 < NTILE:
                    prep(tt + 1)
                hT = hTb[tt % 2]
                cur["hT"], cur["hk"] = hT, f"hT{tt % 2}"
                if j == 1:
                    p0 = c0 - NP
                    fw.dma(cosT[:], self.rope_cos[:, p0:p0 + TT], writes=["cosT"])
                    fw.dma(sinT[:], self.rope_sin[:, p0:p0 + TT], writes=["sinT"])
                evac_flip = 0
                import os
                lvl = int(os.environ.get('DBGA', '9'))
                mlist = list(range(0, 13)) + list(range(14, 38))
                if lvl == 0: mlist = list(range(0, 6))
                if lvl == 1: mlist = list(range(0, 8))
                if lvl == 2: mlist = list(range(0, 13))
                for m in mlist:
                    ps, pk = next_ps()
                    mm_tile(m, ps, pk)
                    eng = fw.act if (evac_flip % 2 == 0) else fw.dve
                    evac_flip += 1
                    if m < 6:
                        i = cnt["s32"] % NST; cnt["s32"] += 1
                        s = st32[i]; sk = ("st32", i)
                        if eng is fw.act:
                            fw.op(eng, lambda a, s=s, ps=ps: a.copy(out=s[:], in_=ps[:]), reads=[pk], writes=[sk])
                        else:
                            fw.op(eng, lambda v, s=s, ps=ps: v.tensor_copy(out=s[:], in_=ps[:]), reads=[pk], writes=[sk])
                        fw.dma(self.zaT[m * 128:(m + 1) * 128, c0:c0 + TT], s[:], reads=[sk], writes=[("zaT", m)])
                    elif m < 8:
                        i = cnt["s16"] % NST; cnt["s16"] += 1
                        s = st16[i]; sk = ("st16", i)
                        fw.op(fw.act, lambda a, s=s, ps=ps: a.copy(out=s[:], in_=ps[:]), reads=[pk], writes=[sk])
                        fw.dma(self.zbT[(m - 6) * 128:(m - 5) * 128, c0:c0 + TT], s[:], reads=[sk], writes=[("zbT", m - 6)])
                    elif m < 13:
                        i = cnt["s16"] % NST; cnt["s16"] += 1
                        s = st16[i]; sk = ("st16", i)
                        ti = cnt["t32"] % 2; cnt["t32"] += 1
                        t32 = tmp32[ti]; tk = ("tmp32", ti)
                        if m < 12 and os.environ.get('DBGQ') != '1':
                            fw.op(fw.act, lambda a, t32=t32, ps=ps: a.activation(out=t32[:], in_=ps[:], func=AF.Square), reads=[pk], writes=[tk])
                            pn, pnk = self.pb[6], ("pb", 6)
                            fw.op(fw.pe, lambda t, t32=t32: t.matmul(pn[:, :], lhsT=self.blk2[:], rhs=t32[:], start=True, stop=True),
                                  reads=[tk, "blk2"], writes=[pnk])
                            ni = cnt["nrm"] % 2; cnt["nrm"] += 1
                            nr = nrm[ni]; nrk = ("nrm", ni)
                            if os.environ.get('DBGQ') != '2':
                                fw.op(fw.act, lambda a, nr=nr: a.mul(out=nr[:], in_=pn[0:2, :], mul=-0.5), reads=[pnk], writes=[nrk])
                            for hh in range(2 if os.environ.get('DBGQ') not in ('2', '3') else 0):
                                h = 2 * (m - 8) + hh
                                fw.dma(self.qA[h, 64:65, c0:c0 + TT], nr[hh:hh + 1, :], reads=[nrk], writes=[("qA", h)])
                        if j == 0:
                            fw.op(fw.dve, lambda v, s=s, ps=ps: v.tensor_copy(out=s[:], in_=ps[:]), reads=[pk], writes=[sk])
                        else:
                            mp = 38 + (m - 8)
                            ps2, pk2 = next_ps()
                            mm_tile(mp, ps2, pk2)
                            fw.op(fw.dve, lambda v, t32=t32, ps=ps: v.tensor_tensor(out=t32[:], in0=ps[:], in1=cosT[:], op=ALU.mult),
                                  reads=[pk, "cosT", tk], writes=[tk])
                            i2 = cnt["s32"] % NST; cnt["s32"] += 1
                            s2 = st32[i2]; s2k = ("st32", i2)
                            fw.op(fw.dve, lambda v, s2=s2, ps2=ps2: v.tensor_tensor(out=s2[:], in0=ps2[:], in1=sinT[:], op=ALU.mult),
                                  reads=[pk2, "sinT"], writes=[s2k])
                            fw.op(fw.pool, lambda g, s=s, t32=t32, s2=s2: g.tensor_tensor(out=s[:], in0=t32[:], in1=s2[:], op=ALU.add),
                                  reads=[tk, s2k], writes=[sk])
                        for hh in range(2):
                            if m < 12:
                                h = 2 * (m - 8) + hh
                                fw.dma(self.qA[h, 0:64, c0:c0 + TT], s[hh * 64:(hh + 1) * 64, :], reads=[sk], writes=[("qA", h)])
                            else:
                                fw.dma(self.kA[hh, 0:64, c0:c0 + TT], s[hh * 64:(hh + 1) * 64, :], reads=[sk], writes=[("kA", hh)])
                    else:
                        i = cnt["s16"] % NST; cnt["s16"] += 1
                        s = st16[i]; sk = ("st16", i)
                        fw.op(fw.act, lambda a, s=s, ps=ps: a.activation(out=s[:], in_=ps[:], func=AF.Sigmoid), reads=[pk], writes=[sk])
                        fw.dma(self.gT[(m - 14) * 128:(m - 13) * 128, c0:c0 + TT], s[:], reads=[sk], writes=[("gT", m - 14)])
                for ts in range(4 if lvl >= 4 else 0):
                    ps, pk = next_ps()
                    for k in range(8):
                        fw.op(fw.pe, lambda t, k=k, ts=ts, ps=ps: t.matmul(ps[:, 0:256], lhsT=hT[:, k, ts * 128:(ts + 1) * 128], rhs=WinB[:, k, 1536:1792],
                                                                         start=(k == 0), stop=(k == 7)),
                              reads=["WinB", cur["hk"]], writes=[pk], inc=(k == 7))
                    i = cnt["kv"] % 2; cnt["kv"] += 1
                    r0 = c0 + ts * 128
                    fw.op(fw.dve, lambda v, i=i, ps=ps: v.tensor_copy(out=kv32[i][:], in_=ps[:, 0:256]), reads=[pk], writes=[("kv32", i)])
                    fw.op(fw.act, lambda a, i=i, ps=ps: a.copy(out=v16[i][:], in_=ps[:, 128:256]), reads=[pk], writes=[("v16", i)])
                    for g_ in range(2):
                        fw.op(fw.act, lambda a, i=i, g_=g_: a.activation(out=junk[:], in_=kv32[i][:, g_ * 64:(g_ + 1) * 64], func=AF.Square,
                                                                          accum_out=nkt[i][:, g_:g_ + 1]),
                              reads=[("kv32", i)], writes=["junkA", ("nkt", i)])
                    fw.op(fw.dve, lambda v, i=i: v.tensor_scalar(out=nkt[i][:], in0=nkt[i][:], scalar1=1.0 / 16.0, scalar2=None, op0=ALU.mult),
                          reads=[("nkt", i)], writes=[("nkt", i)])
                    fw.dma(self.nk[r0:r0 + 128, :], nkt[i][:], reads=[("nkt", i)], writes=["nk"])
                    fw.dma(self.vtok[r0:r0 + 128, :], v16[i][:], reads=[("v16", i)], writes=["vtok"])
                    if tt == 0:
                        fw.dma(self.ok[l, r0:r0 + 128, :], kv32[i][:, 0:128], reads=[("kv32", i)], is_output=True)
                        fw.dma(self.ov[l, r0:r0 + 128, :], kv32[i][:, 128:256], reads=[("kv32", i)], is_output=True)

    def phase_B(self, l):
        nc, fw = self.nc, self.fw
        with ExitStack() as es:
            sb = lambda n, s, d=F32: es.enter_context(nc.sbuf_tensor(self.uniq(n), s, d))
            t32 = sb("bp32", [128, NT])
            t16 = sb("bp16", [128, NT], BF16)
            z16 = sb("bpz", [128, NT], BF16)
            if "hyena" not in self.mixers:
                for m in range(2):
                    fw.dma(t32[:], self.zaT[m * 128:(m + 1) * 128, :], reads=[("zaT", m)], writes=["bp32"])
                    fw.op(fw.dve, lambda v: v.tensor_copy(out=t16[:], in_=t32[:]), reads=["bp32"], writes=["bp16"])
                    fw.dma(self.yaT[m * 128:(m + 1) * 128, :], t16[:], reads=["bp16"], writes=[("yaT", m)])
            if "s5" not in self.mixers:
                for m in range(2):
                    fw.dma(t16[:], self.zbT[m * 128:(m + 1) * 128, :], reads=[("zbT", m)], writes=["bp16"])
                    fw.dma(self.ybT[m * 128:(m + 1) * 128, :], t16[:], reads=["bp16"], writes=[("ybT", m)])
            if "attn" not in self.mixers:
                fw.op(fw.pool, lambda g: g.memset(z16[:], 0.0), writes=["bpz"])
                for m in range(4):
                    fw.dma(self.ycT[m * 128:(m + 1) * 128, :], z16[:], reads=["bpz"], writes=[("ycT", m)])
        if "attn" in self.mixers:
            fw.barrier()
            self.attention(l)
        if "hyena" in self.mixers:
            fw.barrier()
            self.hyena(l)
        if "s5" in self.mixers:
            fw.barrier()
            self.s5(l)

    def attention(self, l):
        nc, fw = self.nc, self.fw
        with ExitStack() as es:
            sb = lambda n, s, d=F32: es.enter_context(nc.sbuf_tensor(self.uniq(n), s, d))
            NKT = NT // 128
            kAs = sb("kAs", [65, NT], BF16)
            qAs = sb("qAs", [65, 4, NT], BF16)
            vt = sb("vt", [128, NKT, 128], BF16)
            nkS = sb("nkS", [128, NKT, 2])
            en = sb("enS", [128, NKT, 2])
            nnk = sb("nnkS", [128, NKT, 2])
            Vp = sb("Vp", [128, NKT, 65], BF16)
            ck32 = sb("ck32", [128, 4, 128])
            cv32 = sb("cv32", [128, 4, 128])
            kcA = sb("kcA", [65, 2, 512], BF16)
            nkc = sb("nkc", [128, 4, 2])
            enc = sb("enc", [128, 4, 2])
            nnkc = sb("nnkc", [128, 4, 2])
            Vpc = sb("Vpc", [128, 2, 4, 65], BF16)
            junk = sb("junkB", [128, 64])
            sinkt = sb("sinkt", [65, 8])
            mprev = sb("mprev", [128, 128], BF16)
            mnext = sb("mnext", [128, 128], BF16)
            m32 = sb("m32", [128, 128])
            sel64 = sb("sel64", [128, 128])
            Rrow = sb("Rrow", [128, 1024])
            PT = [sb(f"PT{i}", [128, 1024], BF16) for i in range(2)]
            sk_t = sb("sk_t", [65, 1024])
            bcs = sb("bcs", [64, 1024])
            ycS = sb("ycS", [64, 4, NT], BF16)

            fw.dma(sinkt[:], self.sink[l:l + 1, :].partition_broadcast(65), writes=["sinkt"])
            fw.dma(m32[:], self.mask_prev[:, :], writes=["m32"])
            fw.op(fw.dve, lambda v: v.tensor_copy(out=mprev[:], in_=m32[:]), reads=["m32"], writes=["mprev"])
            fw.dma(m32[:], self.mask_next[:, :], reads=[], writes=["m32"])
            fw.op(fw.dve, lambda v: v.tensor_copy(out=mnext[:], in_=m32[:]), reads=["m32"], writes=["mnext"])
            fw.op(fw.pool, lambda g: g.memset(sel64[:], 0.0), writes=["sel64"])
            fw.op(fw.pool, lambda g: g.memset(sel64[64:65, :], 1.0), reads=["sel64"], writes=["sel64"])
            fw.op(fw.pool, lambda g: g.memset(Rrow[:], 0.0), writes=["Rrow"])
            fw.dma(vt[:], self.vtok.rearrange("(t p) c -> p t c", p=128), reads=["vtok"], writes=["vt"])
            fw.dma(nkS[:], self.nk.rearrange("(t p) c -> p t c", p=128), reads=["nk"], writes=["nkS"])
            fw.op(fw.act, lambda a: a.activation(out=en[:], in_=nkS[:], func=AF.Exp), reads=["nkS"], writes=["en"])
            fw.op(fw.dve, lambda v: v.tensor_scalar(out=nnk[:], in0=nkS[:], scalar1=-1.0, scalar2=None, op0=ALU.mult), reads=["nkS"], writes=["nnk"])
            fw.dma(ck32[:], self.cache_k[l].rearrange("(t p) c -> p t c", p=128), writes=["ck32"])
            fw.dma(cv32[:], self.cache_v[l].rearrange("(t p) c -> p t c", p=128), writes=["cv32"])
            fw.op(fw.pool, lambda g: g.memset(kcA[64:65, :, :], 1.0), writes=["kcA"])
            for t in range(4):
                for g_ in range(2):
                    fw.op(fw.act, lambda a, t=t, g_=g_: a.activation(out=junk[:], in_=ck32[:, t, g_ * 64:(g_ + 1) * 64], func=AF.Square,
                                                                      accum_out=nkc[:, t, g_:g_ + 1]), reads=["ck32"], writes=["junkB", "nkc"])
                pt, ptk = self.pb[5], ("pb", 5)
                fw.op(fw.pe, lambda t_, t=t: t_.transpose(pt[:, 0:128], ck32[:, t, :], self.ident[:]), reads=["ck32", "ident"], writes=[ptk])
                for g_ in range(2):
                    fw.op(fw.dve, lambda v, t=t, g_=g_: v.tensor_copy(out=kcA[0:64, g_, t * 128:(t + 1) * 128], in_=pt[g_ * 64:(g_ + 1) * 64, 0:128]),
                          reads=[ptk], writes=["kcA"])
            fw.op(fw.dve, lambda v: v.tensor_scalar(out=nkc[:], in0=nkc[:], scalar1=1.0 / 16.0, scalar2=None, op0=ALU.mult), reads=["nkc"], writes=["nkc"])
            fw.op(fw.act, lambda a: a.activation(out=enc[:], in_=nkc[:], func=AF.Exp), reads=["nkc"], writes=["enc"])
            fw.op(fw.dve, lambda v: v.tensor_scalar(out=nnkc[:], in0=nkc[:], scalar1=-1.0, scalar2=None, op0=ALU.mult), reads=["nkc"], writes=["nnkc"])

            for g_ in range(2):
                fw.dma(kAs[:], self.kA[g_], reads=[("kA", g_)], writes=["kAs"])
                for hh in range(4):
                    fw.dma(qAs[:, hh, :], self.qA[4 * g_ + hh], reads=[("qA", 4 * g_ + hh)], writes=["qAs"])
                for t in range(NKT):
                    fw.op(fw.dve, lambda v, t=t: v.tensor_scalar(out=Vp[:, t, 0:64], in0=vt[:, t, g_ * 64:(g_ + 1) * 64], scalar1=en[:, t, g_:g_ + 1],
                                                                  scalar2=None, op0=ALU.mult), reads=["vt", "en"], writes=["Vp"])
                fw.op(fw.dve, lambda v: v.tensor_copy(out=Vp[:, :, 64], in_=en[:, :, g_]), reads=["en"], writes=["Vp"])
                for t in range(4):
                    fw.op(fw.dve, lambda v, t=t: v.tensor_scalar(out=Vpc[:, g_, t, 0:64], in0=cv32[:, t, g_ * 64:(g_ + 1) * 64], scalar1=enc[:, t, g_:g_ + 1],
                                                                  scalar2=None, op0=ALU.mult), reads=["cv32", "enc"], writes=["Vpc"])
                fw.op(fw.dve, lambda v: v.tensor_copy(out=Vpc[:, g_, :, 64], in_=enc[:, :, g_]), reads=["enc"], writes=["Vpc"])

                groups = []
                for s_ in range(2):
                    groups.append((s_ * 256, 256, [("tok", 2 * s_), ("tok", 2 * s_ + 1)]))
                for qb in range(32):
                    kts = [("cache", t) for t in range(4)]
                    if qb > 0:
                        kts.append(("prev", 4 + qb - 1))
                    kts.append(("tok", 4 + qb))
                    if qb < 31:
                        kts.append(("next", 4 + qb + 1))
                    groups.append((NP + qb * 128, 128, kts))
                for gi, (q0, nq, kts) in enumerate(groups):
                    ncol = 4 * nq
                    nb = ncol // 512
                    po = [self.pb[2 + 2 * (gi % 2) + b_] for b_ in range(nb)]
                    pok = [("pb", 2 + 2 * (gi % 2) + b_) for b_ in range(nb)]
                    for ki, (kind, t) in enumerate(kts):
                        pt_ = PT[ki % 2]; ptk_ = ("PT", ki % 2)
                        pss = [self.pb[0], self.pb[1]]; pssk = [("pb", 0), ("pb", 1)]
                        if nb == 1:
                            pss = [self.pb[ki % 2]]; pssk = [("pb", ki % 2)]
                        if kind == "cache":
                            lhs = kcA[0:65, g_, t * 128:(t + 1) * 128]
                            bias = nnkc[:, t, g_:g_ + 1]
                            vprime = Vpc[:, g_, t, :]
                            rk = ["kcA"]; bk = "nnkc"; vk = "Vpc"
                        else:
                            lhs = kAs[0:65, t * 128:(t + 1) * 128]
                            bias = nnk[:, t, g_:g_ + 1]
                            vprime = Vp[:, t, :]
                            rk = ["kAs"]; bk = "nnk"; vk = "Vp"
                        for hh in range(4):
                            col = hh * nq
                            b_ = col // 512
                            fw.op(fw.pe, lambda t_, hh=hh, col=col, b_=b_: t_.matmul(pss[b_][:, col % 512:col % 512 + nq], lhsT=lhs, rhs=qAs[0:65, hh, q0:q0 + nq],
                                                                                   start=True, stop=True),
                                  reads=rk + ["qAs"], writes=[pssk[b_]], inc=(hh % (4 // nb) == (4 // nb) - 1))
                        for b_ in range(nb):
                            fw.op(fw.act, lambda a, b_=b_: a.activation(out=pt_[:, b_ * 512:(b_ + 1) * 512], in_=pss[b_][:], func=AF.Exp, scale=0.125, bias=bias),
                                  reads=[pssk[b_], bk], writes=[ptk_])
                        if kind in ("prev", "next"):
                            mk_ = mprev if kind == "prev" else mnext
                            fw.op(fw.pool, lambda g: g.tensor_tensor(out=pt_[:, 0:512].rearrange("p (h q) -> p h q", h=4),
                                                                     in0=pt_[:, 0:512].rearrange("p (h q) -> p h q", h=4),
                                                                     in1=mk_[:].unsqueeze(1).to_broadcast([128, 4, 128]), op=ALU.mult),
                                  reads=[ptk_, "mprev", "mnext"], writes=[ptk_])
                        for b_ in range(nb):
                            fw.op(fw.pe, lambda t_, b_=b_: t_.matmul(po[b_][0:65, :], lhsT=vprime[:, 0:65], rhs=pt_[:, b_ * 512:(b_ + 1) * 512],
                                                                     start=(ki == 0), stop=(ki == len(kts) - 1)),
                                  reads=[vk, ptk_], writes=[pok[b_]], inc=(ki == len(kts) - 1))
                    for hh in range(4):
                        h = 4 * g_ + hh
                        fw.op(fw.act, lambda a, hh=hh, h=h: a.activation(out=sk_t[64:65, hh * nq:(hh + 1) * nq], in_=qAs[64:65, hh, q0:q0 + nq], func=AF.Exp,
                                                                          scale=0.125, bias=sinkt[64:65, h:h + 1]), reads=["qAs", "sinkt"], writes=["sk_t"])
                    for b_ in range(nb):
                        cs = slice(b_ * 512, (b_ + 1) * 512)
                        fw.op(fw.dve, lambda v, b_=b_, cs=cs: v.tensor_tensor(out=sk_t[64:65, cs], in0=po[b_][64:65, :], in1=sk_t[64:65, cs], op=ALU.add),
                              reads=[pok[b_], "sk_t"], writes=["sk_t"])
                        fw.op(fw.dve, lambda v, cs=cs: v.reciprocal(out=Rrow[64:65, cs], in_=sk_t[64:65, cs]), reads=["sk_t"], writes=["Rrow"])
                        pbc, pbck = self.pb[6 + b_], ("pb", 6 + b_)
                        fw.op(fw.pe, lambda t_, cs=cs, pbc=pbc: t_.matmul(pbc[:], lhsT=sel64[:], rhs=Rrow[:, cs], start=True, stop=True),
                              reads=["sel64", "Rrow"], writes=[pbck])
                        fw.op(fw.act, lambda a, cs=cs, pbc=pbc: a.copy(out=bcs[:, cs], in_=pbc[0:64, :]), reads=[pbck], writes=["bcs"])
                        for hh in range(4 // nb):
                            hg = b_ * (4 // nb) + hh
                            fw.op(fw.dve, lambda v, hh=hh, hg=hg, b_=b_: v.tensor_tensor(out=ycS[:, hg, q0:q0 + nq], in0=po[b_][0:64, hh * nq:(hh + 1) * nq],
                                                                                         in1=bcs[:, b_ * 512 + hh * nq:b_ * 512 + (hh + 1) * nq], op=ALU.mult),
                                  reads=[pok[b_], "bcs"], writes=["ycS"])
                for hh in range(4):
                    h = 4 * g_ + hh
                    fw.dma(self.ycT[h * 64:(h + 1) * 64, :], ycS[:, hh, :], reads=["ycS"], writes=[("ycT", h // 2)])

    HSEG = [(0, 256), (256, 256), (NP, LS)]

    def hyena(self, l):
        nc, fw = self.nc, self.fw
        if not hasattr(self, "hzT"):
            pass
        with ExitStack() as es:
            sb = lambda n, s, d=F32: es.enter_context(nc.sbuf_tensor(self.uniq(n), s, d))
            za = sb("hy_za", [128, NT]); U = sb("hy_U", [128, NT]); cw = sb("hy_cwS", [128, 6, 4])
            fw.dma(cw[:], self.hy_cw[l], writes=["hy_cwS"])
            for m in range(6):
                fw.dma(za[:], self.zaT[m * 128:(m + 1) * 128, :], reads=[("zaT", m)], writes=["hy_za"])
                fw.op(fw.act, lambda a: a.activation(out=U[:], in_=za[:], func=AF.Identity, scale=cw[:, m, 1:2], bias=cw[:, m, 3:4]),
                      reads=["hy_za", "hy_cwS"], writes=["hy_U"])
                for (t0, Ls) in self.HSEG:
                    fw.op(fw.dve, lambda v: v.scalar_tensor_tensor(out=U[:, t0 + 1:t0 + Ls], in0=za[:, t0:t0 + Ls - 1], scalar=cw[:, m, 0:1],
                                                                   in1=U[:, t0 + 1:t0 + Ls], op0=ALU.mult, op1=ALU.add),
                          reads=["hy_za", "hy_cwS", "hy_U"], writes=["hy_U"])
                    fw.op(fw.dve, lambda v: v.scalar_tensor_tensor(out=U[:, t0:t0 + Ls - 1], in0=za[:, t0 + 1:t0 + Ls], scalar=cw[:, m, 2:3],
                                                                   in1=U[:, t0:t0 + Ls - 1], op0=ALU.mult, op1=ALU.add),
                          reads=["hy_za", "hy_cwS", "hy_U"], writes=["hy_U"])
                if m < 2:
                    fw.dma(self.hzT[m * 128:(m + 1) * 128, :], U[:], reads=["hy_U"], writes=[("hzT", m)])
                else:
                    fw.dma(self.ugT[(m - 2) * 128:(m - 1) * 128, :], U[:], reads=["hy_U"], writes=[("ugT", m - 2)])
        fw.barrier()
        for L_ in (256, 4096):
            self.hy_filters(l, L_)
            fw.barrier()
        for o in range(2):
            self.hy_to_tok()
            fw.barrier()
            self.hy_conv256(o)
            fw.barrier()
            self.hy_conv(o, NP, LS)
            fw.barrier()
            with ExitStack() as es:
                sb = lambda n, s, d=F32: es.enter_context(nc.sbuf_tensor(self.uniq(n), s, d))
                z = sb("hy_z", [128, NT]); y = sb("hy_y", [128, NT]); g = sb("hy_g", [128, NT]); skp = sb("hy_skp", [128, 2, 2])
                z16 = sb("hy_z16", [128, NT], BF16)
                fw.dma(skp[:], self.hy_skip[l], writes=["hy_skp"])
                for ch in range(2):
                    fw.dma(z[:], self.hzT[ch * 128:(ch + 1) * 128, :], reads=[("hzT", ch)], writes=["hy_z"])
                    fw.dma(y[:], self.hyT[ch * 128:(ch + 1) * 128, :], reads=[("hyT", ch)], writes=["hy_y"])
                    fw.dma(g[:], self.ugT[(2 * o + ch) * 128:(2 * o + ch + 1) * 128, :], reads=[("ugT", 2 * o + ch)], writes=["hy_g"])
                    fw.op(fw.dve, lambda v: v.scalar_tensor_tensor(out=y[:], in0=z[:], scalar=skp[:, o, ch:ch + 1], in1=y[:], op0=ALU.mult, op1=ALU.add),
                          reads=["hy_z", "hy_skp", "hy_y"], writes=["hy_y"])
                    if o == 0:
                        fw.op(fw.pool, lambda g_: g_.tensor_tensor(out=z[:], in0=y[:], in1=g[:], op=ALU.mult), reads=["hy_y", "hy_g"], writes=["hy_z"])
                        fw.dma(self.hzT[ch * 128:(ch + 1) * 128, :], z[:], reads=["hy_z"], writes=[("hzT", ch)])
                    else:
                        fw.op(fw.pool, lambda g_: g_.tensor_tensor(out=z16[:], in0=y[:], in1=g[:], op=ALU.mult), reads=["hy_y", "hy_g"], writes=["hy_z16"])
                        fw.dma(self.yaT[ch * 128:(ch + 1) * 128, :], z16[:], reads=["hy_z16"], writes=[("yaT", ch)])
            fw.barrier()

    def hy_to_tok(self):
        nc, fw = self.nc, self.fw
        with ExitStack() as es:
            sb = lambda n, s, d=F32: es.enter_context(nc.sbuf_tensor(self.uniq(n), s, d))
            z = sb("ht_z", [128, 2, NT])
            st = [sb(f"ht_st{i}", [128, 256], BF16) for i in range(3)]
            for ch in range(2):
                fw.dma(z[:, ch, :], self.hzT[ch * 128:(ch + 1) * 128, :], reads=[("hzT", ch)], writes=["ht_z"])
            for t in range(NT // 128):
                ps, pk = self.pb[t % 4], ("pb", t % 4)
                for ch in range(2):
                    fw.op(fw.pe, lambda t_, ch=ch: t_.transpose(ps[:, ch * 128:(ch + 1) * 128], z[:, ch, t * 128:(t + 1) * 128], self.ident[:]),
                          reads=["ht_z", "ident"], writes=[pk], inc=(ch == 1))
                s_ = st[t % 3]; sk = ("ht_st", t % 3)
                if t % 2 == 0:
                    fw.op(fw.dve, lambda v: v.tensor_copy(out=s_[:], in_=ps[:, 0:256]), reads=[pk], writes=[sk])
                else:
                    fw.op(fw.act, lambda a: a.activation(out=s_[:], in_=ps[:, 0:256], func=AF.Identity), reads=[pk], writes=[sk])
                fw.dma(self.ztok[t * 128:(t + 1) * 128, :], s_[:], reads=[sk], writes=["ztok"])

    def hy_stageA(self, es, L_, src, C, FAt, xt, stg):
        nc, fw = self.nc, self.fw
        NH = L_ // 128
        srcv = src.rearrange("(h l) c -> h l c", l=128)
        for ch in range(8):
            fw.dma(xt[0:NH, :, 0:C], srcv[:, ch * 16:(ch + 1) * 16, :], reads=["hy_src"], writes=["hy_xt"])
            sg = stg[ch % 2]; sgk = ("hy_stg", ch % 2)
            per = 512 // C
            for i0 in range(0, 16, per):
                bi_ = (i0 // per) % 4
                ps, pk = self.pb[bi_], ("pb", bi_)
                for ii in range(per):
                    i = i0 + ii
                    n_lo = ch * 16 + i
                    fw.op(fw.pe, lambda t_: t_.matmul(ps[:, ii * C:(ii + 1) * C], lhsT=FAt[:, n_lo, :], rhs=xt[:, i, 0:C], start=True, stop=True,
                                                     skip_group_check=True),
                          reads=["hy_FAt", "hy_xt"], writes=[pk], inc=(ii == per - 1))
                ov_ = sg[:, i0:i0 + per, 0:C]
                iv_ = ps[:, 0:per * C].rearrange("p (j c) -> p j c", j=per)
                if (i0 // per) % 2 == 0:
                    fw.op(fw.dve, lambda v: v.tensor_copy(out=ov_, in_=iv_), reads=[pk], writes=[sgk])
                else:
                    fw.op(fw.act, lambda a: a.activation(out=ov_, in_=iv_, func=AF.Identity), reads=[pk], writes=[sgk])
            fw.dma(self.Ascr[:, ch * 16:(ch + 1) * 16, 0:C], sg[:, :, 0:C], reads=[sgk], writes=["Ascr"])

    def hy_filters(self, l, L_):
        nc, fw = self.nc, self.fw
        NJT = L_ // 128
        KA = L_ // 64
        cwid = min(512, L_)
        TWO_PI = 2.0 * math.pi
        with ExitStack() as es:
            sb = lambda n, s, d=F32: es.enter_context(nc.sbuf_tensor(self.uniq(n), s, d))
            featT = sb("hf_feat", [128, L_]); negt = sb("hf_negt", [128, NJT])
            w1 = sb("hf_w1", [128, 128]); w2 = sb("hf_w2", [128, 128]); w3 = sb("hf_w3", [128, 1024])
            vec = sb("hf_vec", [128, 3]); fb = sb("hf_fb", [128, 2]); dabs = sb("hf_dabs", [128, 1024])
            h1T = sb("hf_h1T", [128, L_]); h2T = sb("hf_h2T", [128, L_])
            yt = sb("hf_yt", [128, 512]); ki = sb("hf_ki", [128, 512], I32); kf = sb("hf_kf", [128, 512])
            fw.dma(featT[:], self.hy_featT[L_][:, :], writes=["hf_feat"])
            fw.dma(negt[:], self.hy_negt[L_][:, :], writes=["hf_negt"])
            fw.dma(w1[:], self.hy_w1[l], writes=["hf_w1"]); fw.dma(w2[:], self.hy_w2[l], writes=["hf_w2"]); fw.dma(w3[:], self.hy_w3[l], writes=["hf_w3"])
            fw.dma(vec[:], self.hy_vec[l], writes=["hf_vec"])
            fw.dma(dabs[:], self.hy_decay[l:l + 1, :].partition_broadcast(128), writes=["hf_dabs"])
            fw.op(fw.act, lambda a: a.activation(out=dabs[:], in_=dabs[:], func=AF.Abs), reads=["hf_dabs"], writes=["hf_dabs"])
            fw.op(fw.dve, lambda v: v.tensor_tensor(out=fb[:], in0=vec[:, 0:2], in1=vec[:, 2:3].to_broadcast([128, 2]), op=ALU.mult), reads=["hf_vec"], writes=["hf_fb"])

            def sin_layer(srcT, W, col, dstT, sk, wk, dk):
                for ch in range(L_ // cwid):
                    cs = slice(ch * cwid, (ch + 1) * cwid)
                    ps, pk = self.pb[ch % 2], ("pb", ch % 2)
                    fw.op(fw.pe, lambda t_: t_.matmul(ps[:, 0:cwid], lhsT=W[:], rhs=srcT[:, cs], start=True, stop=True), reads=[sk, wk], writes=[pk])
                    fw.op(fw.act, lambda a: a.activation(out=yt[:, 0:cwid], in_=ps[:, 0:cwid], func=AF.Identity, scale=vec[:, 2:3], bias=fb[:, col:col + 1]),
                          reads=[pk, "hf_vec", "hf_fb"], writes=["hf_yt"])
                    fw.op(fw.dve, lambda v: v.tensor_scalar(out=ki[:, 0:cwid], in0=yt[:, 0:cwid], scalar1=1.0 / TWO_PI, scalar2=None, op0=ALU.mult),
                          reads=["hf_yt"], writes=["hf_ki"])
                    fw.op(fw.dve, lambda v: v.tensor_copy(out=kf[:, 0:cwid], in_=ki[:, 0:cwid]), reads=["hf_ki"], writes=["hf_kf"])
                    fw.op(fw.dve, lambda v: v.scalar_tensor_tensor(out=yt[:, 0:cwid], in0=kf[:, 0:cwid], scalar=-TWO_PI, in1=yt[:, 0:cwid], op0=ALU.mult, op1=ALU.add),
                          reads=["hf_kf", "hf_yt"], writes=["hf_yt"])
                    fw.op(fw.dve, lambda v: v.tensor_scalar(out=yt[:, 0:cwid], in0=yt[:, 0:cwid], scalar1=3.141592, scalar2=-3.141592, op0=ALU.min, op1=ALU.max),
                          reads=["hf_yt"], writes=["hf_yt"])
                    fw.op(fw.act, lambda a: a.activation(out=dstT[:, cs], in_=yt[:, 0:cwid], func=AF.Sin), reads=["hf_yt"], writes=[dk])

            sin_layer(featT, w1, 0, h1T, "hf_feat", "hf_w1", "hf_h1T")
            sin_layer(h1T, w2, 1, h2T, "hf_h1T", "hf_w2", "hf_h2T")

            hrow = [sb(f"hf_hrow{i}", [128, 1024]) for i in range(2)]
            habs = sb("hf_habs", [128, 1024]); Et = sb("hf_E", [128, 512])
            pS = [self.pb[6], self.pb[7]]; pSk = [("pb", 6), ("pb", 7)]
            for j in range(NJT):
                hr = hrow[j % 2]; hk = ("hf_hrow", j % 2)
                for half in range(2):
                    cs = slice(half * 512, (half + 1) * 512)
                    ps, pk = self.pb[2 + half], ("pb", 2 + half)
                    fw.op(fw.pe, lambda t_: t_.matmul(ps[:], lhsT=h2T[:, j * 128:(j + 1) * 128], rhs=w3[:, cs], start=True, stop=True),
                          reads=["hf_h2T", "hf_w3"], writes=[pk])
                    fw.op(fw.act, lambda a: a.activation(out=Et[:], in_=dabs[:, cs], func=AF.Exp, scale=negt[:, j:j + 1]), reads=["hf_dabs", "hf_negt"], writes=["hf_E"])
                    fw.op(fw.dve, lambda v: v.tensor_tensor(out=hr[:, cs], in0=ps[:], in1=Et[:], op=ALU.mult), reads=[pk, "hf_E"], writes=[hk])
                if j == 0:
                    fw.op(fw.pool, lambda g: g.memset(hr[0:1, 256:512], 0.0), reads=[hk], writes=[hk])
                    fw.op(fw.pool, lambda g: g.memset(hr[0:1, 768:1024], 0.0), reads=[hk], writes=[hk])
                fw.op(fw.act, lambda a: a.activation(out=habs[:], in_=hr[:], func=AF.Abs), reads=[hk], writes=["hf_habs"])
                for half in range(2):
                    fw.op(fw.pe, lambda t_: t_.matmul(pS[half][:], lhsT=self.ones32[:], rhs=habs[:, half * 512:(half + 1) * 512], start=(j == 0), stop=(j == NJT - 1)),
                          reads=["hf_habs", "ones32"], writes=[pSk[half]])
                fw.dma(self.hraw[L_][j * 128:(j + 1) * 128, :], hr[:], reads=[hk], writes=["hraw"])
            Ssb = sb("hf_S", [128, 1024]); rn = sb("hf_rn", [128, 2, 256]); rnb = sb("hf_rnb", [128, 1024])
            for half in range(2):
                fw.op(fw.act, lambda a: a.copy(out=Ssb[:, half * 512:(half + 1) * 512], in_=pS[half][:]), reads=[pSk[half]], writes=["hf_S"])
            Sv = Ssb[:].rearrange("p (o d c) -> p o d c", o=2, d=2)
            fw.op(fw.dve, lambda v: v.tensor_tensor(out=rn[:], in0=Sv[:, :, 0, :], in1=Sv[:, :, 1, :], op=ALU.add), reads=["hf_S"], writes=["hf_rn"])
            fw.op(fw.dve, lambda v: v.reciprocal(out=rn[:], in_=rn[:]), reads=["hf_rn"], writes=["hf_rn"])
            rv = rnb[:].rearrange("p (o d c) -> p o d c", o=2, d=2)
            for d_ in range(2):
                fw.op(fw.dve, lambda v: v.tensor_copy(out=rv[:, :, d_, :], in_=rn[:]), reads=["hf_rn"], writes=["hf_rnb"])
            hb16 = [sb(f"hf_hb16{i}", [128, 1024], BF16) for i in range(2)]
            for j in range(NJT):
                hr = hrow[j % 2]; hk = ("hf_hrow", j % 2)
                fw.dma(hr[:], self.hraw[L_][j * 128:(j + 1) * 128, :], reads=["hraw"], writes=[hk])
                fw.op(fw.dve, lambda v: v.tensor_tensor(out=hb16[j % 2][:], in0=hr[:], in1=rnb[:], op=ALU.mult), reads=[hk, "hf_rnb"], writes=[("hf_hb16", j % 2)])
                fw.dma(self.ftok[L_][j * 128:(j + 1) * 128, :], hb16[j % 2][:], reads=[("hf_hb16", j % 2)], writes=["ftok"])
        fw.barrier()
        if L_ == 256:
            self.hy_spectra256()
            fw.barrier()
            return
        with ExitStack() as es:
            sb = lambda n, s, d=F32: es.enter_context(nc.sbuf_tensor(self.uniq(n), s, d))
            FAt = sb("hy_FAt", [128, 128, 128], BF16); GB = sb("hy_GBs", [128, 12, 128], BF16)
            xt = sb("hy_xt", [128, 16, 512], BF16); stg = [sb(f"hy_stg{i}", [128, 16, 512], BF16) for i in range(2)]
            A2 = sb("hy_A2", [128, 2, 4, 512], BF16)
            kst = [sb(f"hy_kst{i}", [128, 2, 4, 256]) for i in range(2)]
            fw.dma(FAt[:], self.hy_FA[L_][:, :, :], writes=["hy_FAt"])
            fw.dma(GB[:], self.hy_GB[:, :, :], writes=["hy_GBs"])
            fw.op(fw.pool, lambda g: g.memset(xt[:], 0.0), writes=["hy_xt"])
            for o in range(2):
                self.hy_stageA(es, L_, self.ftok[L_][:, o * 512:(o + 1) * 512], 512, FAt, xt, stg)
                nkc = min(4, KA)
                for kc in range(KA // nkc):
                    for r in range(2):
                        fw.dma(A2[:, r, 0:nkc, :], self.Ascr[r * 64 + kc * nkc:r * 64 + (kc + 1) * nkc, :, :].rearrange("k l c -> l k c"),
                               reads=["Ascr"], writes=["hy_A2"])
                    ks = kst[kc % 2]; ksk = ("hy_kst", kc % 2)
                    for kk in range(nkc):
                        pa, pak = self.pb[(2 * kk) % 8], ("pb", (2 * kk) % 8)
                        pb_, pbk = self.pb[(2 * kk + 1) % 8], ("pb", (2 * kk + 1) % 8)
                        ops = [(0, 0, 4), (1, 0, 5), (0, 256, 4), (1, 256, 5)]
                        for ii, (r, c0_, gi) in enumerate(ops):
                            fw.op(fw.pe, lambda t_: t_.matmul(pa[:, 0:256], lhsT=GB[:, gi, :], rhs=A2[:, r, kk, c0_:c0_ + 256], start=(ii == 0), stop=(ii == 3)),
                                  reads=["hy_GBs", "hy_A2"], writes=[pak], inc=(ii == 3))
                        ops = [(0, 0, 6), (1, 0, 7), (0, 256, 8), (1, 256, 9)]
                        for ii, (r, c0_, gi) in enumerate(ops):
                            fw.op(fw.pe, lambda t_: t_.matmul(pb_[:, 0:256], lhsT=GB[:, gi, :], rhs=A2[:, r, kk, c0_:c0_ + 256], start=(ii == 0), stop=(ii == 3)),
                                  reads=["hy_GBs", "hy_A2"], writes=[pbk], inc=(ii == 3))
                        fw.op(fw.act, lambda a: a.copy(out=ks[:, 0, kk, :], in_=pa[:, 0:256]), reads=[pak], writes=[ksk])
                        fw.op(fw.dve, lambda v: v.tensor_copy(out=ks[:, 1, kk, :], in_=pb_[:, 0:256]), reads=[pbk], writes=[ksk])
                    for ab in range(2):
                        fw.dma(self.Ksp[L_][o, ab, :, kc * nkc:(kc + 1) * nkc, :], ks[:, ab, 0:nkc, :], reads=[ksk], writes=["Ksp"])
                fw.barrier()

    def hy_spectra256(self):
        nc, fw = self.nc, self.fw
        with ExitStack() as es:
            sb = lambda n, s, d=F32: es.enter_context(nc.sbuf_tensor(self.uniq(n), s, d))
            T = sb("hy_T", [128, 6, 1024], BF16)
            ft = sb("hy_ft", [128, 2, 1024], BF16)
            ks = sb("hy_ks", [128, 2, 2, 4, 256])
            fw.dma(T[:], self.hy_T256[:, :, :], writes=["hy_T"])
            fw.dma(ft[:], self.ftok[256].rearrange("(h p) c -> p h c", p=128), reads=["ftok"], writes=["hy_ft"])
            Tv = lambda var, nh, ka: T[:, var, (nh * 4 + ka) * 128:(nh * 4 + ka + 1) * 128]
            for o in range(2):
                for ka in range(4):
                    pa, pak = self.pb[(2 * ka) % 8], ("pb", (2 * ka) % 8)
                    pb_, pbk = self.pb[(2 * ka + 1) % 8], ("pb", (2 * ka + 1) % 8)
                    seq = [(2, nh, 0) for nh in range(2)] + [(2, nh, 256) for nh in range(2)]
                    for ii, (var, nh, c0_) in enumerate(seq):
                        fw.op(fw.pe, lambda t_: t_.matmul(pa[:, 0:256], lhsT=Tv(var, nh, ka), rhs=ft[:, nh, o * 512 + c0_:o * 512 + c0_ + 256], start=(ii == 0), stop=(ii == 3)),
                              reads=["hy_T", "hy_ft"], writes=[pak], inc=(ii == 3))
                    seq = [(3, nh, 0) for nh in range(2)] + [(4, nh, 256) for nh in range(2)]
                    for ii, (var, nh, c0_) in enumerate(seq):
                        fw.op(fw.pe, lambda t_: t_.matmul(pb_[:, 0:256], lhsT=Tv(var, nh, ka), rhs=ft[:, nh, o * 512 + c0_:o * 512 + c0_ + 256], start=(ii == 0), stop=(ii == 3)),
                              reads=["hy_T", "hy_ft"], writes=[pbk], inc=(ii == 3))
                    fw.op(fw.act, lambda a: a.copy(out=ks[:, o, 0, ka, :], in_=pa[:, 0:256]), reads=[pak], writes=["hy_ks"])
                    fw.op(fw.dve, lambda v: v.tensor_copy(out=ks[:, o, 1, ka, :], in_=pb_[:, 0:256]), reads=[pbk], writes=["hy_ks"])
                for ab in range(2):
                    fw.dma(self.Ksp[256][o, ab], ks[:, o, ab], reads=["hy_ks"], writes=["Ksp"])

    def hy_conv256(self, o):
        nc, fw = self.nc, self.fw
        with ExitStack() as es:
            sb = lambda n, s, d=F32: es.enter_context(nc.sbuf_tensor(self.uniq(n), s, d))
            T = sb("hy_T", [128, 6, 1024], BF16)
            xt = sb("hy_x256", [128, 4, 256], BF16)
            Kab = sb("hy_K256", [128, 2, 4, 256])
            Y = sb("hy_Y256", [128, 2, 4, 256], BF16)
            t1 = [sb(f"hy_t1{i}", [128, 256]) for i in range(2)]; t2 = [sb(f"hy_t2{i}", [128, 256]) for i in range(2)]
            y32 = sb("hy_y256", [128, 2, 512])
            fw.dma(T[:], self.hy_T256[:, :, :], writes=["hy_T"])
            fw.dma(xt[:], self.ztok[0:512, :].rearrange("(h p) c -> p h c", p=128), reads=["ztok"], writes=["hy_x256"])
            for ab in range(2):
                fw.dma(Kab[:, ab], self.Ksp[256][o, ab], reads=["Ksp"], writes=["hy_K256"])
            Tv = lambda var, nh, ka: T[:, var, (nh * 4 + ka) * 128:(nh * 4 + ka + 1) * 128]
            for sq in range(2):
                for ka in range(4):
                    i_ = (sq * 4 + ka) % 2
                    pa, pak = self.pb[2 * i_], ("pb", 2 * i_)
                    pw, pwk = self.pb[2 * i_ + 1], ("pb", 2 * i_ + 1)
                    for nh in range(2):
                        fw.op(fw.pe, lambda t_: t_.matmul(pa[:, 0:256], lhsT=Tv(0, nh, ka), rhs=xt[:, 2 * sq + nh, :], start=(nh == 0), stop=(nh == 1)),
                              reads=["hy_T", "hy_x256"], writes=[pak], inc=(nh == 1))
                    for nh in range(2):
                        fw.op(fw.pe, lambda t_: t_.matmul(pw[:, 0:256], lhsT=Tv(1, nh, ka), rhs=xt[:, 2 * sq + nh, :], start=(nh == 0), stop=(nh == 1)),
                              reads=["hy_T", "hy_x256"], writes=[pwk], inc=(nh == 1))
                    a_, b_ = t1[i_], t2[i_]
                    fw.op(fw.dve, lambda v: v.tensor_tensor(out=a_[:], in0=pa[:, 0:256], in1=Kab[:, 0, ka, :], op=ALU.mult), reads=[pak, "hy_K256"], writes=[("hy_t1", i_)])
                    fw.op(fw.dve, lambda v: v.tensor_tensor(out=b_[:], in0=pw[:, 0:256], in1=Kab[:, 1, ka, :], op=ALU.mult), reads=[pwk, "hy_K256"], writes=[("hy_t2", i_)])
                    fw.op(fw.pool, lambda g: g.tensor_tensor(out=Y[:, sq, ka, :], in0=a_[:], in1=b_[:], op=ALU.add), reads=[("hy_t1", i_), ("hy_t2", i_)], writes=["hy_Y256"])
                for chalf in range(2):
                    ps, pk = self.pb[4 + (2 * sq + chalf) % 4], ("pb", 4 + (2 * sq + chalf) % 4)
                    for ka in range(4):
                        fw.op(fw.pe, lambda t_: t_.matmul(ps[:, 0:256], lhsT=Y[:, sq, ka, chalf * 128:(chalf + 1) * 128], rhs=T[:, 5, ka * 256:(ka + 1) * 256], start=(ka == 0), stop=(ka == 3)),
                              reads=["hy_Y256", "hy_T"], writes=[pk], inc=(ka == 3))
                    if chalf == 0:
                        fw.op(fw.dve, lambda v: v.tensor_copy(out=y32[:, chalf, sq * 256:(sq + 1) * 256], in_=ps[:, 0:256]), reads=[pk], writes=["hy_y256"])
                    else:
                        fw.op(fw.act, lambda a: a.copy(out=y32[:, chalf, sq * 256:(sq + 1) * 256], in_=ps[:, 0:256]), reads=[pk], writes=["hy_y256"])
            for chalf in range(2):
                fw.dma(self.hyT[chalf * 128:(chalf + 1) * 128, 0:512], y32[:, chalf, :], reads=["hy_y256"], writes=[("hyT", chalf)])

    def hy_conv(self, o, t0, L_):
        nc, fw = self.nc, self.fw
        NH = L_ // 128
        KA = L_ // 64
        with ExitStack() as es:
            sb = lambda n, s, d=F32: es.enter_context(nc.sbuf_tensor(self.uniq(n), s, d))
            FAt = sb("hy_FAt", [128, 128, 128], BF16); GB = sb("hy_GBs", [128, 12, 128], BF16)
            FAi = sb("hy_FAi", [128, 128, NH], BF16)
            xt = sb("hy_xt", [128, 16, 256], BF16); stg = [sb(f"hy_stg{i}", [128, 16, 256], BF16) for i in range(2)]
            A2 = sb("hy_A2", [128, 2, 8, 256], BF16)
            Kab = sb("hy_Kab", [128, 2, 8, 256])
            Y = sb("hy_Y", [128, KA, 256], BF16)
            t1 = [sb(f"hy_t1{i}", [128, 256]) for i in range(2)]; t2 = [sb(f"hy_t2{i}", [128, 256]) for i in range(2)]
            stB = [sb(f"hy_stB{i}", [128, 2, 8, 256], BF16) for i in range(2)]
            Bq2 = sb("hy_Bq2", [128, 16, 256], BF16)
            y32 = sb("hy_y32", [128, 2, L_])
            fw.dma(FAt[:], self.hy_FA[L_][:, :, :], writes=["hy_FAt"])
            fw.dma(FAi[:], self.hy_FAi[L_][:, :, :], writes=["hy_FAi"])
            fw.dma(GB[:], self.hy_GB[:, :, :], writes=["hy_GBs"])
            fw.op(fw.pool, lambda g: g.memset(xt[:], 0.0), writes=["hy_xt"])
            fw.op(fw.pool, lambda g: g.memset(Bq2[:], 0.0), writes=["hy_Bq2"])
            self.hy_stageA(es, L_, self.ztok[t0:t0 + L_, :], 256, FAt, xt, stg)
            nkc = min(8, KA)
            for kc in range(KA // nkc):
                for r in range(2):
                    fw.dma(A2[:, r, 0:nkc, :], self.Ascr[r * 64 + kc * nkc:r * 64 + (kc + 1) * nkc, :, 0:256].rearrange("k l c -> l k c"),
                           reads=["Ascr"], writes=["hy_A2"])
                    fw.dma(Kab[:, r, 0:nkc, :], self.Ksp[L_][o, r, :, kc * nkc:(kc + 1) * nkc, :], reads=["Ksp"], writes=["hy_Kab"])
                for kk in range(nkc):
                    ka = kc * nkc + kk
                    pa, pak = self.pb[(2 * kk) % 4], ("pb", (2 * kk) % 4)
                    pw, pwk = self.pb[(2 * kk + 1) % 4], ("pb", (2 * kk + 1) % 4)
                    for ii, (r, gi) in enumerate([(0, 0), (1, 1)]):
                        fw.op(fw.pe, lambda t_: t_.matmul(pa[:, 0:256], lhsT=GB[:, gi, :], rhs=A2[:, r, kk, :], start=(ii == 0), stop=(ii == 1)),
                              reads=["hy_GBs", "hy_A2"], writes=[pak], inc=(ii == 1))
                    for ii, (r, gi) in enumerate([(0, 2), (1, 3)]):
                        fw.op(fw.pe, lambda t_: t_.matmul(pw[:, 0:256], lhsT=GB[:, gi, :], rhs=A2[:, r, kk, :], start=(ii == 0), stop=(ii == 1)),
                              reads=["hy_GBs", "hy_A2"], writes=[pwk], inc=(ii == 1))
                    a_, b_ = t1[kk % 2], t2[kk % 2]
                    fw.op(fw.dve, lambda v: v.tensor_tensor(out=a_[:], in0=pa[:, 0:256], in1=Kab[:, 0, kk, :], op=ALU.mult), reads=[pak, "hy_Kab"], writes=[("hy_t1", kk % 2)])
                    fw.op(fw.dve, lambda v: v.tensor_tensor(out=b_[:], in0=pw[:, 0:256], in1=Kab[:, 1, kk, :], op=ALU.mult), reads=[pwk, "hy_Kab"], writes=[("hy_t2", kk % 2)])
                    fw.op(fw.pool, lambda g: g.tensor_tensor(out=Y[:, ka, :], in0=a_[:], in1=b_[:], op=ALU.add), reads=[("hy_t1", kk % 2), ("hy_t2", kk % 2)], writes=["hy_Y"])
            for kc in range(KA // nkc):
                sB = stB[kc % 2]; sBk = ("hy_stB", kc % 2)
                for kk in range(nkc):
                    ka = kc * nkc + kk
                    for r in range(2):
                        ps, pk = self.pb[4 + (2 * kk + r) % 4], ("pb", 4 + (2 * kk + r) % 4)
                        fw.op(fw.pe, lambda t_: t_.matmul(ps[:, 0:256], lhsT=GB[:, 10 + r, :], rhs=Y[:, ka, :], start=True, stop=True), reads=["hy_GBs", "hy_Y"], writes=[pk])
                        if r == 0:
                            fw.op(fw.dve, lambda v: v.tensor_copy(out=sB[:, r, kk, :], in_=ps[:, 0:256]), reads=[pk], writes=[sBk])
                        else:
                            fw.op(fw.act, lambda a: a.activation(out=sB[:, r, kk, :], in_=ps[:, 0:256], func=AF.Identity), reads=[pk], writes=[sBk])
                for r in range(2):
                    fw.dma(self.Bscr[:, r, kc * nkc:(kc + 1) * nkc, :], sB[:, r, 0:nkc, :], reads=[sBk], writes=["Bscr"])
            for ch in range(8):
                for r in range(2):
                    fw.dma(Bq2[r * 64:r * 64 + KA, :, :], self.Bscr[ch * 16:(ch + 1) * 16, r, 0:KA, :].rearrange("l k c -> k l c"), reads=["Bscr"], writes=["hy_Bq2"])
                for chalf in range(2):
                    ps, pk = self.pb[(2 * ch + chalf) % 4], ("pb", (2 * ch + chalf) % 4)
                    for i in range(16):
                        n_lo = ch * 16 + i
                        fw.op(fw.pe, lambda t_: t_.matmul(ps[:, i * NH:(i + 1) * NH], lhsT=Bq2[:, i, chalf * 128:(chalf + 1) * 128], rhs=FAi[:, n_lo, :], start=True, stop=True),
                              reads=["hy_Bq2", "hy_FAi"], writes=[pk], inc=(i == 15))
                    ov_ = y32[:, chalf, :].rearrange("p (h l) -> p l h", l=128)[:, ch * 16:(ch + 1) * 16, :]
                    iv_ = ps[:, 0:16 * NH].rearrange("p (j h) -> p j h", h=NH)
                    if chalf == 0:
                        fw.op(fw.dve, lambda v: v.tensor_copy(out=ov_, in_=iv_), reads=[pk], writes=["hy_y32"])
                    else:
                        fw.op(fw.act, lambda a: a.activation(out=ov_, in_=iv_, func=AF.Identity), reads=[pk], writes=["hy_y32"])
            for chalf in range(2):
                fw.dma(self.hyT[chalf * 128:(chalf + 1) * 128, t0:t0 + L_], y32[:, chalf, :], reads=["hy_y32"], writes=[("hyT", chalf)])

    def s5(self, l):
        nc, fw = self.nc, self.fw
        NA = NT // 8
        TWO_PI = 2.0 * math.pi
        with ExitStack() as es:
            sb = lambda n, s, d=F32: es.enter_context(nc.sbuf_tensor(self.uniq(n), s, d))
            zb = sb("s5_zb", [128, 2, NT], BF16)
            CQ = sb("s5_CQ", [128, 16, 2, 8, 128], BF16)
            BD = sb("s5_BD", [128, 4, 8, 128], BF16)
            HM = sb("s5_HM", [128, 16, 2, NA], BF16)
            sc = sb("s5_scS", [128, 3, 16]); h0 = sb("s5_h0S", [128, 2, 16]); skp = sb("s5_skS", [128, 2])
            idx = sb("s5_idxS", [128, 513])
            W = {}
            for nm in ("st", "ar", "th", "rho", "c1", "s1", "pr1", "pi1", "nr", "dd", "fr", "fi", "t1", "t2", "rho8", "phr", "th2"):
                W[nm] = sb("s5w_" + nm, [128, 16])
            wi = sb("s5w_i", [128, 16], I32)
            PWr = sb("s5_PWr", [128, 9, 16]); PWi = sb("s5_PWi", [128, 9, 16]); NPWi = sb("s5_NPWi", [128, 9, 16])
            for h in range(2):
                fw.dma(zb[:, h, :], self.zbT[h * 128:(h + 1) * 128, :], reads=[("zbT", h)], writes=["s5_zb"])
            fw.dma(sc[:], self.s5_sc[l], writes=["s5_sc"]); fw.dma(h0[:], self.s5_h0[l], writes=["s5_h0"])
            fw.dma(skp[:], self.s5_skip[l], writes=["s5_skp"]); fw.dma(idx[:], self.s5_idx[:, :], writes=["s5_idx"])
            K = "s5_small"
            dv = lambda fn, r=(K, "s5_sc"), w=(K,): fw.op(fw.dve, fn, reads=list(r), writes=list(w))
            ac = lambda fn, r=(K, "s5_sc"), w=(K,): fw.op(fw.act, fn, reads=list(r), writes=list(w))
            lre, lim, lst = sc[:, 0, :], sc[:, 1, :], sc[:, 2, :]

            def reduce_sin(dst, src, tmpf, tmpi):
                dv(lambda v: v.tensor_scalar(out=tmpi, in0=src, scalar1=1.0 / TWO_PI, scalar2=None, op0=ALU.mult))
                dv(lambda v: v.tensor_copy(out=tmpf, in_=tmpi))
                dv(lambda v: v.scalar_tensor_tensor(out=src, in0=tmpf, scalar=-TWO_PI, in1=src, op0=ALU.mult, op1=ALU.add))
                dv(lambda v: v.tensor_scalar(out=src, in0=src, scalar1=3.141592, scalar2=-3.141592, op0=ALU.min, op1=ALU.max))
                ac(lambda a: a.activation(out=dst, in_=src, func=AF.Sin))

            ac(lambda a: a.activation(out=W["st"][:], in_=lst, func=AF.Exp))
            dv(lambda v: v.tensor_tensor(out=W["ar"][:], in0=lre, in1=W["st"][:], op=ALU.mult))
            dv(lambda v: v.tensor_tensor(out=W["th"][:], in0=lim, in1=W["st"][:], op=ALU.mult))
            ac(lambda a: a.activation(out=W["rho"][:], in_=W["ar"][:], func=AF.Exp))
            ac(lambda a: a.activation(out=W["rho8"][:], in_=W["ar"][:], func=AF.Exp, scale=8.0))
            dv(lambda v: v.tensor_scalar(out=W["phr"][:], in0=W["th"][:], scalar1=8.0, scalar2=None, op0=ALU.mult))
            dv(lambda v: v.tensor_scalar(out=W["th2"][:], in0=W["th"][:], scalar1=math.pi / 2, scalar2=None, op0=ALU.add))
            reduce_sin(W["s1"][:], W["th"][:], W["t1"][:], wi[:])
            reduce_sin(W["c1"][:], W["th2"][:], W["t1"][:], wi[:])
            dv(lambda v: v.tensor_scalar(out=wi[:], in0=W["phr"][:], scalar1=1.0 / TWO_PI, scalar2=None, op0=ALU.mult))
            dv(lambda v: v.tensor_copy(out=W["t1"][:], in_=wi[:]))
            dv(lambda v: v.scalar_tensor_tensor(out=W["phr"][:], in0=W["t1"][:], scalar=-TWO_PI, in1=W["phr"][:], op0=ALU.mult, op1=ALU.add))
            dv(lambda v: v.tensor_tensor(out=W["pr1"][:], in0=W["rho"][:], in1=W["c1"][:], op=ALU.mult))
            dv(lambda v: v.tensor_tensor(out=W["pi1"][:], in0=W["rho"][:], in1=W["s1"][:], op=ALU.mult))
            dv(lambda v: v.tensor_scalar(out=W["nr"][:], in0=W["pr1"][:], scalar1=-1.0, scalar2=None, op0=ALU.add))
            dv(lambda v: v.tensor_tensor(out=W["dd"][:], in0=lre, in1=lre, op=ALU.mult))
            dv(lambda v: v.tensor_tensor(out=W["t1"][:], in0=lim, in1=lim, op=ALU.mult))
            dv(lambda v: v.tensor_tensor(out=W["dd"][:], in0=W["dd"][:], in1=W["t1"][:], op=ALU.add))
            dv(lambda v: v.reciprocal(out=W["dd"][:], in_=W["dd"][:]))
            dv(lambda v: v.tensor_tensor(out=W["t1"][:], in0=W["nr"][:], in1=lre, op=ALU.mult))
            dv(lambda v: v.tensor_tensor(out=W["t2"][:], in0=W["pi1"][:], in1=lim, op=ALU.mult))
            dv(lambda v: v.tensor_tensor(out=W["t1"][:], in0=W["t1"][:], in1=W["t2"][:], op=ALU.add))
            dv(lambda v: v.tensor_tensor(out=W["fr"][:], in0=W["t1"][:], in1=W["dd"][:], op=ALU.mult))
            dv(lambda v: v.tensor_tensor(out=W["t1"][:], in0=W["pi1"][:], in1=lre, op=ALU.mult))
            dv(lambda v: v.tensor_tensor(out=W["t2"][:], in0=W["nr"][:], in1=lim, op=ALU.mult))
            dv(lambda v: v.tensor_tensor(out=W["t1"][:], in0=W["t1"][:], in1=W["t2"][:], op=ALU.subtract))
            dv(lambda v: v.tensor_tensor(out=W["fi"][:], in0=W["t1"][:], in1=W["dd"][:], op=ALU.mult))
            dv(lambda v: v.memset(PWr[:, 0, :], 1.0)); dv(lambda v: v.memset(PWi[:, 0, :], 0.0))
            for j in range(8):
                dv(lambda v: v.tensor_tensor(out=W["t1"][:], in0=PWr[:, j, :], in1=W["pr1"][:], op=ALU.mult))
                dv(lambda v: v.tensor_tensor(out=W["t2"][:], in0=PWi[:, j, :], in1=W["pi1"][:], op=ALU.mult))
                dv(lambda v: v.tensor_tensor(out=PWr[:, j + 1, :], in0=W["t1"][:], in1=W["t2"][:], op=ALU.subtract))
                dv(lambda v: v.tensor_tensor(out=W["t1"][:], in0=PWr[:, j, :], in1=W["pi1"][:], op=ALU.mult))
                dv(lambda v: v.tensor_tensor(out=W["t2"][:], in0=PWi[:, j, :], in1=W["pr1"][:], op=ALU.mult))
                dv(lambda v: v.tensor_tensor(out=PWi[:, j + 1, :], in0=W["t1"][:], in1=W["t2"][:], op=ALU.add))
            dv(lambda v: v.tensor_scalar(out=NPWi[:], in0=PWi[:], scalar1=-1.0, scalar2=None, op0=ALU.mult))

            def ucols(d, h, r, a_lo, n):
                if d == 0:
                    s0 = r + 8 * a_lo
                    return zb[:, h, s0:s0 + 8 * (n - 1) + 1:8]
                s0 = NT - 1 - r - 8 * a_lo
                e0 = s0 - 8 * n
                return zb[:, h, s0:e0:-8] if e0 >= 0 else zb[:, h, s0::-8]

            with ExitStack() as es2:
                sb2 = lambda n, s, d=F32: es2.enter_context(nc.sbuf_tensor(self.uniq(n), s, d))
                Bm = [sb2(f"s5_Bm{i}", [128, 2, 128]) for i in range(2)]
                Cm = [sb2(f"s5_Cm{i}", [128, 2, 128]) for i in range(2)]
                nCi = [sb2(f"s5_nCi{i}", [128, 128]) for i in range(2)]
                Bb = sb2("s5_Bb", [128, 2, 128]); tA = sb2("s5_tA", [128, 128]); tB = sb2("s5_tB", [128, 128])
                X1 = sb2("s5_X1", [128, 8, 128]); X2 = sb2("s5_X2", [128, 8, 128]); X3 = X1; X4 = X2
                Pm = [sb2("s5_Pm0", [128, 2, 8, 128])] * 2
                WV = [sb2("s5_WV0", [128, 2, 8, 128], BF16)] * 2
                Vs = sb2("s5_Vs", [128, 2, NA])
                Tt = sb2("s5_Tt", [128, 2, 513]); ti = sb2("s5_ti", [128, 2, 513], I32); tf = sb2("s5_tf", [128, 2, 513])
                TA = sb2("s5_TA", [128, 2, NA]); TH = sb2("s5_TH", [128, 2, NA + 3])
                Wr = sb2("s5_Wr", [128, 2, NA]); w1 = sb2("s5_w1", [128, NA + 3]); w2 = sb2("s5_w2", [128, NA + 3])
                Gm = sb2("s5_Gm", [128, 2, NA + 3]); Hf = sb2("s5_Hf", [128, 2, NA + 3])
                for d in range(2):
                    if d == 0:
                        segs = [(0, 32, 0), (32, 32, 1), (64, 512, 2)]
                    else:
                        segs = [(0, 512, 2), (512, 32, 1), (544, 32, 0)]
                    for pr in range(8):
                        col = d * 8 + pr
                        h = pr // 4
                        q = pr % 2
                        Bk, Ck, Pk, WVk = ("s5_Bm", q), ("s5_Cm", q), ("s5_Pm", 0), ("s5_WV", 0)
                        for ri in range(2):
                            fw.dma(Bm[q][:, ri, :], self.s5_B[l, d, pr, ri], writes=[Bk])
                            fw.dma(Cm[q][:, ri, :], self.s5_C[l, d, pr, ri], writes=[Ck])
                        fr, fi = W["fr"][:, col:col + 1], W["fi"][:, col:col + 1]
                        fw.op(fw.dve, lambda v: v.tensor_scalar(out=tA[:], in0=Bm[q][:, 1, :], scalar1=fi, scalar2=None, op0=ALU.mult), reads=[Bk, K], writes=["s5_tA"])
                        fw.op(fw.dve, lambda v: v.scalar_tensor_tensor(out=Bb[:, 0, :], in0=Bm[q][:, 0, :], scalar=fr, in1=tA[:], op0=ALU.mult, op1=ALU.subtract),
                              reads=[Bk, K, "s5_tA"], writes=["s5_Bb"])
                        fw.op(fw.dve, lambda v: v.tensor_scalar(out=tB[:], in0=Bm[q][:, 0, :], scalar1=fi, scalar2=None, op0=ALU.mult), reads=[Bk, K], writes=["s5_tB"])
                        fw.op(fw.dve, lambda v: v.scalar_tensor_tensor(out=Bb[:, 1, :], in0=Bm[q][:, 1, :], scalar=fr, in1=tB[:], op0=ALU.mult, op1=ALU.add),
                              reads=[Bk, K, "s5_tB"], writes=["s5_Bb"])
                        fw.op(fw.pool, lambda g: g.tensor_scalar(out=nCi[q][:], in0=Cm[q][:, 1, :], scalar1=-1.0, scalar2=None, op0=ALU.mult), reads=[Ck], writes=[("s5_nCi", q)])
                        bc3 = lambda m2: m2.unsqueeze(1).to_broadcast([128, 8, 128])
                        pw3 = lambda tab, j0: tab[:, j0:j0 + 8, col].unsqueeze(2).to_broadcast([128, 8, 128])
                        fw.op(fw.dve, lambda v: v.tensor_tensor(out=X1[:], in0=bc3(Bb[:, 0, :]), in1=pw3(PWr, 0), op=ALU.mult), reads=["s5_Bb", K], writes=["s5_X1"])
                        fw.op(fw.dve, lambda v: v.tensor_tensor(out=X2[:], in0=bc3(Bb[:, 1, :]), in1=pw3(PWi, 0), op=ALU.mult), reads=["s5_Bb", K], writes=["s5_X2"])
                        fw.op(fw.pool, lambda g: g.tensor_tensor(out=Pm[q][:, 0, :, :], in0=X1[:], in1=X2[:], op=ALU.subtract), reads=["s5_X1", "s5_X2"], writes=[Pk])
                        fw.op(fw.dve, lambda v: v.tensor_tensor(out=X3[:], in0=bc3(Bb[:, 1, :]), in1=pw3(PWr, 0), op=ALU.mult), reads=["s5_Bb", K], writes=["s5_X1"])
                        fw.op(fw.dve, lambda v: v.tensor_tensor(out=X4[:], in0=bc3(Bb[:, 0, :]), in1=pw3(PWi, 0), op=ALU.mult), reads=["s5_Bb", K], writes=["s5_X2"])
                        fw.op(fw.pool, lambda g: g.tensor_tensor(out=Pm[q][:, 1, :, :], in0=X3[:], in1=X4[:], op=ALU.add), reads=["s5_X1", "s5_X2"], writes=[Pk])
                        fw.op(fw.dve, lambda v: v.tensor_tensor(out=X1[:], in0=bc3(Cm[q][:, 0, :]), in1=pw3(PWr, 1), op=ALU.mult), reads=[Ck, K, "s5_X1"], writes=["s5_X1"])
                        fw.op(fw.dve, lambda v: v.tensor_tensor(out=X2[:], in0=bc3(Cm[q][:, 1, :]), in1=pw3(PWi, 1), op=ALU.mult), reads=[Ck, K, "s5_X2"], writes=["s5_X2"])
                        fw.op(fw.pool, lambda g: g.tensor_tensor(out=CQ[:, col, 0, :, :], in0=X1[:], in1=X2[:], op=ALU.subtract), reads=["s5_X1", "s5_X2"], writes=["s5_CQ"])
                        fw.op(fw.dve, lambda v: v.tensor_tensor(out=X3[:], in0=bc3(Cm[q][:, 0, :]), in1=pw3(NPWi, 1), op=ALU.mult), reads=[Ck, K, "s5_X1"], writes=["s5_X1"])
                        fw.op(fw.dve, lambda v: v.tensor_tensor(out=X4[:], in0=bc3(Cm[q][:, 1, :]), in1=pw3(PWr, 1), op=ALU.mult), reads=[Ck, K, "s5_X2"], writes=["s5_X2"])
                        fw.op(fw.pool, lambda g: g.tensor_tensor(out=CQ[:, col, 1, :, :], in0=X3[:], in1=X4[:], op=ALU.subtract), reads=["s5_X1", "s5_X2"], writes=["s5_CQ"])
                        for ri in range(2):
                            for half in range(2):
                                ps, pk = self.pb[half], ("pb", half)
                                for jj in range(4):
                                    rp = half * 4 + jj
                                    fw.op(fw.pe, lambda t_: t_.transpose(ps[:, jj * 128:(jj + 1) * 128], Pm[q][:, ri, 7 - rp, :], self.ident[:]),
                                          reads=[Pk, "ident"], writes=[pk], inc=(jj == 3))
                                fw.op(fw.act, lambda a: a.activation(out=WV[q][:, ri, half * 4:(half + 1) * 4, :], in_=ps[:].rearrange("p (j c) -> p j c", j=4), func=AF.Identity),
                                      reads=[pk], writes=[WVk])
                        for j in range(8):
                            ps, pk = self.pb[6 + j // 4], ("pb", 6 + j // 4)
                            first = (pr % 4 == 0) and (j % 4 == 0)
                            last = (pr % 4 == 3) and (j % 4 == 3)
                            fw.op(fw.pe, lambda t_: t_.matmul(ps[:, (j % 4) * 128:(j % 4 + 1) * 128], lhsT=Pm[q][:, 0, j, :], rhs=Cm[q][:, 0, :], start=first, stop=False,
                                                             skip_group_check=True), reads=[Pk, Ck], writes=[pk], inc=False)
                            fw.op(fw.pe, lambda t_: t_.matmul(ps[:, (j % 4) * 128:(j % 4 + 1) * 128], lhsT=Pm[q][:, 1, j, :], rhs=nCi[q][:], start=False, stop=last,
                                                             skip_group_check=True), reads=[Pk, ("s5_nCi", q)], writes=[pk], inc=(j % 4 == 3))
                        if pr % 4 == 3:
                            for hb in range(2):
                                fw.op(fw.act, lambda a: a.activation(out=BD[:, d * 2 + h, hb * 4:(hb + 1) * 4, :], in_=self.pb[6 + hb][:].rearrange("p (j c) -> p j c", j=4),
                                                                      func=AF.Identity), reads=[("pb", 6 + hb)], writes=["s5_BD"])
                        for ri in range(2):
                            for (b0, nb_, bank) in ((0, 512, 2 + 2 * ri), (512, 64, 3 + 2 * ri)):
                                ps, pk = self.pb[bank], ("pb", bank)
                                for rp in range(8):
                                    fw.op(fw.pe, lambda t_: t_.matmul(ps[:, 0:nb_], lhsT=WV[q][:, ri, rp, :], rhs=ucols(d, h, rp, b0, nb_), start=(rp == 0), stop=(rp == 7)),
                                          reads=[WVk, "s5_zb"], writes=[pk], inc=(rp == 7))
                                if ri == 0:
                                    fw.op(fw.act, lambda a: a.copy(out=Vs[:, ri, b0:b0 + nb_], in_=ps[:, 0:nb_]), reads=[pk], writes=["s5_Vs"])
                                else:
                                    fw.op(fw.dve, lambda v: v.tensor_copy(out=Vs[:, ri, b0:b0 + nb_], in_=ps[:, 0:nb_]), reads=[pk], writes=["s5_Vs"])
                        fw.op(fw.dve, lambda v: v.tensor_scalar(out=Tt[:, 1, :], in0=idx[:], scalar1=W["phr"][:, col:col + 1], scalar2=None, op0=ALU.mult), reads=["s5_idx", K], writes=["s5_Tt"])
                        fw.op(fw.dve, lambda v: v.tensor_scalar(out=Tt[:, 0, :], in0=Tt[:, 1, :], scalar1=math.pi / 2, scalar2=None, op0=ALU.add), reads=["s5_Tt"], writes=["s5_Tt"])
                        fw.op(fw.dve, lambda v: v.tensor_scalar(out=ti[:], in0=Tt[:], scalar1=1.0 / TWO_PI, scalar2=None, op0=ALU.mult), reads=["s5_Tt"], writes=["s5_ti"])
                        fw.op(fw.dve, lambda v: v.tensor_copy(out=tf[:], in_=ti[:]), reads=["s5_ti"], writes=["s5_tf"])
                        fw.op(fw.dve, lambda v: v.scalar_tensor_tensor(out=Tt[:], in0=tf[:], scalar=-TWO_PI, in1=Tt[:], op0=ALU.mult, op1=ALU.add), reads=["s5_tf", "s5_Tt"], writes=["s5_Tt"])
                        fw.op(fw.dve, lambda v: v.tensor_scalar(out=Tt[:], in0=Tt[:], scalar1=3.141592, scalar2=-3.141592, op0=ALU.min, op1=ALU.max), reads=["s5_Tt"], writes=["s5_Tt"])
                        fw.op(fw.act, lambda a: a.activation(out=Tt[:], in_=Tt[:], func=AF.Sin), reads=["s5_Tt"], writes=["s5_Tt"])
                        hoff = 0
                        for (a0, A1, sid) in segs:
                            fw.op(fw.act, lambda a: a.copy(out=TA[:, :, a0:a0 + A1], in_=Tt[:, :, 1:A1 + 1]), reads=["s5_Tt"], writes=["s5_TA"])
                            fw.op(fw.pool, lambda g: g.tensor_copy(out=TH[:, :, hoff:hoff + A1 + 1], in_=Tt[:, :, 0:A1 + 1]), reads=["s5_Tt"], writes=["s5_TH"])
                            hoff += A1 + 1
                        fw.op(fw.dve, lambda v: v.tensor_tensor(out=w1[:, 0:NA], in0=Vs[:, 0, :], in1=TA[:, 0, :], op=ALU.mult), reads=["s5_Vs", "s5_TA"], writes=["s5_w1"])
                        fw.op(fw.dve, lambda v: v.tensor_tensor(out=w2[:, 0:NA], in0=Vs[:, 1, :], in1=TA[:, 1, :], op=ALU.mult), reads=["s5_Vs", "s5_TA"], writes=["s5_w2"])
                        fw.op(fw.pool, lambda g: g.tensor_tensor(out=Wr[:, 0, :], in0=w1[:, 0:NA], in1=w2[:, 0:NA], op=ALU.add), reads=["s5_w1", "s5_w2"], writes=["s5_Wr"])
                        fw.op(fw.dve, lambda v: v.tensor_tensor(out=w1[:, 0:NA], in0=Vs[:, 1, :], in1=TA[:, 0, :], op=ALU.mult), reads=["s5_Vs", "s5_TA", "s5_w1"], writes=["s5_w1"])
                        fw.op(fw.dve, lambda v: v.tensor_tensor(out=w2[:, 0:NA], in0=Vs[:, 0, :], in1=TA[:, 1, :], op=ALU.mult), reads=["s5_Vs", "s5_TA", "s5_w2"], writes=["s5_w2"])
                        fw.op(fw.pool, lambda g: g.tensor_tensor(out=Wr[:, 1, :], in0=w1[:, 0:NA], in1=w2[:, 0:NA], op=ALU.subtract), reads=["s5_w1", "s5_w2"], writes=["s5_Wr"])
                        hoff = 0
                        for (a0, A1, sid) in segs:
                            for ri in range(2):
                                if sid == 2:
                                    fw.op(fw.act, lambda a: a.copy(out=Gm[:, ri, hoff:hoff + 1], in_=h0[:, ri, col:col + 1]), reads=["s5_h0"], writes=["s5_Gm"])
                                    init = h0[:, ri, col:col + 1]
                                else:
                                    fw.op(fw.pool, lambda g: g.memset(Gm[:, ri, hoff:hoff + 1], 0.0), writes=["s5_Gm"])
                                    init = 0.0
                                fw.op(fw.dve, lambda v: v.tensor_tensor_scan(out=Gm[:, ri, hoff + 1:hoff + 1 + A1], data0=W["rho8"][:, col:col + 1].to_broadcast([128, A1]),
                                                                               data1=Wr[:, ri, a0:a0 + A1], initial=init, op0=ALU.mult, op1=ALU.add),
                                      reads=["s5_Wr", K, "s5_h0"], writes=["s5_Gm"])
                            hoff += A1 + 1
                        NH_ = NA + 3
                        fw.op(fw.dve, lambda v: v.tensor_tensor(out=w1[:], in0=Gm[:, 0, :], in1=TH[:, 0, :], op=ALU.mult), reads=["s5_Gm", "s5_TH", "s5_w1"], writes=["s5_w1"])
                        fw.op(fw.dve, lambda v: v.tensor_tensor(out=w2[:], in0=Gm[:, 1, :], in1=TH[:, 1, :], op=ALU.mult), reads=["s5_Gm", "s5_TH", "s5_w2"], writes=["s5_w2"])
                        fw.op(fw.pool, lambda g: g.tensor_tensor(out=Hf[:, 0, :], in0=w1[:], in1=w2[:], op=ALU.subtract), reads=["s5_w1", "s5_w2"], writes=["s5_Hf"])
                        fw.op(fw.dve, lambda v: v.tensor_tensor(out=w1[:], in0=Gm[:, 0, :], in1=TH[:, 1, :], op=ALU.mult), reads=["s5_Gm", "s5_TH", "s5_w1"], writes=["s5_w1"])
                        fw.op(fw.dve, lambda v: v.tensor_tensor(out=w2[:], in0=Gm[:, 1, :], in1=TH[:, 0, :], op=ALU.mult), reads=["s5_Gm", "s5_TH", "s5_w2"], writes=["s5_w2"])
                        fw.op(fw.pool, lambda g: g.tensor_tensor(out=Hf[:, 1, :], in0=w1[:], in1=w2[:], op=ALU.add), reads=["s5_w1", "s5_w2"], writes=["s5_Hf"])
                        hoff = 0
                        for (a0, A1, sid) in segs:
                            fw.op(fw.act, lambda a: a.activation(out=HM[:, col, :, a0:a0 + A1], in_=Hf[:, :, hoff:hoff + A1], func=AF.Identity), reads=["s5_Hf"], writes=["s5_HM"])
                            if sid < 2:
                                for ri in range(2):
                                    fw.dma(self.osf[l, sid, ri, d, 2 * pr:2 * pr + 2, :].rearrange("e (n o) -> (e n) o", o=1), Hf[:, ri, hoff + A1:hoff + A1 + 1],
                                           reads=["s5_Hf"], is_output=True)
                            hoff += A1 + 1
            fw.barrier()
            with ExitStack() as es3:
                sb3 = lambda n, s, d=F32: es3.enter_context(nc.sbuf_tensor(self.uniq(n), s, d))
                GW = sb3("s5_GW", [128, 2, 512], BF16); gb = sb3("s5_gbS", [128, 4])
                y16 = [sb3(f"s5_y16{i}", [128, 2, 512], BF16) for i in range(2)]
                sg = [sb3(f"s5_sg{i}", [128, 512]) for i in range(2)]
                yo = [sb3(f"s5_yo{i}", [128, 512], BF16) for i in range(2)]
                self.load_w_bf16(GW, self.s5_gw[l], "s5_GW", 1)
                fw.dma(gb[:], self.s5_gb[l], writes=["s5_gb"])
                for blk in range(NT // 512):
                    c0 = blk * 512
                    yb_ = y16[blk % 2]; ybk = ("s5_y16", blk % 2)
                    for h in range(2):
                        ps, pk = self.pb[h], ("pb", h)
                        first = True
                        for d in range(2):
                            a_lo = blk * 64 if d == 0 else NA - 64 * (blk + 1)
                            for r in range(8):
                                if d == 0:
                                    oap = ps[:, r:r + 8 * 63 + 1:8]
                                else:
                                    s0 = 511 - r
                                    oap = ps[:, s0:(7 - r) - 1:-8] if (7 - r) - 1 >= 0 else ps[:, s0::-8]
                                for r2 in range(r + 1):
                                    fw.op(fw.pe, lambda t_: t_.matmul(oap, lhsT=BD[:, d * 2 + h, r - r2, :], rhs=ucols(d, h, r2, a_lo, 64), start=first, stop=False,
                                                                     skip_group_check=True), reads=["s5_BD", "s5_zb"], writes=[pk], inc=False)
                                    first = False
                                for pr in range(4 * h, 4 * h + 4):
                                    col = d * 8 + pr
                                    for ri in range(2):
                                        lastmm = (d == 1 and r == 7 and pr == 4 * h + 3 and ri == 1)
                                        fw.op(fw.pe, lambda t_: t_.matmul(oap, lhsT=CQ[:, col, ri, r, :], rhs=HM[:, col, ri, a_lo:a_lo + 64], start=False, stop=lastmm,
                                                                         skip_group_check=True), reads=["s5_CQ", "s5_HM"], writes=[pk], inc=lastmm)
                        fw.op(fw.dve, lambda v: v.scalar_tensor_tensor(out=yb_[:, h, :], in0=zb[:, h, c0:c0 + 512], scalar=skp[:, h:h + 1], in1=ps[:], op0=ALU.mult, op1=ALU.add),
                              reads=["s5_zb", "s5_skp", pk], writes=[ybk])
                    for mt in range(2):
                        pa, pak = self.pb[2 + mt], ("pb", 2 + mt)
                        pg, pgk = self.pb[4 + mt], ("pb", 4 + mt)
                        for h in range(2):
                            fw.op(fw.pe, lambda t_: t_.matmul(pa[:], lhsT=GW[:, h, mt * 128:(mt + 1) * 128], rhs=yb_[:, h, :], start=(h == 0), stop=(h == 1)),
                                  reads=["s5_GW", ybk], writes=[pak], inc=(h == 1))
                        for h in range(2):
                            fw.op(fw.pe, lambda t_: t_.matmul(pg[:], lhsT=GW[:, h, (2 + mt) * 128:(3 + mt) * 128], rhs=yb_[:, h, :], start=(h == 0), stop=(h == 1)),
                                  reads=["s5_GW", ybk], writes=[pgk], inc=(h == 1))
                        fw.op(fw.act, lambda a: a.activation(out=sg[mt][:], in_=pg[:], func=AF.Sigmoid, bias=gb[:, 2 + mt:3 + mt], scale=1.0), reads=[pgk, "s5_gb"], writes=[("s5_sg", mt)])
                        fw.op(fw.dve, lambda v: v.scalar_tensor_tensor(out=yo[mt][:], in0=pa[:], scalar=gb[:, mt:mt + 1], in1=sg[mt][:], op0=ALU.add, op1=ALU.mult),
                              reads=[pak, "s5_gb", ("s5_sg", mt)], writes=[("s5_yo", mt)])
                        fw.dma(self.ybT[mt * 128:(mt + 1) * 128, c0:c0 + 512], yo[mt][:], reads=[("s5_yo", mt)], writes=[("ybT", mt)])

    def phase_C1(self, l):
        nc, fw = self.nc, self.fw
        with ExitStack() as es:
            sb = lambda n, s, d=F32: es.enter_context(nc.sbuf_tensor(self.uniq(n), s, d))
            PA = sb("PA", [128, 2, D], BF16)
            PB = sb("PB", [128, 2, D], BF16)
            PC = sb("PC", [128, 4, D], BF16)
            WO = sb("WO", [128, 8, D], BF16)
            xt2 = [sb(f"xtC{i}", [128, 8, TT]) for i in range(2)]
            ya2 = [sb(f"yaC{i}", [128, 2, TT], BF16) for i in range(2)]
            yb2 = [sb(f"ybC{i}", [128, 2, TT], BF16) for i in range(2)]
            yc2 = [sb(f"ycC{i}", [128, 4, TT], BF16) for i in range(2)]
            gt2 = [sb(f"gtC{i}", [128, 24, TT], BF16) for i in range(2)]
            mg = sb("mgC", [128, 8, TT], BF16)
            t1 = [sb(f"t1C{i}", [128, TT]) for i in range(2)]
            t2 = [sb(f"t2C{i}", [128, TT]) for i in range(2)]
            t3 = [sb(f"t3C{i}", [128, TT]) for i in range(2)]
            self.load_w_bf16(PA, self.proj_a[l], "PA", 1)
            self.load_w_bf16(PB, self.proj_b[l], "PB", 1)
            self.load_w_bf16(PC, self.proj_c[l], "PC", 1)
            self.load_w_bf16(WO, self.w_out[l], "WO", 1)
            def loadC1(tt):
                bi = tt % 2
                c0 = tt * TT
                for k in range(8):
                    fw.dma(xt2[bi][:, k, :], self.xT[k * 128:(k + 1) * 128, c0:c0 + TT], reads=[("xT", k)], writes=[f"xt{bi}"])
                for k in range(2):
                    fw.dma(ya2[bi][:, k, :], self.yaT[k * 128:(k + 1) * 128, c0:c0 + TT], reads=[("yaT", k)], writes=[f"ya{bi}"])
                    fw.dma(yb2[bi][:, k, :], self.ybT[k * 128:(k + 1) * 128, c0:c0 + TT], reads=[("ybT", k)], writes=[f"yb{bi}"])
                for k in range(4):
                    fw.dma(yc2[bi][:, k, :], self.ycT[k * 128:(k + 1) * 128, c0:c0 + TT], reads=[("ycT", k)], writes=[f"yc{bi}"])
                for m in range(24):
                    fw.dma(gt2[bi][:, m, :], self.gT[m * 128:(m + 1) * 128, c0:c0 + TT], reads=[("gT", m)], writes=[f"gt{bi}"])

            loadC1(0)
            for tt in range(NTILE):
                c0 = tt * TT
                j = 0 if tt == 0 else 1
                if tt + 1 < NTILE:
                    loadC1(tt + 1)
                bi = tt % 2
                xt, ya, yb, yc, gt = xt2[bi], ya2[bi], yb2[bi], yc2[bi], gt2[bi]
                kxt, kya, kyb, kyc, kgt = f"xt{bi}", f"ya{bi}", f"yb{bi}", f"yc{bi}", f"gt{bi}"
                for mf in range(8):
                    par = mf % 2
                    pa, pb_, pc = self.pb[par * 3 + 0], self.pb[par * 3 + 1], self.pb[par * 3 + 2]
                    ka, kb, kc = ("pb", par * 3), ("pb", par * 3 + 1), ("pb", par * 3 + 2)
                    cs = slice(mf * 128, (mf + 1) * 128)
                    for k in range(2):
                        fw.op(fw.pe, lambda t, k=k: t.matmul(pa[:], lhsT=PA[:, k, cs], rhs=ya[:, k, :], start=(k == 0), stop=(k == 1)),
                              reads=["PA", kya], writes=[ka], inc=(k == 1))
                    for k in range(2):
                        fw.op(fw.pe, lambda t, k=k: t.matmul(pb_[:], lhsT=PB[:, k, cs], rhs=yb[:, k, :], start=(k == 0), stop=(k == 1)),
                              reads=["PB", kyb], writes=[kb], inc=(k == 1))
                    for k in range(4):
                        fw.op(fw.pe, lambda t, k=k: t.matmul(pc[:], lhsT=PC[:, k, cs], rhs=yc[:, k, :], start=(k == 0), stop=(k == 3)),
                              reads=["PC", kyc], writes=[kc], inc=(k == 3))
                    a_, b_, c_ = t1[par], t2[par], t3[par]
                    fw.op(fw.dve, lambda v: v.tensor_tensor(out=a_[:], in0=pa[:], in1=gt[:, mf, :], op=ALU.mult), reads=[ka, kgt], writes=[("t1", par)])
                    fw.op(fw.dve, lambda v: v.tensor_tensor(out=b_[:], in0=pb_[:], in1=gt[:, 8 + mf, :], op=ALU.mult), reads=[kb, kgt], writes=[("t2", par)])
                    fw.op(fw.dve, lambda v: v.tensor_tensor(out=c_[:], in0=pc[:], in1=gt[:, 16 + mf, :], op=ALU.mult), reads=[kc, kgt], writes=[("t3", par)])
                    fw.op(fw.pool, lambda g: g.tensor_tensor(out=a_[:], in0=a_[:], in1=b_[:], op=ALU.add), reads=[("t1", par), ("t2", par)], writes=[("t1", par)])
                    fw.op(fw.pool, lambda g: g.tensor_tensor(out=mg[:, mf, :], in0=a_[:], in1=c_[:], op=ALU.add), reads=[("t1", par), ("t3", par)], writes=["mg"])
                for mf in range(8):
                    ps, pk = self.pb[6 + mf % 2], ("pb", 6 + mf % 2)
                    for k in range(8):
                        fw.op(fw.pe, lambda t, k=k: t.matmul(ps[:], lhsT=WO[:, k, mf * 128:(mf + 1) * 128], rhs=mg[:, k, :], start=(k == 0), stop=(k == 7)),
                              reads=["WO", "mg"], writes=[pk], inc=(k == 7))
                    fw.op(fw.dve, lambda v: v.scalar_tensor_tensor(out=xt[:, mf, :], in0=ps[:], scalar=self.modT[:, 16 + mf, j:j + 1], in1=xt[:, mf, :],
                                                                   op0=ALU.mult, op1=ALU.add), reads=[pk, "modT", kxt], writes=[kxt])
                for k in range(8):
                    fw.dma(self.xT[k * 128:(k + 1) * 128, c0:c0 + TT], xt[:, k, :], reads=[kxt], writes=[("xT", k)])

    def phase_C2(self, l):
        nc, fw = self.nc, self.fw
        with ExitStack() as es:
            sb = lambda n, s, d=F32: es.enter_context(nc.sbuf_tensor(self.uniq(n), s, d))
            WG = sb("WG", [128, 8, FF], BF16)
            WU = sb("WU", [128, 8, FF], BF16)
            xtb = [sb(f"xtD{i}", [128, 8, TT]) for i in range(2)]
            xsqb = [sb(f"xsqD{i}", [128, 8, TT]) for i in range(2)]
            rinb = [sb(f"rinD{i}", [128, TT]) for i in range(2)]
            hTb = [sb(f"hTD{i}", [128, 8, TT], BF16) for i in range(2)]

            def prep(tt):
                bi = tt % 2
                for k in range(8):
                    fw.dma(xtb[bi][:, k, :], self.xT[k * 128:(k + 1) * 128, tt * TT:(tt + 1) * TT], reads=[("xT", k)], writes=[f"xt{bi}"])
                self.norm_mod(es, xtb[bi], self.a2, 24, 0 if tt == 0 else 1, hTb[bi], "D", bufs=(xsqb[bi], rinb[bi]), sfx=str(bi))
            sl = [sb(f"slD{i}", [128, TT]) for i in range(2)]
            ao = [sb(f"aoD{i}", [128, TT], BF16) for i in range(3)]
            self.load_w_bf16(WG, self.w_gate[l], "WG", 2)
            self.load_w_bf16(WU, self.w_up[l], "WU", 2)
            prep(0)
            for tt in range(NTILE):
                c0 = tt * TT
                j = 0 if tt == 0 else 1
                if tt + 1 < NTILE:
                    prep(tt + 1)
                hT = hTb[tt % 2]
                hk = f"hT{tt % 2}"
                for jj in range(NJ):
                    par = jj % 3
                    pg, pu = self.pb[par * 2], self.pb[par * 2 + 1]
                    kg, ku = ("pb", par * 2), ("pb", par * 2 + 1)
                    cs = slice(jj * 128, (jj + 1) * 128)
                    for k in range(8):
                        fw.op(fw.pe, lambda t, k=k: t.matmul(pg[:], lhsT=WG[:, k, cs], rhs=hT[:, k, :], start=(k == 0), stop=(k == 7)),
                              reads=["WG", hk], writes=[kg], inc=(k == 7))
                    for k in range(8):
                        fw.op(fw.pe, lambda t, k=k: t.matmul(pu[:], lhsT=WU[:, k, cs], rhs=hT[:, k, :], start=(k == 0), stop=(k == 7)),
                              reads=["WU", hk], writes=[ku], inc=(k == 7))
                    s_ = sl[jj % 2]; sk = ("sl", jj % 2)
                    a_ = ao[jj % 3]; ak = ("ao", jj % 3)
                    fw.op(fw.act, lambda a: a.activation(out=s_[:], in_=pg[:], func=AF.Silu), reads=[kg], writes=[sk])
                    fw.op(fw.dve, lambda v: v.tensor_tensor(out=a_[:], in0=pu[:], in1=s_[:], op=ALU.mult), reads=[ku, sk], writes=[ak])
                    fw.dma(self.aT[jj * 128:(jj + 1) * 128, c0:c0 + TT], a_[:], reads=[ak], writes=[("aT", jj)])

    def phase_C3(self, l):
        nc, fw = self.nc, self.fw
        with ExitStack() as es:
            sb = lambda n, s, d=F32: es.enter_context(nc.sbuf_tensor(self.uniq(n), s, d))
            WD = sb("WD", [128, NJ, D], BF16)
            xt2 = [sb(f"xtE{i}", [128, 8, TT]) for i in range(2)]
            at2 = [sb(f"atE{i}", [128, NJ, TT], BF16) for i in range(2)]
            self.load_w_bf16(WD, self.w_down[l], "WD", 1)

            def loadC3(tt):
                bi = tt % 2
                c0 = tt * TT
                for k in range(8):
                    fw.dma(xt2[bi][:, k, :], self.xT[k * 128:(k + 1) * 128, c0:c0 + TT], reads=[("xT", k)], writes=[f"xt{bi}"])
                for jj in range(NJ):
                    fw.dma(at2[bi][:, jj, :], self.aT[jj * 128:(jj + 1) * 128, c0:c0 + TT], reads=[("aT", jj)], writes=[f"at{bi}"])

            loadC3(0)
            for tt in range(NTILE):
                c0 = tt * TT
                j = 0 if tt == 0 else 1
                if tt + 1 < NTILE:
                    loadC3(tt + 1)
                bi = tt % 2
                xt, at = xt2[bi], at2[bi]
                kxt, kat = f"xt{bi}", f"at{bi}"
                for mf in range(8):
                    ps, pk = self.pb[mf % 4], ("pb", mf % 4)
                    for jj in range(NJ):
                        fw.op(fw.pe, lambda t, jj=jj: t.matmul(ps[:], lhsT=WD[:, jj, mf * 128:(mf + 1) * 128], rhs=at[:, jj, :], start=(jj == 0), stop=(jj == NJ - 1)),
                              reads=["WD", kat], writes=[pk], inc=(jj == NJ - 1))
                    fw.op(fw.dve, lambda v: v.scalar_tensor_tensor(out=xt[:, mf, :], in0=ps[:], scalar=self.modT[:, 40 + mf, j:j + 1], in1=xt[:, mf, :],
                                                                   op0=ALU.mult, op1=ALU.add), reads=[pk, "modT", kxt], writes=[kxt])
                for k in range(8):
                    fw.dma(self.xT[k * 128:(k + 1) * 128, c0:c0 + TT], xt[:, k, :], reads=[kxt], writes=[("xT", k)])

    def phase_final(self):
        nc, fw = self.nc, self.fw
        with ExitStack() as es:
            sb = lambda n, s, d=F32: es.enter_context(nc.sbuf_tensor(self.uniq(n), s, d))
            xt = sb("xtF", [128, 8, TT])
            self._xsq = sb("xsqF", [128, 8, TT])
            self._rin = sb("rinF", [128, TT])
            hT = sb("hTF", [128, 8, TT])
            gf = sb("gfin", [128, 8])
            fw.dma(gf[:], self.final_g[:, :], writes=["gfin"])
            for tt in range(NTILE):
                c0 = tt * TT
                for k in range(8):
                    fw.dma(xt[:, k, :], self.xT[k * 128:(k + 1) * 128, c0:c0 + TT], reads=[("xT", k)], writes=["xt"])
                self.norm_mod(es, xt, gf, None, 0, hT, "F")
                for k in range(8):
                    fw.dma(self.yT[k * 128:(k + 1) * 128, c0:c0 + TT], hT[:, k, :], reads=["hT"], is_output=True)


_SHARED_KEYS_PER_CORE = ("xT_in", "cT", "cache_k", "cache_v", "s5_h0")


def host_inputs(inputs, core, depth=DEPTH, shared=None):
    if shared is not None and shared.get("ready"):
        return _host_core_only(inputs, core, depth, shared)
    m = _host_inputs_full(inputs, core, depth)
    if shared is not None:
        shared.update({k: v for k, v in m.items() if k not in _SHARED_KEYS_PER_CORE})
        shared["ready"] = True
    return m


def _host_core_only(inputs, core, depth, shared):
    f = lambda a: np.ascontiguousarray(np.asarray(a, dtype=np.float32))
    xp = np.asarray(inputs["x_prompt"])[2 * core:2 * core + 2].reshape(NP, D)
    b = core % 2
    xs = np.asarray(inputs["x_sample"])[b]
    xT = np.concatenate([xp, xs], 0).T
    cT = np.stack([np.asarray(inputs["c_ctx"]), np.asarray(inputs["c"])[b]], 1)

    def pair(a):
        return a.reshape(depth, 2, 8, 2, 64).transpose(0, 3, 4, 1, 2).reshape(depth, 128, 16)
    m = {k: v for k, v in shared.items() if k != "ready"}
    m["xT_in"] = f(xT)
    m["cT"] = f(cT.reshape(8, 128, 2).transpose(1, 0, 2))
    m["cache_k"] = f(np.asarray(inputs["cache_k"])[b, :depth].reshape(depth, 512, 128))
    m["cache_v"] = f(np.asarray(inputs["cache_v"])[b, :depth].reshape(depth, 512, 128))
    m["s5_h0"] = f(np.stack([pair(np.asarray(inputs["state_s5_re"])[b, :depth]), pair(np.asarray(inputs["state_s5_im"])[b, :depth])], 2))
    return m


def _host_inputs_full(inputs, core, depth=DEPTH):
    f = lambda a: np.ascontiguousarray(np.asarray(a, dtype=np.float32))
    xp = np.asarray(inputs["x_prompt"])[2 * core:2 * core + 2].reshape(NP, D)
    b = core % 2
    xs = np.asarray(inputs["x_sample"])[b]
    xT = np.concatenate([xp, xs], 0).T
    cT = np.stack([np.asarray(inputs["c_ctx"]), np.asarray(inputs["c"])[b]], 1)
    perm = _perm_cols()
    w_in = np.asarray(inputs["w_in"])[:depth]
    w_ext = np.concatenate([w_in, w_in[:, :, perm]], axis=2)
    cos2, sin2 = _rope_tables()
    def pair(a):
        return a.reshape(depth, 2, 8, 2, 64).transpose(0, 3, 4, 1, 2).reshape(depth, 128, 16)
    lre = pair(np.asarray(inputs["s5_lam_re"])[:depth]); lim = pair(np.asarray(inputs["s5_lam_im"])[:depth])
    lst = pair(np.broadcast_to(np.asarray(inputs["s5_log_step"])[:depth][..., None], (depth, 2, 16, 64)))
    s5_sc = np.stack([lre, lim, lst], 2)
    s5_h0 = np.stack([pair(np.asarray(inputs["state_s5_re"])[b, :depth]), pair(np.asarray(inputs["state_s5_im"])[b, :depth])], 2)
    s5_B = np.zeros((depth, 2, 8, 2, 128, 128), np.float32); s5_C = np.zeros((depth, 2, 8, 2, 128, 128), np.float32)
    bre = np.asarray(inputs["s5_b_re"])[:depth]; bim = np.asarray(inputs["s5_b_im"])[:depth]
    cre = np.asarray(inputs["s5_c_re"])[:depth]; cim = np.asarray(inputs["s5_c_im"])[:depth]
    for g_ in range(16):
        pr_, e_, gl_ = g_ // 2, g_ % 2, g_ % 8
        s5_B[:, :, pr_, 0, e_ * 64:(e_ + 1) * 64, gl_ * 16:(gl_ + 1) * 16] = bre[:, :, g_]
        s5_B[:, :, pr_, 1, e_ * 64:(e_ + 1) * 64, gl_ * 16:(gl_ + 1) * 16] = bim[:, :, g_]
        s5_C[:, :, pr_, 0, e_ * 64:(e_ + 1) * 64, gl_ * 16:(gl_ + 1) * 16] = cre[:, :, g_].transpose(0, 1, 3, 2)
        s5_C[:, :, pr_, 1, e_ * 64:(e_ + 1) * 64, gl_ * 16:(gl_ + 1) * 16] = cim[:, :, g_].transpose(0, 1, 3, 2)
    pk = lambda a: a.reshape(a.shape[0], -1, 128).transpose(0, 2, 1)
    import ml_dtypes
    bf = lambda a: np.ascontiguousarray(np.asarray(a, dtype=np.float32).astype(ml_dtypes.bfloat16))

    def zp(a, r, c):
        o = np.zeros(a.shape[:-2] + (r, c), np.float32)
        o[..., :a.shape[-2], :a.shape[-1]] = a
        return o
    m = {
        "xT_in": f(xT), "cT": f(cT.reshape(8, 128, 2).transpose(1, 0, 2)),
        "ada_w": f(np.asarray(inputs["ada_w"])[:depth]), "ada_b": f(pk(np.asarray(inputs["ada_b"])[:depth])),
        "norm1_g": f(pk(np.asarray(inputs["norm1_g"])[:depth])), "norm2_g": f(pk(np.asarray(inputs["norm2_g"])[:depth])),
        "final_g": f(pk(np.asarray(inputs["final_norm_g"])[None])[0]),
        "w_in": f(w_ext),
        "proj_a": f(np.asarray(inputs["proj_a"])[:depth]), "proj_b": f(np.asarray(inputs["proj_b"])[:depth]),
        "proj_c": f(np.asarray(inputs["proj_c"])[:depth]), "w_out": f(np.asarray(inputs["w_out"])[:depth]),
        "w_gate": f(np.asarray(inputs["ffn_w_gate"])[:depth]), "w_up": f(np.asarray(inputs["ffn_w_up"])[:depth]),
        "w_down": f(np.asarray(inputs["ffn_w_down"])[:depth]),
        "rope_cos": cos2, "rope_sin": sin2,
        "hy_GB": bf(_hy_gb()),
        "hy_T256": bf(_hy_t256()),
        "hy_w1": f(zp(np.asarray(inputs["hy_pos_w1"])[:depth], 128, 128)),
        "hy_w2": f(zp(np.asarray(inputs["hy_pos_w2"])[:depth], 128, 128)),
        "hy_w3": f(zp(np.asarray(inputs["hy_pos_w3"])[:depth], 128, 1024)),
        "hy_vec": f(zp(np.stack([np.asarray(inputs["hy_pos_b1"])[:depth], np.asarray(inputs["hy_pos_b2"])[:depth],
                                 np.asarray(inputs["hy_sin_freq"])[:depth]], -1), 128, 3)),
        "hy_decay": f(np.asarray(inputs["hy_decay"])[:depth]),
        "hy_cw": f(np.concatenate([np.asarray(inputs["hy_conv_w"])[:depth].transpose(0, 2, 1),
                                   np.asarray(inputs["hy_conv_b"])[:depth][:, :, None]], -1).reshape(depth, 6, 128, 4).transpose(0, 2, 1, 3)),
        "hy_skip": f(np.asarray(inputs["hy_skip"])[:depth].reshape(depth, 2, 2, 128).transpose(0, 3, 1, 2)),
        "s5_sc": f(s5_sc), "s5_B": f(s5_B), "s5_C": f(s5_C), "s5_h0": f(s5_h0),
        "s5_skip": f(np.asarray(inputs["s5_skip"])[:depth].reshape(depth, 2, 128).transpose(0, 2, 1)),
        "s5_gw": f(np.asarray(inputs["s5_glu_w"])[:depth]),
        "s5_gb": f(np.asarray(inputs["s5_glu_b"])[:depth].reshape(depth, 4, 128).transpose(0, 2, 1)),
        "s5_idx": f(np.broadcast_to(np.arange(513, dtype=np.float32), (128, 513))),
        "cache_k": f(np.asarray(inputs["cache_k"])[b, :depth].reshape(depth, 512, 128)),
        "cache_v": f(np.asarray(inputs["cache_v"])[b, :depth].reshape(depth, 512, 128)),
        "sink": f(np.asarray(inputs["attn_sink"])[:depth]),
        "mask_prev": f(np.tril(np.ones((128, 128)))),
        "mask_next": f(np.triu(np.ones((128, 128)))),
    }
    for L_ in (256, 4096):
        FA, FAi = _hy_tables(L_)
        fT, negt = _hy_feat(L_)
        m[f"hy_FA{L_}"] = bf(FA); m[f"hy_FAi{L_}"] = bf(FAi)
        m[f"hy_featT{L_}"] = f(fT); m[f"hy_negt{L_}"] = f(negt)
    return m


_PROG = {}


def get_prog(depth=DEPTH, mixers=("attn", "hyena", "s5"), taps=()):
    key = (depth, tuple(mixers), tuple(taps))
    if key not in _PROG:
        _PROG[key] = Prog(depth, mixers, taps)
    return _PROG[key]


def kernel(**inputs):
    prog = get_prog()
    in_maps = []
    shared = {}
    for c in range(8):
        m = host_inputs(inputs, c, DEPTH, shared)
        in_maps.append({k: m[k] for k in prog.din})
    res = run_bass_kernel_spmd(prog.nc, in_maps, core_ids=list(range(8)))
    R = res.results
    B, S = 16, 256
    y_prompt = np.zeros((B, S, D), np.float32)
    y_sample = np.zeros((2, LS, D), np.float32)
    nk_ = np.zeros((B, DEPTH, S, 2, 64), np.float32)
    nv_ = np.zeros((B, DEPTH, S, 2, 64), np.float32)
    sre = np.zeros((B, DEPTH, 2, 16, 64), np.float32)
    sim = np.zeros((B, DEPTH, 2, 16, 64), np.float32)
    for c in range(8):
        yT = R[c]["yT"]
        y_prompt[2 * c:2 * c + 2] = yT[:, :NP].T.reshape(2, S, D)
        if c < 2:
            y_sample[c] = yT[:, NP:].T
        ok = R[c]["ok"].reshape(DEPTH, 2, S, 2, 64)
        ov = R[c]["ov"].reshape(DEPTH, 2, S, 2, 64)
        nk_[2 * c:2 * c + 2] = ok.transpose(1, 0, 2, 3, 4)
        nv_[2 * c:2 * c + 2] = ov.transpose(1, 0, 2, 3, 4)
        osf = R[c]["osf"]
        sre[2 * c:2 * c + 2] = osf[:, :, 0].transpose(1, 0, 2, 3, 4)
        sim[2 * c:2 * c + 2] = osf[:, :, 1].transpose(1, 0, 2, 3, 4)
    return (y_prompt, y_sample, nk_, nv_, sre, sim)
```
